# Optimizing a Trainium2 kernel written in Bass

```python
import jax, jax.numpy as jnp
from jax import lax
import numpy as np

D_MODEL = 1024
BATCH = 4
SEQ = 4096
DEPTH = 4

N_MIXERS = 3
N_LAYERS_A = (DEPTH + 2) // 3
N_LAYERS_B = (DEPTH + 1) // 3
N_LAYERS_C = DEPTH // 3
RMS_EPS = 1e-6

LRU_WIDTH = D_MODEL
LRU_HEADS = 4
LRU_BLOCK = LRU_WIDTH // LRU_HEADS
LRU_CONV = 4
LRU_C = 8.0

ATT_HEAD_DIM = 64
ATT_HEADS = D_MODEL // ATT_HEAD_DIM
DILATED_GROUPS = ((128, 1), (512, 4), (2048, 16))
N_GROUPS = 3
BAND = 128
ROPE_THETA = 10000.0
NEG_INF = -1e30

RWKV_HEAD = 64
RWKV_HEADS = D_MODEL // RWKV_HEAD
DECAY_LORA = 64
AAA_LORA = 64
GATE_LORA = 128
GN_EPS = 64e-5

FFN_DIM = 2816
FFN_CONV = 3
PLE_DIM = 256

kernel_name = "hybrid_rglru_dilated_rwkv7_trunk"


def rmsnorm(x, g):
    xf = x.astype(jnp.float32)
    y = xf * lax.rsqrt(jnp.mean(xf * xf, axis=-1, keepdims=True) + RMS_EPS)
    return (y * g.astype(jnp.float32)).astype(x.dtype)


def causal_dwconv(x, w, b):
    K, C = w.shape
    y = lax.conv_general_dilated(
        x, w[:, None, :].astype(x.dtype), window_strides=(1,), padding=((K - 1, 0),),
        dimension_numbers=('NWC', 'WIO', 'NWC'), feature_group_count=C)
    return y + b.astype(x.dtype)


def rotary(x, cos, sin):
    x1, x2 = jnp.split(x.astype(jnp.float32), 2, axis=-1)
    return jnp.concatenate([x1 * cos - x2 * sin, x2 * cos + x1 * sin], axis=-1).astype(x.dtype)


def rglru_mixer(x, w_in, conv_w, conv_b, gate_w, gate_b, lam, w_out):
    B, S, _ = x.shape
    f32 = jnp.float32
    y_gate, xr = jnp.split(x @ w_in, 2, axis=-1)
    y_gate = jax.nn.gelu(y_gate)
    xr = causal_dwconv(xr, conv_w, conv_b)
    xh = xr.astype(f32).reshape(B, S, LRU_HEADS, LRU_BLOCK)
    gates = jnp.einsum('bshi,hio->bsho', xh, gate_w.astype(f32)) + gate_b.astype(f32)
    r = jax.nn.sigmoid(gates[..., :LRU_BLOCK]).reshape(B, S, LRU_WIDTH)
    i = jax.nn.sigmoid(gates[..., LRU_BLOCK:]).reshape(B, S, LRU_WIDTH)
    log_a = -LRU_C * r * jax.nn.softplus(-lam.astype(f32))
    a = jnp.exp(log_a)
    bterm = jnp.sqrt(-jnp.expm1(2.0 * log_a)) * (i * xh.reshape(B, S, LRU_WIDTH))

    def combine(c1, c2):
        a1, b1 = c1
        a2, b2 = c2
        return a1 * a2, a2 * b1 + b2

    _, h = lax.associative_scan(combine, (a, bterm), axis=1)
    return (h.astype(x.dtype) * y_gate) @ w_out


def dilated_window_attention(q, k, v, window, dilation):
    B, S, H, Dh = q.shape
    f32 = jnp.float32
    L = S // dilation
    reach = window // dilation
    nblk = -(-L // BAND)
    Lp = nblk * BAND

    def strided(t):
        return t.reshape(B, L, dilation, H, Dh).transpose(0, 2, 1, 3, 4)

    qs = jnp.pad(strided(q), ((0, 0), (0, 0), (0, Lp - L), (0, 0), (0, 0)))
    ks = jnp.pad(strided(k), ((0, 0), (0, 0), (BAND, Lp - L), (0, 0), (0, 0)))
    vs = jnp.pad(strided(v), ((0, 0), (0, 0), (BAND, Lp - L), (0, 0), (0, 0)))
    scale = Dh ** -0.5
    qpos = jnp.arange(BAND)
    kpos = jnp.arange(2 * BAND)
    dist = BAND + qpos[:, None] - kpos[None, :]
    band = (dist >= 0) & (dist <= reach)

    def block(n):
        s0 = n * BAND
        qb = lax.dynamic_slice_in_dim(qs, s0, BAND, axis=2)
        kb = lax.dynamic_slice_in_dim(ks, s0, 2 * BAND, axis=2)
        vb = lax.dynamic_slice_in_dim(vs, s0, 2 * BAND, axis=2)
        logits = jnp.einsum('brqhd,brkhd->brhqk', qb, kb, preferred_element_type=f32) * scale
        valid = band & ((s0 - BAND + kpos) >= 0)[None, :]
        logits = jnp.where(valid, logits, NEG_INF)
        lse = jax.nn.logsumexp(logits, axis=-1)
        probs = jnp.exp(logits - lse[..., None])
        o = jnp.einsum('brhqk,brkhd->brqhd', probs, vb.astype(f32))
        return o, lse

    o, lse = lax.map(block, jnp.arange(nblk))
    o = o.transpose(1, 2, 0, 3, 4, 5).reshape(B, dilation, Lp, H, Dh)[:, :, :L]
    o = o.transpose(0, 2, 1, 3, 4).reshape(B, S, H, Dh)
    lse = lse.transpose(1, 2, 0, 4, 3).reshape(B, dilation, Lp, H)[:, :, :L]
    lse = lse.transpose(0, 2, 1, 3).reshape(B, S, H)
    return o, lse


def dilated_attention_mixer(x, cos, sin, w_qkv, w_out):
    B, S, D = x.shape
    H, Dh = ATT_HEADS, ATT_HEAD_DIM
    qk_cols = N_GROUPS * H * Dh
    qkv = x @ w_qkv
    q = qkv[..., :qk_cols].reshape(B, S, N_GROUPS, H, Dh)
    k = qkv[..., qk_cols:2 * qk_cols].reshape(B, S, N_GROUPS, H, Dh)
    v = qkv[..., 2 * qk_cols:].reshape(B, S, H, Dh)
    c = cos[:, :, None, None, :]
    s = sin[:, :, None, None, :]
    q = rotary(q, c, s)
    k = rotary(k, c, s)
    outs, lses = [], []
    for g, (window, dilation) in enumerate(DILATED_GROUPS):
        o_g, lse_g = dilated_window_attention(q[:, :, g], k[:, :, g], v, window, dilation)
        outs.append(o_g)
        lses.append(lse_g)
    wts = jax.nn.softmax(jnp.stack(lses, axis=0), axis=0)
    o = jnp.einsum('gbsh,gbshd->bshd', wts, jnp.stack(outs, axis=0))
    return o.reshape(B, S, H * Dh).astype(x.dtype) @ w_out


def rwkv7_mixer(x, mu, w_rkv, w0, w1, w2, a0, a1, a2, g1, g2, k_k, k_a, r_k, ln_w, ln_b, w_out):
    B, S, D = x.shape
    H, N = RWKV_HEADS, RWKV_HEAD
    f32 = jnp.float32
    xx = jnp.pad(x, ((0, 0), (1, 0), (0, 0)))[:, :-1] - x
    xmix = x[None] + xx[None] * mu[:, None, None, :].astype(x.dtype)
    rkv = jnp.einsum('cbsd,cde->cbse', xmix[:3], w_rkv)
    r, k, v = rkv[0], rkv[1], rkv[2]
    xw, xa, xg = xmix[3], xmix[4], xmix[5]
    w = -jax.nn.softplus(-(w0 + jnp.tanh(xw @ w1) @ w2).astype(f32)) - 0.5
    decay = jnp.exp(-jnp.exp(w)).reshape(B, S, H, N)
    a = jax.nn.sigmoid((a0 + (xa @ a1) @ a2).astype(f32))
    g = jax.nn.sigmoid(xg @ g1) @ g2
    kk = (k.astype(f32) * k_k.astype(f32)).reshape(B, S, H, N)
    kk = kk / jnp.maximum(jnp.sqrt(jnp.sum(kk * kk, axis=-1, keepdims=True)), 1e-12)
    a_h = a.reshape(B, S, H, N)
    k_h = (k.astype(f32) * (1.0 + (a - 1.0) * k_a.astype(f32))).reshape(B, S, H, N)
    r_h = r.astype(f32).reshape(B, S, H, N)
    v_h = v.astype(f32).reshape(B, S, H, N)
    st_a = -kk
    st_b = kk * a_h

    def step(state, inp):
        r_t, w_t, k_t, v_t, a_t, b_t = inp
        sa = jnp.einsum('bhij,bhj->bhi', state, a_t)
        state = state * w_t[:, :, None, :] + sa[..., None] * b_t[:, :, None, :] + v_t[..., None] * k_t[:, :, None, :]
        y = jnp.einsum('bhij,bhj->bhi', state, r_t)
        return state, y

    tm = lambda t: jnp.swapaxes(t, 0, 1)
    xs = (tm(r_h), tm(decay), tm(k_h), tm(v_h), tm(st_a), tm(st_b))
    _, y = lax.scan(step, jnp.zeros((B, H, N, N), f32), xs)
    y = tm(y)
    mean = jnp.mean(y, axis=-1, keepdims=True)
    var = jnp.mean(jnp.square(y - mean), axis=-1, keepdims=True)
    y = ((y - mean) * lax.rsqrt(var + GN_EPS)).reshape(B, S, D) * ln_w.astype(f32) + ln_b.astype(f32)
    bonus = jnp.sum(r_h * k_h * r_k.astype(f32), axis=-1, keepdims=True) * v_h
    y = y + bonus.reshape(B, S, D)
    return (y.astype(x.dtype) * g) @ w_out


def conv_glu_ffn(x, w_up, conv_w, conv_b, w_down):
    u = causal_dwconv(x @ w_up, conv_w, conv_b)
    gate, up = jnp.split(u, 2, axis=-1)
    return (jax.nn.silu(gate) * up) @ w_down


def setup_inputs(seed: int = 0) -> dict:
    key = jax.random.key(seed)
    ks = iter(jax.random.split(key, 64))
    f32 = jnp.float32
    D = D_MODEL

    def nrm(shape, scale):
        return jax.random.normal(next(ks), shape, f32) * scale

    def gain(shape):
        return 1.0 + nrm(shape, 0.05)

    x = nrm((BATCH, SEQ, D), 1.0)
    p = nrm((DEPTH, BATCH, SEQ, PLE_DIM), 1.0)
    offset = jax.random.randint(next(ks), (BATCH, 1), 0, 1024, dtype=jnp.int32)
    positions = offset + jnp.arange(SEQ, dtype=jnp.int32)[None, :]

    u = jax.random.uniform(next(ks), (N_LAYERS_A, LRU_WIDTH), f32, 0.9, 0.999)
    a_base = u ** (1.0 / LRU_C)
    return {
        "x": x, "p": p, "positions": positions,
        "norm_mix": gain((DEPTH, D)), "norm_ffn": gain((DEPTH, D)), "norm_ple": gain((DEPTH, D)),
        "norm_final": gain((D,)),
        "a_w_in": nrm((N_LAYERS_A, D, 2 * LRU_WIDTH), D ** -0.5),
        "a_conv_w": nrm((N_LAYERS_A, LRU_CONV, LRU_WIDTH), LRU_CONV ** -0.5),
        "a_conv_b": nrm((N_LAYERS_A, LRU_WIDTH), 0.01),
        "a_gate_w": nrm((N_LAYERS_A, LRU_HEADS, LRU_BLOCK, 2 * LRU_BLOCK), LRU_BLOCK ** -0.5),
        "a_gate_b": nrm((N_LAYERS_A, LRU_HEADS, 2 * LRU_BLOCK), 0.01),
        "a_lambda": jnp.log(a_base) - jnp.log1p(-a_base),
        "a_w_out": nrm((N_LAYERS_A, LRU_WIDTH, D), LRU_WIDTH ** -0.5),
        "b_w_qkv": nrm((N_LAYERS_B, D, (2 * N_GROUPS + 1) * ATT_HEADS * ATT_HEAD_DIM), D ** -0.5),
        "b_w_out": nrm((N_LAYERS_B, ATT_HEADS * ATT_HEAD_DIM, D), D ** -0.5),
        "c_mu": jax.random.uniform(next(ks), (N_LAYERS_C, 6, D), f32, 0.0, 1.0),
        "c_w_rkv": nrm((N_LAYERS_C, 3, D, D), D ** -0.5),
        "c_w0": jax.random.uniform(next(ks), (N_LAYERS_C, D), f32, -6.0, -1.0),
        "c_w1": nrm((N_LAYERS_C, D, DECAY_LORA), 0.1 * D ** -0.5),
        "c_w2": nrm((N_LAYERS_C, DECAY_LORA, D), 0.1 * DECAY_LORA ** -0.5),
        "c_a0": nrm((N_LAYERS_C, D), 0.1),
        "c_a1": nrm((N_LAYERS_C, D, AAA_LORA), 0.1 * D ** -0.5),
        "c_a2": nrm((N_LAYERS_C, AAA_LORA, D), 0.1 * AAA_LORA ** -0.5),
        "c_g1": nrm((N_LAYERS_C, D, GATE_LORA), D ** -0.5),
        "c_g2": nrm((N_LAYERS_C, GATE_LORA, D), GATE_LORA ** -0.5),
        "c_k_k": 0.85 + nrm((N_LAYERS_C, D), 0.05),
        "c_k_a": gain((N_LAYERS_C, D)),
        "c_r_k": nrm((N_LAYERS_C, RWKV_HEADS, RWKV_HEAD), 0.1),
        "c_ln_w": gain((N_LAYERS_C, D)),
        "c_ln_b": nrm((N_LAYERS_C, D), 0.01),
        "c_w_out": nrm((N_LAYERS_C, D, D), D ** -0.5),
        "f_w_up": nrm((DEPTH, D, 2 * FFN_DIM), D ** -0.5),
        "f_conv_w": nrm((DEPTH, FFN_CONV, 2 * FFN_DIM), FFN_CONV ** -0.5),
        "f_conv_b": nrm((DEPTH, 2 * FFN_DIM), 0.01),
        "f_w_down": nrm((DEPTH, FFN_DIM, D), FFN_DIM ** -0.5),
        "ple_w_proj": nrm((DEPTH, PLE_DIM, D), PLE_DIM ** -0.5),
        "ple_w_gate": nrm((DEPTH, D, D), D ** -0.5),
    }


def reference(x, p, positions, norm_mix, norm_ffn, norm_ple, norm_final,
              a_w_in, a_conv_w, a_conv_b, a_gate_w, a_gate_b, a_lambda, a_w_out,
              b_w_qkv, b_w_out,
              c_mu, c_w_rkv, c_w0, c_w1, c_w2, c_a0, c_a1, c_a2, c_g1, c_g2,
              c_k_k, c_k_a, c_r_k, c_ln_w, c_ln_b, c_w_out,
              f_w_up, f_conv_w, f_conv_b, f_w_down, ple_w_proj, ple_w_gate):
    inv_freq = ROPE_THETA ** (-jnp.arange(0, ATT_HEAD_DIM, 2, dtype=jnp.float32) / ATT_HEAD_DIM)
    ang = positions.astype(jnp.float32)[..., None] * inv_freq
    cos, sin = jnp.cos(ang), jnp.sin(ang)

    h = x
    for i in range(DEPTH):
        kind, j = i % N_MIXERS, i // N_MIXERS
        hn = rmsnorm(h, norm_mix[i])
        if kind == 0:
            m = rglru_mixer(hn, a_w_in[j], a_conv_w[j], a_conv_b[j], a_gate_w[j], a_gate_b[j],
                            a_lambda[j], a_w_out[j])
        elif kind == 1:
            m = dilated_attention_mixer(hn, cos, sin, b_w_qkv[j], b_w_out[j])
        else:
            m = rwkv7_mixer(hn, c_mu[j], c_w_rkv[j], c_w0[j], c_w1[j], c_w2[j], c_a0[j], c_a1[j],
                            c_a2[j], c_g1[j], c_g2[j], c_k_k[j], c_k_a[j], c_r_k[j], c_ln_w[j],
                            c_ln_b[j], c_w_out[j])
        h = h + m
        h = h + conv_glu_ffn(rmsnorm(h, norm_ffn[i]), f_w_up[i], f_conv_w[i], f_conv_b[i], f_w_down[i])
        ple_gate = jax.nn.sigmoid(rmsnorm(h, norm_ple[i]) @ ple_w_gate[i])
        h = h + ple_gate * (p[i].astype(h.dtype) @ ple_w_proj[i])
    return rmsnorm(h, norm_final)
```

```python
import numpy as np
import concourse.bass as bass
import concourse.mybir as mybir
from concourse.bass_utils import run_bass_kernel_spmd

F32 = mybir.dt.float32
BF16 = mybir.dt.bfloat16
I32 = mybir.dt.int32
AF = mybir.ActivationFunctionType
ALU = mybir.AluOpType
AX = mybir.AxisListType


class Buf:
    __slots__ = ("w", "r", "name")

    def __init__(self, name=""):
        self.w = None
        self.r = {}
        self.name = name


class _Eng:
    def __init__(self, name, eng, sem):
        self.name, self.e, self.sem = name, eng, sem
        self.cnt = 0
        self.seen = {}
        self.pending = []

    def wait(self, ev):
        sem, val = ev
        k = id(sem)
        if self.seen.get(k, 0) >= val:
            return
        self.e.wait_ge(sem, val)
        self.seen[k] = val


class Sched:
    NDMA = 8

    def __init__(self, nc, stack):
        self.nc = nc
        self.engs = {}
        for name, eng in (("pe", nc.tensor), ("act", nc.scalar), ("dve", nc.vector),
                          ("pool", nc.gpsimd), ("sp", nc.sync)):
            sem = stack.enter_context(nc.semaphore("sem_" + name))
            self.engs[name] = _Eng(name, eng, sem)
        self.dma_slots = {}
        for q in ("sp", "pool", "act"):
            sl = []
            for i in range(self.NDMA):
                sem = stack.enter_context(nc.semaphore("dq_%s%d" % (q, i)))
                sl.append([sem, 0])
            self.dma_slots[q] = [sl, 0]
        self.n_inst = 0

    def op(self, engname, fn, reads=(), writes=(), inc=True, acc=False):
        E = self.engs[engname]
        for b in reads:
            if b.w is not None:
                E.wait(b.w)
        for b in writes:
            if b.w is not None and not (acc and b.w[0] is E.sem):
                E.wait(b.w)
            for ev in b.r.values():
                if ev[0] is not E.sem:
                    E.wait(ev)
        inst = fn(E.e)
        self.n_inst += 1
        if inc:
            E.cnt += 1
            inst.then_inc(E.sem, 1)
            ev = (E.sem, E.cnt)
            E.pending.append((reads, writes))
            for rd, wr in E.pending:
                for b in rd:
                    b.r[id(E.sem)] = ev
                for b in wr:
                    b.w = ev
                    b.r = {}
            E.pending = []
            E.seen[id(E.sem)] = max(E.seen.get(id(E.sem), 0), 0)
        else:
            E.pending.append((reads, writes))
        return inst

    def dma(self, q, out, in_, reads=(), writes=()):
        E = self.engs[q]
        slots, idx = self.dma_slots[q]
        slot = slots[idx % self.NDMA]
        self.dma_slots[q][1] = idx + 1
        if slot[1] > 0:
            E.wait((slot[0], slot[1]))
        for b in reads:
            if b.w is not None:
                E.wait(b.w)
        for b in writes:
            if b.w is not None:
                E.wait(b.w)
            for ev in b.r.values():
                E.wait(ev)
        inst = E.e.dma_start(out=out, in_=in_)
        self.n_inst += 1
        slot[1] += 16
        inst.then_inc(slot[0], 16)
        ev = (slot[0], slot[1])
        for b in reads:
            b.r[id(slot[0])] = ev
        for b in writes:
            b.w = ev
            b.r = {}
        return ev

    def wait_all(self, engname, bufs):
        E = self.engs[engname]
        for b in bufs:
            if b.w is not None:
                E.wait(b.w)
            for ev in b.r.values():
                E.wait(ev)


class Ring:
    _uid = [0]

    def __init__(self, nc, stack, name, shape, dtype, n, psum=False):
        self.items = []
        Ring._uid[0] += 1
        name = "%s_u%d_" % (name, Ring._uid[0])
        for i in range(n):
            if psum:
                t = stack.enter_context(nc.psum_tensor("%s%d" % (name, i), shape, dtype))
            else:
                t = stack.enter_context(nc.sbuf_tensor("%s%d" % (name, i), shape, dtype))
            self.items.append((t, Buf("%s%d" % (name, i))))
        self.i = 0

    def next(self):
        it = self.items[self.i % len(self.items)]
        self.i += 1
        return it


class SubRing(Ring):
    def __init__(self, items):
        self.items = list(items)
        self.i = 0


D = 1024
T = 4096
NB = 4
DEPTH = 4
KT = D // 128
TT = 512
NTT = T // TT
FFN = 2816
FT = FFN // 128
PLE = 256
RMS_EPS = 1e-6
GN_EPS = 64e-5
LRU_C = 8.0


class ColPack:
    def __init__(self):
        self.cols = []
        self.idx = {}

    def add(self, name, vec):
        vec = np.ascontiguousarray(vec, dtype=np.float32).reshape(-1)
        assert vec.size % 128 == 0
        n = vec.size // 128
        self.idx[name] = (len(self.cols), n)
        for i in range(n):
            self.cols.append(vec[i * 128:(i + 1) * 128])

    def array(self):
        return np.ascontiguousarray(np.stack(self.cols, axis=1))


def col_layout():
    L = []
    for i in range(DEPTH):
        L += [("norm_mix%d" % i, KT), ("norm_ffn%d" % i, KT), ("norm_ple%d" % i, KT)]
        for k in range(3):
            L.append(("f_conv_w%d_%d" % (i, k), 2 * FT))
        L.append(("f_conv_b%d" % i, 2 * FT))
    L.append(("norm_final", KT))
    for j in range(2):
        for k in range(4):
            L.append(("a_conv_w%d_%d" % (j, k), KT))
        L += [("a_conv_b%d" % j, KT), ("a_gate_b%d" % j, 2 * KT), ("a_lambda%d" % j, KT)]
    for c in range(6):
        L.append(("c_mu%d" % c, KT))
    for nm in ("c_w0", "c_a0", "c_k_k", "c_k_a", "c_r_k", "c_ln_w", "c_ln_b"):
        L.append((nm, KT))
    L.append(("invf", 1))
    L.append(("rsgn", 1))
    off = {}
    o = 0
    for nm, n in L:
        off[nm] = (o, n)
        o += n
    return off, o


COLS, NCOLS = col_layout()


def pack_cols(inp):
    cp = ColPack()
    for i in range(DEPTH):
        cp.add("norm_mix%d" % i, inp["norm_mix"][i])
        cp.add("norm_ffn%d" % i, inp["norm_ffn"][i])
        cp.add("norm_ple%d" % i, inp["norm_ple"][i])
        for k in range(3):
            cp.add("f_conv_w%d_%d" % (i, k), inp["f_conv_w"][i, k])
        cp.add("f_conv_b%d" % i, inp["f_conv_b"][i])
    cp.add("norm_final", inp["norm_final"])
    for j in range(2):
        for k in range(4):
            cp.add("a_conv_w%d_%d" % (j, k), inp["a_conv_w"][j, k])
        cp.add("a_conv_b%d" % j, inp["a_conv_b"][j])
        gb = inp["a_gate_b"][j].reshape(4, 2, 256)
        cp.add("a_gate_b%d" % j, np.concatenate([gb[:, 0].reshape(-1), gb[:, 1].reshape(-1)]))
        cp.add("a_lambda%d" % j, inp["a_lambda"][j])
    for c in range(6):
        cp.add("c_mu%d" % c, inp["c_mu"][0, c])
    for nm in ("c_w0", "c_a0", "c_k_k", "c_k_a", "c_r_k", "c_ln_w", "c_ln_b"):
        cp.add(nm, inp[nm][0])
    invf = (10000.0 ** (-np.arange(0, 64, 2, dtype=np.float32) / 64)).astype(np.float32)
    cp.add("invf", np.tile(invf, 4))
    cp.add("rsgn", np.tile(np.concatenate([-np.ones(32, np.float32), np.ones(32, np.float32)]), 2))
    assert cp.idx == COLS, "col layout mismatch"
    return cp.array()


from contextlib import ExitStack


class _Cut(Exception):
    pass


class Prog:
    def cut(self, name):
        if self.dbg == name:
            raise _Cut()

    def __init__(self, layers=DEPTH, dbg=None):
        self.layers = list(range(layers)) if isinstance(layers, int) else list(layers)
        self.dbg = dbg
        nc = bass.Bass("TRN2", target_bir_lowering=False)
        self.nc = nc
        dt = nc.dram_tensor
        self.xT = dt("xT", [KT, 128, T], F32, kind="ExternalInput").ap()
        self.pT = dt("pT", [DEPTH, 2, 128, T], F32, kind="ExternalInput").ap()
        self.pos = dt("pos", [1, T], I32, kind="ExternalInput").ap()
        self.colsD = dt("cols", [128, NCOLS], F32, kind="ExternalInput").ap()
        class _LazyW(dict):
            def __init__(s2, shapes):
                s2.shapes = shapes
            def __missing__(s2, nm):
                s2[nm] = dt(nm, s2.shapes[nm], F32, kind="ExternalInput").ap()
                return s2[nm]
        shapes = {}
        for nm, shp in (("a_w_in", [2, D, 2 * D]), ("a_gate_w", [2, 4, 256, 512]), ("a_w_out", [2, D, D]),
                        ("b_w_qkv", [D, 7 * D]), ("b_w_qks", [D, 6 * D]), ("b_w_out", [D, D]),
                        ("c_w_rkv", [3, D, D]), ("c_w1", [D, 64]), ("c_w2", [64, D]), ("c_a1", [D, 64]),
                        ("c_a2", [64, D]), ("c_g1", [D, 128]), ("c_g2", [128, D]), ("c_w_out", [D, D]),
                        ("f_w_up", [DEPTH, D, 2 * FFN]), ("f_w_down", [DEPTH, FFN, D]),
                        ("ple_w_proj", [DEPTH, PLE, D]), ("ple_w_gate", [DEPTH, D, D])):
            shapes[nm] = shp
        self.W = _LazyW(shapes)
        self.yT = dt("yT", [KT, 128, T], F32, kind="ExternalOutput").ap()
        self.hT = dt("hT", [KT, 128, T], F32, kind="Internal").ap()
        self.hnT = dt("hnT", [KT, 128, T], BF16, kind="Internal").ap()
        self.gT = dt("gT", [KT, 128, T], BF16, kind="Internal").ap()
        self.actT = dt("actT", [FT, 128, T], BF16, kind="Internal").ap()
        self.s32 = [dt("s32_%d" % i, [KT, 128, T], F32, kind="Internal").ap() for i in range(6)]
        self.s16 = [dt("s16_%d" % i, [KT, 128, T], BF16, kind="Internal").ap() for i in range(8)]
        self.build()

    def col(self, name, k=0, n=1):
        o, sz = COLS[name]
        assert k + n <= sz
        return self.cols[:, o + k:o + k + n]

    def barrier(self):
        S = self.S
        evs = [(E.sem, E.cnt) for E in S.engs.values() if E.cnt > 0]
        for q in S.dma_slots:
            for sl in S.dma_slots[q][0]:
                if sl[1] > 0:
                    evs.append((sl[0], sl[1]))
        for E in S.engs.values():
            assert not E.pending
            for ev in evs:
                E.wait(ev)

    def dump(self, name, ap, bufs, shape, dtype):
        if not self.dbg:
            return
        t = self.nc.dram_tensor("dbg_" + name, list(shape), dtype, kind="Internal").ap()
        self.S.dma("sp", t, ap, reads=list(bufs))

    def sb(self, st, name, shape, dtype):
        Ring._uid[0] += 1
        return st.enter_context(self.nc.sbuf_tensor("%s_u%d" % (name, Ring._uid[0]), shape, dtype))

    def load_w(self, ring, wap2d, kt, e0, ew):
        t, b = ring.next()
        src = wap2d.rearrange("(k p) e -> p k e", p=128)[:, :, e0:e0 + ew]
        self.S.dma("pool", t[:, 0:kt, 0:ew], src, writes=[b])
        return t, b

    def rmsnorm(self, st_rings, h, bh, gain, out, bout):
        S = self.S
        sqr, psr, rsr = st_rings
        ps, bps = psr.next()
        for k in range(KT):
            sq, bsq = sqr.next()
            S.op("act", lambda e: e.activation(out=sq[:], in_=h[:, k, :], func=AF.Square), reads=[bh], writes=[bsq])
            S.op("pe", lambda e: e.matmul(ps[:, :], lhsT=self.ones_bf[:, :], rhs=sq[:], start=(k == 0), stop=(k == KT - 1)),
                 reads=[bsq, self.bconst], writes=[bps], acc=True)
        rs, brs = rsr.next()
        S.op("act", lambda e: e.activation(out=rs[:], in_=ps[:, :], func=AF.Sqrt, scale=1.0 / D, bias=self.eps_col),
             reads=[bps, self.bconst], writes=[brs])
        S.op("dve", lambda e: e.reciprocal(out=rs[:], in_=rs[:]), reads=[brs], writes=[brs])
        for k in range(KT):
            S.op("dve", lambda e: e.scalar_tensor_tensor(out=out[:, k, :], in0=h[:, k, :], scalar=self.col(gain, k),
                                                         in1=rs[:], op0=ALU.mult, op1=ALU.mult),
                 reads=[bh, brs, self.bconst], writes=[bout])

    def norm_rings(self, st, tag):
        nc = self.nc
        return (Ring(nc, st, "sq" + tag, [128, TT], BF16, 3),
                self.psA, Ring(nc, st, "rs" + tag, [128, TT], F32, 2))

    def tview(self, ap3, j):
        return ap3.rearrange("k p t -> p k t")[:, :, j * TT:(j + 1) * TT]

    def build(self):
        nc = self.nc
        with ExitStack() as top:
            S = Sched(nc, top)
            self.S = S
            self.cols = self.sb(top, "cols", [128, NCOLS], F32)
            self.cst = self.sb(top, "cst", [128, 8], F32)
            self.ones_bf = self.sb(top, "ones_bf", [128, 128], BF16)
            self.bconst = Buf("const")
            S.dma("sp", self.cols[:], self.colsD[:, :], writes=[self.bconst])
            S.op("pool", lambda e: e.memset(self.cst[:, 0:1], RMS_EPS), writes=[self.bconst])
            S.op("pool", lambda e: e.memset(self.cst[:, 1:2], 1.0), writes=[self.bconst])
            S.op("pool", lambda e: e.memset(self.cst[:, 2:3], 0.0), writes=[self.bconst])
            S.op("pool", lambda e: e.memset(self.cst[:, 3:4], GN_EPS), writes=[self.bconst])
            S.op("pool", lambda e: e.memset(self.ones_bf[:], 1.0), writes=[self.bconst])
            self.eps_col = self.cst[:, 0:1]
            self.one_col = self.cst[:, 1:2]
            self.zero_col = self.cst[:, 2:3]
            self.gneps_col = self.cst[:, 3:4]
            self.psbig = [top.enter_context(nc.psum_tensor("psbig%d" % q, [128, 2 * TT], F32)) for q in range(4)]
            self.psA = SubRing([(self.psbig[q // 2][:, (q % 2) * TT:(q % 2 + 1) * TT], Buf("ps%d" % q)) for q in range(8)])
            self.barrier()
            self.phase_norm0(self.layers[0])
            h_src = self.xT
            for li, i in enumerate(self.layers):
                kind, j = i % 3, i // 3
                if kind == 0:
                    self.mixer_a(i, j)
                    wout = self.W["a_w_out"][j]
                elif kind == 1:
                    self.mixer_b(i)
                    wout = self.W["b_w_out"]
                else:
                    self.mixer_c(i)
                    wout = self.W["c_w_out"]
                self.mixer_out(i, wout, h_src)
                h_src = self.hT
                self.ffn_up(i)
                last = (li == len(self.layers) - 1)
                self.tail(i, last, None if last else self.layers[li + 1])
            self.barrier()

    def phase_norm0(self, i0):
        nc, S = self.nc, self.S
        with ExitStack() as st:
            hr = Ring(nc, st, "n0h", [128, KT, TT], F32, 2)
            orr = Ring(nc, st, "n0o", [128, KT, TT], BF16, 2)
            rings = self.norm_rings(st, "n0")
            def ld(j):
                h, bh = hr.next()
                S.dma("sp", h[:], self.tview(self.xT, j), writes=[bh])
                return h, bh
            nxt = ld(0)
            for j in range(NTT):
                h, bh = nxt
                if j + 1 < NTT:
                    nxt = ld(j + 1)
                o, bo = orr.next()
                self.rmsnorm(rings, h, bh, "norm_mix%d" % i0, o, bo)
                S.dma("sp", self.tview(self.hnT, j), o[:], reads=[bo])
            self.barrier()

    def mixer_out(self, i, wout, h_src):
        nc, S = self.nc, self.S
        with ExitStack() as st:
            w = self.sb(st, "mo_w", [128, KT, D], BF16)
            bw = Buf()
            for k in range(KT):
                S.dma("pool", w[:, k, :], wout[k * 128:(k + 1) * 128, :], writes=[bw])
            gr = Ring(nc, st, "mo_g", [128, KT, TT], BF16, 2)
            hr = Ring(nc, st, "mo_h", [128, KT, TT], F32, 2)
            orr = Ring(nc, st, "mo_o", [128, KT, TT], BF16, 2)
            rings = self.norm_rings(st, "mo")
            def ld(j):
                g, bg = gr.next()
                S.dma("sp", g[:], self.tview(self.gT, j), writes=[bg])
                h, bh = hr.next()
                S.dma("sp", h[:], self.tview(h_src, j), writes=[bh])
                return g, bg, h, bh
            nxt = ld(0)
            for j in range(NTT):
                g, bg, h, bh = nxt
                if j + 1 < NTT:
                    nxt = ld(j + 1)
                for e in range(KT):
                    ps, bps = self.psA.next()
                    for k in range(KT):
                        S.op("pe", lambda en: en.matmul(ps[:, :], lhsT=w[:, k, e * 128:(e + 1) * 128], rhs=g[:, k, :],
                                                        start=(k == 0), stop=(k == KT - 1)),
                             reads=[bw, bg], writes=[bps], inc=(k == KT - 1), acc=True)
                    S.op("dve", lambda en: en.tensor_tensor(out=h[:, e, :], in0=ps[:, :], in1=h[:, e, :], op=ALU.add),
                         reads=[bps, bh], writes=[bh])
                S.dma("sp", self.tview(self.hT, j), h[:], reads=[bh])
                o, bo = orr.next()
                self.rmsnorm(rings, h, bh, "norm_ffn%d" % i, o, bo)
                S.dma("sp", self.tview(self.hnT, j), o[:], reads=[bo])
            self.barrier()

    def ffn_up(self, i):
        nc, S = self.nc, self.S
        wup = self.W["f_w_up"][i]
        with ExitStack() as st:
            hn = self.sb(st, "fu_hn", [128, KT, T], BF16)
            bhn = [Buf() for _ in range(KT)]
            for k in range(KT):
                S.dma("sp", hn[:, k, :], self.hnT[k, :, :], writes=[bhn[k]])
            wr = Ring(nc, st, "fu_w", [128, KT, 128], BF16, 4)
            stg = [[self.sb(st, "fu_stg%d_%d" % (s, b), [128, 2 + T], F32) for b in range(2)] for s in range(2)]
            bstg = [[[Buf() for _ in range(NTT + 1)] for b in range(2)] for s in range(2)]
            for s in range(2):
                for b in range(2):
                    S.op("pool", lambda e: e.memset(stg[s][b][:, 0:2], 0.0), writes=[bstg[s][b][0]])
            accr = Ring(nc, st, "fu_acc", [128, TT], F32, 4)
            sgr = Ring(nc, st, "fu_sg", [128, TT], F32, 2)
            outr = Ring(nc, st, "fu_out", [128, TT], BF16, 3)
            ldw = lambda f: [self.load_w(wr, wup, KT, (s * FT + f) * 128, 128) for s in range(2)]
            wnxt = ldw(0)
            for f in range(FT):
                wt = wnxt
                if f + 1 < FT:
                    wnxt = ldw(f + 1)
                accs = [None, None]
                for j in range(NTT):
                    for s in range(2):
                        w, bw = wt[s]
                        ps, bps = self.psA.next()
                        for k in range(KT):
                            S.op("pe", lambda en: en.matmul(ps[:, :], lhsT=w[:, k, :], rhs=hn[:, k, j * TT:(j + 1) * TT],
                                                            start=(k == 0), stop=(k == KT - 1)),
                                 reads=[bw] + bhn, writes=[bps], inc=(k == KT - 1), acc=True)
                        sg_t, sg_b = stg[s][f % 2], bstg[s][f % 2]
                        c0 = 2 + j * TT
                        S.op("act", lambda en: en.activation(out=sg_t[:, c0:c0 + TT], in_=ps[:, :], func=AF.Copy),
                             reads=[bps], writes=[sg_b[j + 1]])
                        acc, bacc = accr.next()
                        ci = s * FT + f
                        S.op("act", lambda en: en.activation(out=acc[:], in_=ps[:, :], func=AF.Identity,
                                                             scale=self.col("f_conv_w%d_2" % i, ci),
                                                             bias=self.col("f_conv_b%d" % i, ci)),
                             reads=[bps, self.bconst], writes=[bacc])
                        for kk in (1, 0):
                            S.op("dve", lambda en: en.scalar_tensor_tensor(
                                out=acc[:], in0=sg_t[:, j * TT + kk:j * TT + kk + TT],
                                scalar=self.col("f_conv_w%d_%d" % (i, kk), ci), in1=acc[:], op0=ALU.mult, op1=ALU.add),
                                reads=[sg_b[j], sg_b[j + 1], bacc, self.bconst], writes=[bacc])
                        accs[s] = (acc, bacc)
                    sg, bsg = sgr.next()
                    S.op("act", lambda en: en.activation(out=sg[:], in_=accs[0][0][:], func=AF.Silu),
                         reads=[accs[0][1]], writes=[bsg])
                    o, bo = outr.next()
                    S.op("dve", lambda en: en.tensor_tensor(out=o[:], in0=sg[:], in1=accs[1][0][:], op=ALU.mult),
                         reads=[bsg, accs[1][1]], writes=[bo])
                    S.dma("sp", self.actT[f, :, j * TT:(j + 1) * TT], o[:], reads=[bo])
            self.barrier()

    def tail(self, i, last, inext):
        nc, S = self.nc, self.S
        with ExitStack() as st:
            wd = self.sb(st, "tl_wd", [128, FT, D], BF16)
            wg = self.sb(st, "tl_wg", [128, KT, D], BF16)
            wp = self.sb(st, "tl_wp", [128, 2, D], BF16)
            bwd, bwg, bwp = Buf(), Buf(), Buf()
            for f in range(FT):
                S.dma("pool", wd[:, f, :], self.W["f_w_down"][i, f * 128:(f + 1) * 128, :], writes=[bwd])
            for k in range(KT):
                S.dma("pool", wg[:, k, :], self.W["ple_w_gate"][i, k * 128:(k + 1) * 128, :], writes=[bwg])
            for k in range(2):
                S.dma("pool", wp[:, k, :], self.W["ple_w_proj"][i, k * 128:(k + 1) * 128, :], writes=[bwp])
            ar = Ring(nc, st, "tl_a", [128, FT, TT], BF16, 2)
            hr = Ring(nc, st, "tl_h", [128, KT, TT], F32, 2)
            pr = Ring(nc, st, "tl_p", [128, 2, TT], BF16, 2)
            n3r = Ring(nc, st, "tl_n3", [128, KT, TT], BF16, 1)
            sgr = Ring(nc, st, "tl_sg", [128, TT], F32, 2)
            if last:
                orr = Ring(nc, st, "tl_o", [128, KT, TT], F32, 1)
            else:
                orr = Ring(nc, st, "tl_o", [128, KT, TT], BF16, 2)
            rings = self.norm_rings(st, "tl")
            def ld(j):
                a, ba = ar.next()
                S.dma("sp", a[:], self.tview(self.actT, j), writes=[ba])
                h, bh = hr.next()
                S.dma("sp", h[:], self.tview(self.hT, j), writes=[bh])
                p, bp = pr.next()
                S.dma("pool", p[:], self.tview(self.pT[i], j), writes=[bp])
                return a, ba, h, bh, p, bp
            nxt = ld(0)
            for j in range(NTT):
                a, ba, h, bh, p, bp = nxt
                if j + 1 < NTT:
                    nxt = ld(j + 1)
                for e in range(KT):
                    ps, bps = self.psA.next()
                    for f in range(FT):
                        S.op("pe", lambda en: en.matmul(ps[:, :], lhsT=wd[:, f, e * 128:(e + 1) * 128], rhs=a[:, f, :],
                                                        start=(f == 0), stop=(f == FT - 1)),
                             reads=[bwd, ba], writes=[bps], inc=(f == FT - 1), acc=True)
                    S.op("dve", lambda en: en.tensor_tensor(out=h[:, e, :], in0=ps[:, :], in1=h[:, e, :], op=ALU.add),
                         reads=[bps, bh], writes=[bh])
                n3, bn3 = n3r.next()
                self.rmsnorm(rings, h, bh, "norm_ple%d" % i, n3, bn3)
                for e in range(KT):
                    ps, bps = self.psA.next()
                    for k in range(KT):
                        S.op("pe", lambda en: en.matmul(ps[:, :], lhsT=wg[:, k, e * 128:(e + 1) * 128], rhs=n3[:, k, :],
                                                        start=(k == 0), stop=(k == KT - 1)),
                             reads=[bwg, bn3], writes=[bps], inc=(k == KT - 1), acc=True)
                    sg, bsg = sgr.next()
                    S.op("act", lambda en: en.activation(out=sg[:], in_=ps[:, :], func=AF.Sigmoid),
                         reads=[bps], writes=[bsg])
                    ps2, bps2 = self.psA.next()
                    for k in range(2):
                        S.op("pe", lambda en: en.matmul(ps2[:, :], lhsT=wp[:, k, e * 128:(e + 1) * 128], rhs=p[:, k, :],
                                                        start=(k == 0), stop=(k == 1)),
                             reads=[bwp, bp], writes=[bps2], inc=(k == 1), acc=True)
                    S.op("dve", lambda en: en.tensor_tensor(out=sg[:], in0=ps2[:, :], in1=sg[:], op=ALU.mult),
                         reads=[bps2, bsg], writes=[bsg])
                    S.op("dve", lambda en: en.tensor_tensor(out=h[:, e, :], in0=sg[:], in1=h[:, e, :], op=ALU.add),
                         reads=[bsg, bh], writes=[bh])
                o, bo = orr.next()
                if last:
                    self.rmsnorm(rings, h, bh, "norm_final", o, bo)
                    S.dma("sp", self.tview(self.yT, j), o[:], reads=[bo])
                else:
                    S.dma("sp", self.tview(self.hT, j), h[:], reads=[bh])
                    self.rmsnorm(rings, h, bh, "norm_mix%d" % inext, o, bo)
                    S.dma("sp", self.tview(self.hnT, j), o[:], reads=[bo])
            self.barrier()

    def mixer_a(self, i, j):
        nc, S = self.nc, self.S
        ygT, xcT, xcbT = self.s32[0], self.s32[1], self.s16[0]
        win = self.W["a_w_in"][j]
        with ExitStack() as st:
            hn = self.sb(st, "a1_hn", [128, KT, T], BF16)
            bhn = [Buf() for _ in range(KT)]
            for k in range(KT):
                S.dma("sp", hn[:, k, :], self.hnT[k, :, :], writes=[bhn[k]])
            wr = Ring(nc, st, "a1_w", [128, KT, 128], BF16, 4)
            stg = [self.sb(st, "a1_stg%d" % b, [128, 3 + T], F32) for b in range(2)]
            bstg = [[Buf() for _ in range(NTT + 1)] for b in range(2)]
            for b in range(2):
                S.op("pool", lambda e: e.memset(stg[b][:, 0:3], 0.0), writes=[bstg[b][0]])
            ygr = Ring(nc, st, "a1_yg", [128, TT], F32, 3)
            accr = Ring(nc, st, "a1_acc", [128, TT], F32, 3)
            xbr = Ring(nc, st, "a1_xb", [128, TT], BF16, 3)
            wnxt = self.load_w(wr, win, KT, 0, 128)
            for e in range(2 * KT):
                w, bw = wnxt
                if e + 1 < 2 * KT:
                    wnxt = self.load_w(wr, win, KT, (e + 1) * 128, 128)
                for jt in range(NTT):
                    ps, bps = self.psA.next()
                    for k in range(KT):
                        S.op("pe", lambda en: en.matmul(ps[:, :], lhsT=w[:, k, :], rhs=hn[:, k, jt * TT:(jt + 1) * TT],
                                                        start=(k == 0), stop=(k == KT - 1)),
                             reads=[bw] + bhn, writes=[bps], inc=(k == KT - 1), acc=True)
                    if e < KT:
                        yg, byg = ygr.next()
                        S.op("act", lambda en: en.activation(out=yg[:], in_=ps[:, :], func=AF.Gelu_apprx_tanh),
                             reads=[bps], writes=[byg])
                        S.dma("sp", ygT[e, :, jt * TT:(jt + 1) * TT], yg[:], reads=[byg])
                    else:
                        ft = e - KT
                        sg_t, sg_b = stg[ft % 2], bstg[ft % 2]
                        c0 = 3 + jt * TT
                        S.op("act", lambda en: en.activation(out=sg_t[:, c0:c0 + TT], in_=ps[:, :], func=AF.Copy),
                             reads=[bps], writes=[sg_b[jt + 1]])
                        acc, bacc = accr.next()
                        S.op("act", lambda en: en.activation(out=acc[:], in_=ps[:, :], func=AF.Identity,
                                                             scale=self.col("a_conv_w%d_3" % j, ft),
                                                             bias=self.col("a_conv_b%d" % j, ft)),
                             reads=[bps, self.bconst], writes=[bacc])
                        for kk in (2, 1, 0):
                            S.op("dve", lambda en: en.scalar_tensor_tensor(
                                out=acc[:], in0=sg_t[:, jt * TT + kk:jt * TT + kk + TT],
                                scalar=self.col("a_conv_w%d_%d" % (j, kk), ft), in1=acc[:], op0=ALU.mult, op1=ALU.add),
                                reads=[sg_b[jt], sg_b[jt + 1], bacc, self.bconst], writes=[bacc])
                        xb, bxb = xbr.next()
                        S.op("pool", lambda en: en.tensor_copy(out=xb[:], in_=acc[:]), reads=[bacc], writes=[bxb])
                        S.dma("sp", xcT[ft, :, jt * TT:(jt + 1) * TT], acc[:], reads=[bacc])
                        S.dma("sp", xcbT[ft, :, jt * TT:(jt + 1) * TT], xb[:], reads=[bxb])
            self.barrier()
        with ExitStack() as st:
            cc = self.sb(st, "a2_cc", [128, 5, KT], F32)
            bc = Buf()
            lam = self.col("a_lambda%d" % j, 0, KT)
            ev, l1, t2, mk, ccol = (cc[:, q, :] for q in range(5))
            S.op("act", lambda e: e.activation(out=ev, in_=lam, func=AF.Exp, scale=-1.0), reads=[self.bconst], writes=[bc])
            S.op("act", lambda e: e.activation(out=l1, in_=ev, func=AF.Ln, bias=self.one_col), reads=[bc, self.bconst], writes=[bc])
            S.op("dve", lambda e: e.tensor_scalar(out=t2, in0=ev, scalar1=1.0 / 3.0, scalar2=-0.5, op0=ALU.mult, op1=ALU.add), reads=[bc], writes=[bc])
            S.op("dve", lambda e: e.tensor_tensor(out=t2, in0=t2, in1=ev, op=ALU.mult), reads=[bc], writes=[bc])
            S.op("dve", lambda e: e.tensor_scalar(out=t2, in0=t2, scalar1=1.0, scalar2=None, op0=ALU.add), reads=[bc], writes=[bc])
            S.op("dve", lambda e: e.tensor_tensor(out=t2, in0=t2, in1=ev, op=ALU.mult), reads=[bc], writes=[bc])
            S.op("dve", lambda e: e.tensor_scalar(out=mk, in0=ev, scalar1=0.02, scalar2=None, op0=ALU.is_lt), reads=[bc], writes=[bc])
            S.op("dve", lambda e: e.tensor_tensor(out=t2, in0=t2, in1=l1, op=ALU.subtract), reads=[bc], writes=[bc])
            S.op("dve", lambda e: e.tensor_tensor(out=t2, in0=t2, in1=mk, op=ALU.mult), reads=[bc], writes=[bc])
            S.op("dve", lambda e: e.tensor_tensor(out=l1, in0=l1, in1=t2, op=ALU.add), reads=[bc], writes=[bc])
            S.op("dve", lambda e: e.tensor_scalar(out=ccol, in0=l1, scalar1=-LRU_C, scalar2=None, op0=ALU.mult), reads=[bc], writes=[bc])

            xbr = Ring(nc, st, "a2_xb", [128, 2, T], BF16, 2)
            gwr = Ring(nc, st, "a2_gw", [128, 2, 512], BF16, 2)
            afr = Ring(nc, st, "a2_af", [128, T], F32, 2)
            bfr = Ring(nc, st, "a2_bf", [128, T], F32, 2)
            hfr = Ring(nc, st, "a2_hf", [128, T], F32, 2)
            tr = Ring(nc, st, "a2_t", [128, TT], F32, 6)
            xcr = Ring(nc, st, "a2_xc", [128, TT], F32, 3)
            ygr = Ring(nc, st, "a2_yg", [128, TT], F32, 3)
            gor = Ring(nc, st, "a2_go", [128, TT], BF16, 3)

            def ldh(hd):
                xb, bxb = xbr.next()
                for k in range(2):
                    S.dma("sp", xb[:, k, :], xcbT[2 * hd + k, :, :], writes=[bxb])
                gw, bgw = self.load_w(gwr, self.W["a_gate_w"][j, hd], 2, 0, 512)
                return xb, bxb, gw, bgw
            nxt = ldh(0)
            for hd in range(4):
                xb, bxb, gw, bgw = nxt
                if hd + 1 < 4:
                    nxt = ldh(hd + 1)
                for ft in range(2):
                    ftg = 2 * hd + ft
                    af, _ = afr.next()
                    bf, _ = bfr.next()
                    baf = [Buf() for _ in range(NTT)]
                    bbf = [Buf() for _ in range(NTT)]
                    hf, bhf = hfr.next()
                    if getattr(self, "_a2_prev", None) is not None:
                        pass
                    for jt in range(NTT):
                        sl = slice(jt * TT, (jt + 1) * TT)
                        xc, bxc = xcr.next()
                        S.dma("sp", xc[:], xcT[ftg, :, sl], writes=[bxc])
                        psr, bpsr = self.psA.next()
                        psi, bpsi = self.psA.next()
                        for k in range(2):
                            S.op("pe", lambda en: en.matmul(psr[:, :], lhsT=gw[:, k, ft * 128:(ft + 1) * 128], rhs=xb[:, k, sl],
                                                            start=(k == 0), stop=(k == 1)),
                                 reads=[bgw, bxb], writes=[bpsr], inc=(k == 1), acc=True)
                        for k in range(2):
                            S.op("pe", lambda en: en.matmul(psi[:, :], lhsT=gw[:, k, 256 + ft * 128:256 + (ft + 1) * 128], rhs=xb[:, k, sl],
                                                            start=(k == 0), stop=(k == 1)),
                                 reads=[bgw, bxb], writes=[bpsi], inc=(k == 1), acc=True)
                        r, br = tr.next()
                        S.op("act", lambda en: en.activation(out=r[:], in_=psr[:, :], func=AF.Sigmoid,
                                                             bias=self.col("a_gate_b%d" % j, ftg)),
                             reads=[bpsr, self.bconst], writes=[br])
                        S.op("act", lambda en: en.activation(out=af[:, sl], in_=r[:], func=AF.Exp, scale=cc[:, 4, ftg:ftg + 1]),
                             reads=[br, bc] + self._ring_guard(afr, af), writes=[baf[jt]])
                        ig, big = tr.next()
                        S.op("act", lambda en: en.activation(out=ig[:], in_=psi[:, :], func=AF.Sigmoid,
                                                             bias=self.col("a_gate_b%d" % j, KT + ftg)),
                             reads=[bpsi, self.bconst], writes=[big])
                        sq, bsq = tr.next()
                        S.op("act", lambda en: en.activation(out=sq[:], in_=af[:, sl], func=AF.Square), reads=[baf[jt]], writes=[bsq])
                        S.op("act", lambda en: en.activation(out=sq[:], in_=sq[:], func=AF.Sqrt, scale=-1.0, bias=self.one_col),
                             reads=[bsq, self.bconst], writes=[bsq])
                        S.op("dve", lambda en: en.tensor_tensor(out=ig[:], in0=ig[:], in1=xc[:], op=ALU.mult),
                             reads=[big, bxc], writes=[big])
                        S.op("dve", lambda en: en.tensor_tensor(out=bf[:, sl], in0=ig[:], in1=sq[:], op=ALU.mult),
                             reads=[big, bsq] + self._ring_guard(bfr, bf), writes=[bbf[jt]])
                    S.op("dve", lambda en: en.tensor_tensor_scan(out=hf[:], data0=af[:], data1=bf[:], initial=0.0,
                                                                  op0=ALU.mult, op1=ALU.add),
                         reads=baf + bbf, writes=[bhf])
                    self._ring_set(afr, af, bhf)
                    self._ring_set(bfr, bf, bhf)
                    for jt in range(NTT):
                        sl = slice(jt * TT, (jt + 1) * TT)
                        yg, byg = ygr.next()
                        S.dma("sp", yg[:], ygT[ftg, :, sl], writes=[byg])
                        go, bgo = gor.next()
                        S.op("pool", lambda en: en.tensor_tensor(out=go[:], in0=hf[:, sl], in1=yg[:], op=ALU.mult),
                             reads=[bhf, byg], writes=[bgo])
                        S.dma("sp", self.gT[ftg, :, sl], go[:], reads=[bgo])
            self.barrier()

    def _ring_guard(self, ring, tile):
        d = getattr(self, "_rg", None)
        if d is None:
            d = self._rg = {}
        b = d.get(id(tile))
        return [b] if b is not None else []

    def _ring_set(self, ring, tile, buf):
        if getattr(self, "_rg", None) is None:
            self._rg = {}
        self._rg[id(tile)] = buf


    def mixer_b(self, i):
        with ExitStack() as st:
            try:
                self._mixer_b_body(i, st)
            except _Cut:
                pass
            self.barrier()

    def _mixer_b_body(self, i, st):
        nc, S = self.nc, self.S
        wqkv, wqks = self.W["b_w_qkv"], self.W["b_w_qks"]
        banks = self.psA.items
        if True:
            cosT = self.sb(st, "b_cos", [128, T], BF16)
            sinT = self.sb(st, "b_sin", [128, T], BF16)
            btab = Buf()
            with ExitStack() as s2:
                posi = self.sb(s2, "b_posi", [128, T], I32)
                ang = self.sb(s2, "b_ang", [128, T], F32)
                kk = self.sb(s2, "b_kk", [128, T], F32)
                cst = self.sb(s2, "b_cst", [128, 2], F32)
                bt = Buf()
                S.dma("sp", posi[:], self.pos[0:1, :].to_broadcast([128, T]), writes=[bt])
                S.op("pool", lambda e: e.memset(cst[:, 0:1], float(np.pi / 2)), writes=[bt])
                S.op("dve", lambda e: e.tensor_copy(out=ang[:], in_=posi[:]), reads=[bt], writes=[bt])
                S.op("dve", lambda e: e.tensor_scalar(out=ang[:], in0=ang[:], scalar1=self.col("invf"), scalar2=None, op0=ALU.mult),
                     reads=[bt, self.bconst], writes=[bt])
                MAGIC = 12582912.0
                S.op("dve", lambda e: e.tensor_scalar(out=kk[:], in0=ang[:], scalar1=float(1.0 / (2 * np.pi)), scalar2=MAGIC,
                                                      op0=ALU.mult, op1=ALU.add), reads=[bt], writes=[bt])
                S.op("dve", lambda e: e.tensor_scalar(out=kk[:], in0=kk[:], scalar1=-MAGIC, scalar2=None, op0=ALU.add),
                     reads=[bt], writes=[bt])
                C1 = 6.28125
                C2 = float(2 * np.pi - C1)
                S.op("dve", lambda e: e.scalar_tensor_tensor(out=ang[:], in0=kk[:], scalar=-C1, in1=ang[:], op0=ALU.mult, op1=ALU.add),
                     reads=[bt], writes=[bt])
                S.op("dve", lambda e: e.scalar_tensor_tensor(out=ang[:], in0=kk[:], scalar=-C2, in1=ang[:], op0=ALU.mult, op1=ALU.add),
                     reads=[bt], writes=[bt])
                S.op("dve", lambda e: e.tensor_scalar(out=ang[:], in0=ang[:], scalar1=float(np.pi), scalar2=float(-np.pi),
                                                      op0=ALU.min, op1=ALU.max), reads=[bt], writes=[bt])
                S.op("act", lambda e: e.activation(out=sinT[:], in_=ang[:], func=AF.Sin, scale=self.col("rsgn")),
                     reads=[bt, self.bconst], writes=[btab])
                S.op("dve", lambda e: e.scalar_tensor_tensor(out=kk[:], in0=ang[:], scalar=-1.0, in1=ang[:], op0=ALU.mult, op1=ALU.max), reads=[bt], writes=[bt])
                S.op("act", lambda e: e.activation(out=cosT[:], in_=kk[:], func=AF.Sin, scale=-1.0, bias=cst[:, 0:1]),
                     reads=[bt], writes=[btab])
                self.barrier()
            self.cut("cutA")
            self.dump("cos", cosT[:], [btab], [128, T], BF16)
            self.dump("sin", sinT[:], [btab], [128, T], BF16)
            hn = self.sb(st, "b_hn", [128, KT, T], BF16)
            bhn = [Buf() for _ in range(KT)]
            for k in range(KT):
                S.dma("sp", hn[:, k, :], self.hnT[k, :, :], writes=[bhn[k]])
            band = self.sb(st, "b_band", [128, 2, 256], BF16)
            bband = Buf()
            S.op("pool", lambda e: e.memset(band[:], 1.0), writes=[bband])
            S.op("pool", lambda e: e.affine_select(out=band[:], in_=band[:], pattern=[[0, 2], [1, 256]], compare_op=ALU.is_ge,
                                                   fill=0.0, base=0, channel_multiplier=-1), reads=[bband], writes=[bband])
            S.op("pool", lambda e: e.affine_select(out=band[:], in_=band[:], pattern=[[0, 2], [-1, 256]], compare_op=ALU.is_ge,
                                                   fill=0.0, base=128, channel_multiplier=1), reads=[bband], writes=[bband])
            wr = Ring(nc, st, "b_w", [128, KT, 128], BF16, 4)
            wvr = Ring(nc, st, "b_wv", [128, KT, 128], BF16, 2)
            qr = Ring(nc, st, "b_q", [128, T], BF16, 2)
            kr = Ring(nc, st, "b_k", [128, T], BF16, 2)
            vr = Ring(nc, st, "b_v", [128, 32, 128], BF16, 2)
            accden = self.sb(st, "b_accden", [128, 2, T], F32)
            bacc = Buf()
            tr = Ring(nc, st, "b_t", [128, TT], F32, 4)
            er = Ring(nc, st, "b_e", [128, 2, 256], BF16, 3)
            outr = Ring(nc, st, "b_o", [128, TT], BF16, 2)
            psP = SubRing(banks[0:2])
            psS = SubRing([(self.psbig[q][:, :].rearrange("p (a b) -> p a b", a=2), Buf()) for q in (1, 2)])
            psOD = SubRing([(banks[q][0].rearrange("p (a b) -> p a b", a=2), Buf()) for q in (6, 7)])

            def proj_rot(col0, dst, bdst, d):
                w, bw = self.load_w(wr, wqkv, KT, col0, 128)
                ws, bws = self.load_w(wr, wqks, KT, col0, 128)
                for jt in range(NTT):
                    sl = slice(jt * TT, (jt + 1) * TT)
                    ps, bps = psP.next()
                    ps2, bps2 = psP.next()
                    for (pp, bpp, ww, bww) in ((ps, bps, w, bw), (ps2, bps2, ws, bws)):
                        for k in range(KT):
                            S.op("pe", lambda en: en.matmul(pp[:, :], lhsT=ww[:, k, :], rhs=hn[:, k, sl],
                                                            start=(k == 0), stop=(k == KT - 1)),
                                 reads=[bww] + bhn, writes=[bpp], inc=(k == KT - 1), acc=True)
                    t1, bt1 = tr.next()
                    t2, bt2 = tr.next()
                    S.op("dve", lambda en: en.tensor_tensor(out=t1[:], in0=ps[:, :], in1=cosT[:, sl], op=ALU.mult),
                         reads=[bps, btab], writes=[bt1])
                    S.op("dve", lambda en: en.tensor_tensor(out=t2[:], in0=ps2[:, :], in1=sinT[:, sl], op=ALU.mult),
                         reads=[bps2, btab], writes=[bt2])
                    n = TT // d
                    dv = dst[:].rearrange("p (r l) -> p r l", r=d)[:, :, jt * n:(jt + 1) * n]
                    v1 = t1[:].rearrange("p (j r) -> p r j", r=d)
                    v2 = t2[:].rearrange("p (j r) -> p r j", r=d)
                    S.op("pool", lambda en: en.tensor_tensor(out=dv, in0=v1, in1=v2, op=ALU.add),
                         reads=[bt1, bt2], writes=[bdst])

            for hp in range(KT):
                wv, bwv = self.load_w(wvr, wqkv, KT, 6 * D + hp * 128, 128)
                S.op("pool", lambda e: e.memset(accden[:], 0.0), writes=[bacc])
                for g, d in enumerate((1, 4, 16)):
                    L = T // d
                    nb = L // 128
                    qT, bq = qr.next()
                    kT, bk = kr.next()
                    proj_rot(g * D + hp * 128, qT, bq, d)
                    proj_rot(3 * D + g * D + hp * 128, kT, bk, d)
                    self.cut("cutB")
                    if hp == 0:
                        self.dump("q%d" % g, qT[:], [bq], [128, T], BF16)
                        self.dump("k%d" % g, kT[:], [bk], [128, T], BF16)
                    vt, bvt = vr.next()
                    for n0 in range(0, 32, 4):
                        ps, bps = psP.next()
                        for q4 in range(4):
                            n = n0 + q4
                            r, kb = n // nb, n % nb
                            start = kb * 128 * d + r
                            for k in range(KT):
                                S.op("pe", lambda en: en.matmul(ps[:, q4 * 128:(q4 + 1) * 128],
                                                                lhsT=hn[:, k, start:start + 127 * d + 1:d], rhs=wv[:, k, :],
                                                                start=(k == 0), stop=(k == KT - 1)),
                                     reads=[bwv] + bhn, writes=[bps], inc=(k == KT - 1 and q4 == 3), acc=True)
                        S.op("act", lambda en: en.activation(out=vt[:, n0:n0 + 4, :], in_=ps[:, :].rearrange("p (a b) -> p a b", a=4),
                                                             func=AF.Copy), reads=[bps], writes=[bvt])
                    self.cut("cutC")
                    for r in range(d):
                        pod = bpod = npod = nbpod = None
                        fresh = [True, True]
                        nfresh = [True, True]
                        for kb in range(nb):
                            n = r * nb + kb
                            kcol = r * L + kb * 128
                            nq = 256 if kb + 1 < nb else 128
                            pS, bpS = psS.next()
                            for hh in range(2):
                                S.op("pe", lambda en: en.matmul(pS[:, hh, 0:nq],
                                                                lhsT=kT[64 * hh:64 * hh + 64, kcol:kcol + 128],
                                                                rhs=qT[64 * hh:64 * hh + 64, kcol:kcol + nq],
                                                                start=True, stop=True),
                                     reads=[bk, bq], writes=[bpS], inc=(hh == 1), acc=True)
                            E, bE = er.next()
                            S.op("act", lambda en: en.activation(out=E[:, :, 0:nq], in_=pS[:, :, 0:nq],
                                                                 func=AF.Exp, scale=0.125), reads=[bpS], writes=[bE])
                            S.op("pool", lambda en: en.tensor_tensor(out=E[:, :, 0:nq], in0=E[:, :, 0:nq], in1=band[:, :, 0:nq], op=ALU.mult),
                                 reads=[bE, bband], writes=[bE])
                            if kb == 0:
                                pod, bpod = psOD.next()
                                fresh = [True, True]
                            halves = [(kb, 0)]
                            if kb + 1 < nb:
                                halves.append((kb + 1, 1))
                            for (qb, hf) in halves:
                                if qb % 2 == 0 and hf == 1:
                                    npod, nbpod = psOD.next()
                                    nfresh = [True, True]
                                    tp, tbp, fr = npod, nbpod, nfresh
                                else:
                                    tp, tbp, fr = pod, bpod, fresh
                                c0 = (qb % 2) * 128
                                for hh in range(2):
                                    for od in range(2):
                                        lh = vt[:, n, 64 * hh:64 * hh + 64] if od == 0 else self.ones_bf[:, 0:64]
                                        st_ = fr[hh]
                                        fr[hh] = False
                                        S.op("pe", lambda en: en.matmul(tp[64 * hh:64 * hh + 64, od, c0:c0 + 128], lhsT=lh,
                                                                        rhs=E[:, hh, hf * 128:(hf + 1) * 128], start=st_, stop=True,
                                                                        skip_group_check=True),
                                             reads=[bvt, bE, self.bconst], writes=[tbp], inc=(hh == 1 and od == 1), acc=True)
                            if kb % 2 == 1 or kb == nb - 1:
                                qb0 = (kb // 2) * 2
                                ncols = (kb - qb0 + 1) * 128
                                t0 = qb0 * 128 * d + r
                                asl = accden[:, :, t0:t0 + (ncols - 1) * d + 1:d]
                                S.op("dve", lambda en: en.tensor_tensor(out=asl, in0=pod[:, :, 0:ncols], in1=asl, op=ALU.add),
                                     reads=[bpod, bacc], writes=[bacc])
                                if kb + 1 < nb:
                                    pod, bpod, fresh = npod, nbpod, nfresh
                    self.cut("cutD%d" % g)
                if hp == 0:
                    self.dump("acc", accden[:, 0, :], [bacc], [128, T], F32)
                    self.dump("den", accden[:, 1, :], [bacc], [128, T], F32)
                    self.dump("vt", vt[:], [bvt], [128, 32, 128], BF16)
                for jt in range(NTT):
                    sl = slice(jt * TT, (jt + 1) * TT)
                    S.op("dve", lambda en: en.reciprocal(out=accden[:, 1, sl], in_=accden[:, 1, sl]), reads=[bacc], writes=[bacc])
                    o, bo = outr.next()
                    S.op("dve", lambda en: en.tensor_tensor(out=o[:], in0=accden[:, 0, sl], in1=accden[:, 1, sl], op=ALU.mult),
                         reads=[bacc], writes=[bo])
                    S.dma("sp", self.gT[hp, :, sl], o[:], reads=[bo])
            self.barrier()


    CH = 128
    NCH = T // 128
    LAM = float(np.exp(-0.5))

    def mixer_c(self, i):
        if not hasattr(self, "c_AR"):
            dt = self.nc.dram_tensor
            self.c_AR = dt("c_AR", [KT, 128, 2 * T], BF16, kind="Internal").ap()
            self.c_vtok = dt("c_vtok", [T // 128, 128, D], BF16, kind="Internal").ap()
            self.c_gc = dt("c_gc", [KT, 128, T // 128], F32, kind="Internal").ap()
        self.mixer_c1(i)
        self.mixer_c2(i)

    def mixer_c2(self, i):
        nc, S = self.nc, self.S
        BTd, KTd, gD, bonD = self.s16[1], self.s16[2], self.s16[3], self.s32[0]
        NCH = self.NCH
        with ExitStack() as st:
            bm = Buf()
            MSK = self.sb(st, "c2_msk", [128, 2, 4, 128], BF16)
            LM = self.sb(st, "c2_lm", [128, 2, 128], BF16)
            IDN = self.sb(st, "c2_idn", [128, 2, 128], BF16)
            bonesf = self.sb(st, "c2_bones", [128, 128], F32)
            S.op("pool", lambda e: e.memset(MSK[:], 1.0), writes=[bm])
            for par in range(2):
                S.op("pool", lambda e: e.affine_select(out=MSK[:, :, par::2, :], in_=MSK[:, :, par::2, :],
                                                       pattern=[[0, 2], [0, 2], [1, 128]], compare_op=ALU.is_ge, fill=0.0,
                                                       base=par - 1, channel_multiplier=-1), reads=[bm], writes=[bm])
            S.op("pool", lambda e: e.memset(LM[:], 1.0), writes=[bm])
            S.op("pool", lambda e: e.affine_select(out=LM[:], in_=LM[:], pattern=[[0, 2], [-1, 128]], compare_op=ALU.is_ge, fill=0.0,
                                                   base=-1, channel_multiplier=1), reads=[bm], writes=[bm])
            S.op("pool", lambda e: e.memset(IDN[:], 1.0), writes=[bm])
            S.op("pool", lambda e: e.affine_select(out=IDN[:], in_=IDN[:], pattern=[[0, 2], [-1, 128]], compare_op=ALU.is_equal, fill=0.0,
                                                   base=0, channel_multiplier=1), reads=[bm], writes=[bm])
            S.op("pool", lambda e: e.memset(bonesf[:], 0.0), writes=[bm])
            S.op("pool", lambda e: e.memset(bonesf[0:64, 0:64], 1.0 / 64), writes=[bm])
            S.op("pool", lambda e: e.memset(bonesf[64:128, 64:128], 1.0 / 64), writes=[bm])
            ident = IDN[:, 0, :]
            arr = Ring(nc, st, "c2_ar", [128, NCH, 2, 128], BF16, 2)
            btr = Ring(nc, st, "c2_bt", [128, T], BF16, 2)
            ktr = Ring(nc, st, "c2_kt", [128, T], BF16, 2)
            vtr = Ring(nc, st, "c2_vt", [128, NCH, 128], BF16, 2)
            gcr = Ring(nc, st, "c2_gc", [128, NCH], F32, 2)
            scr = Ring(nc, st, "c2_sc", [128, 2, 4, 128], BF16, 2)
            mlr = Ring(nc, st, "c2_ml", [128, 2, 2, 128], BF16, 3)
            ttr = Ring(nc, st, "c2_tt", [128, 2, 128], BF16, 3)
            tokr = Ring(nc, st, "c2_tok", [128, 2, 128], BF16, 2)
            wur = Ring(nc, st, "c2_wu", [128, 64], BF16, 6)
            Pst = self.sb(st, "c2_pst", [128, 64], F32)
            PG = self.sb(st, "c2_pg", [128, 64], F32)
            Pbf = self.sb(st, "c2_pbf", [128, 64], BF16)
            bP = [Buf(), Buf()]
            bPG = [Buf(), Buf()]
            ysr = Ring(nc, st, "c2_ys", [128, TT], F32, 2)
            gtr = Ring(nc, st, "c2_gt", [128, TT], F32, 8)
            bonr = Ring(nc, st, "c2_bon", [128, TT], F32, 2)
            ggr = Ring(nc, st, "c2_gg", [128, TT], BF16, 2)
            outr = Ring(nc, st, "c2_out", [128, TT], BF16, 2)
            scA = self.psbig[0][:, :].rearrange("p (a b) -> p a b", a=2)
            bscA = Buf()
            scB = self.psbig[1][:, :].rearrange("p (a b) -> p a b", a=2)
            bscB = Buf()
            trp = self.psbig[1][:, 256:512].bitcast(BF16).rearrange("p (a b) -> p a b", a=4)
            btrp = bscB
            mlp = self.psbig[2][:, 0:512].rearrange("p (a b c) -> p a b c", a=2, b=2)
            bmlp = Buf()
            ttp = self.psbig[2][:, 512:768].rearrange("p (a b) -> p a b", a=2)
            bttp = Buf()
            sq = self.psbig[3][:, :].rearrange("p (a b) -> p a b", a=2)
            bsq = [Buf(), Buf()]

            def ldt(e_):
                ar, bar = arr.next()
                S.dma("sp", ar[:].rearrange("p a b c -> p (a b c)"), self.c_AR[e_, :, :], writes=[bar])
                bt_, bbt = btr.next()
                S.dma("sp", bt_[:], BTd[e_, :, :], writes=[bbt])
                kt_, bkt = ktr.next()
                S.dma("sp", kt_[:], KTd[e_, :, :], writes=[bkt])
                vt, bvt = vtr.next()
                S.dma("sp", vt[:], self.c_vtok.rearrange("c p f -> p c f")[:, :, e_ * 128:(e_ + 1) * 128], writes=[bvt])
                gc, bgc = gcr.next()
                S.dma("sp", gc[:], self.c_gc[e_, :, :], writes=[bgc])
                return ar, bar, bt_, bbt, kt_, bkt, vt, bvt, gc, bgc
            nxt = ldt(0)
            for e_ in range(KT):
                ar, bar, bt_, bbt, kt_, bkt, vt, bvt, gc, bgc = nxt
                if e_ + 1 < KT:
                    nxt = ldt(e_ + 1)
                for hh in range(2):
                    P = slice(64 * hh, 64 * hh + 64)
                    S.op("pool", lambda e: e.memset(Pst[P, :], 0.0), writes=[bP[hh]])
                    S.op("pool", lambda e: e.memset(Pbf[P, :], 0.0), writes=[bP[hh]])
                ys = bys = None
                for c in range(NCH):
                    cs_ = slice(c * 128, (c + 1) * 128)
                    for hh in range(2):
                        P = slice(64 * hh, 64 * hh + 64)
                        S.op("pe", lambda e: e.matmul(scA[:, hh, 0:256], lhsT=bt_[P, cs_], rhs=ar[P, c, :, :].rearrange("p a b -> p (a b)"),
                                                      start=True, stop=True), reads=[bbt, bar], writes=[bscA], inc=False, acc=True)
                    for hh in range(2):
                        P = slice(64 * hh, 64 * hh + 64)
                        S.op("pe", lambda e: e.matmul(scA[:, hh, 256:512], lhsT=kt_[P, cs_], rhs=ar[P, c, :, :].rearrange("p a b -> p (a b)"),
                                                      start=True, stop=True), reads=[bkt, bar], writes=[bscA], inc=(hh == 1), acc=True)
                    for hh in range(2):
                        P = slice(64 * hh, 64 * hh + 64)
                        S.op("pe", lambda e: e.matmul(scB[:, hh, 0:128], lhsT=ar[P, c, 0, :], rhs=bt_[P, cs_],
                                                      start=True, stop=True), reads=[bbt, bar], writes=[bscB], inc=(hh == 1), acc=True)
                    S.op("pe", lambda e: e.transpose(trp[:, 0, :], bt_[:, cs_], ident), reads=[bbt, bm], writes=[btrp], inc=False, acc=True)
                    S.op("pe", lambda e: e.transpose(trp[:, 1, :], kt_[:, cs_], ident), reads=[bkt, bm], writes=[btrp], acc=True)
                    SC, bSC = scr.next()
                    S.op("dve", lambda e: e.tensor_tensor(out=SC[:].rearrange("p a b c -> p a (b c)"), in0=scA[:, :, :],
                                                          in1=MSK[:].rearrange("p a b c -> p a (b c)"), op=ALU.mult),
                         reads=[bscA, bm], writes=[bSC])
                    ML, bML = mlr.next()
                    S.op("dve", lambda e: e.tensor_tensor(out=ML[:, :, 1, :], in0=scB[:, :, 0:128], in1=LM[:], op=ALU.mult),
                         reads=[bscB, bm], writes=[bML])
                    S.op("pool", lambda e: e.tensor_copy(out=ML[:, :, 0, :], in_=SC[:, :, 0, :]), reads=[bSC], writes=[bML])
                    tok, btok = tokr.next()
                    S.op("act", lambda e: e.activation(out=tok[:], in_=trp[:, 0:2, :], func=AF.Copy), reads=[btrp], writes=[btok])
                    TTc, bTT = ttr.next()
                    S.op("pool", lambda e: e.tensor_tensor(out=TTc[:], in0=SC[:, :, 0, :], in1=IDN[:], op=ALU.add), reads=[bSC, bm], writes=[bTT])
                    for lev in range(1, 7):
                        MLn, bMLn = mlr.next()
                        for hh in range(2):
                            if lev < 6:
                                S.op("pe", lambda e: e.matmul(mlp[:, hh, 0, :], lhsT=ML[:, hh, 1, :], rhs=ML[:, hh, 0, :], start=True, stop=True),
                                     reads=[bML], writes=[bmlp], inc=False, acc=True)
                            S.op("pe", lambda e: e.matmul(mlp[:, hh, 1, :], lhsT=ML[:, hh, 0, :], rhs=ML[:, hh, 1, :], start=True, stop=True),
                                 reads=[bML], writes=[bmlp], inc=(hh == 1), acc=True)
                        if lev < 6:
                            S.op("act", lambda e: e.activation(out=MLn[:], in_=mlp[:, :, :, :], func=AF.Copy), reads=[bmlp], writes=[bMLn])
                        else:
                            S.op("act", lambda e: e.activation(out=MLn[:, :, 1, :], in_=mlp[:, :, 1, :], func=AF.Copy), reads=[bmlp], writes=[bMLn])
                        for hh in range(2):
                            S.op("pe", lambda e: e.matmul(ttp[:, hh, :], lhsT=MLn[:, hh, 1, :], rhs=TTc[:, hh, :], start=True, stop=True),
                                 reads=[bMLn, bTT], writes=[bttp], inc=(hh == 1), acc=True)
                        TTn, bTTn = ttr.next()
                        S.op("dve", lambda e: e.tensor_tensor(out=TTn[:], in0=ttp[:, :, :], in1=TTc[:], op=ALU.add), reads=[bttp, bTT], writes=[bTTn])
                        ML, bML, TTc, bTT = MLn, bMLn, TTn, bTTn
                    if c % 4 == 0:
                        ys, bys = ysr.next()
                    for hh in range(2):
                        P = slice(64 * hh, 64 * hh + 64)
                        vh = vt[:, c, 64 * hh:64 * hh + 64]
                        S.op("pe", lambda e: e.matmul(sq[:, hh, 0:64], lhsT=SC[:, hh, 2, :], rhs=vh, start=True, stop=False),
                             reads=[bSC, bvt], writes=[bsq[hh]], inc=False, acc=True)
                        S.op("pe", lambda e: e.matmul(sq[:, hh, 0:64], lhsT=ar[P, c, 0, :], rhs=Pbf[P, :], start=False, stop=True),
                             reads=[bar, bP[hh]], writes=[bsq[hh]], acc=True)
                        Wsb, bW = wur.next()
                        S.op("act", lambda e: e.activation(out=Wsb[:], in_=sq[:, hh, 0:64], func=AF.Copy), reads=[bsq[hh]], writes=[bW])
                        S.op("pe", lambda e: e.matmul(sq[:, hh, 64:128], lhsT=TTc[:, hh, :], rhs=Wsb[:], start=True, stop=True),
                             reads=[bTT, bW], writes=[bsq[hh]], acc=True)
                        Usb, bU = wur.next()
                        S.op("act", lambda e: e.activation(out=Usb[:], in_=sq[:, hh, 64:128], func=AF.Copy), reads=[bsq[hh]], writes=[bU])
                        S.op("pe", lambda e: e.matmul(sq[P, hh, 128:256], lhsT=vh, rhs=SC[:, hh, 3, :], start=True, stop=False),
                             reads=[bvt, bSC], writes=[bsq[hh]], inc=False, acc=True)
                        S.op("pe", lambda e: e.matmul(sq[P, hh, 128:256], lhsT=Usb[:], rhs=SC[:, hh, 1, :], start=False, stop=False),
                             reads=[bU, bSC], writes=[bsq[hh]], inc=False, acc=True)
                        S.op("pe", lambda e: e.matmul(sq[P, hh, 128:256], lhsT=Pbf[P, :], rhs=ar[P, c, 1, :], start=False, stop=True),
                             reads=[bP[hh], bar], writes=[bsq[hh]], acc=True)
                        S.op("act", lambda e: e.activation(out=ys[P, (c % 4) * 128:(c % 4 + 1) * 128], in_=sq[P, hh, 128:256], func=AF.Copy),
                             reads=[bsq[hh]], writes=[bys])
                        S.op("dve", lambda e: e.tensor_scalar(out=PG[P, :], in0=Pst[P, :], scalar1=gc[P, c:c + 1], scalar2=None, op0=ALU.mult),
                             reads=[bP[hh], bgc], writes=[bPG[hh]])
                        S.op("pe", lambda e: e.matmul(sq[P, hh, 256:320], lhsT=tok[:, 0, 64 * hh:64 * hh + 64], rhs=Usb[:], start=True, stop=False),
                             reads=[btok, bU], writes=[bsq[hh]], inc=False, acc=True)
                        S.op("pe", lambda e: e.matmul(sq[P, hh, 256:320], lhsT=tok[:, 1, 64 * hh:64 * hh + 64], rhs=vh, start=False, stop=True),
                             reads=[btok, bvt], writes=[bsq[hh]], acc=True)
                        S.op("dve", lambda e: e.scalar_tensor_tensor(out=Pst[P, :], in0=sq[P, hh, 256:320], scalar=gc[P, c:c + 1], in1=PG[P, :],
                                                                     op0=ALU.mult, op1=ALU.add),
                             reads=[bsq[hh], bgc, bPG[hh]], writes=[bP[hh]])
                        S.op("act", lambda e: e.activation(out=Pbf[P, :], in_=Pst[P, :], func=AF.Copy), reads=[bP[hh]], writes=[bP[hh]])
                    if c % 4 == 3:
                        jt = c // 4
                        sl = slice(jt * TT, (jt + 1) * TT)
                        bon, bbon = bonr.next()
                        S.dma("sp", bon[:], bonD[e_, :, sl], writes=[bbon])
                        gg, bgg = ggr.next()
                        S.dma("sp", gg[:], gD[e_, :, sl], writes=[bgg])
                        mean_ps, ex2_ps = scA[:, 0, :], scA[:, 1, :]
                        ysq, bysq = gtr.next()
                        S.op("act", lambda e: e.activation(out=ysq[:], in_=ys[:], func=AF.Square), reads=[bys], writes=[bysq])
                        S.op("pe", lambda e: e.matmul(mean_ps, lhsT=bonesf[:, :], rhs=ys[:], start=True, stop=True),
                             reads=[bm, bys], writes=[bscA], inc=False, acc=True)
                        S.op("pe", lambda e: e.matmul(ex2_ps, lhsT=bonesf[:, :], rhs=ysq[:], start=True, stop=True),
                             reads=[bm, bysq], writes=[bscA], acc=True)
                        msq, bmsq = gtr.next()
                        S.op("act", lambda e: e.activation(out=msq[:], in_=mean_ps, func=AF.Square), reads=[bscA], writes=[bmsq])
                        S.op("dve", lambda e: e.tensor_tensor(out=msq[:], in0=ex2_ps, in1=msq[:], op=ALU.subtract), reads=[bscA, bmsq], writes=[bmsq])
                        S.op("dve", lambda e: e.tensor_scalar(out=msq[:], in0=msq[:], scalar1=0.0, scalar2=None, op0=ALU.max), reads=[bmsq], writes=[bmsq])
                        S.op("act", lambda e: e.activation(out=msq[:], in_=msq[:], func=AF.Sqrt, bias=self.gneps_col), reads=[bmsq, self.bconst], writes=[bmsq])
                        S.op("dve", lambda e: e.reciprocal(out=msq[:], in_=msq[:]), reads=[bmsq], writes=[bmsq])
                        yc, byc = gtr.next()
                        S.op("dve", lambda e: e.tensor_tensor(out=yc[:], in0=mean_ps, in1=ys[:], op=ALU.subtract), reads=[bscA, bys], writes=[byc])
                        S.op("dve", lambda e: e.tensor_tensor(out=yc[:], in0=yc[:], in1=msq[:], op=ALU.mult), reads=[byc, bmsq], writes=[byc])
                        S.op("dve", lambda e: e.tensor_scalar(out=yc[:], in0=yc[:], scalar1=-1.0, scalar2=self.col("c_ln_w", e_), op0=ALU.mult, op1=ALU.mult),
                             reads=[byc, self.bconst], writes=[byc])
                        S.op("dve", lambda e: e.scalar_tensor_tensor(out=yc[:], in0=yc[:], scalar=self.col("c_ln_b", e_), in1=bon[:], op0=ALU.add, op1=ALU.add),
                             reads=[byc, bbon, self.bconst], writes=[byc])
                        o, bo = outr.next()
                        S.op("dve", lambda e: e.tensor_tensor(out=o[:], in0=yc[:], in1=gg[:], op=ALU.mult), reads=[byc, bgg], writes=[bo])
                        S.dma("sp", self.gT[e_, :, sl], o[:], reads=[bo])
            self.barrier()

    def mixer_c1(self, i):
        nc, S = self.nc, self.S
        W = self.W
        LAM = self.LAM
        BTd, KTd, gD, bonD = self.s16[1], self.s16[2], self.s16[3], self.s32[0]
        with ExitStack() as st:
            wbuf = Buf()
            wrkv = [self.sb(st, "c_wrkv%d" % c, [128, KT, D], BF16) for c in range(3)]
            for c in range(3):
                for k in range(KT):
                    S.dma("pool", wrkv[c][:, k, :], W["c_w_rkv"][c, k * 128:(k + 1) * 128, :], writes=[wbuf])
            w1 = self.sb(st, "c_w1", [128, KT, 64], BF16)
            a1 = self.sb(st, "c_a1", [128, KT, 64], BF16)
            g1 = self.sb(st, "c_g1", [128, KT, 128], BF16)
            w2 = self.sb(st, "c_w2", [128, D], BF16)
            a2 = self.sb(st, "c_a2", [128, D], BF16)
            g2 = self.sb(st, "c_g2", [128, D], BF16)
            S.dma("pool", w1[:], W["c_w1"].rearrange("(k p) e -> p k e", p=128), writes=[wbuf])
            S.dma("pool", a1[:], W["c_a1"].rearrange("(k p) e -> p k e", p=128), writes=[wbuf])
            S.dma("pool", g1[:], W["c_g1"].rearrange("(k p) e -> p k e", p=128), writes=[wbuf])
            S.dma("pool", w2[0:64, :], W["c_w2"][:, :], writes=[wbuf])
            S.dma("pool", a2[0:64, :], W["c_a2"][:, :], writes=[wbuf])
            S.dma("pool", g2[:, :], W["c_g2"][:, :], writes=[wbuf])
            cm01 = self.sb(st, "c_cm01", [128, TT], F32)
            bones = self.sb(st, "c_bones", [128, 128], BF16)
            bm = Buf()
            S.op("pool", lambda e: e.memset(cm01[:], 1.0), writes=[bm])
            for q in range(4):
                S.op("pool", lambda e: e.memset(cm01[:, q * 128:q * 128 + 1], 0.0), writes=[bm])
            S.op("pool", lambda e: e.memset(bones[:], 0.0), writes=[bm])
            S.op("pool", lambda e: e.memset(bones[0:64, 0:64], 1.0), writes=[bm])
            S.op("pool", lambda e: e.memset(bones[64:128, 64:128], 1.0), writes=[bm])
            hr = Ring(nc, st, "c_hn", [128, KT, TT + 1], BF16, 2)
            dd = self.sb(st, "c_d", [128, KT, TT], F32)
            bdd = Buf()
            xm = [self.sb(st, "c_xm%d" % c, [128, KT, TT], BF16) for c in range(6)]
            bxm = [Buf() for _ in range(6)]
            lor = [self.sb(st, "c_lor%d" % c, [128, TT], BF16) for c in range(3)]
            blor = [Buf() for _ in range(3)]
            tr = Ring(nc, st, "c_t", [128, TT], F32, 14)
            br = Ring(nc, st, "c_b", [128, TT], BF16, 8)
            arr = Ring(nc, st, "c_ar", [128, 4, 2, 128], BF16, 2)
            vtr = Ring(nc, st, "c_vt", [128, D], BF16, 2)
            gct = self.sb(st, "c_gct", [128, KT, T // 128], F32)
            bgct = Buf()
            P_ = self.psA

            def ldh(jt):
                h, bh = hr.next()
                if jt == 0:
                    S.op("pool", lambda e: e.memset(h[:, :, 0:1], 0.0), writes=[bh])
                    S.dma("sp", h[:, :, 1:TT + 1], self.tview(self.hnT, 0), writes=[bh])
                else:
                    S.dma("sp", h[:, :, :], self.hnT.rearrange("k p t -> p k t")[:, :, jt * TT - 1:(jt + 1) * TT], writes=[bh])
                return h, bh
            nxt = ldh(0)
            for jt in range(NTT):
                sl = slice(jt * TT, (jt + 1) * TT)
                h, bh = nxt
                if jt + 1 < NTT:
                    nxt = ldh(jt + 1)
                for k in range(KT):
                    S.op("dve", lambda e: e.tensor_tensor(out=dd[:, k, :], in0=h[:, k, 0:TT], in1=h[:, k, 1:TT + 1], op=ALU.subtract),
                         reads=[bh], writes=[bdd])
                for c in (3, 4, 5, 2, 0, 1):
                    for k in range(KT):
                        S.op("dve", lambda e: e.scalar_tensor_tensor(out=xm[c][:, k, :], in0=dd[:, k, :], scalar=self.col("c_mu%d" % c, k),
                                                                     in1=h[:, k, 1:TT + 1], op0=ALU.mult, op1=ALU.add),
                             reads=[bdd, bh, self.bconst], writes=[bxm[c]])
                for li, (wt, c, fn, m) in enumerate(((w1, 3, AF.Tanh, 64), (a1, 4, AF.Copy, 64), (g1, 5, AF.Sigmoid, 128))):
                    ps, bps = P_.next()
                    for k in range(KT):
                        S.op("pe", lambda e: e.matmul(ps[0:m, :], lhsT=wt[:, k, :], rhs=xm[c][:, k, :], start=(k == 0), stop=(k == KT - 1)),
                             reads=[wbuf, bxm[c]], writes=[bps], inc=(k == KT - 1), acc=True)
                    S.op("act", lambda e: e.activation(out=lor[li][0:m, :], in_=ps[0:m, :], func=fn), reads=[bps], writes=[blor[li]])
                for blk in range(4):
                    vt, bvt = vtr.next()
                    for half in range(2):
                        ps, bps = P_.next()
                        for k in range(KT):
                            S.op("pe", lambda e: e.matmul(ps[:, :], lhsT=xm[2][:, k, blk * 128:(blk + 1) * 128],
                                                          rhs=wrkv[2][:, k, half * 512:(half + 1) * 512], start=(k == 0), stop=(k == KT - 1)),
                                 reads=[wbuf, bxm[2]], writes=[bps], inc=(k == KT - 1), acc=True)
                        S.op("act", lambda e: e.activation(out=vt[:, half * 512:(half + 1) * 512], in_=ps[:, :], func=AF.Copy),
                             reads=[bps], writes=[bvt])
                    S.dma("sp", self.c_vtok[jt * 4 + blk, :, :], vt[:], reads=[bvt])
                for e_ in range(KT):
                    es = slice(e_ * 128, (e_ + 1) * 128)
                    pss = []
                    for c in range(3):
                        ps, bps = P_.next()
                        for k in range(KT):
                            S.op("pe", lambda e: e.matmul(ps[:, :], lhsT=wrkv[c][:, k, es], rhs=xm[c][:, k, :], start=(k == 0), stop=(k == KT - 1)),
                                 reads=[wbuf, bxm[c]], writes=[bps], inc=(k == KT - 1), acc=True)
                        pss.append((ps, bps))
                    (r_ps, br_), (k_ps, bk_), (v_ps, bv_) = pss
                    wl_ps, bwl = P_.next()
                    S.op("pe", lambda e: e.matmul(wl_ps[:, :], lhsT=w2[0:64, es], rhs=lor[0][0:64, :], start=True, stop=True),
                         reads=[wbuf, blor[0]], writes=[bwl], acc=True)
                    al_ps, bal = P_.next()
                    S.op("pe", lambda e: e.matmul(al_ps[:, :], lhsT=a2[0:64, es], rhs=lor[1][0:64, :], start=True, stop=True),
                         reads=[wbuf, blor[1]], writes=[bal], acc=True)
                    g_ps, bg_ = P_.next()
                    S.op("pe", lambda e: e.matmul(g_ps[:, :], lhsT=g2[:, es], rhs=lor[2][:, :], start=True, stop=True),
                         reads=[wbuf, blor[2]], writes=[bg_], acc=True)
                    gb, bgb = br.next()
                    S.op("act", lambda e: e.activation(out=gb[:], in_=g_ps[:, :], func=AF.Copy), reads=[bg_], writes=[bgb])
                    S.dma("sp", gD[e_, :, sl], gb[:], reads=[bgb])
                    sg, bsg = tr.next()
                    S.op("act", lambda e: e.activation(out=sg[:], in_=wl_ps[:, :], func=AF.Sigmoid, bias=self.col("c_w0", e_)),
                         reads=[bwl, self.bconst], writes=[bsg])
                    al, bal2 = tr.next()
                    S.op("act", lambda e: e.activation(out=al[:], in_=al_ps[:, :], func=AF.Sigmoid, bias=self.col("c_a0", e_)),
                         reads=[bal, self.bconst], writes=[bal2])
                    kf, bkf = tr.next()
                    S.op("act", lambda e: e.activation(out=kf[:], in_=k_ps[:, :], func=AF.Copy), reads=[bk_], writes=[bkf])
                    kk, bkk = tr.next()
                    S.op("dve", lambda e: e.tensor_scalar(out=kk[:], in0=kf[:], scalar1=self.col("c_k_k", e_), scalar2=None, op0=ALU.mult),
                         reads=[bkf, self.bconst], writes=[bkk])
                    k2, bk2 = br.next()
                    S.op("act", lambda e: e.activation(out=k2[:], in_=kk[:], func=AF.Square), reads=[bkk], writes=[bk2])
                    ss_ps, bss = P_.next()
                    S.op("pe", lambda e: e.matmul(ss_ps[:, :], lhsT=bones[:, :], rhs=k2[:], start=True, stop=True),
                         reads=[bm, bk2], writes=[bss], acc=True)
                    rn, brn = tr.next()
                    S.op("act", lambda e: e.activation(out=rn[:], in_=ss_ps[:, :], func=AF.Sqrt), reads=[bss], writes=[brn])
                    S.op("dve", lambda e: e.tensor_scalar(out=rn[:], in0=rn[:], scalar1=1e-12, scalar2=None, op0=ALU.max), reads=[brn], writes=[brn])
                    S.op("dve", lambda e: e.reciprocal(out=rn[:], in_=rn[:]), reads=[brn], writes=[brn])
                    S.op("dve", lambda e: e.tensor_tensor(out=kk[:], in0=kk[:], in1=rn[:], op=ALU.mult), reads=[bkk, brn], writes=[bkk])
                    cs, bcs = tr.next()
                    S.op("dve", lambda e: e.tensor_tensor_scan(out=cs[:], data0=cm01[:], data1=sg[:], initial=0.0, op0=ALU.mult, op1=ALU.add),
                         reads=[bm, bsg], writes=[bcs])
                    csx, bcsx = tr.next()
                    S.op("dve", lambda e: e.tensor_tensor(out=csx[:], in0=cs[:], in1=sg[:], op=ALU.subtract), reads=[bcs, bsg], writes=[bcsx])
                    eG, beG = tr.next()
                    S.op("act", lambda e: e.activation(out=eG[:], in_=cs[:], func=AF.Exp, scale=-LAM), reads=[bcs], writes=[beG])
                    eGi, beGi = tr.next()
                    S.op("act", lambda e: e.activation(out=eGi[:], in_=cs[:], func=AF.Exp, scale=LAM), reads=[bcs], writes=[beGi])
                    S.op("act", lambda e: e.activation(out=csx[:], in_=csx[:], func=AF.Exp, scale=-LAM), reads=[bcsx], writes=[bcsx])
                    ar, bar = arr.next()
                    S.op("dve", lambda e: e.scalar_tensor_tensor(out=ar[:, :, 0, :], in0=kk[:].rearrange("p (c t) -> p c t", c=4), scalar=-1.0,
                                                                 in1=csx[:].rearrange("p (c t) -> p c t", c=4), op0=ALU.mult, op1=ALU.mult),
                         reads=[bkk, bcsx], writes=[bar])
                    S.op("dve", lambda e: e.tensor_tensor(out=kk[:], in0=kk[:], in1=al[:], op=ALU.mult), reads=[bkk, bal2], writes=[bkk])
                    bt_, bbt = br.next()
                    S.op("dve", lambda e: e.tensor_tensor(out=bt_[:], in0=kk[:], in1=eGi[:], op=ALU.mult), reads=[bkk, beGi], writes=[bbt])
                    S.dma("sp", BTd[e_, :, sl], bt_[:], reads=[bbt])
                    S.op("dve", lambda e: e.tensor_scalar(out=al[:], in0=al[:], scalar1=-1.0, scalar2=self.col("c_k_a", e_), op0=ALU.add, op1=ALU.mult),
                         reads=[bal2, self.bconst], writes=[bal2])
                    S.op("dve", lambda e: e.scalar_tensor_tensor(out=kf[:], in0=al[:], scalar=1.0, in1=kf[:], op0=ALU.add, op1=ALU.mult),
                         reads=[bal2, bkf], writes=[bkf])
                    kt_, bkt = br.next()
                    S.op("dve", lambda e: e.tensor_tensor(out=kt_[:], in0=kf[:], in1=eGi[:], op=ALU.mult), reads=[bkf, beGi], writes=[bkt])
                    S.dma("sp", KTd[e_, :, sl], kt_[:], reads=[bkt])
                    S.op("dve", lambda e: e.tensor_tensor(out=ar[:, :, 1, :], in0=r_ps[:, :].rearrange("p (c t) -> p c t", c=4),
                                                          in1=eG[:].rearrange("p (c t) -> p c t", c=4), op=ALU.mult),
                         reads=[br_, beG], writes=[bar])
                    S.dma("sp", self.c_AR[e_, :, jt * 1024:(jt + 1) * 1024], ar[:].rearrange("p a b c -> p (a b c)"), reads=[bar])
                    rk, brk = br.next()
                    S.op("dve", lambda e: e.scalar_tensor_tensor(out=rk[:], in0=r_ps[:, :], scalar=self.col("c_r_k", e_), in1=kf[:],
                                                                 op0=ALU.mult, op1=ALU.mult), reads=[br_, bkf, self.bconst], writes=[brk])
                    rk_ps, brkp = P_.next()
                    S.op("pe", lambda e: e.matmul(rk_ps[:, :], lhsT=bones[:, :], rhs=rk[:], start=True, stop=True),
                         reads=[bm, brk], writes=[brkp], acc=True)
                    vf, bvf = tr.next()
                    S.op("act", lambda e: e.activation(out=vf[:], in_=v_ps[:, :], func=AF.Copy), reads=[bv_], writes=[bvf])
                    S.op("dve", lambda e: e.tensor_tensor(out=vf[:], in0=rk_ps[:, :], in1=vf[:], op=ALU.mult), reads=[brkp, bvf], writes=[bvf])
                    S.dma("sp", bonD[e_, :, sl], vf[:], reads=[bvf])
                    S.op("act", lambda e: e.activation(out=gct[:, e_, jt * 4:(jt + 1) * 4], in_=eG[:, 127:TT:128], func=AF.Copy),
                         reads=[beG], writes=[bgct])
            for e_ in range(KT):
                S.dma("sp", self.c_gc[e_, :, :], gct[:, e_, :], reads=[bgct])
            self.barrier()


def make_in_maps(inp):
    f32 = np.float32
    cols = pack_cols(inp)
    wqkv = np.ascontiguousarray(inp["b_w_qkv"][0], dtype=f32)
    qk = wqkv[:, :6 * D].reshape(D, 6 * 16, 2, 32)
    wqks = np.ascontiguousarray(qk[:, :, ::-1, :].reshape(D, 6 * D))
    shared = {
        "cols": cols,
        "a_w_in": inp["a_w_in"], "a_gate_w": inp["a_gate_w"], "a_w_out": inp["a_w_out"],
        "b_w_qkv": wqkv, "b_w_qks": wqks, "b_w_out": inp["b_w_out"][0],
        "c_w_rkv": inp["c_w_rkv"][0], "c_w1": inp["c_w1"][0], "c_w2": inp["c_w2"][0],
        "c_a1": inp["c_a1"][0], "c_a2": inp["c_a2"][0], "c_g1": inp["c_g1"][0], "c_g2": inp["c_g2"][0],
        "c_w_out": inp["c_w_out"][0],
        "f_w_up": inp["f_w_up"], "f_w_down": inp["f_w_down"],
        "ple_w_proj": inp["ple_w_proj"], "ple_w_gate": inp["ple_w_gate"],
    }
    shared = {k: np.ascontiguousarray(v, dtype=f32) for k, v in shared.items()}
    maps = []
    for c in range(8):
        b = c % NB
        m = dict(shared)
        m["xT"] = np.ascontiguousarray(np.asarray(inp["x"][b], dtype=f32).T).reshape(KT, 128, T)
        m["pT"] = np.ascontiguousarray(np.transpose(np.asarray(inp["p"][:, b], dtype=f32), (0, 2, 1))).reshape(DEPTH, 2, 128, T)
        m["pos"] = np.ascontiguousarray(np.asarray(inp["positions"][b], dtype=np.int32)).reshape(1, T)
        maps.append(m)
    return maps


_PROG_CACHE = {}


def run_prog(inp, layers=DEPTH, dbg=None):
    key = (str(layers), dbg)
    if key not in _PROG_CACHE:
        _PROG_CACHE[key] = Prog(layers, dbg)
    prog = _PROG_CACHE[key]
    maps = make_in_maps(inp)
    used = set(prog.W.keys()) | {"xT", "pT", "pos", "cols"}
    maps = [{k: v for k, v in m.items() if k in used} for m in maps]
    res = run_bass_kernel_spmd(prog.nc, maps, core_ids=list(range(8)))
    out = np.stack([np.asarray(res.results[b]["yT"]).reshape(D, T).T for b in range(NB)])
    return np.ascontiguousarray(out.astype(np.float32)), res


def kernel(**inputs):
    inp = {k: np.asarray(v) for k, v in inputs.items()}
    out, _ = run_prog(inp)
    return out
```

```python
import numpy as np
import concourse.bass as bass
import concourse.mybir as mybir
from concourse.bass_utils import run_bass_kernel_spmd

F32 = mybir.dt.float32
BF16 = mybir.dt.bfloat16
I32 = mybir.dt.int32
AF = mybir.ActivationFunctionType
ALU = mybir.AluOpType
AX = mybir.AxisListType


class Buf:
    __slots__ = ("w", "r", "name")

    def __init__(self, name=""):
        self.w = None
        self.r = {}
        self.name = name


class _Eng:
    def __init__(self, name, eng, sem):
        self.name, self.e, self.sem = name, eng, sem
        self.cnt = 0
        self.seen = {}
        self.pending = []

    def wait(self, ev):
        sem, val = ev
        k = id(sem)
        if self.seen.get(k, 0) >= val:
            return
        self.e.wait_ge(sem, val)
        self.seen[k] = val


class Sched:
    NDMA = 8

    def __init__(self, nc, stack):
        self.nc = nc
        self.engs = {}
        for name, eng in (("pe", nc.tensor), ("act", nc.scalar), ("dve", nc.vector),
                          ("pool", nc.gpsimd), ("sp", nc.sync)):
            sem = stack.enter_context(nc.semaphore("sem_" + name))
            self.engs[name] = _Eng(name, eng, sem)
        self.dma_slots = {}
        for q in ("sp", "pool", "act"):
            sl = []
            for i in range(self.NDMA):
                sem = stack.enter_context(nc.semaphore("dq_%s%d" % (q, i)))
                sl.append([sem, 0])
            self.dma_slots[q] = [sl, 0]
        self.n_inst = 0

    def op(self, engname, fn, reads=(), writes=(), inc=True, acc=False):
        E = self.engs[engname]
        for b in reads:
            if b.w is not None:
                E.wait(b.w)
        for b in writes:
            if b.w is not None and not (acc and b.w[0] is E.sem):
                E.wait(b.w)
            for ev in b.r.values():
                if ev[0] is not E.sem:
                    E.wait(ev)
        inst = fn(E.e)
        self.n_inst += 1
        if inc:
            E.cnt += 1
            inst.then_inc(E.sem, 1)
            ev = (E.sem, E.cnt)
            E.pending.append((reads, writes))
            for rd, wr in E.pending:
                for b in rd:
                    b.r[id(E.sem)] = ev
                for b in wr:
                    b.w = ev
                    b.r = {}
            E.pending = []
            E.seen[id(E.sem)] = max(E.seen.get(id(E.sem), 0), 0)
        else:
            E.pending.append((reads, writes))
        return inst

    def dma(self, q, out, in_, reads=(), writes=()):
        E = self.engs[q]
        slots, idx = self.dma_slots[q]
        slot = slots[idx % self.NDMA]
        self.dma_slots[q][1] = idx + 1
        if slot[1] > 0:
            E.wait((slot[0], slot[1]))
        for b in reads:
            if b.w is not None:
                E.wait(b.w)
        for b in writes:
            if b.w is not None:
                E.wait(b.w)
            for ev in b.r.values():
                E.wait(ev)
        inst = E.e.dma_start(out=out, in_=in_)
        self.n_inst += 1
        slot[1] += 16
        inst.then_inc(slot[0], 16)
        ev = (slot[0], slot[1])
        for b in reads:
            b.r[id(slot[0])] = ev
        for b in writes:
            b.w = ev
            b.r = {}
        return ev

    def wait_all(self, engname, bufs):
        E = self.engs[engname]
        for b in bufs:
            if b.w is not None:
                E.wait(b.w)
            for ev in b.r.values():
                E.wait(ev)


class Ring:
    _uid = [0]

    def __init__(self, nc, stack, name, shape, dtype, n, psum=False):
        self.items = []
        Ring._uid[0] += 1
        name = "%s_u%d_" % (name, Ring._uid[0])
        for i in range(n):
            if psum:
                t = stack.enter_context(nc.psum_tensor("%s%d" % (name, i), shape, dtype))
            else:
                t = stack.enter_context(nc.sbuf_tensor("%s%d" % (name, i), shape, dtype))
            self.items.append((t, Buf("%s%d" % (name, i))))
        self.i = 0

    def next(self):
        it = self.items[self.i % len(self.items)]
        self.i += 1
        return it


class SubRing(Ring):
    def __init__(self, items):
        self.items = list(items)
        self.i = 0


D = 1024
T = 4096
NB = 4
DEPTH = 4
KT = D // 128
TT = 512
NTT = T // TT
FFN = 2816
FT = FFN // 128
PLE = 256
RMS_EPS = 1e-6
GN_EPS = 64e-5
LRU_C = 8.0


class ColPack:
    def __init__(self):
        self.cols = []
        self.idx = {}

    def add(self, name, vec):
        vec = np.ascontiguousarray(vec, dtype=np.float32).reshape(-1)
        assert vec.size % 128 == 0
        n = vec.size // 128
        self.idx[name] = (len(self.cols), n)
        for i in range(n):
            self.cols.append(vec[i * 128:(i + 1) * 128])

    def array(self):
        return np.ascontiguousarray(np.stack(self.cols, axis=1))


def col_layout():
    L = []
    for i in range(DEPTH):
        L += [("norm_mix%d" % i, KT), ("norm_ffn%d" % i, KT), ("norm_ple%d" % i, KT)]
        for k in range(3):
            L.append(("f_conv_w%d_%d" % (i, k), 2 * FT))
        L.append(("f_conv_b%d" % i, 2 * FT))
    L.append(("norm_final", KT))
    for j in range(2):
        for k in range(4):
            L.append(("a_conv_w%d_%d" % (j, k), KT))
        L += [("a_conv_b%d" % j, KT), ("a_gate_b%d" % j, 2 * KT), ("a_lambda%d" % j, KT)]
    for c in range(6):
        L.append(("c_mu%d" % c, KT))
    for nm in ("c_w0", "c_a0", "c_k_k", "c_k_a", "c_r_k", "c_ln_w", "c_ln_b"):
        L.append((nm, KT))
    L.append(("invf", 1))
    L.append(("rsgn", 1))
    off = {}
    o = 0
    for nm, n in L:
        off[nm] = (o, n)
        o += n
    return off, o


COLS, NCOLS = col_layout()


def pack_cols(inp):
    cp = ColPack()
    for i in range(DEPTH):
        cp.add("norm_mix%d" % i, inp["norm_mix"][i])
        cp.add("norm_ffn%d" % i, inp["norm_ffn"][i])
        cp.add("norm_ple%d" % i, inp["norm_ple"][i])
        for k in range(3):
            cp.add("f_conv_w%d_%d" % (i, k), inp["f_conv_w"][i, k])
        cp.add("f_conv_b%d" % i, inp["f_conv_b"][i])
    cp.add("norm_final", inp["norm_final"])
    for j in range(2):
        for k in range(4):
            cp.add("a_conv_w%d_%d" % (j, k), inp["a_conv_w"][j, k])
        cp.add("a_conv_b%d" % j, inp["a_conv_b"][j])
        gb = inp["a_gate_b"][j].reshape(4, 2, 256)
        cp.add("a_gate_b%d" % j, np.concatenate([gb[:, 0].reshape(-1), gb[:, 1].reshape(-1)]))
        cp.add("a_lambda%d" % j, inp["a_lambda"][j])
    for c in range(6):
        cp.add("c_mu%d" % c, inp["c_mu"][0, c])
    for nm in ("c_w0", "c_a0", "c_k_k", "c_k_a", "c_r_k", "c_ln_w", "c_ln_b"):
        cp.add(nm, inp[nm][0])
    invf = (10000.0 ** (-np.arange(0, 64, 2, dtype=np.float32) / 64)).astype(np.float32)
    cp.add("invf", np.tile(invf, 4))
    cp.add("rsgn", np.tile(np.concatenate([-np.ones(32, np.float32), np.ones(32, np.float32)]), 2))
    assert cp.idx == COLS, "col layout mismatch"
    return cp.array()


from contextlib import ExitStack


class _Cut(Exception):
    pass


class Prog:
    def cut(self, name):
        if self.dbg == name:
            raise _Cut()

    def __init__(self, layers=DEPTH, dbg=None):
        self.layers = list(range(layers)) if isinstance(layers, int) else list(layers)
        self.dbg = dbg
        nc = bass.Bass("TRN2", target_bir_lowering=False)
        self.nc = nc
        dt = nc.dram_tensor
        self.xT = dt("xT", [KT, 128, T], F32, kind="ExternalInput").ap()
        self.pT = dt("pT", [DEPTH, 2, 128, T], F32, kind="ExternalInput").ap()
        self.pos = dt("pos", [1, T], I32, kind="ExternalInput").ap()
        self.colsD = dt("cols", [128, NCOLS], F32, kind="ExternalInput").ap()
        class _LazyW(dict):
            def __init__(s2, shapes):
                s2.shapes = shapes
            def __missing__(s2, nm):
                s2[nm] = dt(nm, s2.shapes[nm], F32, kind="ExternalInput").ap()
                return s2[nm]
        shapes = {}
        for nm, shp in (("a_w_in", [2, D, 2 * D]), ("a_gate_w", [2, 4, 256, 512]), ("a_w_out", [2, D, D]),
                        ("b_w_qkv", [D, 7 * D]), ("b_w_qks", [D, 6 * D]), ("b_w_out", [D, D]),
                        ("c_w_rkv", [3, D, D]), ("c_w1", [D, 64]), ("c_w2", [64, D]), ("c_a1", [D, 64]),
                        ("c_a2", [64, D]), ("c_g1", [D, 128]), ("c_g2", [128, D]), ("c_w_out", [D, D]),
                        ("f_w_up", [DEPTH, D, 2 * FFN]), ("f_w_down", [DEPTH, FFN, D]),
                        ("ple_w_proj", [DEPTH, PLE, D]), ("ple_w_gate", [DEPTH, D, D])):
            shapes[nm] = shp
        self.W = _LazyW(shapes)
        self.yT = dt("yT", [KT, 128, T], F32, kind="ExternalOutput").ap()
        self.hT = dt("hT", [KT, 128, T], F32, kind="Internal").ap()
        self.hnT = dt("hnT", [KT, 128, T], BF16, kind="Internal").ap()
        self.gT = dt("gT", [KT, 128, T], BF16, kind="Internal").ap()
        self.actT = dt("actT", [FT, 128, T], BF16, kind="Internal").ap()
        self.s32 = [dt("s32_%d" % i, [KT, 128, T], F32, kind="Internal").ap() for i in range(6)]
        self.s16 = [dt("s16_%d" % i, [KT, 128, T], BF16, kind="Internal").ap() for i in range(8)]
        self.build()

    def col(self, name, k=0, n=1):
        o, sz = COLS[name]
        assert k + n <= sz
        return self.cols[:, o + k:o + k + n]

    def barrier(self):
        S = self.S
        evs = [(E.sem, E.cnt) for E in S.engs.values() if E.cnt > 0]
        for q in S.dma_slots:
            for sl in S.dma_slots[q][0]:
                if sl[1] > 0:
                    evs.append((sl[0], sl[1]))
        for E in S.engs.values():
            assert not E.pending
            for ev in evs:
                E.wait(ev)

    def dump(self, name, ap, bufs, shape, dtype):
        if not self.dbg:
            return
        t = self.nc.dram_tensor("dbg_" + name, list(shape), dtype, kind="Internal").ap()
        self.S.dma("sp", t, ap, reads=list(bufs))

    def sb(self, st, name, shape, dtype):
        Ring._uid[0] += 1
        return st.enter_context(self.nc.sbuf_tensor("%s_u%d" % (name, Ring._uid[0]), shape, dtype))

    def load_w(self, ring, wap2d, kt, e0, ew):
        t, b = ring.next()
        src = wap2d.rearrange("(k p) e -> p k e", p=128)[:, :, e0:e0 + ew]
        self.S.dma("pool", t[:, 0:kt, 0:ew], src, writes=[b])
        return t, b

    def rmsnorm(self, st_rings, h, bh, gain, out, bout):
        S = self.S
        sqr, psr, rsr = st_rings
        ps, bps = psr.next()
        for k in range(KT):
            sq, bsq = sqr.next()
            S.op("act", lambda e: e.activation(out=sq[:], in_=h[:, k, :], func=AF.Square), reads=[bh], writes=[bsq])
            S.op("pe", lambda e: e.matmul(ps[:, :], lhsT=self.ones_bf[:, :], rhs=sq[:], start=(k == 0), stop=(k == KT - 1)),
                 reads=[bsq, self.bconst], writes=[bps], acc=True)
        rs, brs = rsr.next()
        S.op("act", lambda e: e.activation(out=rs[:], in_=ps[:, :], func=AF.Sqrt, scale=1.0 / D, bias=self.eps_col),
             reads=[bps, self.bconst], writes=[brs])
        S.op("dve", lambda e: e.reciprocal(out=rs[:], in_=rs[:]), reads=[brs], writes=[brs])
        for k in range(KT):
            S.op("dve", lambda e: e.scalar_tensor_tensor(out=out[:, k, :], in0=h[:, k, :], scalar=self.col(gain, k),
                                                         in1=rs[:], op0=ALU.mult, op1=ALU.mult),
                 reads=[bh, brs, self.bconst], writes=[bout])

    def norm_rings(self, st, tag):
        nc = self.nc
        return (Ring(nc, st, "sq" + tag, [128, TT], BF16, 3),
                self.psA, Ring(nc, st, "rs" + tag, [128, TT], F32, 2))

    def tview(self, ap3, j):
        return ap3.rearrange("k p t -> p k t")[:, :, j * TT:(j + 1) * TT]

    def build(self):
        nc = self.nc
        with ExitStack() as top:
            S = Sched(nc, top)
            self.S = S
            self.cols = self.sb(top, "cols", [128, NCOLS], F32)
            self.cst = self.sb(top, "cst", [128, 8], F32)
            self.ones_bf = self.sb(top, "ones_bf", [128, 128], BF16)
            self.bconst = Buf("const")
            S.dma("sp", self.cols[:], self.colsD[:, :], writes=[self.bconst])
            S.op("pool", lambda e: e.memset(self.cst[:, 0:1], RMS_EPS), writes=[self.bconst])
            S.op("pool", lambda e: e.memset(self.cst[:, 1:2], 1.0), writes=[self.bconst])
            S.op("pool", lambda e: e.memset(self.cst[:, 2:3], 0.0), writes=[self.bconst])
            S.op("pool", lambda e: e.memset(self.cst[:, 3:4], GN_EPS), writes=[self.bconst])
            S.op("pool", lambda e: e.memset(self.ones_bf[:], 1.0), writes=[self.bconst])
            self.eps_col = self.cst[:, 0:1]
            self.one_col = self.cst[:, 1:2]
            self.zero_col = self.cst[:, 2:3]
            self.gneps_col = self.cst[:, 3:4]
            self.psbig = [top.enter_context(nc.psum_tensor("psbig%d" % q, [128, 2 * TT], F32)) for q in range(4)]
            self.psA = SubRing([(self.psbig[q // 2][:, (q % 2) * TT:(q % 2 + 1) * TT], Buf("ps%d" % q)) for q in range(8)])
            self.barrier()
            self.phase_norm0(self.layers[0])
            h_src = self.xT
            for li, i in enumerate(self.layers):
                kind, j = i % 3, i // 3
                if kind == 0:
                    self.mixer_a(i, j)
                    wout = self.W["a_w_out"][j]
                elif kind == 1:
                    self.mixer_b(i)
                    wout = self.W["b_w_out"]
                else:
                    self.mixer_c(i)
                    wout = self.W["c_w_out"]
                self.mixer_out(i, wout, h_src)
                h_src = self.hT
                self.ffn_up(i)
                last = (li == len(self.layers) - 1)
                self.tail(i, last, None if last else self.layers[li + 1])
            self.barrier()

    def phase_norm0(self, i0):
        nc, S = self.nc, self.S
        with ExitStack() as st:
            hr = Ring(nc, st, "n0h", [128, KT, TT], F32, 2)
            orr = Ring(nc, st, "n0o", [128, KT, TT], BF16, 2)
            rings = self.norm_rings(st, "n0")
            def ld(j):
                h, bh = hr.next()
                S.dma("sp", h[:], self.tview(self.xT, j), writes=[bh])
                return h, bh
            nxt = ld(0)
            for j in range(NTT):
                h, bh = nxt
                if j + 1 < NTT:
                    nxt = ld(j + 1)
                o, bo = orr.next()
                self.rmsnorm(rings, h, bh, "norm_mix%d" % i0, o, bo)
                S.dma("sp", self.tview(self.hnT, j), o[:], reads=[bo])
            self.barrier()

    def mixer_out(self, i, wout, h_src):
        nc, S = self.nc, self.S
        with ExitStack() as st:
            w = self.sb(st, "mo_w", [128, KT, D], BF16)
            bw = Buf()
            for k in range(KT):
                S.dma("pool", w[:, k, :], wout[k * 128:(k + 1) * 128, :], writes=[bw])
            gr = Ring(nc, st, "mo_g", [128, KT, TT], BF16, 2)
            hr = Ring(nc, st, "mo_h", [128, KT, TT], F32, 2)
            orr = Ring(nc, st, "mo_o", [128, KT, TT], BF16, 2)
            rings = self.norm_rings(st, "mo")
            def ld(j):
                g, bg = gr.next()
                S.dma("sp", g[:], self.tview(self.gT, j), writes=[bg])
                h, bh = hr.next()
                S.dma("sp", h[:], self.tview(h_src, j), writes=[bh])
                return g, bg, h, bh
            nxt = ld(0)
            for j in range(NTT):
                g, bg, h, bh = nxt
                if j + 1 < NTT:
                    nxt = ld(j + 1)
                for e in range(KT):
                    ps, bps = self.psA.next()
                    for k in range(KT):
                        S.op("pe", lambda en: en.matmul(ps[:, :], lhsT=w[:, k, e * 128:(e + 1) * 128], rhs=g[:, k, :],
                                                        start=(k == 0), stop=(k == KT - 1)),
                             reads=[bw, bg], writes=[bps], inc=(k == KT - 1), acc=True)
                    S.op("dve", lambda en: en.tensor_tensor(out=h[:, e, :], in0=ps[:, :], in1=h[:, e, :], op=ALU.add),
                         reads=[bps, bh], writes=[bh])
                S.dma("sp", self.tview(self.hT, j), h[:], reads=[bh])
                o, bo = orr.next()
                self.rmsnorm(rings, h, bh, "norm_ffn%d" % i, o, bo)
                S.dma("sp", self.tview(self.hnT, j), o[:], reads=[bo])
            self.barrier()

    def ffn_up(self, i):
        nc, S = self.nc, self.S
        wup = self.W["f_w_up"][i]
        with ExitStack() as st:
            hn = self.sb(st, "fu_hn", [128, KT, T], BF16)
            bhn = [Buf() for _ in range(KT)]
            for k in range(KT):
                S.dma("sp", hn[:, k, :], self.hnT[k, :, :], writes=[bhn[k]])
            wr = Ring(nc, st, "fu_w", [128, KT, 128], BF16, 4)
            stg = [[self.sb(st, "fu_stg%d_%d" % (s, b), [128, 2 + T], F32) for b in range(2)] for s in range(2)]
            bstg = [[[Buf() for _ in range(NTT + 1)] for b in range(2)] for s in range(2)]
            for s in range(2):
                for b in range(2):
                    S.op("pool", lambda e: e.memset(stg[s][b][:, 0:2], 0.0), writes=[bstg[s][b][0]])
            accr = Ring(nc, st, "fu_acc", [128, TT], F32, 4)
            sgr = Ring(nc, st, "fu_sg", [128, TT], F32, 2)
            outr = Ring(nc, st, "fu_out", [128, TT], BF16, 3)
            ldw = lambda f: [self.load_w(wr, wup, KT, (s * FT + f) * 128, 128) for s in range(2)]
            wnxt = ldw(0)
            for f in range(FT):
                wt = wnxt
                if f + 1 < FT:
                    wnxt = ldw(f + 1)
                accs = [None, None]
                for j in range(NTT):
                    for s in range(2):
                        w, bw = wt[s]
                        ps, bps = self.psA.next()
                        for k in range(KT):
                            S.op("pe", lambda en: en.matmul(ps[:, :], lhsT=w[:, k, :], rhs=hn[:, k, j * TT:(j + 1) * TT],
                                                            start=(k == 0), stop=(k == KT - 1)),
                                 reads=[bw] + bhn, writes=[bps], inc=(k == KT - 1), acc=True)
                        sg_t, sg_b = stg[s][f % 2], bstg[s][f % 2]
                        c0 = 2 + j * TT
                        S.op("act", lambda en: en.activation(out=sg_t[:, c0:c0 + TT], in_=ps[:, :], func=AF.Copy),
                             reads=[bps], writes=[sg_b[j + 1]])
                        acc, bacc = accr.next()
                        ci = s * FT + f
                        S.op("act", lambda en: en.activation(out=acc[:], in_=ps[:, :], func=AF.Identity,
                                                             scale=self.col("f_conv_w%d_2" % i, ci),
                                                             bias=self.col("f_conv_b%d" % i, ci)),
                             reads=[bps, self.bconst], writes=[bacc])
                        for kk in (1, 0):
                            S.op("dve", lambda en: en.scalar_tensor_tensor(
                                out=acc[:], in0=sg_t[:, j * TT + kk:j * TT + kk + TT],
                                scalar=self.col("f_conv_w%d_%d" % (i, kk), ci), in1=acc[:], op0=ALU.mult, op1=ALU.add),
                                reads=[sg_b[j], sg_b[j + 1], bacc, self.bconst], writes=[bacc])
                        accs[s] = (acc, bacc)
                    sg, bsg = sgr.next()
                    S.op("act", lambda en: en.activation(out=sg[:], in_=accs[0][0][:], func=AF.Silu),
                         reads=[accs[0][1]], writes=[bsg])
                    o, bo = outr.next()
                    S.op("dve", lambda en: en.tensor_tensor(out=o[:], in0=sg[:], in1=accs[1][0][:], op=ALU.mult),
                         reads=[bsg, accs[1][1]], writes=[bo])
                    S.dma("sp", self.actT[f, :, j * TT:(j + 1) * TT], o[:], reads=[bo])
            self.barrier()

    def tail(self, i, last, inext):
        nc, S = self.nc, self.S
        with ExitStack() as st:
            wd = self.sb(st, "tl_wd", [128, FT, D], BF16)
            wg = self.sb(st, "tl_wg", [128, KT, D], BF16)
            wp = self.sb(st, "tl_wp", [128, 2, D], BF16)
            bwd, bwg, bwp = Buf(), Buf(), Buf()
            for f in range(FT):
                S.dma("pool", wd[:, f, :], self.W["f_w_down"][i, f * 128:(f + 1) * 128, :], writes=[bwd])
            for k in range(KT):
                S.dma("pool", wg[:, k, :], self.W["ple_w_gate"][i, k * 128:(k + 1) * 128, :], writes=[bwg])
            for k in range(2):
                S.dma("pool", wp[:, k, :], self.W["ple_w_proj"][i, k * 128:(k + 1) * 128, :], writes=[bwp])
            ar = Ring(nc, st, "tl_a", [128, FT, TT], BF16, 2)
            hr = Ring(nc, st, "tl_h", [128, KT, TT], F32, 2)
            pr = Ring(nc, st, "tl_p", [128, 2, TT], BF16, 2)
            n3r = Ring(nc, st, "tl_n3", [128, KT, TT], BF16, 1)
            sgr = Ring(nc, st, "tl_sg", [128, TT], F32, 2)
            if last:
                orr = Ring(nc, st, "tl_o", [128, KT, TT], F32, 1)
            else:
                orr = Ring(nc, st, "tl_o", [128, KT, TT], BF16, 2)
            rings = self.norm_rings(st, "tl")
            def ld(j):
                a, ba = ar.next()
                S.dma("sp", a[:], self.tview(self.actT, j), writes=[ba])
                h, bh = hr.next()
                S.dma("sp", h[:], self.tview(self.hT, j), writes=[bh])
                p, bp = pr.next()
                S.dma("pool", p[:], self.tview(self.pT[i], j), writes=[bp])
                return a, ba, h, bh, p, bp
            nxt = ld(0)
            for j in range(NTT):
                a, ba, h, bh, p, bp = nxt
                if j + 1 < NTT:
                    nxt = ld(j + 1)
                for e in range(KT):
                    ps, bps = self.psA.next()
                    for f in range(FT):
                        S.op("pe", lambda en: en.matmul(ps[:, :], lhsT=wd[:, f, e * 128:(e + 1) * 128], rhs=a[:, f, :],
                                                        start=(f == 0), stop=(f == FT - 1)),
                             reads=[bwd, ba], writes=[bps], inc=(f == FT - 1), acc=True)
                    S.op("dve", lambda en: en.tensor_tensor(out=h[:, e, :], in0=ps[:, :], in1=h[:, e, :], op=ALU.add),
                         reads=[bps, bh], writes=[bh])
                n3, bn3 = n3r.next()
                self.rmsnorm(rings, h, bh, "norm_ple%d" % i, n3, bn3)
                for e in range(KT):
                    ps, bps = self.psA.next()
                    for k in range(KT):
                        S.op("pe", lambda en: en.matmul(ps[:, :], lhsT=wg[:, k, e * 128:(e + 1) * 128], rhs=n3[:, k, :],
                                                        start=(k == 0), stop=(k == KT - 1)),
                             reads=[bwg, bn3], writes=[bps], inc=(k == KT - 1), acc=True)
                    sg, bsg = sgr.next()
                    S.op("act", lambda en: en.activation(out=sg[:], in_=ps[:, :], func=AF.Sigmoid),
                         reads=[bps], writes=[bsg])
                    ps2, bps2 = self.psA.next()
                    for k in range(2):
                        S.op("pe", lambda en: en.matmul(ps2[:, :], lhsT=wp[:, k, e * 128:(e + 1) * 128], rhs=p[:, k, :],
                                                        start=(k == 0), stop=(k == 1)),
                             reads=[bwp, bp], writes=[bps2], inc=(k == 1), acc=True)
                    S.op("dve", lambda en: en.tensor_tensor(out=sg[:], in0=ps2[:, :], in1=sg[:], op=ALU.mult),
                         reads=[bps2, bsg], writes=[bsg])
                    S.op("dve", lambda en: en.tensor_tensor(out=h[:, e, :], in0=sg[:], in1=h[:, e, :], op=ALU.add),
                         reads=[bsg, bh], writes=[bh])
                o, bo = orr.next()
                if last:
                    self.rmsnorm(rings, h, bh, "norm_final", o, bo)
                    S.dma("sp", self.tview(self.yT, j), o[:], reads=[bo])
                else:
                    S.dma("sp", self.tview(self.hT, j), h[:], reads=[bh])
                    self.rmsnorm(rings, h, bh, "norm_mix%d" % inext, o, bo)
                    S.dma("sp", self.tview(self.hnT, j), o[:], reads=[bo])
            self.barrier()

    def mixer_a(self, i, j):
        nc, S = self.nc, self.S
        ygT, xcT, xcbT = self.s32[0], self.s32[1], self.s16[0]
        win = self.W["a_w_in"][j]
        with ExitStack() as st:
            hn = self.sb(st, "a1_hn", [128, KT, T], BF16)
            bhn = [Buf() for _ in range(KT)]
            for k in range(KT):
                S.dma("sp", hn[:, k, :], self.hnT[k, :, :], writes=[bhn[k]])
            wr = Ring(nc, st, "a1_w", [128, KT, 128], BF16, 4)
            stg = [self.sb(st, "a1_stg%d" % b, [128, 3 + T], F32) for b in range(2)]
            bstg = [[Buf() for _ in range(NTT + 1)] for b in range(2)]
            for b in range(2):
                S.op("pool", lambda e: e.memset(stg[b][:, 0:3], 0.0), writes=[bstg[b][0]])
            ygr = Ring(nc, st, "a1_yg", [128, TT], F32, 3)
            accr = Ring(nc, st, "a1_acc", [128, TT], F32, 3)
            xbr = Ring(nc, st, "a1_xb", [128, TT], BF16, 3)
            wnxt = self.load_w(wr, win, KT, 0, 128)
            for e in range(2 * KT):
                w, bw = wnxt
                if e + 1 < 2 * KT:
                    wnxt = self.load_w(wr, win, KT, (e + 1) * 128, 128)
                for jt in range(NTT):
                    ps, bps = self.psA.next()
                    for k in range(KT):
                        S.op("pe", lambda en: en.matmul(ps[:, :], lhsT=w[:, k, :], rhs=hn[:, k, jt * TT:(jt + 1) * TT],
                                                        start=(k == 0), stop=(k == KT - 1)),
                             reads=[bw] + bhn, writes=[bps], inc=(k == KT - 1), acc=True)
                    if e < KT:
                        yg, byg = ygr.next()
                        S.op("act", lambda en: en.activation(out=yg[:], in_=ps[:, :], func=AF.Gelu_apprx_tanh),
                             reads=[bps], writes=[byg])
                        S.dma("sp", ygT[e, :, jt * TT:(jt + 1) * TT], yg[:], reads=[byg])
                    else:
                        ft = e - KT
                        sg_t, sg_b = stg[ft % 2], bstg[ft % 2]
                        c0 = 3 + jt * TT
                        S.op("act", lambda en: en.activation(out=sg_t[:, c0:c0 + TT], in_=ps[:, :], func=AF.Copy),
                             reads=[bps], writes=[sg_b[jt + 1]])
                        acc, bacc = accr.next()
                        S.op("act", lambda en: en.activation(out=acc[:], in_=ps[:, :], func=AF.Identity,
                                                             scale=self.col("a_conv_w%d_3" % j, ft),
                                                             bias=self.col("a_conv_b%d" % j, ft)),
                             reads=[bps, self.bconst], writes=[bacc])
                        for kk in (2, 1, 0):
                            S.op("dve", lambda en: en.scalar_tensor_tensor(
                                out=acc[:], in0=sg_t[:, jt * TT + kk:jt * TT + kk + TT],
                                scalar=self.col("a_conv_w%d_%d" % (j, kk), ft), in1=acc[:], op0=ALU.mult, op1=ALU.add),
                                reads=[sg_b[jt], sg_b[jt + 1], bacc, self.bconst], writes=[bacc])
                        xb, bxb = xbr.next()
                        S.op("pool", lambda en: en.tensor_copy(out=xb[:], in_=acc[:]), reads=[bacc], writes=[bxb])
                        S.dma("sp", xcT[ft, :, jt * TT:(jt + 1) * TT], acc[:], reads=[bacc])
                        S.dma("sp", xcbT[ft, :, jt * TT:(jt + 1) * TT], xb[:], reads=[bxb])
            self.barrier()
        with ExitStack() as st:
            cc = self.sb(st, "a2_cc", [128, 5, KT], F32)
            bc = Buf()
            lam = self.col("a_lambda%d" % j, 0, KT)
            ev, l1, t2, mk, ccol = (cc[:, q, :] for q in range(5))
            S.op("act", lambda e: e.activation(out=ev, in_=lam, func=AF.Exp, scale=-1.0), reads=[self.bconst], writes=[bc])
            S.op("act", lambda e: e.activation(out=l1, in_=ev, func=AF.Ln, bias=self.one_col), reads=[bc, self.bconst], writes=[bc])
            S.op("dve", lambda e: e.tensor_scalar(out=t2, in0=ev, scalar1=1.0 / 3.0, scalar2=-0.5, op0=ALU.mult, op1=ALU.add), reads=[bc], writes=[bc])
            S.op("dve", lambda e: e.tensor_tensor(out=t2, in0=t2, in1=ev, op=ALU.mult), reads=[bc], writes=[bc])
            S.op("dve", lambda e: e.tensor_scalar(out=t2, in0=t2, scalar1=1.0, scalar2=None, op0=ALU.add), reads=[bc], writes=[bc])
            S.op("dve", lambda e: e.tensor_tensor(out=t2, in0=t2, in1=ev, op=ALU.mult), reads=[bc], writes=[bc])
            S.op("dve", lambda e: e.tensor_scalar(out=mk, in0=ev, scalar1=0.02, scalar2=None, op0=ALU.is_lt), reads=[bc], writes=[bc])
            S.op("dve", lambda e: e.tensor_tensor(out=t2, in0=t2, in1=l1, op=ALU.subtract), reads=[bc], writes=[bc])
            S.op("dve", lambda e: e.tensor_tensor(out=t2, in0=t2, in1=mk, op=ALU.mult), reads=[bc], writes=[bc])
            S.op("dve", lambda e: e.tensor_tensor(out=l1, in0=l1, in1=t2, op=ALU.add), reads=[bc], writes=[bc])
            S.op("dve", lambda e: e.tensor_scalar(out=ccol, in0=l1, scalar1=-LRU_C, scalar2=None, op0=ALU.mult), reads=[bc], writes=[bc])

            xbr = Ring(nc, st, "a2_xb", [128, 2, T], BF16, 2)
            gwr = Ring(nc, st, "a2_gw", [128, 2, 512], BF16, 2)
            afr = Ring(nc, st, "a2_af", [128, T], F32, 2)
            bfr = Ring(nc, st, "a2_bf", [128, T], F32, 2)
            hfr = Ring(nc, st, "a2_hf", [128, T], F32, 2)
            tr = Ring(nc, st, "a2_t", [128, TT], F32, 6)
            xcr = Ring(nc, st, "a2_xc", [128, TT], F32, 3)
            ygr = Ring(nc, st, "a2_yg", [128, TT], F32, 3)
            gor = Ring(nc, st, "a2_go", [128, TT], BF16, 3)

            def ldh(hd):
                xb, bxb = xbr.next()
                for k in range(2):
                    S.dma("sp", xb[:, k, :], xcbT[2 * hd + k, :, :], writes=[bxb])
                gw, bgw = self.load_w(gwr, self.W["a_gate_w"][j, hd], 2, 0, 512)
                return xb, bxb, gw, bgw
            nxt = ldh(0)
            for hd in range(4):
                xb, bxb, gw, bgw = nxt
                if hd + 1 < 4:
                    nxt = ldh(hd + 1)
                for ft in range(2):
                    ftg = 2 * hd + ft
                    af, _ = afr.next()
                    bf, _ = bfr.next()
                    baf = [Buf() for _ in range(NTT)]
                    bbf = [Buf() for _ in range(NTT)]
                    hf, bhf = hfr.next()
                    if getattr(self, "_a2_prev", None) is not None:
                        pass
                    for jt in range(NTT):
                        sl = slice(jt * TT, (jt + 1) * TT)
                        xc, bxc = xcr.next()
                        S.dma("sp", xc[:], xcT[ftg, :, sl], writes=[bxc])
                        psr, bpsr = self.psA.next()
                        psi, bpsi = self.psA.next()
                        for k in range(2):
                            S.op("pe", lambda en: en.matmul(psr[:, :], lhsT=gw[:, k, ft * 128:(ft + 1) * 128], rhs=xb[:, k, sl],
                                                            start=(k == 0), stop=(k == 1)),
                                 reads=[bgw, bxb], writes=[bpsr], inc=(k == 1), acc=True)
                        for k in range(2):
                            S.op("pe", lambda en: en.matmul(psi[:, :], lhsT=gw[:, k, 256 + ft * 128:256 + (ft + 1) * 128], rhs=xb[:, k, sl],
                                                            start=(k == 0), stop=(k == 1)),
                                 reads=[bgw, bxb], writes=[bpsi], inc=(k == 1), acc=True)
                        r, br = tr.next()
                        S.op("act", lambda en: en.activation(out=r[:], in_=psr[:, :], func=AF.Sigmoid,
                                                             bias=self.col("a_gate_b%d" % j, ftg)),
                             reads=[bpsr, self.bconst], writes=[br])
                        S.op("act", lambda en: en.activation(out=af[:, sl], in_=r[:], func=AF.Exp, scale=cc[:, 4, ftg:ftg + 1]),
                             reads=[br, bc] + self._ring_guard(afr, af), writes=[baf[jt]])
                        ig, big = tr.next()
                        S.op("act", lambda en: en.activation(out=ig[:], in_=psi[:, :], func=AF.Sigmoid,
                                                             bias=self.col("a_gate_b%d" % j, KT + ftg)),
                             reads=[bpsi, self.bconst], writes=[big])
                        sq, bsq = tr.next()
                        S.op("act", lambda en: en.activation(out=sq[:], in_=af[:, sl], func=AF.Square), reads=[baf[jt]], writes=[bsq])
                        S.op("act", lambda en: en.activation(out=sq[:], in_=sq[:], func=AF.Sqrt, scale=-1.0, bias=self.one_col),
                             reads=[bsq, self.bconst], writes=[bsq])
                        S.op("dve", lambda en: en.tensor_tensor(out=ig[:], in0=ig[:], in1=xc[:], op=ALU.mult),
                             reads=[big, bxc], writes=[big])
                        S.op("dve", lambda en: en.tensor_tensor(out=bf[:, sl], in0=ig[:], in1=sq[:], op=ALU.mult),
                             reads=[big, bsq] + self._ring_guard(bfr, bf), writes=[bbf[jt]])
                    S.op("dve", lambda en: en.tensor_tensor_scan(out=hf[:], data0=af[:], data1=bf[:], initial=0.0,
                                                                  op0=ALU.mult, op1=ALU.add),
                         reads=baf + bbf, writes=[bhf])
                    self._ring_set(afr, af, bhf)
                    self._ring_set(bfr, bf, bhf)
                    for jt in range(NTT):
                        sl = slice(jt * TT, (jt + 1) * TT)
                        yg, byg = ygr.next()
                        S.dma("sp", yg[:], ygT[ftg, :, sl], writes=[byg])
                        go, bgo = gor.next()
                        S.op("pool", lambda en: en.tensor_tensor(out=go[:], in0=hf[:, sl], in1=yg[:], op=ALU.mult),
                             reads=[bhf, byg], writes=[bgo])
                        S.dma("sp", self.gT[ftg, :, sl], go[:], reads=[bgo])
            self.barrier()

    def _ring_guard(self, ring, tile):
        d = getattr(self, "_rg", None)
        if d is None:
            d = self._rg = {}
        b = d.get(id(tile))
        return [b] if b is not None else []

    def _ring_set(self, ring, tile, buf):
        if getattr(self, "_rg", None) is None:
            self._rg = {}
        self._rg[id(tile)] = buf


    def mixer_b(self, i):
        with ExitStack() as st:
            try:
                self._mixer_b_body(i, st)
            except _Cut:
                pass
            self.barrier()

    def _mixer_b_body(self, i, st):
        nc, S = self.nc, self.S
        wqkv, wqks = self.W["b_w_qkv"], self.W["b_w_qks"]
        banks = self.psA.items
        if True:
            cosT = self.sb(st, "b_cos", [128, T], BF16)
            sinT = self.sb(st, "b_sin", [128, T], BF16)
            btab = Buf()
            with ExitStack() as s2:
                posi = self.sb(s2, "b_posi", [128, T], I32)
                ang = self.sb(s2, "b_ang", [128, T], F32)
                kk = self.sb(s2, "b_kk", [128, T], F32)
                cst = self.sb(s2, "b_cst", [128, 2], F32)
                bt = Buf()
                S.dma("sp", posi[:], self.pos[0:1, :].to_broadcast([128, T]), writes=[bt])
                S.op("pool", lambda e: e.memset(cst[:, 0:1], float(np.pi / 2)), writes=[bt])
                S.op("dve", lambda e: e.tensor_copy(out=ang[:], in_=posi[:]), reads=[bt], writes=[bt])
                S.op("dve", lambda e: e.tensor_scalar(out=ang[:], in0=ang[:], scalar1=self.col("invf"), scalar2=None, op0=ALU.mult),
                     reads=[bt, self.bconst], writes=[bt])
                MAGIC = 12582912.0
                S.op("dve", lambda e: e.tensor_scalar(out=kk[:], in0=ang[:], scalar1=float(1.0 / (2 * np.pi)), scalar2=MAGIC,
                                                      op0=ALU.mult, op1=ALU.add), reads=[bt], writes=[bt])
                S.op("dve", lambda e: e.tensor_scalar(out=kk[:], in0=kk[:], scalar1=-MAGIC, scalar2=None, op0=ALU.add),
                     reads=[bt], writes=[bt])
                C1 = 6.28125
                C2 = float(2 * np.pi - C1)
                S.op("dve", lambda e: e.scalar_tensor_tensor(out=ang[:], in0=kk[:], scalar=-C1, in1=ang[:], op0=ALU.mult, op1=ALU.add),
                     reads=[bt], writes=[bt])
                S.op("dve", lambda e: e.scalar_tensor_tensor(out=ang[:], in0=kk[:], scalar=-C2, in1=ang[:], op0=ALU.mult, op1=ALU.add),
                     reads=[bt], writes=[bt])
                S.op("dve", lambda e: e.tensor_scalar(out=ang[:], in0=ang[:], scalar1=float(np.pi), scalar2=float(-np.pi),
                                                      op0=ALU.min, op1=ALU.max), reads=[bt], writes=[bt])
                S.op("act", lambda e: e.activation(out=sinT[:], in_=ang[:], func=AF.Sin, scale=self.col("rsgn")),
                     reads=[bt, self.bconst], writes=[btab])
                S.op("dve", lambda e: e.scalar_tensor_tensor(out=kk[:], in0=ang[:], scalar=-1.0, in1=ang[:], op0=ALU.mult, op1=ALU.max), reads=[bt], writes=[bt])
                S.op("act", lambda e: e.activation(out=cosT[:], in_=kk[:], func=AF.Sin, scale=-1.0, bias=cst[:, 0:1]),
                     reads=[bt], writes=[btab])
                self.barrier()
            self.cut("cutA")
            self.dump("cos", cosT[:], [btab], [128, T], BF16)
            self.dump("sin", sinT[:], [btab], [128, T], BF16)
            hn = self.sb(st, "b_hn", [128, KT, T], BF16)
            bhn = [Buf() for _ in range(KT)]
            for k in range(KT):
                S.dma("sp", hn[:, k, :], self.hnT[k, :, :], writes=[bhn[k]])
            band = self.sb(st, "b_band", [128, 2, 256], BF16)
            bband = Buf()
            S.op("pool", lambda e: e.memset(band[:], 1.0), writes=[bband])
            S.op("pool", lambda e: e.affine_select(out=band[:], in_=band[:], pattern=[[0, 2], [1, 256]], compare_op=ALU.is_ge,
                                                   fill=0.0, base=0, channel_multiplier=-1), reads=[bband], writes=[bband])
            S.op("pool", lambda e: e.affine_select(out=band[:], in_=band[:], pattern=[[0, 2], [-1, 256]], compare_op=ALU.is_ge,
                                                   fill=0.0, base=128, channel_multiplier=1), reads=[bband], writes=[bband])
            wr = Ring(nc, st, "b_w", [128, KT, 128], BF16, 4)
            wvr = Ring(nc, st, "b_wv", [128, KT, 128], BF16, 2)
            qr = Ring(nc, st, "b_q", [128, T], BF16, 2)
            kr = Ring(nc, st, "b_k", [128, T], BF16, 2)
            vr = Ring(nc, st, "b_v", [128, 32, 128], BF16, 2)
            accden = self.sb(st, "b_accden", [128, 2, T], F32)
            bacc = Buf()
            tr = Ring(nc, st, "b_t", [128, TT], F32, 4)
            er = Ring(nc, st, "b_e", [128, 2, 256], BF16, 3)
            outr = Ring(nc, st, "b_o", [128, TT], BF16, 2)
            psP = SubRing(banks[0:2])
            psS = SubRing([(self.psbig[q][:, :].rearrange("p (a b) -> p a b", a=2), Buf()) for q in (1, 2)])
            psOD = SubRing([(banks[q][0].rearrange("p (a b) -> p a b", a=2), Buf()) for q in (6, 7)])

            def proj_rot(col0, dst, bdst, d):
                w, bw = self.load_w(wr, wqkv, KT, col0, 128)
                ws, bws = self.load_w(wr, wqks, KT, col0, 128)
                for jt in range(NTT):
                    sl = slice(jt * TT, (jt + 1) * TT)
                    ps, bps = psP.next()
                    ps2, bps2 = psP.next()
                    for (pp, bpp, ww, bww) in ((ps, bps, w, bw), (ps2, bps2, ws, bws)):
                        for k in range(KT):
                            S.op("pe", lambda en: en.matmul(pp[:, :], lhsT=ww[:, k, :], rhs=hn[:, k, sl],
                                                            start=(k == 0), stop=(k == KT - 1)),
                                 reads=[bww] + bhn, writes=[bpp], inc=(k == KT - 1), acc=True)
                    t1, bt1 = tr.next()
                    t2, bt2 = tr.next()
                    S.op("dve", lambda en: en.tensor_tensor(out=t1[:], in0=ps[:, :], in1=cosT[:, sl], op=ALU.mult),
                         reads=[bps, btab], writes=[bt1])
                    S.op("dve", lambda en: en.tensor_tensor(out=t2[:], in0=ps2[:, :], in1=sinT[:, sl], op=ALU.mult),
                         reads=[bps2, btab], writes=[bt2])
                    n = TT // d
                    dv = dst[:].rearrange("p (r l) -> p r l", r=d)[:, :, jt * n:(jt + 1) * n]
                    v1 = t1[:].rearrange("p (j r) -> p r j", r=d)
                    v2 = t2[:].rearrange("p (j r) -> p r j", r=d)
                    S.op("pool", lambda en: en.tensor_tensor(out=dv, in0=v1, in1=v2, op=ALU.add),
                         reads=[bt1, bt2], writes=[bdst])

            for hp in range(KT):
                wv, bwv = self.load_w(wvr, wqkv, KT, 6 * D + hp * 128, 128)
                S.op("pool", lambda e: e.memset(accden[:], 0.0), writes=[bacc])
                for g, d in enumerate((1, 4, 16)):
                    L = T // d
                    nb = L // 128
                    qT, bq = qr.next()
                    kT, bk = kr.next()
                    proj_rot(g * D + hp * 128, qT, bq, d)
                    proj_rot(3 * D + g * D + hp * 128, kT, bk, d)
                    self.cut("cutB")
                    if hp == 0:
                        self.dump("q%d" % g, qT[:], [bq], [128, T], BF16)
                        self.dump("k%d" % g, kT[:], [bk], [128, T], BF16)
                    vt, bvt = vr.next()
                    for n0 in range(0, 32, 4):
                        ps, bps = psP.next()
                        for q4 in range(4):
                            n = n0 + q4
                            r, kb = n // nb, n % nb
                            start = kb * 128 * d + r
                            for k in range(KT):
                                S.op("pe", lambda en: en.matmul(ps[:, q4 * 128:(q4 + 1) * 128],
                                                                lhsT=hn[:, k, start:start + 127 * d + 1:d], rhs=wv[:, k, :],
                                                                start=(k == 0), stop=(k == KT - 1)),
                                     reads=[bwv] + bhn, writes=[bps], inc=(k == KT - 1 and q4 == 3), acc=True)
                        S.op("act", lambda en: en.activation(out=vt[:, n0:n0 + 4, :], in_=ps[:, :].rearrange("p (a b) -> p a b", a=4),
                                                             func=AF.Copy), reads=[bps], writes=[bvt])
                    self.cut("cutC")
                    steps = [(r, kb) for r in range(d) for kb in range(nb)]

                    def stage1(r, kb):
                        kcol = r * L + kb * 128
                        nq = 256 if kb + 1 < nb else 128
                        pS, bpS = psS.next()
                        for hh in range(2):
                            S.op("pe", lambda en: en.matmul(pS[:, hh, 0:nq],
                                                            lhsT=kT[64 * hh:64 * hh + 64, kcol:kcol + 128],
                                                            rhs=qT[64 * hh:64 * hh + 64, kcol:kcol + nq],
                                                            start=True, stop=True),
                                 reads=[bk, bq], writes=[bpS], inc=(hh == 1), acc=True)
                        E, bE = er.next()
                        S.op("act", lambda en: en.activation(out=E[:, :, 0:nq], in_=pS[:, :, 0:nq],
                                                             func=AF.Exp, scale=0.125), reads=[bpS], writes=[bE])
                        meng = "dve" if (kb % 2 == 0) else "pool"
                        S.op(meng, lambda en: en.tensor_tensor(out=E[:, :, 0:nq], in0=E[:, :, 0:nq], in1=band[:, :, 0:nq], op=ALU.mult),
                             reads=[bE, bband], writes=[bE])
                        return E, bE

                    stt = {"pod": None, "bpod": None, "npod": None, "nbpod": None, "fresh": None, "nfresh": None}

                    def stage2(r, kb, E, bE):
                        n = r * nb + kb
                        if kb == 0:
                            stt["pod"], stt["bpod"] = psOD.next()
                            stt["fresh"] = [True, True]
                        halves = [(kb, 0)]
                        if kb + 1 < nb:
                            halves.append((kb + 1, 1))
                        for (qb, hf) in halves:
                            if qb % 2 == 0 and hf == 1:
                                stt["npod"], stt["nbpod"] = psOD.next()
                                stt["nfresh"] = [True, True]
                                tp, tbp, fr = stt["npod"], stt["nbpod"], stt["nfresh"]
                            else:
                                tp, tbp, fr = stt["pod"], stt["bpod"], stt["fresh"]
                            c0 = (qb % 2) * 128
                            for hh in range(2):
                                for od in range(2):
                                    lh = vt[:, n, 64 * hh:64 * hh + 64] if od == 0 else self.ones_bf[:, 0:64]
                                    st_ = fr[hh]
                                    fr[hh] = False
                                    S.op("pe", lambda en: en.matmul(tp[64 * hh:64 * hh + 64, od, c0:c0 + 128], lhsT=lh,
                                                                    rhs=E[:, hh, hf * 128:(hf + 1) * 128], start=st_, stop=True,
                                                                    skip_group_check=True),
                                         reads=[bvt, bE, self.bconst], writes=[tbp], inc=(hh == 1 and od == 1), acc=True)
                        if kb % 2 == 1 or kb == nb - 1:
                            qb0 = (kb // 2) * 2
                            ncols = (kb - qb0 + 1) * 128
                            t0 = qb0 * 128 * d + r
                            asl = accden[:, :, t0:t0 + (ncols - 1) * d + 1:d]
                            pod, bpod = stt["pod"], stt["bpod"]
                            S.op("dve", lambda en: en.tensor_tensor(out=asl, in0=pod[:, :, 0:ncols], in1=asl, op=ALU.add),
                                 reads=[bpod, bacc], writes=[bacc])
                            if kb + 1 < nb:
                                stt["pod"], stt["bpod"], stt["fresh"] = stt["npod"], stt["nbpod"], stt["nfresh"]

                    cur = stage1(*steps[0])
                    for si, (r, kb) in enumerate(steps):
                        nxt_ = stage1(*steps[si + 1]) if si + 1 < len(steps) else None
                        stage2(r, kb, *cur)
                        cur = nxt_
                    self.cut("cutD%d" % g)
                if hp == 0:
                    self.dump("acc", accden[:, 0, :], [bacc], [128, T], F32)
                    self.dump("den", accden[:, 1, :], [bacc], [128, T], F32)
                    self.dump("vt", vt[:], [bvt], [128, 32, 128], BF16)
                for jt in range(NTT):
                    sl = slice(jt * TT, (jt + 1) * TT)
                    S.op("dve", lambda en: en.reciprocal(out=accden[:, 1, sl], in_=accden[:, 1, sl]), reads=[bacc], writes=[bacc])
                    o, bo = outr.next()
                    S.op("dve", lambda en: en.tensor_tensor(out=o[:], in0=accden[:, 0, sl], in1=accden[:, 1, sl], op=ALU.mult),
                         reads=[bacc], writes=[bo])
                    S.dma("sp", self.gT[hp, :, sl], o[:], reads=[bo])
            self.barrier()


    CH = 128
    NCH = T // 128
    LAM = float(np.exp(-0.5))

    def mixer_c(self, i):
        if not hasattr(self, "c_AR"):
            dt = self.nc.dram_tensor
            self.c_AR = dt("c_AR", [KT, 128, 2 * T], BF16, kind="Internal").ap()
            self.c_vtok = dt("c_vtok", [T // 128, 128, D], BF16, kind="Internal").ap()
            self.c_gc = dt("c_gc", [KT, 128, T // 128], F32, kind="Internal").ap()
        self.mixer_c1(i)
        self.mixer_c2(i)

    def mixer_c2(self, i):
        nc, S = self.nc, self.S
        BTd, KTd, gD, bonD = self.s16[1], self.s16[2], self.s16[3], self.s32[0]
        NCH = self.NCH
        with ExitStack() as st:
            bm = Buf()
            MSK = self.sb(st, "c2_msk", [128, 2, 4, 128], BF16)
            LM = self.sb(st, "c2_lm", [128, 2, 128], BF16)
            IDN = self.sb(st, "c2_idn", [128, 2, 128], BF16)
            bonesf = self.sb(st, "c2_bones", [128, 128], F32)
            S.op("pool", lambda e: e.memset(MSK[:], 1.0), writes=[bm])
            for par in range(2):
                S.op("pool", lambda e: e.affine_select(out=MSK[:, :, par::2, :], in_=MSK[:, :, par::2, :],
                                                       pattern=[[0, 2], [0, 2], [1, 128]], compare_op=ALU.is_ge, fill=0.0,
                                                       base=par - 1, channel_multiplier=-1), reads=[bm], writes=[bm])
            S.op("pool", lambda e: e.memset(LM[:], 1.0), writes=[bm])
            S.op("pool", lambda e: e.affine_select(out=LM[:], in_=LM[:], pattern=[[0, 2], [-1, 128]], compare_op=ALU.is_ge, fill=0.0,
                                                   base=-1, channel_multiplier=1), reads=[bm], writes=[bm])
            S.op("pool", lambda e: e.memset(IDN[:], 1.0), writes=[bm])
            S.op("pool", lambda e: e.affine_select(out=IDN[:], in_=IDN[:], pattern=[[0, 2], [-1, 128]], compare_op=ALU.is_equal, fill=0.0,
                                                   base=0, channel_multiplier=1), reads=[bm], writes=[bm])
            S.op("pool", lambda e: e.memset(bonesf[:], 0.0), writes=[bm])
            S.op("pool", lambda e: e.memset(bonesf[0:64, 0:64], 1.0 / 64), writes=[bm])
            S.op("pool", lambda e: e.memset(bonesf[64:128, 64:128], 1.0 / 64), writes=[bm])
            ident = IDN[:, 0, :]
            arr = Ring(nc, st, "c2_ar", [128, NCH, 2, 128], BF16, 2)
            btr = Ring(nc, st, "c2_bt", [128, T], BF16, 2)
            ktr = Ring(nc, st, "c2_kt", [128, T], BF16, 2)
            vtr = Ring(nc, st, "c2_vt", [128, NCH, 128], BF16, 2)
            gcr = Ring(nc, st, "c2_gc", [128, NCH], F32, 2)
            scr = Ring(nc, st, "c2_sc", [128, 2, 4, 128], BF16, 2)
            mlr = Ring(nc, st, "c2_ml", [128, 2, 2, 128], BF16, 3)
            ttr = Ring(nc, st, "c2_tt", [128, 2, 128], BF16, 3)
            tokr = Ring(nc, st, "c2_tok", [128, 2, 128], BF16, 2)
            wur = Ring(nc, st, "c2_wu", [128, 64], BF16, 6)
            Pst = self.sb(st, "c2_pst", [128, 64], F32)
            PG = self.sb(st, "c2_pg", [128, 64], F32)
            Pbf = self.sb(st, "c2_pbf", [128, 64], BF16)
            bP = [Buf(), Buf()]
            bPG = [Buf(), Buf()]
            ysr = Ring(nc, st, "c2_ys", [128, TT], F32, 2)
            gtr = Ring(nc, st, "c2_gt", [128, TT], F32, 8)
            bonr = Ring(nc, st, "c2_bon", [128, TT], F32, 2)
            ggr = Ring(nc, st, "c2_gg", [128, TT], BF16, 2)
            outr = Ring(nc, st, "c2_out", [128, TT], BF16, 2)
            scA = self.psbig[0][:, :].rearrange("p (a b) -> p a b", a=2)
            bscA = Buf()
            scB = self.psbig[1][:, :].rearrange("p (a b) -> p a b", a=2)
            bscB = Buf()
            trp = self.psbig[1][:, 256:512].bitcast(BF16).rearrange("p (a b) -> p a b", a=4)
            btrp = bscB
            mlp = self.psbig[2][:, 0:512].rearrange("p (a b c) -> p a b c", a=2, b=2)
            bmlp = Buf()
            ttp = self.psbig[2][:, 512:768].rearrange("p (a b) -> p a b", a=2)
            bttp = Buf()
            sq = self.psbig[3][:, :].rearrange("p (a b) -> p a b", a=2)
            bsq = [Buf(), Buf()]

            def ldt(e_):
                ar, bar = arr.next()
                S.dma("sp", ar[:].rearrange("p a b c -> p (a b c)"), self.c_AR[e_, :, :], writes=[bar])
                bt_, bbt = btr.next()
                S.dma("sp", bt_[:], BTd[e_, :, :], writes=[bbt])
                kt_, bkt = ktr.next()
                S.dma("sp", kt_[:], KTd[e_, :, :], writes=[bkt])
                vt, bvt = vtr.next()
                S.dma("sp", vt[:], self.c_vtok.rearrange("c p f -> p c f")[:, :, e_ * 128:(e_ + 1) * 128], writes=[bvt])
                gc, bgc = gcr.next()
                S.dma("sp", gc[:], self.c_gc[e_, :, :], writes=[bgc])
                return ar, bar, bt_, bbt, kt_, bkt, vt, bvt, gc, bgc
            nxt = ldt(0)
            for e_ in range(KT):
                ar, bar, bt_, bbt, kt_, bkt, vt, bvt, gc, bgc = nxt
                if e_ + 1 < KT:
                    nxt = ldt(e_ + 1)
                for hh in range(2):
                    P = slice(64 * hh, 64 * hh + 64)
                    S.op("pool", lambda e: e.memset(Pst[P, :], 0.0), writes=[bP[hh]])
                    S.op("pool", lambda e: e.memset(Pbf[P, :], 0.0), writes=[bP[hh]])
                ys = bys = None
                for c in range(NCH):
                    cs_ = slice(c * 128, (c + 1) * 128)
                    for hh in range(2):
                        P = slice(64 * hh, 64 * hh + 64)
                        S.op("pe", lambda e: e.matmul(scA[:, hh, 0:256], lhsT=bt_[P, cs_], rhs=ar[P, c, :, :].rearrange("p a b -> p (a b)"),
                                                      start=True, stop=True), reads=[bbt, bar], writes=[bscA], inc=False, acc=True)
                    for hh in range(2):
                        P = slice(64 * hh, 64 * hh + 64)
                        S.op("pe", lambda e: e.matmul(scA[:, hh, 256:512], lhsT=kt_[P, cs_], rhs=ar[P, c, :, :].rearrange("p a b -> p (a b)"),
                                                      start=True, stop=True), reads=[bkt, bar], writes=[bscA], inc=(hh == 1), acc=True)
                    for hh in range(2):
                        P = slice(64 * hh, 64 * hh + 64)
                        S.op("pe", lambda e: e.matmul(scB[:, hh, 0:128], lhsT=ar[P, c, 0, :], rhs=bt_[P, cs_],
                                                      start=True, stop=True), reads=[bbt, bar], writes=[bscB], inc=(hh == 1), acc=True)
                    S.op("pe", lambda e: e.transpose(trp[:, 0, :], bt_[:, cs_], ident), reads=[bbt, bm], writes=[btrp], inc=False, acc=True)
                    S.op("pe", lambda e: e.transpose(trp[:, 1, :], kt_[:, cs_], ident), reads=[bkt, bm], writes=[btrp], acc=True)
                    SC, bSC = scr.next()
                    S.op("dve", lambda e: e.tensor_tensor(out=SC[:].rearrange("p a b c -> p a (b c)"), in0=scA[:, :, :],
                                                          in1=MSK[:].rearrange("p a b c -> p a (b c)"), op=ALU.mult),
                         reads=[bscA, bm], writes=[bSC])
                    ML, bML = mlr.next()
                    S.op("dve", lambda e: e.tensor_tensor(out=ML[:, :, 1, :], in0=scB[:, :, 0:128], in1=LM[:], op=ALU.mult),
                         reads=[bscB, bm], writes=[bML])
                    S.op("pool", lambda e: e.tensor_copy(out=ML[:, :, 0, :], in_=SC[:, :, 0, :]), reads=[bSC], writes=[bML])
                    tok, btok = tokr.next()
                    S.op("act", lambda e: e.activation(out=tok[:], in_=trp[:, 0:2, :], func=AF.Copy), reads=[btrp], writes=[btok])
                    TTc, bTT = ttr.next()
                    S.op("pool", lambda e: e.tensor_tensor(out=TTc[:], in0=SC[:, :, 0, :], in1=IDN[:], op=ALU.add), reads=[bSC, bm], writes=[bTT])
                    for lev in range(1, 7):
                        MLn, bMLn = mlr.next()
                        for hh in range(2):
                            if lev < 6:
                                S.op("pe", lambda e: e.matmul(mlp[:, hh, 0, :], lhsT=ML[:, hh, 1, :], rhs=ML[:, hh, 0, :], start=True, stop=True),
                                     reads=[bML], writes=[bmlp], inc=False, acc=True)
                            S.op("pe", lambda e: e.matmul(mlp[:, hh, 1, :], lhsT=ML[:, hh, 0, :], rhs=ML[:, hh, 1, :], start=True, stop=True),
                                 reads=[bML], writes=[bmlp], inc=(hh == 1), acc=True)
                        if lev < 6:
                            S.op("act", lambda e: e.activation(out=MLn[:], in_=mlp[:, :, :, :], func=AF.Copy), reads=[bmlp], writes=[bMLn])
                        else:
                            S.op("act", lambda e: e.activation(out=MLn[:, :, 1, :], in_=mlp[:, :, 1, :], func=AF.Copy), reads=[bmlp], writes=[bMLn])
                        for hh in range(2):
                            S.op("pe", lambda e: e.matmul(ttp[:, hh, :], lhsT=MLn[:, hh, 1, :], rhs=TTc[:, hh, :], start=True, stop=True),
                                 reads=[bMLn, bTT], writes=[bttp], inc=(hh == 1), acc=True)
                        TTn, bTTn = ttr.next()
                        S.op("dve", lambda e: e.tensor_tensor(out=TTn[:], in0=ttp[:, :, :], in1=TTc[:], op=ALU.add), reads=[bttp, bTT], writes=[bTTn])
                        ML, bML, TTc, bTT = MLn, bMLn, TTn, bTTn
                    if c % 4 == 0:
                        ys, bys = ysr.next()
                    for hh in range(2):
                        P = slice(64 * hh, 64 * hh + 64)
                        vh = vt[:, c, 64 * hh:64 * hh + 64]
                        S.op("pe", lambda e: e.matmul(sq[:, hh, 0:64], lhsT=SC[:, hh, 2, :], rhs=vh, start=True, stop=False),
                             reads=[bSC, bvt], writes=[bsq[hh]], inc=False, acc=True)
                        S.op("pe", lambda e: e.matmul(sq[:, hh, 0:64], lhsT=ar[P, c, 0, :], rhs=Pbf[P, :], start=False, stop=True),
                             reads=[bar, bP[hh]], writes=[bsq[hh]], acc=True)
                        Wsb, bW = wur.next()
                        S.op("act", lambda e: e.activation(out=Wsb[:], in_=sq[:, hh, 0:64], func=AF.Copy), reads=[bsq[hh]], writes=[bW])
                        S.op("pe", lambda e: e.matmul(sq[:, hh, 64:128], lhsT=TTc[:, hh, :], rhs=Wsb[:], start=True, stop=True),
                             reads=[bTT, bW], writes=[bsq[hh]], acc=True)
                        Usb, bU = wur.next()
                        S.op("act", lambda e: e.activation(out=Usb[:], in_=sq[:, hh, 64:128], func=AF.Copy), reads=[bsq[hh]], writes=[bU])
                        S.op("pe", lambda e: e.matmul(sq[P, hh, 128:256], lhsT=vh, rhs=SC[:, hh, 3, :], start=True, stop=False),
                             reads=[bvt, bSC], writes=[bsq[hh]], inc=False, acc=True)
                        S.op("pe", lambda e: e.matmul(sq[P, hh, 128:256], lhsT=Usb[:], rhs=SC[:, hh, 1, :], start=False, stop=False),
                             reads=[bU, bSC], writes=[bsq[hh]], inc=False, acc=True)
                        S.op("pe", lambda e: e.matmul(sq[P, hh, 128:256], lhsT=Pbf[P, :], rhs=ar[P, c, 1, :], start=False, stop=True),
                             reads=[bP[hh], bar], writes=[bsq[hh]], acc=True)
                        S.op("act", lambda e: e.activation(out=ys[P, (c % 4) * 128:(c % 4 + 1) * 128], in_=sq[P, hh, 128:256], func=AF.Copy),
                             reads=[bsq[hh]], writes=[bys])
                        S.op("dve", lambda e: e.tensor_scalar(out=PG[P, :], in0=Pst[P, :], scalar1=gc[P, c:c + 1], scalar2=None, op0=ALU.mult),
                             reads=[bP[hh], bgc], writes=[bPG[hh]])
                        S.op("pe", lambda e: e.matmul(sq[P, hh, 256:320], lhsT=tok[:, 0, 64 * hh:64 * hh + 64], rhs=Usb[:], start=True, stop=False),
                             reads=[btok, bU], writes=[bsq[hh]], inc=False, acc=True)
                        S.op("pe", lambda e: e.matmul(sq[P, hh, 256:320], lhsT=tok[:, 1, 64 * hh:64 * hh + 64], rhs=vh, start=False, stop=True),
                             reads=[btok, bvt], writes=[bsq[hh]], acc=True)
                        S.op("dve", lambda e: e.scalar_tensor_tensor(out=Pst[P, :], in0=sq[P, hh, 256:320], scalar=gc[P, c:c + 1], in1=PG[P, :],
                                                                     op0=ALU.mult, op1=ALU.add),
                             reads=[bsq[hh], bgc, bPG[hh]], writes=[bP[hh]])
                        S.op("act", lambda e: e.activation(out=Pbf[P, :], in_=Pst[P, :], func=AF.Copy), reads=[bP[hh]], writes=[bP[hh]])
                    if c % 4 == 3:
                        jt = c // 4
                        sl = slice(jt * TT, (jt + 1) * TT)
                        bon, bbon = bonr.next()
                        S.dma("sp", bon[:], bonD[e_, :, sl], writes=[bbon])
                        gg, bgg = ggr.next()
                        S.dma("sp", gg[:], gD[e_, :, sl], writes=[bgg])
                        mean_ps, ex2_ps = scA[:, 0, :], scA[:, 1, :]
                        ysq, bysq = gtr.next()
                        S.op("act", lambda e: e.activation(out=ysq[:], in_=ys[:], func=AF.Square), reads=[bys], writes=[bysq])
                        S.op("pe", lambda e: e.matmul(mean_ps, lhsT=bonesf[:, :], rhs=ys[:], start=True, stop=True),
                             reads=[bm, bys], writes=[bscA], inc=False, acc=True)
                        S.op("pe", lambda e: e.matmul(ex2_ps, lhsT=bonesf[:, :], rhs=ysq[:], start=True, stop=True),
                             reads=[bm, bysq], writes=[bscA], acc=True)
                        msq, bmsq = gtr.next()
                        S.op("act", lambda e: e.activation(out=msq[:], in_=mean_ps, func=AF.Square), reads=[bscA], writes=[bmsq])
                        S.op("dve", lambda e: e.tensor_tensor(out=msq[:], in0=ex2_ps, in1=msq[:], op=ALU.subtract), reads=[bscA, bmsq], writes=[bmsq])
                        S.op("dve", lambda e: e.tensor_scalar(out=msq[:], in0=msq[:], scalar1=0.0, scalar2=None, op0=ALU.max), reads=[bmsq], writes=[bmsq])
                        S.op("act", lambda e: e.activation(out=msq[:], in_=msq[:], func=AF.Sqrt, bias=self.gneps_col), reads=[bmsq, self.bconst], writes=[bmsq])
                        S.op("dve", lambda e: e.reciprocal(out=msq[:], in_=msq[:]), reads=[bmsq], writes=[bmsq])
                        yc, byc = gtr.next()
                        S.op("dve", lambda e: e.tensor_tensor(out=yc[:], in0=mean_ps, in1=ys[:], op=ALU.subtract), reads=[bscA, bys], writes=[byc])
                        S.op("dve", lambda e: e.tensor_tensor(out=yc[:], in0=yc[:], in1=msq[:], op=ALU.mult), reads=[byc, bmsq], writes=[byc])
                        S.op("dve", lambda e: e.tensor_scalar(out=yc[:], in0=yc[:], scalar1=-1.0, scalar2=self.col("c_ln_w", e_), op0=ALU.mult, op1=ALU.mult),
                             reads=[byc, self.bconst], writes=[byc])
                        S.op("dve", lambda e: e.scalar_tensor_tensor(out=yc[:], in0=yc[:], scalar=self.col("c_ln_b", e_), in1=bon[:], op0=ALU.add, op1=ALU.add),
                             reads=[byc, bbon, self.bconst], writes=[byc])
                        o, bo = outr.next()
                        S.op("dve", lambda e: e.tensor_tensor(out=o[:], in0=yc[:], in1=gg[:], op=ALU.mult), reads=[byc, bgg], writes=[bo])
                        S.dma("sp", self.gT[e_, :, sl], o[:], reads=[bo])
            self.barrier()

    def mixer_c1(self, i):
        nc, S = self.nc, self.S
        W = self.W
        LAM = self.LAM
        BTd, KTd, gD, bonD = self.s16[1], self.s16[2], self.s16[3], self.s32[0]
        with ExitStack() as st:
            wbuf = Buf()
            wrkv = [self.sb(st, "c_wrkv%d" % c, [128, KT, D], BF16) for c in range(3)]
            for c in range(3):
                for k in range(KT):
                    S.dma("pool", wrkv[c][:, k, :], W["c_w_rkv"][c, k * 128:(k + 1) * 128, :], writes=[wbuf])
            w1 = self.sb(st, "c_w1", [128, KT, 64], BF16)
            a1 = self.sb(st, "c_a1", [128, KT, 64], BF16)
            g1 = self.sb(st, "c_g1", [128, KT, 128], BF16)
            w2 = self.sb(st, "c_w2", [128, D], BF16)
            a2 = self.sb(st, "c_a2", [128, D], BF16)
            g2 = self.sb(st, "c_g2", [128, D], BF16)
            S.dma("pool", w1[:], W["c_w1"].rearrange("(k p) e -> p k e", p=128), writes=[wbuf])
            S.dma("pool", a1[:], W["c_a1"].rearrange("(k p) e -> p k e", p=128), writes=[wbuf])
            S.dma("pool", g1[:], W["c_g1"].rearrange("(k p) e -> p k e", p=128), writes=[wbuf])
            S.dma("pool", w2[0:64, :], W["c_w2"][:, :], writes=[wbuf])
            S.dma("pool", a2[0:64, :], W["c_a2"][:, :], writes=[wbuf])
            S.dma("pool", g2[:, :], W["c_g2"][:, :], writes=[wbuf])
            cm01 = self.sb(st, "c_cm01", [128, TT], F32)
            bones = self.sb(st, "c_bones", [128, 128], BF16)
            bm = Buf()
            S.op("pool", lambda e: e.memset(cm01[:], 1.0), writes=[bm])
            for q in range(4):
                S.op("pool", lambda e: e.memset(cm01[:, q * 128:q * 128 + 1], 0.0), writes=[bm])
            S.op("pool", lambda e: e.memset(bones[:], 0.0), writes=[bm])
            S.op("pool", lambda e: e.memset(bones[0:64, 0:64], 1.0), writes=[bm])
            S.op("pool", lambda e: e.memset(bones[64:128, 64:128], 1.0), writes=[bm])
            hr = Ring(nc, st, "c_hn", [128, KT, TT + 1], BF16, 2)
            dd = self.sb(st, "c_d", [128, KT, TT], F32)
            bdd = Buf()
            xm = [self.sb(st, "c_xm%d" % c, [128, KT, TT], BF16) for c in range(6)]
            bxm = [Buf() for _ in range(6)]
            lor = [self.sb(st, "c_lor%d" % c, [128, TT], BF16) for c in range(3)]
            blor = [Buf() for _ in range(3)]
            tr = Ring(nc, st, "c_t", [128, TT], F32, 14)
            br = Ring(nc, st, "c_b", [128, TT], BF16, 8)
            arr = Ring(nc, st, "c_ar", [128, 4, 2, 128], BF16, 2)
            vtr = Ring(nc, st, "c_vt", [128, D], BF16, 2)
            gct = self.sb(st, "c_gct", [128, KT, T // 128], F32)
            bgct = Buf()
            P_ = self.psA

            def ldh(jt):
                h, bh = hr.next()
                if jt == 0:
                    S.op("pool", lambda e: e.memset(h[:, :, 0:1], 0.0), writes=[bh])
                    S.dma("sp", h[:, :, 1:TT + 1], self.tview(self.hnT, 0), writes=[bh])
                else:
                    S.dma("sp", h[:, :, :], self.hnT.rearrange("k p t -> p k t")[:, :, jt * TT - 1:(jt + 1) * TT], writes=[bh])
                return h, bh
            nxt = ldh(0)
            for jt in range(NTT):
                sl = slice(jt * TT, (jt + 1) * TT)
                h, bh = nxt
                if jt + 1 < NTT:
                    nxt = ldh(jt + 1)
                for k in range(KT):
                    S.op("dve", lambda e: e.tensor_tensor(out=dd[:, k, :], in0=h[:, k, 0:TT], in1=h[:, k, 1:TT + 1], op=ALU.subtract),
                         reads=[bh], writes=[bdd])
                for c in (3, 4, 5, 2, 0, 1):
                    for k in range(KT):
                        S.op("dve", lambda e: e.scalar_tensor_tensor(out=xm[c][:, k, :], in0=dd[:, k, :], scalar=self.col("c_mu%d" % c, k),
                                                                     in1=h[:, k, 1:TT + 1], op0=ALU.mult, op1=ALU.add),
                             reads=[bdd, bh, self.bconst], writes=[bxm[c]])
                for li, (wt, c, fn, m) in enumerate(((w1, 3, AF.Tanh, 64), (a1, 4, AF.Copy, 64), (g1, 5, AF.Sigmoid, 128))):
                    ps, bps = P_.next()
                    for k in range(KT):
                        S.op("pe", lambda e: e.matmul(ps[0:m, :], lhsT=wt[:, k, :], rhs=xm[c][:, k, :], start=(k == 0), stop=(k == KT - 1)),
                             reads=[wbuf, bxm[c]], writes=[bps], inc=(k == KT - 1), acc=True)
                    S.op("act", lambda e: e.activation(out=lor[li][0:m, :], in_=ps[0:m, :], func=fn), reads=[bps], writes=[blor[li]])
                for blk in range(4):
                    vt, bvt = vtr.next()
                    for half in range(2):
                        ps, bps = P_.next()
                        for k in range(KT):
                            S.op("pe", lambda e: e.matmul(ps[:, :], lhsT=xm[2][:, k, blk * 128:(blk + 1) * 128],
                                                          rhs=wrkv[2][:, k, half * 512:(half + 1) * 512], start=(k == 0), stop=(k == KT - 1)),
                                 reads=[wbuf, bxm[2]], writes=[bps], inc=(k == KT - 1), acc=True)
                        S.op("act", lambda e: e.activation(out=vt[:, half * 512:(half + 1) * 512], in_=ps[:, :], func=AF.Copy),
                             reads=[bps], writes=[bvt])
                    S.dma("sp", self.c_vtok[jt * 4 + blk, :, :], vt[:], reads=[bvt])
                for e_ in range(KT):
                    es = slice(e_ * 128, (e_ + 1) * 128)
                    pss = []
                    for c in range(3):
                        ps, bps = P_.next()
                        for k in range(KT):
                            S.op("pe", lambda e: e.matmul(ps[:, :], lhsT=wrkv[c][:, k, es], rhs=xm[c][:, k, :], start=(k == 0), stop=(k == KT - 1)),
                                 reads=[wbuf, bxm[c]], writes=[bps], inc=(k == KT - 1), acc=True)
                        pss.append((ps, bps))
                    (r_ps, br_), (k_ps, bk_), (v_ps, bv_) = pss
                    wl_ps, bwl = P_.next()
                    S.op("pe", lambda e: e.matmul(wl_ps[:, :], lhsT=w2[0:64, es], rhs=lor[0][0:64, :], start=True, stop=True),
                         reads=[wbuf, blor[0]], writes=[bwl], acc=True)
                    al_ps, bal = P_.next()
                    S.op("pe", lambda e: e.matmul(al_ps[:, :], lhsT=a2[0:64, es], rhs=lor[1][0:64, :], start=True, stop=True),
                         reads=[wbuf, blor[1]], writes=[bal], acc=True)
                    g_ps, bg_ = P_.next()
                    S.op("pe", lambda e: e.matmul(g_ps[:, :], lhsT=g2[:, es], rhs=lor[2][:, :], start=True, stop=True),
                         reads=[wbuf, blor[2]], writes=[bg_], acc=True)
                    gb, bgb = br.next()
                    S.op("act", lambda e: e.activation(out=gb[:], in_=g_ps[:, :], func=AF.Copy), reads=[bg_], writes=[bgb])
                    S.dma("sp", gD[e_, :, sl], gb[:], reads=[bgb])
                    sg, bsg = tr.next()
                    S.op("act", lambda e: e.activation(out=sg[:], in_=wl_ps[:, :], func=AF.Sigmoid, bias=self.col("c_w0", e_)),
                         reads=[bwl, self.bconst], writes=[bsg])
                    al, bal2 = tr.next()
                    S.op("act", lambda e: e.activation(out=al[:], in_=al_ps[:, :], func=AF.Sigmoid, bias=self.col("c_a0", e_)),
                         reads=[bal, self.bconst], writes=[bal2])
                    kf, bkf = tr.next()
                    S.op("act", lambda e: e.activation(out=kf[:], in_=k_ps[:, :], func=AF.Copy), reads=[bk_], writes=[bkf])
                    kk, bkk = tr.next()
                    S.op("dve", lambda e: e.tensor_scalar(out=kk[:], in0=kf[:], scalar1=self.col("c_k_k", e_), scalar2=None, op0=ALU.mult),
                         reads=[bkf, self.bconst], writes=[bkk])
                    k2, bk2 = br.next()
                    S.op("act", lambda e: e.activation(out=k2[:], in_=kk[:], func=AF.Square), reads=[bkk], writes=[bk2])
                    ss_ps, bss = P_.next()
                    S.op("pe", lambda e: e.matmul(ss_ps[:, :], lhsT=bones[:, :], rhs=k2[:], start=True, stop=True),
                         reads=[bm, bk2], writes=[bss], acc=True)
                    rn, brn = tr.next()
                    S.op("act", lambda e: e.activation(out=rn[:], in_=ss_ps[:, :], func=AF.Sqrt), reads=[bss], writes=[brn])
                    S.op("dve", lambda e: e.tensor_scalar(out=rn[:], in0=rn[:], scalar1=1e-12, scalar2=None, op0=ALU.max), reads=[brn], writes=[brn])
                    S.op("dve", lambda e: e.reciprocal(out=rn[:], in_=rn[:]), reads=[brn], writes=[brn])
                    S.op("dve", lambda e: e.tensor_tensor(out=kk[:], in0=kk[:], in1=rn[:], op=ALU.mult), reads=[bkk, brn], writes=[bkk])
                    cs, bcs = tr.next()
                    S.op("dve", lambda e: e.tensor_tensor_scan(out=cs[:], data0=cm01[:], data1=sg[:], initial=0.0, op0=ALU.mult, op1=ALU.add),
                         reads=[bm, bsg], writes=[bcs])
                    csx, bcsx = tr.next()
                    S.op("dve", lambda e: e.tensor_tensor(out=csx[:], in0=cs[:], in1=sg[:], op=ALU.subtract), reads=[bcs, bsg], writes=[bcsx])
                    eG, beG = tr.next()
                    S.op("act", lambda e: e.activation(out=eG[:], in_=cs[:], func=AF.Exp, scale=-LAM), reads=[bcs], writes=[beG])
                    eGi, beGi = tr.next()
                    S.op("act", lambda e: e.activation(out=eGi[:], in_=cs[:], func=AF.Exp, scale=LAM), reads=[bcs], writes=[beGi])
                    S.op("act", lambda e: e.activation(out=csx[:], in_=csx[:], func=AF.Exp, scale=-LAM), reads=[bcsx], writes=[bcsx])
                    ar, bar = arr.next()
                    S.op("dve", lambda e: e.scalar_tensor_tensor(out=ar[:, :, 0, :], in0=kk[:].rearrange("p (c t) -> p c t", c=4), scalar=-1.0,
                                                                 in1=csx[:].rearrange("p (c t) -> p c t", c=4), op0=ALU.mult, op1=ALU.mult),
                         reads=[bkk, bcsx], writes=[bar])
                    S.op("dve", lambda e: e.tensor_tensor(out=kk[:], in0=kk[:], in1=al[:], op=ALU.mult), reads=[bkk, bal2], writes=[bkk])
                    bt_, bbt = br.next()
                    S.op("dve", lambda e: e.tensor_tensor(out=bt_[:], in0=kk[:], in1=eGi[:], op=ALU.mult), reads=[bkk, beGi], writes=[bbt])
                    S.dma("sp", BTd[e_, :, sl], bt_[:], reads=[bbt])
                    S.op("dve", lambda e: e.tensor_scalar(out=al[:], in0=al[:], scalar1=-1.0, scalar2=self.col("c_k_a", e_), op0=ALU.add, op1=ALU.mult),
                         reads=[bal2, self.bconst], writes=[bal2])
                    S.op("dve", lambda e: e.scalar_tensor_tensor(out=kf[:], in0=al[:], scalar=1.0, in1=kf[:], op0=ALU.add, op1=ALU.mult),
                         reads=[bal2, bkf], writes=[bkf])
                    kt_, bkt = br.next()
                    S.op("dve", lambda e: e.tensor_tensor(out=kt_[:], in0=kf[:], in1=eGi[:], op=ALU.mult), reads=[bkf, beGi], writes=[bkt])
                    S.dma("sp", KTd[e_, :, sl], kt_[:], reads=[bkt])
                    S.op("dve", lambda e: e.tensor_tensor(out=ar[:, :, 1, :], in0=r_ps[:, :].rearrange("p (c t) -> p c t", c=4),
                                                          in1=eG[:].rearrange("p (c t) -> p c t", c=4), op=ALU.mult),
                         reads=[br_, beG], writes=[bar])
                    S.dma("sp", self.c_AR[e_, :, jt * 1024:(jt + 1) * 1024], ar[:].rearrange("p a b c -> p (a b c)"), reads=[bar])
                    rk, brk = br.next()
                    S.op("dve", lambda e: e.scalar_tensor_tensor(out=rk[:], in0=r_ps[:, :], scalar=self.col("c_r_k", e_), in1=kf[:],
                                                                 op0=ALU.mult, op1=ALU.mult), reads=[br_, bkf, self.bconst], writes=[brk])
                    rk_ps, brkp = P_.next()
                    S.op("pe", lambda e: e.matmul(rk_ps[:, :], lhsT=bones[:, :], rhs=rk[:], start=True, stop=True),
                         reads=[bm, brk], writes=[brkp], acc=True)
                    vf, bvf = tr.next()
                    S.op("act", lambda e: e.activation(out=vf[:], in_=v_ps[:, :], func=AF.Copy), reads=[bv_], writes=[bvf])
                    S.op("dve", lambda e: e.tensor_tensor(out=vf[:], in0=rk_ps[:, :], in1=vf[:], op=ALU.mult), reads=[brkp, bvf], writes=[bvf])
                    S.dma("sp", bonD[e_, :, sl], vf[:], reads=[bvf])
                    S.op("act", lambda e: e.activation(out=gct[:, e_, jt * 4:(jt + 1) * 4], in_=eG[:, 127:TT:128], func=AF.Copy),
                         reads=[beG], writes=[bgct])
            for e_ in range(KT):
                S.dma("sp", self.c_gc[e_, :, :], gct[:, e_, :], reads=[bgct])
            self.barrier()


def make_in_maps(inp):
    f32 = np.float32
    cols = pack_cols(inp)
    wqkv = np.ascontiguousarray(inp["b_w_qkv"][0], dtype=f32)
    qk = wqkv[:, :6 * D].reshape(D, 6 * 16, 2, 32)
    wqks = np.ascontiguousarray(qk[:, :, ::-1, :].reshape(D, 6 * D))
    shared = {
        "cols": cols,
        "a_w_in": inp["a_w_in"], "a_gate_w": inp["a_gate_w"], "a_w_out": inp["a_w_out"],
        "b_w_qkv": wqkv, "b_w_qks": wqks, "b_w_out": inp["b_w_out"][0],
        "c_w_rkv": inp["c_w_rkv"][0], "c_w1": inp["c_w1"][0], "c_w2": inp["c_w2"][0],
        "c_a1": inp["c_a1"][0], "c_a2": inp["c_a2"][0], "c_g1": inp["c_g1"][0], "c_g2": inp["c_g2"][0],
        "c_w_out": inp["c_w_out"][0],
        "f_w_up": inp["f_w_up"], "f_w_down": inp["f_w_down"],
        "ple_w_proj": inp["ple_w_proj"], "ple_w_gate": inp["ple_w_gate"],
    }
    shared = {k: np.ascontiguousarray(v, dtype=f32) for k, v in shared.items()}
    maps = []
    for c in range(8):
        b = c % NB
        m = dict(shared)
        m["xT"] = np.ascontiguousarray(np.asarray(inp["x"][b], dtype=f32).T).reshape(KT, 128, T)
        m["pT"] = np.ascontiguousarray(np.transpose(np.asarray(inp["p"][:, b], dtype=f32), (0, 2, 1))).reshape(DEPTH, 2, 128, T)
        m["pos"] = np.ascontiguousarray(np.asarray(inp["positions"][b], dtype=np.int32)).reshape(1, T)
        maps.append(m)
    return maps


_PROG_CACHE = {}


def run_prog(inp, layers=DEPTH, dbg=None):
    key = (str(layers), dbg)
    if key not in _PROG_CACHE:
        _PROG_CACHE[key] = Prog(layers, dbg)
    prog = _PROG_CACHE[key]
    maps = make_in_maps(inp)
    used = set(prog.W.keys()) | {"xT", "pT", "pos", "cols"}
    maps = [{k: v for k, v in m.items() if k in used} for m in maps]
    res = run_bass_kernel_spmd(prog.nc, maps, core_ids=list(range(8)))
    out = np.stack([np.asarray(res.results[b]["yT"]).reshape(D, T).T for b in range(NB)])
    return np.ascontiguousarray(out.astype(np.float32)), res


def kernel(**inputs):
    inp = {k: np.asarray(v) for k, v in inputs.items()}
    out, _ = run_prog(inp)
    return out
```

```python
import numpy as np
import concourse.bass as bass
import concourse.mybir as mybir
from concourse.bass_utils import run_bass_kernel_spmd

F32 = mybir.dt.float32
BF16 = mybir.dt.bfloat16
I32 = mybir.dt.int32
AF = mybir.ActivationFunctionType
ALU = mybir.AluOpType
AX = mybir.AxisListType


class Buf:
    __slots__ = ("w", "r", "name")

    def __init__(self, name=""):
        self.w = None
        self.r = {}
        self.name = name


class _Eng:
    def __init__(self, name, eng, sem):
        self.name, self.e, self.sem = name, eng, sem
        self.cnt = 0
        self.seen = {}
        self.pending = []

    def wait(self, ev):
        sem, val = ev
        k = id(sem)
        if self.seen.get(k, 0) >= val:
            return
        self.e.wait_ge(sem, val)
        self.seen[k] = val


class Sched:
    NDMA = 8

    def __init__(self, nc, stack):
        self.nc = nc
        self.engs = {}
        for name, eng in (("pe", nc.tensor), ("act", nc.scalar), ("dve", nc.vector),
                          ("pool", nc.gpsimd), ("sp", nc.sync)):
            sem = stack.enter_context(nc.semaphore("sem_" + name))
            self.engs[name] = _Eng(name, eng, sem)
        self.dma_slots = {}
        for q in ("sp", "pool", "act"):
            sl = []
            for i in range(self.NDMA):
                sem = stack.enter_context(nc.semaphore("dq_%s%d" % (q, i)))
                sl.append([sem, 0])
            self.dma_slots[q] = [sl, 0]
        self.n_inst = 0

    def op(self, engname, fn, reads=(), writes=(), inc=True, acc=False):
        E = self.engs[engname]
        for b in reads:
            if b.w is not None:
                E.wait(b.w)
        for b in writes:
            if b.w is not None and not (acc and b.w[0] is E.sem):
                E.wait(b.w)
            for ev in b.r.values():
                if ev[0] is not E.sem:
                    E.wait(ev)
        inst = fn(E.e)
        self.n_inst += 1
        if inc:
            E.cnt += 1
            inst.then_inc(E.sem, 1)
            ev = (E.sem, E.cnt)
            E.pending.append((reads, writes))
            for rd, wr in E.pending:
                for b in rd:
                    b.r[id(E.sem)] = ev
                for b in wr:
                    b.w = ev
                    b.r = {}
            E.pending = []
            E.seen[id(E.sem)] = max(E.seen.get(id(E.sem), 0), 0)
        else:
            E.pending.append((reads, writes))
        return inst

    def dma(self, q, out, in_, reads=(), writes=()):
        E = self.engs[q]
        slots, idx = self.dma_slots[q]
        slot = slots[idx % self.NDMA]
        self.dma_slots[q][1] = idx + 1
        if slot[1] > 0:
            E.wait((slot[0], slot[1]))
        for b in reads:
            if b.w is not None:
                E.wait(b.w)
        for b in writes:
            if b.w is not None:
                E.wait(b.w)
            for ev in b.r.values():
                E.wait(ev)
        inst = E.e.dma_start(out=out, in_=in_)
        self.n_inst += 1
        slot[1] += 16
        inst.then_inc(slot[0], 16)
        ev = (slot[0], slot[1])
        for b in reads:
            b.r[id(slot[0])] = ev
        for b in writes:
            b.w = ev
            b.r = {}
        return ev

    def wait_all(self, engname, bufs):
        E = self.engs[engname]
        for b in bufs:
            if b.w is not None:
                E.wait(b.w)
            for ev in b.r.values():
                E.wait(ev)


class Ring:
    _uid = [0]

    def __init__(self, nc, stack, name, shape, dtype, n, psum=False):
        self.items = []
        Ring._uid[0] += 1
        name = "%s_u%d_" % (name, Ring._uid[0])
        for i in range(n):
            if psum:
                t = stack.enter_context(nc.psum_tensor("%s%d" % (name, i), shape, dtype))
            else:
                t = stack.enter_context(nc.sbuf_tensor("%s%d" % (name, i), shape, dtype))
            self.items.append((t, Buf("%s%d" % (name, i))))
        self.i = 0

    def next(self):
        it = self.items[self.i % len(self.items)]
        self.i += 1
        return it


class SubRing(Ring):
    def __init__(self, items):
        self.items = list(items)
        self.i = 0


D = 1024
T = 4096
NB = 4
DEPTH = 4
KT = D // 128
TT = 512
NTT = T // TT
FFN = 2816
FT = FFN // 128
PLE = 256
RMS_EPS = 1e-6
GN_EPS = 64e-5
LRU_C = 8.0


class ColPack:
    def __init__(self):
        self.cols = []
        self.idx = {}

    def add(self, name, vec):
        vec = np.ascontiguousarray(vec, dtype=np.float32).reshape(-1)
        assert vec.size % 128 == 0
        n = vec.size // 128
        self.idx[name] = (len(self.cols), n)
        for i in range(n):
            self.cols.append(vec[i * 128:(i + 1) * 128])

    def array(self):
        return np.ascontiguousarray(np.stack(self.cols, axis=1))


def col_layout():
    L = []
    for i in range(DEPTH):
        L += [("norm_mix%d" % i, KT), ("norm_ffn%d" % i, KT), ("norm_ple%d" % i, KT)]
        for k in range(3):
            L.append(("f_conv_w%d_%d" % (i, k), 2 * FT))
        L.append(("f_conv_b%d" % i, 2 * FT))
    L.append(("norm_final", KT))
    for j in range(2):
        for k in range(4):
            L.append(("a_conv_w%d_%d" % (j, k), KT))
        L += [("a_conv_b%d" % j, KT), ("a_gate_b%d" % j, 2 * KT), ("a_lambda%d" % j, KT)]
    for c in range(6):
        L.append(("c_mu%d" % c, KT))
    for nm in ("c_w0", "c_a0", "c_k_k", "c_k_a", "c_r_k", "c_ln_w", "c_ln_b"):
        L.append((nm, KT))
    L.append(("invf", 1))
    L.append(("rsgn", 1))
    off = {}
    o = 0
    for nm, n in L:
        off[nm] = (o, n)
        o += n
    return off, o


COLS, NCOLS = col_layout()


def pack_cols(inp):
    cp = ColPack()
    for i in range(DEPTH):
        cp.add("norm_mix%d" % i, inp["norm_mix"][i])
        cp.add("norm_ffn%d" % i, inp["norm_ffn"][i])
        cp.add("norm_ple%d" % i, inp["norm_ple"][i])
        for k in range(3):
            cp.add("f_conv_w%d_%d" % (i, k), inp["f_conv_w"][i, k])
        cp.add("f_conv_b%d" % i, inp["f_conv_b"][i])
    cp.add("norm_final", inp["norm_final"])
    for j in range(2):
        for k in range(4):
            cp.add("a_conv_w%d_%d" % (j, k), inp["a_conv_w"][j, k])
        cp.add("a_conv_b%d" % j, inp["a_conv_b"][j])
        gb = inp["a_gate_b"][j].reshape(4, 2, 256)
        cp.add("a_gate_b%d" % j, np.concatenate([gb[:, 0].reshape(-1), gb[:, 1].reshape(-1)]))
        cp.add("a_lambda%d" % j, inp["a_lambda"][j])
    for c in range(6):
        cp.add("c_mu%d" % c, inp["c_mu"][0, c])
    for nm in ("c_w0", "c_a0", "c_k_k", "c_k_a", "c_r_k", "c_ln_w", "c_ln_b"):
        cp.add(nm, inp[nm][0])
    invf = (10000.0 ** (-np.arange(0, 64, 2, dtype=np.float32) / 64)).astype(np.float32)
    cp.add("invf", np.tile(invf, 4))
    cp.add("rsgn", np.tile(np.concatenate([-np.ones(32, np.float32), np.ones(32, np.float32)]), 2))
    assert cp.idx == COLS, "col layout mismatch"
    return cp.array()


from contextlib import ExitStack


class _Cut(Exception):
    pass


class Prog:
    def cut(self, name):
        if self.dbg == name:
            raise _Cut()

    def __init__(self, layers=DEPTH, dbg=None):
        self.layers = list(range(layers)) if isinstance(layers, int) else list(layers)
        self.dbg = dbg
        nc = bass.Bass("TRN2", target_bir_lowering=False)
        self.nc = nc
        dt = nc.dram_tensor
        self.xT = dt("xT", [KT, 128, T], F32, kind="ExternalInput").ap()
        self.pT = dt("pT", [DEPTH, 2, 128, T], F32, kind="ExternalInput").ap()
        self.pos = dt("pos", [1, T], I32, kind="ExternalInput").ap()
        self.colsD = dt("cols", [128, NCOLS], F32, kind="ExternalInput").ap()
        class _LazyW(dict):
            def __init__(s2, shapes):
                s2.shapes = shapes
            def __missing__(s2, nm):
                s2[nm] = dt(nm, s2.shapes[nm], F32, kind="ExternalInput").ap()
                return s2[nm]
        shapes = {}
        for nm, shp in (("a_w_in", [2, D, 2 * D]), ("a_gate_w", [2, 4, 256, 512]), ("a_w_out", [2, D, D]),
                        ("b_w_qkv", [D, 7 * D]), ("b_w_qks", [D, 6 * D]), ("b_w_out", [D, D]),
                        ("c_w_rkv", [3, D, D]), ("c_w1", [D, 64]), ("c_w2", [64, D]), ("c_a1", [D, 64]),
                        ("c_a2", [64, D]), ("c_g1", [D, 128]), ("c_g2", [128, D]), ("c_w_out", [D, D]),
                        ("f_w_up", [DEPTH, D, 2 * FFN]), ("f_w_down", [DEPTH, FFN, D]),
                        ("ple_w_proj", [DEPTH, PLE, D]), ("ple_w_gate", [DEPTH, D, D])):
            shapes[nm] = shp
        self.W = _LazyW(shapes)
        self.yT = dt("yT", [KT, 128, T], F32, kind="ExternalOutput").ap()
        self.hT = dt("hT", [KT, 128, T], F32, kind="Internal").ap()
        self.hnT = dt("hnT", [KT, 128, T], BF16, kind="Internal").ap()
        self.gT = dt("gT", [KT, 128, T], BF16, kind="Internal").ap()
        self.actT = dt("actT", [FT, 128, T], BF16, kind="Internal").ap()
        self.s32 = [dt("s32_%d" % i, [KT, 128, T], F32, kind="Internal").ap() for i in range(6)]
        self.s16 = [dt("s16_%d" % i, [KT, 128, T], BF16, kind="Internal").ap() for i in range(8)]
        self.build()

    def col(self, name, k=0, n=1):
        o, sz = COLS[name]
        assert k + n <= sz
        return self.cols[:, o + k:o + k + n]

    def barrier(self):
        S = self.S
        evs = [(E.sem, E.cnt) for E in S.engs.values() if E.cnt > 0]
        for q in S.dma_slots:
            for sl in S.dma_slots[q][0]:
                if sl[1] > 0:
                    evs.append((sl[0], sl[1]))
        for E in S.engs.values():
            assert not E.pending
            for ev in evs:
                E.wait(ev)

    def dump(self, name, ap, bufs, shape, dtype):
        if not self.dbg:
            return
        t = self.nc.dram_tensor("dbg_" + name, list(shape), dtype, kind="Internal").ap()
        self.S.dma("sp", t, ap, reads=list(bufs))

    def sb(self, st, name, shape, dtype):
        Ring._uid[0] += 1
        return st.enter_context(self.nc.sbuf_tensor("%s_u%d" % (name, Ring._uid[0]), shape, dtype))

    def load_w(self, ring, wap2d, kt, e0, ew):
        t, b = ring.next()
        src = wap2d.rearrange("(k p) e -> p k e", p=128)[:, :, e0:e0 + ew]
        self.S.dma("pool", t[:, 0:kt, 0:ew], src, writes=[b])
        return t, b

    def rmsnorm(self, st_rings, h, bh, gain, out, bout):
        S = self.S
        sqr, psr, rsr = st_rings
        ps, bps = psr.next()
        for k in range(KT):
            sq, bsq = sqr.next()
            S.op("act", lambda e: e.activation(out=sq[:], in_=h[:, k, :], func=AF.Square), reads=[bh], writes=[bsq])
            S.op("pe", lambda e: e.matmul(ps[:, :], lhsT=self.ones_bf[:, :], rhs=sq[:], start=(k == 0), stop=(k == KT - 1)),
                 reads=[bsq, self.bconst], writes=[bps], acc=True)
        rs, brs = rsr.next()
        S.op("act", lambda e: e.activation(out=rs[:], in_=ps[:, :], func=AF.Sqrt, scale=1.0 / D, bias=self.eps_col),
             reads=[bps, self.bconst], writes=[brs])
        rs2, brs2 = rsr.next()
        S.op("dve", lambda e: e.reciprocal(out=rs2[:], in_=rs[:]), reads=[brs], writes=[brs2])
        for k in range(KT):
            S.op("dve", lambda e: e.scalar_tensor_tensor(out=out[:, k, :], in0=h[:, k, :], scalar=self.col(gain, k),
                                                         in1=rs2[:], op0=ALU.mult, op1=ALU.mult),
                 reads=[bh, brs2, self.bconst], writes=[bout])

    def norm_rings(self, st, tag):
        nc = self.nc
        return (Ring(nc, st, "sq" + tag, [128, TT], BF16, 3),
                self.psA, Ring(nc, st, "rs" + tag, [128, TT], F32, 4))

    def tview(self, ap3, j):
        return ap3.rearrange("k p t -> p k t")[:, :, j * TT:(j + 1) * TT]

    def build(self):
        nc = self.nc
        with ExitStack() as top:
            S = Sched(nc, top)
            self.S = S
            self.cols = self.sb(top, "cols", [128, NCOLS], F32)
            self.cst = self.sb(top, "cst", [128, 8], F32)
            self.ones_bf = self.sb(top, "ones_bf", [128, 128], BF16)
            self.bconst = Buf("const")
            S.dma("sp", self.cols[:], self.colsD[:, :], writes=[self.bconst])
            S.op("pool", lambda e: e.memset(self.cst[:, 0:1], RMS_EPS), writes=[self.bconst])
            S.op("pool", lambda e: e.memset(self.cst[:, 1:2], 1.0), writes=[self.bconst])
            S.op("pool", lambda e: e.memset(self.cst[:, 2:3], 0.0), writes=[self.bconst])
            S.op("pool", lambda e: e.memset(self.cst[:, 3:4], GN_EPS), writes=[self.bconst])
            S.op("pool", lambda e: e.memset(self.ones_bf[:], 1.0), writes=[self.bconst])
            self.eps_col = self.cst[:, 0:1]
            self.one_col = self.cst[:, 1:2]
            self.zero_col = self.cst[:, 2:3]
            self.gneps_col = self.cst[:, 3:4]
            self.psbig = [top.enter_context(nc.psum_tensor("psbig%d" % q, [128, 2 * TT], F32)) for q in range(4)]
            self.psA = SubRing([(self.psbig[q // 2][:, (q % 2) * TT:(q % 2 + 1) * TT], Buf("ps%d" % q)) for q in range(8)])
            self.barrier()
            self.phase_norm0(self.layers[0])
            h_src = self.xT
            for li, i in enumerate(self.layers):
                kind, j = i % 3, i // 3
                if kind == 0:
                    self.mixer_a(i, j)
                    wout = self.W["a_w_out"][j]
                elif kind == 1:
                    self.mixer_b(i)
                    wout = self.W["b_w_out"]
                else:
                    self.mixer_c(i)
                    wout = self.W["c_w_out"]
                self.mixer_out(i, wout, h_src)
                h_src = self.hT
                self.ffn_up(i)
                last = (li == len(self.layers) - 1)
                self.tail(i, last, None if last else self.layers[li + 1])
            self.barrier()

    def phase_norm0(self, i0):
        nc, S = self.nc, self.S
        with ExitStack() as st:
            hr = Ring(nc, st, "n0h", [128, KT, TT], F32, 2)
            orr = Ring(nc, st, "n0o", [128, KT, TT], BF16, 2)
            rings = self.norm_rings(st, "n0")
            def ld(j):
                h, bh = hr.next()
                S.dma("sp", h[:], self.tview(self.xT, j), writes=[bh])
                return h, bh
            nxt = ld(0)
            for j in range(NTT):
                h, bh = nxt
                if j + 1 < NTT:
                    nxt = ld(j + 1)
                o, bo = orr.next()
                self.rmsnorm(rings, h, bh, "norm_mix%d" % i0, o, bo)
                S.dma("sp", self.tview(self.hnT, j), o[:], reads=[bo])
            self.barrier()

    def mixer_out(self, i, wout, h_src):
        nc, S = self.nc, self.S
        with ExitStack() as st:
            w = self.sb(st, "mo_w", [128, KT, D], BF16)
            bw = Buf()
            for k in range(KT):
                S.dma("pool", w[:, k, :], wout[k * 128:(k + 1) * 128, :], writes=[bw])
            gr = Ring(nc, st, "mo_g", [128, KT, TT], BF16, 2)
            hr = Ring(nc, st, "mo_h", [128, KT, TT], F32, 2)
            orr = Ring(nc, st, "mo_o", [128, KT, TT], BF16, 2)
            rings = self.norm_rings(st, "mo")
            def ld(j):
                g, bg = gr.next()
                S.dma("sp", g[:], self.tview(self.gT, j), writes=[bg])
                h, bh = hr.next()
                S.dma("sp", h[:], self.tview(h_src, j), writes=[bh])
                return g, bg, h, bh
            nxt = ld(0)
            for j in range(NTT):
                g, bg, h, bh = nxt
                if j + 1 < NTT:
                    nxt = ld(j + 1)
                for e in range(KT):
                    ps, bps = self.psA.next()
                    for k in range(KT):
                        S.op("pe", lambda en: en.matmul(ps[:, :], lhsT=w[:, k, e * 128:(e + 1) * 128], rhs=g[:, k, :],
                                                        start=(k == 0), stop=(k == KT - 1)),
                             reads=[bw, bg], writes=[bps], inc=(k == KT - 1), acc=True)
                    S.op("dve", lambda en: en.tensor_tensor(out=h[:, e, :], in0=ps[:, :], in1=h[:, e, :], op=ALU.add),
                         reads=[bps, bh], writes=[bh])
                S.dma("sp", self.tview(self.hT, j), h[:], reads=[bh])
                o, bo = orr.next()
                self.rmsnorm(rings, h, bh, "norm_ffn%d" % i, o, bo)
                S.dma("sp", self.tview(self.hnT, j), o[:], reads=[bo])
            self.barrier()

    def ffn_up(self, i):
        nc, S = self.nc, self.S
        wup = self.W["f_w_up"][i]
        with ExitStack() as st:
            hn = self.sb(st, "fu_hn", [128, KT, T], BF16)
            bhn = [Buf() for _ in range(KT)]
            for k in range(KT):
                S.dma("sp", hn[:, k, :], self.hnT[k, :, :], writes=[bhn[k]])
            wr = Ring(nc, st, "fu_w", [128, KT, 128], BF16, 4)
            stg = [[self.sb(st, "fu_stg%d_%d" % (s, b), [128, 2 + T], F32) for b in range(2)] for s in range(2)]
            bstg = [[[Buf() for _ in range(NTT + 1)] for b in range(2)] for s in range(2)]
            for s in range(2):
                for b in range(2):
                    S.op("pool", lambda e: e.memset(stg[s][b][:, 0:2], 0.0), writes=[bstg[s][b][0]])
            accr = Ring(nc, st, "fu_acc", [128, TT], F32, 4)
            sgr = Ring(nc, st, "fu_sg", [128, TT], F32, 2)
            outr = Ring(nc, st, "fu_out", [128, TT], BF16, 3)
            ldw = lambda f: [self.load_w(wr, wup, KT, (s * FT + f) * 128, 128) for s in range(2)]
            wnxt = ldw(0)
            for f in range(FT):
                wt = wnxt
                if f + 1 < FT:
                    wnxt = ldw(f + 1)
                accs = [None, None]
                for j in range(NTT):
                    for s in range(2):
                        w, bw = wt[s]
                        ps, bps = self.psA.next()
                        for k in range(KT):
                            S.op("pe", lambda en: en.matmul(ps[:, :], lhsT=w[:, k, :], rhs=hn[:, k, j * TT:(j + 1) * TT],
                                                            start=(k == 0), stop=(k == KT - 1)),
                                 reads=[bw] + bhn, writes=[bps], inc=(k == KT - 1), acc=True)
                        sg_t, sg_b = stg[s][f % 2], bstg[s][f % 2]
                        c0 = 2 + j * TT
                        S.op("act", lambda en: en.activation(out=sg_t[:, c0:c0 + TT], in_=ps[:, :], func=AF.Copy),
                             reads=[bps], writes=[sg_b[j + 1]])
                        acc, bacc = accr.next()
                        ci = s * FT + f
                        S.op("act", lambda en: en.activation(out=acc[:], in_=ps[:, :], func=AF.Identity,
                                                             scale=self.col("f_conv_w%d_2" % i, ci),
                                                             bias=self.col("f_conv_b%d" % i, ci)),
                             reads=[bps, self.bconst], writes=[bacc])
                        for kk in (1, 0):
                            S.op("dve", lambda en: en.scalar_tensor_tensor(
                                out=acc[:], in0=sg_t[:, j * TT + kk:j * TT + kk + TT],
                                scalar=self.col("f_conv_w%d_%d" % (i, kk), ci), in1=acc[:], op0=ALU.mult, op1=ALU.add),
                                reads=[sg_b[j], sg_b[j + 1], bacc, self.bconst], writes=[bacc])
                        accs[s] = (acc, bacc)
                    sg, bsg = sgr.next()
                    S.op("act", lambda en: en.activation(out=sg[:], in_=accs[0][0][:], func=AF.Silu),
                         reads=[accs[0][1]], writes=[bsg])
                    o, bo = outr.next()
                    S.op("dve", lambda en: en.tensor_tensor(out=o[:], in0=sg[:], in1=accs[1][0][:], op=ALU.mult),
                         reads=[bsg, accs[1][1]], writes=[bo])
                    S.dma("sp", self.actT[f, :, j * TT:(j + 1) * TT], o[:], reads=[bo])
            self.barrier()

    def tail(self, i, last, inext):
        nc, S = self.nc, self.S
        with ExitStack() as st:
            wd = self.sb(st, "tl_wd", [128, FT, D], BF16)
            wg = self.sb(st, "tl_wg", [128, KT, D], BF16)
            wp = self.sb(st, "tl_wp", [128, 2, D], BF16)
            bwd, bwg, bwp = Buf(), Buf(), Buf()
            for f in range(FT):
                S.dma("pool", wd[:, f, :], self.W["f_w_down"][i, f * 128:(f + 1) * 128, :], writes=[bwd])
            for k in range(KT):
                S.dma("pool", wg[:, k, :], self.W["ple_w_gate"][i, k * 128:(k + 1) * 128, :], writes=[bwg])
            for k in range(2):
                S.dma("pool", wp[:, k, :], self.W["ple_w_proj"][i, k * 128:(k + 1) * 128, :], writes=[bwp])
            ar = Ring(nc, st, "tl_a", [128, FT, TT], BF16, 2)
            hr = Ring(nc, st, "tl_h", [128, KT, TT], F32, 2)
            pr = Ring(nc, st, "tl_p", [128, 2, TT], BF16, 2)
            n3r = Ring(nc, st, "tl_n3", [128, KT, TT], BF16, 1)
            sgr = Ring(nc, st, "tl_sg", [128, TT], F32, 2)
            if last:
                orr = Ring(nc, st, "tl_o", [128, KT, TT], F32, 1)
            else:
                orr = Ring(nc, st, "tl_o", [128, KT, TT], BF16, 2)
            rings = self.norm_rings(st, "tl")
            def ld(j):
                a, ba = ar.next()
                S.dma("sp", a[:], self.tview(self.actT, j), writes=[ba])
                h, bh = hr.next()
                S.dma("sp", h[:], self.tview(self.hT, j), writes=[bh])
                p, bp = pr.next()
                S.dma("pool", p[:], self.tview(self.pT[i], j), writes=[bp])
                return a, ba, h, bh, p, bp
            nxt = ld(0)
            for j in range(NTT):
                a, ba, h, bh, p, bp = nxt
                if j + 1 < NTT:
                    nxt = ld(j + 1)
                for e in range(KT):
                    ps, bps = self.psA.next()
                    for f in range(FT):
                        S.op("pe", lambda en: en.matmul(ps[:, :], lhsT=wd[:, f, e * 128:(e + 1) * 128], rhs=a[:, f, :],
                                                        start=(f == 0), stop=(f == FT - 1)),
                             reads=[bwd, ba], writes=[bps], inc=(f == FT - 1), acc=True)
                    S.op("dve", lambda en: en.tensor_tensor(out=h[:, e, :], in0=ps[:, :], in1=h[:, e, :], op=ALU.add),
                         reads=[bps, bh], writes=[bh])
                n3, bn3 = n3r.next()
                self.rmsnorm(rings, h, bh, "norm_ple%d" % i, n3, bn3)
                for e in range(KT):
                    ps, bps = self.psA.next()
                    for k in range(KT):
                        S.op("pe", lambda en: en.matmul(ps[:, :], lhsT=wg[:, k, e * 128:(e + 1) * 128], rhs=n3[:, k, :],
                                                        start=(k == 0), stop=(k == KT - 1)),
                             reads=[bwg, bn3], writes=[bps], inc=(k == KT - 1), acc=True)
                    sg, bsg = sgr.next()
                    S.op("act", lambda en: en.activation(out=sg[:], in_=ps[:, :], func=AF.Sigmoid),
                         reads=[bps], writes=[bsg])
                    ps2, bps2 = self.psA.next()
                    for k in range(2):
                        S.op("pe", lambda en: en.matmul(ps2[:, :], lhsT=wp[:, k, e * 128:(e + 1) * 128], rhs=p[:, k, :],
                                                        start=(k == 0), stop=(k == 1)),
                             reads=[bwp, bp], writes=[bps2], inc=(k == 1), acc=True)
                    S.op("dve", lambda en: en.tensor_tensor(out=sg[:], in0=ps2[:, :], in1=sg[:], op=ALU.mult),
                         reads=[bps2, bsg], writes=[bsg])
                    S.op("dve", lambda en: en.tensor_tensor(out=h[:, e, :], in0=sg[:], in1=h[:, e, :], op=ALU.add),
                         reads=[bsg, bh], writes=[bh])
                o, bo = orr.next()
                if last:
                    self.rmsnorm(rings, h, bh, "norm_final", o, bo)
                    S.dma("sp", self.tview(self.yT, j), o[:], reads=[bo])
                else:
                    S.dma("sp", self.tview(self.hT, j), h[:], reads=[bh])
                    self.rmsnorm(rings, h, bh, "norm_mix%d" % inext, o, bo)
                    S.dma("sp", self.tview(self.hnT, j), o[:], reads=[bo])
            self.barrier()

    def mixer_a(self, i, j):
        nc, S = self.nc, self.S
        ygT, xcT, xcbT = self.s32[0], self.s32[1], self.s16[0]
        win = self.W["a_w_in"][j]
        with ExitStack() as st:
            hn = self.sb(st, "a1_hn", [128, KT, T], BF16)
            bhn = [Buf() for _ in range(KT)]
            for k in range(KT):
                S.dma("sp", hn[:, k, :], self.hnT[k, :, :], writes=[bhn[k]])
            wr = Ring(nc, st, "a1_w", [128, KT, 128], BF16, 4)
            stg = [self.sb(st, "a1_stg%d" % b, [128, 3 + T], F32) for b in range(2)]
            bstg = [[Buf() for _ in range(NTT + 1)] for b in range(2)]
            for b in range(2):
                S.op("pool", lambda e: e.memset(stg[b][:, 0:3], 0.0), writes=[bstg[b][0]])
            ygr = Ring(nc, st, "a1_yg", [128, TT], F32, 3)
            accr = Ring(nc, st, "a1_acc", [128, TT], F32, 3)
            xbr = Ring(nc, st, "a1_xb", [128, TT], BF16, 3)
            wnxt = self.load_w(wr, win, KT, 0, 128)
            for e in range(2 * KT):
                w, bw = wnxt
                if e + 1 < 2 * KT:
                    wnxt = self.load_w(wr, win, KT, (e + 1) * 128, 128)
                for jt in range(NTT):
                    ps, bps = self.psA.next()
                    for k in range(KT):
                        S.op("pe", lambda en: en.matmul(ps[:, :], lhsT=w[:, k, :], rhs=hn[:, k, jt * TT:(jt + 1) * TT],
                                                        start=(k == 0), stop=(k == KT - 1)),
                             reads=[bw] + bhn, writes=[bps], inc=(k == KT - 1), acc=True)
                    if e < KT:
                        yg, byg = ygr.next()
                        S.op("act", lambda en: en.activation(out=yg[:], in_=ps[:, :], func=AF.Gelu_apprx_tanh),
                             reads=[bps], writes=[byg])
                        S.dma("sp", ygT[e, :, jt * TT:(jt + 1) * TT], yg[:], reads=[byg])
                    else:
                        ft = e - KT
                        sg_t, sg_b = stg[ft % 2], bstg[ft % 2]
                        c0 = 3 + jt * TT
                        S.op("act", lambda en: en.activation(out=sg_t[:, c0:c0 + TT], in_=ps[:, :], func=AF.Copy),
                             reads=[bps], writes=[sg_b[jt + 1]])
                        acc, bacc = accr.next()
                        S.op("act", lambda en: en.activation(out=acc[:], in_=ps[:, :], func=AF.Identity,
                                                             scale=self.col("a_conv_w%d_3" % j, ft),
                                                             bias=self.col("a_conv_b%d" % j, ft)),
                             reads=[bps, self.bconst], writes=[bacc])
                        for kk in (2, 1, 0):
                            S.op("dve", lambda en: en.scalar_tensor_tensor(
                                out=acc[:], in0=sg_t[:, jt * TT + kk:jt * TT + kk + TT],
                                scalar=self.col("a_conv_w%d_%d" % (j, kk), ft), in1=acc[:], op0=ALU.mult, op1=ALU.add),
                                reads=[sg_b[jt], sg_b[jt + 1], bacc, self.bconst], writes=[bacc])
                        xb, bxb = xbr.next()
                        S.op("pool", lambda en: en.tensor_copy(out=xb[:], in_=acc[:]), reads=[bacc], writes=[bxb])
                        S.dma("sp", xcT[ft, :, jt * TT:(jt + 1) * TT], acc[:], reads=[bacc])
                        S.dma("sp", xcbT[ft, :, jt * TT:(jt + 1) * TT], xb[:], reads=[bxb])
            self.barrier()
        with ExitStack() as st:
            cc = self.sb(st, "a2_cc", [128, 5, KT], F32)
            bc = Buf()
            lam = self.col("a_lambda%d" % j, 0, KT)
            ev, l1, t2, mk, ccol = (cc[:, q, :] for q in range(5))
            S.op("act", lambda e: e.activation(out=ev, in_=lam, func=AF.Exp, scale=-1.0), reads=[self.bconst], writes=[bc])
            S.op("act", lambda e: e.activation(out=l1, in_=ev, func=AF.Ln, bias=self.one_col), reads=[bc, self.bconst], writes=[bc])
            S.op("dve", lambda e: e.tensor_scalar(out=t2, in0=ev, scalar1=1.0 / 3.0, scalar2=-0.5, op0=ALU.mult, op1=ALU.add), reads=[bc], writes=[bc])
            S.op("dve", lambda e: e.tensor_tensor(out=t2, in0=t2, in1=ev, op=ALU.mult), reads=[bc], writes=[bc])
            S.op("dve", lambda e: e.tensor_scalar(out=t2, in0=t2, scalar1=1.0, scalar2=None, op0=ALU.add), reads=[bc], writes=[bc])
            S.op("dve", lambda e: e.tensor_tensor(out=t2, in0=t2, in1=ev, op=ALU.mult), reads=[bc], writes=[bc])
            S.op("dve", lambda e: e.tensor_scalar(out=mk, in0=ev, scalar1=0.02, scalar2=None, op0=ALU.is_lt), reads=[bc], writes=[bc])
            S.op("dve", lambda e: e.tensor_tensor(out=t2, in0=t2, in1=l1, op=ALU.subtract), reads=[bc], writes=[bc])
            S.op("dve", lambda e: e.tensor_tensor(out=t2, in0=t2, in1=mk, op=ALU.mult), reads=[bc], writes=[bc])
            S.op("dve", lambda e: e.tensor_tensor(out=l1, in0=l1, in1=t2, op=ALU.add), reads=[bc], writes=[bc])
            S.op("dve", lambda e: e.tensor_scalar(out=ccol, in0=l1, scalar1=-LRU_C, scalar2=None, op0=ALU.mult), reads=[bc], writes=[bc])

            xbr = Ring(nc, st, "a2_xb", [128, 2, T], BF16, 2)
            gwr = Ring(nc, st, "a2_gw", [128, 2, 512], BF16, 2)
            afr = Ring(nc, st, "a2_af", [128, T], F32, 2)
            bfr = Ring(nc, st, "a2_bf", [128, T], F32, 2)
            hfr = Ring(nc, st, "a2_hf", [128, T], F32, 1)
            sqr_ = Ring(nc, st, "a2_sq", [128, T], F32, 1)
            ygr = Ring(nc, st, "a2_yg", [128, T], F32, 1)
            gor = Ring(nc, st, "a2_go", [128, T], BF16, 1)
            tr = Ring(nc, st, "a2_t", [128, TT], F32, 3)
            xcr = Ring(nc, st, "a2_xc", [128, TT], F32, 3)

            def ldh(hd):
                xb, bxb = xbr.next()
                for k in range(2):
                    S.dma("sp", xb[:, k, :], xcbT[2 * hd + k, :, :], writes=[bxb])
                gw, bgw = self.load_w(gwr, self.W["a_gate_w"][j, hd], 2, 0, 512)
                return xb, bxb, gw, bgw
            nxt = ldh(0)
            for hd in range(4):
                xb, bxb, gw, bgw = nxt
                if hd + 1 < 4:
                    nxt = ldh(hd + 1)
                for ft in range(2):
                    ftg = 2 * hd + ft
                    af, baf = afr.next()
                    bf, bbf = bfr.next()
                    yg, byg = ygr.next()
                    S.dma("sp", yg[:], ygT[ftg, :, :], writes=[byg])
                    for jt in range(NTT):
                        sl = slice(jt * TT, (jt + 1) * TT)
                        xc, bxc = xcr.next()
                        S.dma("sp", xc[:], xcT[ftg, :, sl], writes=[bxc])
                        psr, bpsr = self.psA.next()
                        psi, bpsi = self.psA.next()
                        for k in range(2):
                            S.op("pe", lambda en: en.matmul(psr[:, :], lhsT=gw[:, k, ft * 128:(ft + 1) * 128], rhs=xb[:, k, sl],
                                                            start=(k == 0), stop=(k == 1)),
                                 reads=[bgw, bxb], writes=[bpsr], inc=(k == 1), acc=True)
                        for k in range(2):
                            S.op("pe", lambda en: en.matmul(psi[:, :], lhsT=gw[:, k, 256 + ft * 128:256 + (ft + 1) * 128], rhs=xb[:, k, sl],
                                                            start=(k == 0), stop=(k == 1)),
                                 reads=[bgw, bxb], writes=[bpsi], inc=(k == 1), acc=True)
                        S.op("act", lambda en: en.activation(out=af[:, sl], in_=psr[:, :], func=AF.Sigmoid,
                                                             bias=self.col("a_gate_b%d" % j, ftg)),
                             reads=[bpsr, self.bconst], writes=[baf])
                        ig, big = tr.next()
                        S.op("act", lambda en: en.activation(out=ig[:], in_=psi[:, :], func=AF.Sigmoid,
                                                             bias=self.col("a_gate_b%d" % j, KT + ftg)),
                             reads=[bpsi, self.bconst], writes=[big])
                        S.op("dve", lambda en: en.tensor_tensor(out=bf[:, sl], in0=ig[:], in1=xc[:], op=ALU.mult),
                             reads=[big, bxc], writes=[bbf])
                    S.op("act", lambda en: en.activation(out=af[:], in_=af[:], func=AF.Exp, scale=cc[:, 4, ftg:ftg + 1]),
                         reads=[baf, bc], writes=[baf])
                    sq, bsq = sqr_.next()
                    S.op("act", lambda en: en.activation(out=sq[:], in_=af[:], func=AF.Square), reads=[baf], writes=[bsq])
                    S.op("act", lambda en: en.activation(out=sq[:], in_=sq[:], func=AF.Sqrt, scale=-1.0, bias=self.one_col),
                         reads=[bsq, self.bconst], writes=[bsq])
                    S.op("dve", lambda en: en.tensor_tensor(out=bf[:], in0=bf[:], in1=sq[:], op=ALU.mult), reads=[bbf, bsq], writes=[bbf])
                    hf, bhf = hfr.next()
                    S.op("dve", lambda en: en.tensor_tensor_scan(out=hf[:], data0=af[:], data1=bf[:], initial=0.0,
                                                                  op0=ALU.mult, op1=ALU.add),
                         reads=[baf, bbf], writes=[bhf])
                    go, bgo = gor.next()
                    S.op("pool", lambda en: en.tensor_tensor(out=go[:], in0=hf[:], in1=yg[:], op=ALU.mult),
                         reads=[bhf, byg], writes=[bgo])
                    S.dma("sp", self.gT[ftg, :, :], go[:], reads=[bgo])
            self.barrier()

    def _ring_guard(self, ring, tile):
        d = getattr(self, "_rg", None)
        if d is None:
            d = self._rg = {}
        b = d.get(id(tile))
        return [b] if b is not None else []

    def _ring_set(self, ring, tile, buf):
        if getattr(self, "_rg", None) is None:
            self._rg = {}
        self._rg[id(tile)] = buf


    def mixer_b(self, i):
        with ExitStack() as st:
            try:
                self._mixer_b_body(i, st)
            except _Cut:
                pass
            self.barrier()

    def _mixer_b_body(self, i, st):
        nc, S = self.nc, self.S
        wqkv, wqks = self.W["b_w_qkv"], self.W["b_w_qks"]
        banks = self.psA.items
        if True:
            cosT = self.sb(st, "b_cos", [128, T], BF16)
            sinT = self.sb(st, "b_sin", [128, T], BF16)
            btab = Buf()
            with ExitStack() as s2:
                posi = self.sb(s2, "b_posi", [128, T], I32)
                ang = self.sb(s2, "b_ang", [128, T], F32)
                kk = self.sb(s2, "b_kk", [128, T], F32)
                cst = self.sb(s2, "b_cst", [128, 2], F32)
                bt = Buf()
                S.dma("sp", posi[:], self.pos[0:1, :].to_broadcast([128, T]), writes=[bt])
                S.op("pool", lambda e: e.memset(cst[:, 0:1], float(np.pi / 2)), writes=[bt])
                S.op("dve", lambda e: e.tensor_copy(out=ang[:], in_=posi[:]), reads=[bt], writes=[bt])
                S.op("dve", lambda e: e.tensor_scalar(out=ang[:], in0=ang[:], scalar1=self.col("invf"), scalar2=None, op0=ALU.mult),
                     reads=[bt, self.bconst], writes=[bt])
                MAGIC = 12582912.0
                S.op("dve", lambda e: e.tensor_scalar(out=kk[:], in0=ang[:], scalar1=float(1.0 / (2 * np.pi)), scalar2=MAGIC,
                                                      op0=ALU.mult, op1=ALU.add), reads=[bt], writes=[bt])
                S.op("dve", lambda e: e.tensor_scalar(out=kk[:], in0=kk[:], scalar1=-MAGIC, scalar2=None, op0=ALU.add),
                     reads=[bt], writes=[bt])
                C1 = 6.28125
                C2 = float(2 * np.pi - C1)
                S.op("dve", lambda e: e.scalar_tensor_tensor(out=ang[:], in0=kk[:], scalar=-C1, in1=ang[:], op0=ALU.mult, op1=ALU.add),
                     reads=[bt], writes=[bt])
                S.op("dve", lambda e: e.scalar_tensor_tensor(out=ang[:], in0=kk[:], scalar=-C2, in1=ang[:], op0=ALU.mult, op1=ALU.add),
                     reads=[bt], writes=[bt])
                S.op("dve", lambda e: e.tensor_scalar(out=ang[:], in0=ang[:], scalar1=float(np.pi), scalar2=float(-np.pi),
                                                      op0=ALU.min, op1=ALU.max), reads=[bt], writes=[bt])
                S.op("act", lambda e: e.activation(out=sinT[:], in_=ang[:], func=AF.Sin, scale=self.col("rsgn")),
                     reads=[bt, self.bconst], writes=[btab])
                S.op("dve", lambda e: e.scalar_tensor_tensor(out=kk[:], in0=ang[:], scalar=-1.0, in1=ang[:], op0=ALU.mult, op1=ALU.max), reads=[bt], writes=[bt])
                S.op("act", lambda e: e.activation(out=cosT[:], in_=kk[:], func=AF.Sin, scale=-1.0, bias=cst[:, 0:1]),
                     reads=[bt], writes=[btab])
                self.barrier()
            self.cut("cutA")
            self.dump("cos", cosT[:], [btab], [128, T], BF16)
            self.dump("sin", sinT[:], [btab], [128, T], BF16)
            hn = self.sb(st, "b_hn", [128, KT, T], BF16)
            bhn = [Buf() for _ in range(KT)]
            for k in range(KT):
                S.dma("sp", hn[:, k, :], self.hnT[k, :, :], writes=[bhn[k]])
            band = self.sb(st, "b_band", [128, 2, 256], BF16)
            bband = Buf()
            S.op("pool", lambda e: e.memset(band[:], 1.0), writes=[bband])
            S.op("pool", lambda e: e.affine_select(out=band[:], in_=band[:], pattern=[[0, 2], [1, 256]], compare_op=ALU.is_ge,
                                                   fill=0.0, base=0, channel_multiplier=-1), reads=[bband], writes=[bband])
            S.op("pool", lambda e: e.affine_select(out=band[:], in_=band[:], pattern=[[0, 2], [-1, 256]], compare_op=ALU.is_ge,
                                                   fill=0.0, base=128, channel_multiplier=1), reads=[bband], writes=[bband])
            wr = Ring(nc, st, "b_w", [128, KT, 128], BF16, 4)
            wvr = Ring(nc, st, "b_wv", [128, KT, 128], BF16, 2)
            qr = Ring(nc, st, "b_q", [128, T], BF16, 2)
            kr = Ring(nc, st, "b_k", [128, T], BF16, 2)
            vr = Ring(nc, st, "b_v", [128, 32, 128], BF16, 2)
            accden = self.sb(st, "b_accden", [128, 2, T], F32)
            bacc = Buf()
            tr = Ring(nc, st, "b_t", [128, TT], F32, 4)
            er = Ring(nc, st, "b_e", [128, 2, 256], BF16, 3)
            outr = Ring(nc, st, "b_o", [128, TT], BF16, 2)
            psP = SubRing(banks[0:2])
            psS = SubRing([(self.psbig[q][:, :].rearrange("p (a b) -> p a b", a=2), Buf()) for q in (1, 2)])
            psOD = SubRing([(banks[q][0].rearrange("p (a b) -> p a b", a=2), Buf()) for q in (6, 7)])

            def proj_rot(col0, dst, bdst, d):
                w, bw = self.load_w(wr, wqkv, KT, col0, 128)
                ws, bws = self.load_w(wr, wqks, KT, col0, 128)
                for jt in range(NTT):
                    sl = slice(jt * TT, (jt + 1) * TT)
                    ps, bps = psP.next()
                    ps2, bps2 = psP.next()
                    for (pp, bpp, ww, bww) in ((ps, bps, w, bw), (ps2, bps2, ws, bws)):
                        for k in range(KT):
                            S.op("pe", lambda en: en.matmul(pp[:, :], lhsT=ww[:, k, :], rhs=hn[:, k, sl],
                                                            start=(k == 0), stop=(k == KT - 1)),
                                 reads=[bww] + bhn, writes=[bpp], inc=(k == KT - 1), acc=True)
                    t1, bt1 = tr.next()
                    t2, bt2 = tr.next()
                    S.op("dve", lambda en: en.tensor_tensor(out=t1[:], in0=ps[:, :], in1=cosT[:, sl], op=ALU.mult),
                         reads=[bps, btab], writes=[bt1])
                    S.op("dve", lambda en: en.tensor_tensor(out=t2[:], in0=ps2[:, :], in1=sinT[:, sl], op=ALU.mult),
                         reads=[bps2, btab], writes=[bt2])
                    n = TT // d
                    dv = dst[:].rearrange("p (r l) -> p r l", r=d)[:, :, jt * n:(jt + 1) * n]
                    v1 = t1[:].rearrange("p (j r) -> p r j", r=d)
                    v2 = t2[:].rearrange("p (j r) -> p r j", r=d)
                    S.op("pool", lambda en: en.tensor_tensor(out=dv, in0=v1, in1=v2, op=ALU.add),
                         reads=[bt1, bt2], writes=[bdst])

            for hp in range(KT):
                wv, bwv = self.load_w(wvr, wqkv, KT, 6 * D + hp * 128, 128)
                S.op("pool", lambda e: e.memset(accden[:], 0.0), writes=[bacc])
                for g, d in enumerate((1, 4, 16)):
                    L = T // d
                    nb = L // 128
                    qT, bq = qr.next()
                    kT, bk = kr.next()
                    proj_rot(g * D + hp * 128, qT, bq, d)
                    proj_rot(3 * D + g * D + hp * 128, kT, bk, d)
                    self.cut("cutB")
                    if hp == 0:
                        self.dump("q%d" % g, qT[:], [bq], [128, T], BF16)
                        self.dump("k%d" % g, kT[:], [bk], [128, T], BF16)
                    vt, bvt = vr.next()
                    for n0 in range(0, 32, 4):
                        ps, bps = psP.next()
                        for q4 in range(4):
                            n = n0 + q4
                            r, kb = n // nb, n % nb
                            start = kb * 128 * d + r
                            for k in range(KT):
                                S.op("pe", lambda en: en.matmul(ps[:, q4 * 128:(q4 + 1) * 128],
                                                                lhsT=hn[:, k, start:start + 127 * d + 1:d], rhs=wv[:, k, :],
                                                                start=(k == 0), stop=(k == KT - 1)),
                                     reads=[bwv] + bhn, writes=[bps], inc=(k == KT - 1 and q4 == 3), acc=True)
                        S.op("act", lambda en: en.activation(out=vt[:, n0:n0 + 4, :], in_=ps[:, :].rearrange("p (a b) -> p a b", a=4),
                                                             func=AF.Copy), reads=[bps], writes=[bvt])
                    self.cut("cutC")
                    steps = [(r, kb) for r in range(d) for kb in range(nb)]

                    def stage1(r, kb):
                        kcol = r * L + kb * 128
                        nq = 256 if kb + 1 < nb else 128
                        pS, bpS = psS.next()
                        for hh in range(2):
                            S.op("pe", lambda en: en.matmul(pS[:, hh, 0:nq],
                                                            lhsT=kT[64 * hh:64 * hh + 64, kcol:kcol + 128],
                                                            rhs=qT[64 * hh:64 * hh + 64, kcol:kcol + nq],
                                                            start=True, stop=True),
                                 reads=[bk, bq], writes=[bpS], inc=(hh == 1), acc=True)
                        E, bE = er.next()
                        S.op("act", lambda en: en.activation(out=E[:, :, 0:nq], in_=pS[:, :, 0:nq],
                                                             func=AF.Exp, scale=0.125), reads=[bpS], writes=[bE])
                        meng = "dve" if (kb % 2 == 0) else "pool"
                        S.op(meng, lambda en: en.tensor_tensor(out=E[:, :, 0:nq], in0=E[:, :, 0:nq], in1=band[:, :, 0:nq], op=ALU.mult),
                             reads=[bE, bband], writes=[bE])
                        return E, bE

                    stt = {"pod": None, "bpod": None, "npod": None, "nbpod": None, "fresh": None, "nfresh": None}

                    def stage2(r, kb, E, bE):
                        n = r * nb + kb
                        if kb == 0:
                            stt["pod"], stt["bpod"] = psOD.next()
                            stt["fresh"] = [True, True]
                        halves = [(kb, 0)]
                        if kb + 1 < nb:
                            halves.append((kb + 1, 1))
                        for (qb, hf) in halves:
                            if qb % 2 == 0 and hf == 1:
                                stt["npod"], stt["nbpod"] = psOD.next()
                                stt["nfresh"] = [True, True]
                                tp, tbp, fr = stt["npod"], stt["nbpod"], stt["nfresh"]
                            else:
                                tp, tbp, fr = stt["pod"], stt["bpod"], stt["fresh"]
                            c0 = (qb % 2) * 128
                            for hh in range(2):
                                for od in range(2):
                                    lh = vt[:, n, 64 * hh:64 * hh + 64] if od == 0 else self.ones_bf[:, 0:64]
                                    st_ = fr[hh]
                                    fr[hh] = False
                                    S.op("pe", lambda en: en.matmul(tp[64 * hh:64 * hh + 64, od, c0:c0 + 128], lhsT=lh,
                                                                    rhs=E[:, hh, hf * 128:(hf + 1) * 128], start=st_, stop=True,
                                                                    skip_group_check=True),
                                         reads=[bvt, bE, self.bconst], writes=[tbp], inc=(hh == 1 and od == 1), acc=True)
                        if kb % 2 == 1 or kb == nb - 1:
                            qb0 = (kb // 2) * 2
                            ncols = (kb - qb0 + 1) * 128
                            t0 = qb0 * 128 * d + r
                            asl = accden[:, :, t0:t0 + (ncols - 1) * d + 1:d]
                            pod, bpod = stt["pod"], stt["bpod"]
                            S.op("dve", lambda en: en.tensor_tensor(out=asl, in0=pod[:, :, 0:ncols], in1=asl, op=ALU.add),
                                 reads=[bpod, bacc], writes=[bacc])
                            if kb + 1 < nb:
                                stt["pod"], stt["bpod"], stt["fresh"] = stt["npod"], stt["nbpod"], stt["nfresh"]

                    cur = stage1(*steps[0])
                    for si, (r, kb) in enumerate(steps):
                        nxt_ = stage1(*steps[si + 1]) if si + 1 < len(steps) else None
                        stage2(r, kb, *cur)
                        cur = nxt_
                    self.cut("cutD%d" % g)
                if hp == 0:
                    self.dump("acc", accden[:, 0, :], [bacc], [128, T], F32)
                    self.dump("den", accden[:, 1, :], [bacc], [128, T], F32)
                    self.dump("vt", vt[:], [bvt], [128, 32, 128], BF16)
                for jt in range(NTT):
                    sl = slice(jt * TT, (jt + 1) * TT)
                    rc, brc = tr.next()
                    S.op("dve", lambda en: en.reciprocal(out=rc[:], in_=accden[:, 1, sl]), reads=[bacc], writes=[brc])
                    o, bo = outr.next()
                    S.op("dve", lambda en: en.tensor_tensor(out=o[:], in0=accden[:, 0, sl], in1=rc[:], op=ALU.mult),
                         reads=[bacc, brc], writes=[bo])
                    S.dma("sp", self.gT[hp, :, sl], o[:], reads=[bo])
            self.barrier()


    CH = 128
    NCH = T // 128
    LAM = float(np.exp(-0.5))

    def mixer_c(self, i):
        if not hasattr(self, "c_AR"):
            dt = self.nc.dram_tensor
            self.c_AR = dt("c_AR", [KT, 128, 2 * T], BF16, kind="Internal").ap()
            self.c_vtok = dt("c_vtok", [T // 128, 128, D], BF16, kind="Internal").ap()
            self.c_gc = dt("c_gc", [KT, 128, T // 128], F32, kind="Internal").ap()
        self.mixer_c1(i)
        self.mixer_c2(i)

    def mixer_c2(self, i):
        nc, S = self.nc, self.S
        BTd, KTd, gD, bonD = self.s16[1], self.s16[2], self.s16[3], self.s32[0]
        NCH = self.NCH
        with ExitStack() as st:
            bm = Buf()
            MSK = self.sb(st, "c2_msk", [128, 2, 4, 128], BF16)
            LM = self.sb(st, "c2_lm", [128, 2, 128], BF16)
            IDN = self.sb(st, "c2_idn", [128, 2, 128], BF16)
            bonesf = self.sb(st, "c2_bones", [128, 128], F32)
            S.op("pool", lambda e: e.memset(MSK[:], 1.0), writes=[bm])
            for par in range(2):
                S.op("pool", lambda e: e.affine_select(out=MSK[:, :, par::2, :], in_=MSK[:, :, par::2, :],
                                                       pattern=[[0, 2], [0, 2], [1, 128]], compare_op=ALU.is_ge, fill=0.0,
                                                       base=par - 1, channel_multiplier=-1), reads=[bm], writes=[bm])
            S.op("pool", lambda e: e.memset(LM[:], 1.0), writes=[bm])
            S.op("pool", lambda e: e.affine_select(out=LM[:], in_=LM[:], pattern=[[0, 2], [-1, 128]], compare_op=ALU.is_ge, fill=0.0,
                                                   base=-1, channel_multiplier=1), reads=[bm], writes=[bm])
            S.op("pool", lambda e: e.memset(IDN[:], 1.0), writes=[bm])
            S.op("pool", lambda e: e.affine_select(out=IDN[:], in_=IDN[:], pattern=[[0, 2], [-1, 128]], compare_op=ALU.is_equal, fill=0.0,
                                                   base=0, channel_multiplier=1), reads=[bm], writes=[bm])
            S.op("pool", lambda e: e.memset(bonesf[:], 0.0), writes=[bm])
            S.op("pool", lambda e: e.memset(bonesf[0:64, 0:64], 1.0 / 64), writes=[bm])
            S.op("pool", lambda e: e.memset(bonesf[64:128, 64:128], 1.0 / 64), writes=[bm])
            ident = IDN[:, 0, :]
            arr = Ring(nc, st, "c2_ar", [128, NCH, 2, 128], BF16, 2)
            btr = Ring(nc, st, "c2_bt", [128, T], BF16, 2)
            ktr = Ring(nc, st, "c2_kt", [128, T], BF16, 2)
            vtr = Ring(nc, st, "c2_vt", [128, NCH, 128], BF16, 2)
            gcr = Ring(nc, st, "c2_gc", [128, NCH], F32, 2)
            scr = Ring(nc, st, "c2_sc", [128, 2, 4, 128], BF16, 2)
            mlr = Ring(nc, st, "c2_ml", [128, 2, 2, 128], BF16, 3)
            ttr = Ring(nc, st, "c2_tt", [128, 2, 128], BF16, 3)
            tokr = Ring(nc, st, "c2_tok", [128, 2, 128], BF16, 2)
            wur = Ring(nc, st, "c2_wu", [128, 64], BF16, 6)
            Pst = self.sb(st, "c2_pst", [128, 64], F32)
            PG = self.sb(st, "c2_pg", [128, 64], F32)
            Pbf = self.sb(st, "c2_pbf", [128, 64], BF16)
            bP = [Buf(), Buf()]
            bPG = [Buf(), Buf()]
            ysr = Ring(nc, st, "c2_ys", [128, TT], F32, 2)
            gtr = Ring(nc, st, "c2_gt", [128, TT], F32, 8)
            bonr = Ring(nc, st, "c2_bon", [128, TT], F32, 2)
            ggr = Ring(nc, st, "c2_gg", [128, TT], BF16, 2)
            outr = Ring(nc, st, "c2_out", [128, TT], BF16, 2)
            scA = self.psbig[0][:, :].rearrange("p (a b) -> p a b", a=2)
            bscA = Buf()
            scB = self.psbig[1][:, :].rearrange("p (a b) -> p a b", a=2)
            bscB = Buf()
            trp = self.psbig[1][:, 256:512].bitcast(BF16).rearrange("p (a b) -> p a b", a=4)
            btrp = bscB
            mlp = self.psbig[2][:, 0:512].rearrange("p (a b c) -> p a b c", a=2, b=2)
            bmlp = Buf()
            ttp = self.psbig[2][:, 512:768].rearrange("p (a b) -> p a b", a=2)
            bttp = Buf()
            sq = self.psbig[3][:, :].rearrange("p (a b) -> p a b", a=2)
            bsq = [Buf(), Buf()]

            def ldt(e_):
                ar, bar = arr.next()
                S.dma("sp", ar[:].rearrange("p a b c -> p (a b c)"), self.c_AR[e_, :, :], writes=[bar])
                bt_, bbt = btr.next()
                S.dma("sp", bt_[:], BTd[e_, :, :], writes=[bbt])
                kt_, bkt = ktr.next()
                S.dma("sp", kt_[:], KTd[e_, :, :], writes=[bkt])
                vt, bvt = vtr.next()
                S.dma("sp", vt[:], self.c_vtok.rearrange("c p f -> p c f")[:, :, e_ * 128:(e_ + 1) * 128], writes=[bvt])
                gc, bgc = gcr.next()
                S.dma("sp", gc[:], self.c_gc[e_, :, :], writes=[bgc])
                return ar, bar, bt_, bbt, kt_, bkt, vt, bvt, gc, bgc
            nxt = ldt(0)
            for e_ in range(KT):
                ar, bar, bt_, bbt, kt_, bkt, vt, bvt, gc, bgc = nxt
                if e_ + 1 < KT:
                    nxt = ldt(e_ + 1)
                for hh in range(2):
                    P = slice(64 * hh, 64 * hh + 64)
                    S.op("pool", lambda e: e.memset(Pst[P, :], 0.0), writes=[bP[hh]])
                    S.op("pool", lambda e: e.memset(Pbf[P, :], 0.0), writes=[bP[hh]])
                ys = bys = None
                for c in range(NCH):
                    cs_ = slice(c * 128, (c + 1) * 128)
                    for hh in range(2):
                        P = slice(64 * hh, 64 * hh + 64)
                        S.op("pe", lambda e: e.matmul(scA[:, hh, 0:256], lhsT=bt_[P, cs_], rhs=ar[P, c, :, :].rearrange("p a b -> p (a b)"),
                                                      start=True, stop=True), reads=[bbt, bar], writes=[bscA], inc=False, acc=True)
                    for hh in range(2):
                        P = slice(64 * hh, 64 * hh + 64)
                        S.op("pe", lambda e: e.matmul(scA[:, hh, 256:512], lhsT=kt_[P, cs_], rhs=ar[P, c, :, :].rearrange("p a b -> p (a b)"),
                                                      start=True, stop=True), reads=[bkt, bar], writes=[bscA], inc=(hh == 1), acc=True)
                    for hh in range(2):
                        P = slice(64 * hh, 64 * hh + 64)
                        S.op("pe", lambda e: e.matmul(scB[:, hh, 0:128], lhsT=ar[P, c, 0, :], rhs=bt_[P, cs_],
                                                      start=True, stop=True), reads=[bbt, bar], writes=[bscB], inc=(hh == 1), acc=True)
                    S.op("pe", lambda e: e.transpose(trp[:, 0, :], bt_[:, cs_], ident), reads=[bbt, bm], writes=[btrp], inc=False, acc=True)
                    S.op("pe", lambda e: e.transpose(trp[:, 1, :], kt_[:, cs_], ident), reads=[bkt, bm], writes=[btrp], acc=True)
                    SC, bSC = scr.next()
                    S.op("dve", lambda e: e.tensor_tensor(out=SC[:].rearrange("p a b c -> p a (b c)"), in0=scA[:, :, :],
                                                          in1=MSK[:].rearrange("p a b c -> p a (b c)"), op=ALU.mult),
                         reads=[bscA, bm], writes=[bSC])
                    ML, bML = mlr.next()
                    S.op("dve", lambda e: e.tensor_tensor(out=ML[:, :, 1, :], in0=scB[:, :, 0:128], in1=LM[:], op=ALU.mult),
                         reads=[bscB, bm], writes=[bML])
                    S.op("pool", lambda e: e.tensor_copy(out=ML[:, :, 0, :], in_=SC[:, :, 0, :]), reads=[bSC], writes=[bML])
                    tok, btok = tokr.next()
                    S.op("act", lambda e: e.activation(out=tok[:], in_=trp[:, 0:2, :], func=AF.Copy), reads=[btrp], writes=[btok])
                    TTc, bTT = ttr.next()
                    S.op("pool", lambda e: e.tensor_tensor(out=TTc[:], in0=SC[:, :, 0, :], in1=IDN[:], op=ALU.add), reads=[bSC, bm], writes=[bTT])
                    for lev in range(1, 7):
                        MLn, bMLn = mlr.next()
                        for hh in range(2):
                            if lev < 6:
                                S.op("pe", lambda e: e.matmul(mlp[:, hh, 0, :], lhsT=ML[:, hh, 1, :], rhs=ML[:, hh, 0, :], start=True, stop=True),
                                     reads=[bML], writes=[bmlp], inc=False, acc=True)
                            S.op("pe", lambda e: e.matmul(mlp[:, hh, 1, :], lhsT=ML[:, hh, 0, :], rhs=ML[:, hh, 1, :], start=True, stop=True),
                                 reads=[bML], writes=[bmlp], inc=(hh == 1), acc=True)
                        if lev < 6:
                            S.op("act", lambda e: e.activation(out=MLn[:], in_=mlp[:, :, :, :], func=AF.Copy), reads=[bmlp], writes=[bMLn])
                        else:
                            S.op("act", lambda e: e.activation(out=MLn[:, :, 1, :], in_=mlp[:, :, 1, :], func=AF.Copy), reads=[bmlp], writes=[bMLn])
                        for hh in range(2):
                            S.op("pe", lambda e: e.matmul(ttp[:, hh, :], lhsT=MLn[:, hh, 1, :], rhs=TTc[:, hh, :], start=True, stop=True),
                                 reads=[bMLn, bTT], writes=[bttp], inc=(hh == 1), acc=True)
                        TTn, bTTn = ttr.next()
                        S.op("dve", lambda e: e.tensor_tensor(out=TTn[:], in0=ttp[:, :, :], in1=TTc[:], op=ALU.add), reads=[bttp, bTT], writes=[bTTn])
                        ML, bML, TTc, bTT = MLn, bMLn, TTn, bTTn
                    if c % 4 == 0:
                        ys, bys = ysr.next()
                    for hh in range(2):
                        P = slice(64 * hh, 64 * hh + 64)
                        vh = vt[:, c, 64 * hh:64 * hh + 64]
                        S.op("pe", lambda e: e.matmul(sq[:, hh, 0:64], lhsT=SC[:, hh, 2, :], rhs=vh, start=True, stop=False),
                             reads=[bSC, bvt], writes=[bsq[hh]], inc=False, acc=True)
                        S.op("pe", lambda e: e.matmul(sq[:, hh, 0:64], lhsT=ar[P, c, 0, :], rhs=Pbf[P, :], start=False, stop=True),
                             reads=[bar, bP[hh]], writes=[bsq[hh]], acc=True)
                        Wsb, bW = wur.next()
                        S.op("act", lambda e: e.activation(out=Wsb[:], in_=sq[:, hh, 0:64], func=AF.Copy), reads=[bsq[hh]], writes=[bW])
                        S.op("pe", lambda e: e.matmul(sq[:, hh, 64:128], lhsT=TTc[:, hh, :], rhs=Wsb[:], start=True, stop=True),
                             reads=[bTT, bW], writes=[bsq[hh]], acc=True)
                        Usb, bU = wur.next()
                        S.op("act", lambda e: e.activation(out=Usb[:], in_=sq[:, hh, 64:128], func=AF.Copy), reads=[bsq[hh]], writes=[bU])
                        S.op("pe", lambda e: e.matmul(sq[P, hh, 128:256], lhsT=vh, rhs=SC[:, hh, 3, :], start=True, stop=False),
                             reads=[bvt, bSC], writes=[bsq[hh]], inc=False, acc=True)
                        S.op("pe", lambda e: e.matmul(sq[P, hh, 128:256], lhsT=Usb[:], rhs=SC[:, hh, 1, :], start=False, stop=False),
                             reads=[bU, bSC], writes=[bsq[hh]], inc=False, acc=True)
                        S.op("pe", lambda e: e.matmul(sq[P, hh, 128:256], lhsT=Pbf[P, :], rhs=ar[P, c, 1, :], start=False, stop=True),
                             reads=[bP[hh], bar], writes=[bsq[hh]], acc=True)
                        S.op("act", lambda e: e.activation(out=ys[P, (c % 4) * 128:(c % 4 + 1) * 128], in_=sq[P, hh, 128:256], func=AF.Copy),
                             reads=[bsq[hh]], writes=[bys])
                        S.op("dve", lambda e: e.tensor_scalar(out=PG[P, :], in0=Pst[P, :], scalar1=gc[P, c:c + 1], scalar2=None, op0=ALU.mult),
                             reads=[bP[hh], bgc], writes=[bPG[hh]])
                        S.op("pe", lambda e: e.matmul(sq[P, hh, 256:320], lhsT=tok[:, 0, 64 * hh:64 * hh + 64], rhs=Usb[:], start=True, stop=False),
                             reads=[btok, bU], writes=[bsq[hh]], inc=False, acc=True)
                        S.op("pe", lambda e: e.matmul(sq[P, hh, 256:320], lhsT=tok[:, 1, 64 * hh:64 * hh + 64], rhs=vh, start=False, stop=True),
                             reads=[btok, bvt], writes=[bsq[hh]], acc=True)
                        S.op("dve", lambda e: e.scalar_tensor_tensor(out=Pst[P, :], in0=sq[P, hh, 256:320], scalar=gc[P, c:c + 1], in1=PG[P, :],
                                                                     op0=ALU.mult, op1=ALU.add),
                             reads=[bsq[hh], bgc, bPG[hh]], writes=[bP[hh]])
                        S.op("act", lambda e: e.activation(out=Pbf[P, :], in_=Pst[P, :], func=AF.Copy), reads=[bP[hh]], writes=[bP[hh]])
                    if c % 4 == 3:
                        jt = c // 4
                        sl = slice(jt * TT, (jt + 1) * TT)
                        bon, bbon = bonr.next()
                        S.dma("sp", bon[:], bonD[e_, :, sl], writes=[bbon])
                        gg, bgg = ggr.next()
                        S.dma("sp", gg[:], gD[e_, :, sl], writes=[bgg])
                        mean_ps, ex2_ps = scA[:, 0, :], scA[:, 1, :]
                        ysq, bysq = gtr.next()
                        S.op("act", lambda e: e.activation(out=ysq[:], in_=ys[:], func=AF.Square), reads=[bys], writes=[bysq])
                        S.op("pe", lambda e: e.matmul(mean_ps, lhsT=bonesf[:, :], rhs=ys[:], start=True, stop=True),
                             reads=[bm, bys], writes=[bscA], inc=False, acc=True)
                        S.op("pe", lambda e: e.matmul(ex2_ps, lhsT=bonesf[:, :], rhs=ysq[:], start=True, stop=True),
                             reads=[bm, bysq], writes=[bscA], acc=True)
                        msq, bmsq = gtr.next()
                        S.op("act", lambda e: e.activation(out=msq[:], in_=mean_ps, func=AF.Square), reads=[bscA], writes=[bmsq])
                        S.op("dve", lambda e: e.tensor_tensor(out=msq[:], in0=ex2_ps, in1=msq[:], op=ALU.subtract), reads=[bscA, bmsq], writes=[bmsq])
                        S.op("dve", lambda e: e.tensor_scalar(out=msq[:], in0=msq[:], scalar1=0.0, scalar2=None, op0=ALU.max), reads=[bmsq], writes=[bmsq])
                        S.op("act", lambda e: e.activation(out=msq[:], in_=msq[:], func=AF.Sqrt, bias=self.gneps_col), reads=[bmsq, self.bconst], writes=[bmsq])
                        msq_in, bmsq_in = msq, bmsq
                        msq, bmsq = gtr.next()
                        S.op("dve", lambda e: e.reciprocal(out=msq[:], in_=msq_in[:]), reads=[bmsq_in], writes=[bmsq])
                        yc, byc = gtr.next()
                        S.op("dve", lambda e: e.tensor_tensor(out=yc[:], in0=mean_ps, in1=ys[:], op=ALU.subtract), reads=[bscA, bys], writes=[byc])
                        S.op("dve", lambda e: e.tensor_tensor(out=yc[:], in0=yc[:], in1=msq[:], op=ALU.mult), reads=[byc, bmsq], writes=[byc])
                        S.op("dve", lambda e: e.tensor_scalar(out=yc[:], in0=yc[:], scalar1=-1.0, scalar2=self.col("c_ln_w", e_), op0=ALU.mult, op1=ALU.mult),
                             reads=[byc, self.bconst], writes=[byc])
                        S.op("dve", lambda e: e.scalar_tensor_tensor(out=yc[:], in0=yc[:], scalar=self.col("c_ln_b", e_), in1=bon[:], op0=ALU.add, op1=ALU.add),
                             reads=[byc, bbon, self.bconst], writes=[byc])
                        o, bo = outr.next()
                        S.op("dve", lambda e: e.tensor_tensor(out=o[:], in0=yc[:], in1=gg[:], op=ALU.mult), reads=[byc, bgg], writes=[bo])
                        S.dma("sp", self.gT[e_, :, sl], o[:], reads=[bo])
            self.barrier()

    def mixer_c1(self, i):
        nc, S = self.nc, self.S
        W = self.W
        LAM = self.LAM
        BTd, KTd, gD, bonD = self.s16[1], self.s16[2], self.s16[3], self.s32[0]
        with ExitStack() as st:
            wbuf = Buf()
            wrkv = [self.sb(st, "c_wrkv%d" % c, [128, KT, D], BF16) for c in range(3)]
            for c in range(3):
                for k in range(KT):
                    S.dma("pool", wrkv[c][:, k, :], W["c_w_rkv"][c, k * 128:(k + 1) * 128, :], writes=[wbuf])
            w1 = self.sb(st, "c_w1", [128, KT, 64], BF16)
            a1 = self.sb(st, "c_a1", [128, KT, 64], BF16)
            g1 = self.sb(st, "c_g1", [128, KT, 128], BF16)
            w2 = self.sb(st, "c_w2", [128, D], BF16)
            a2 = self.sb(st, "c_a2", [128, D], BF16)
            g2 = self.sb(st, "c_g2", [128, D], BF16)
            S.dma("pool", w1[:], W["c_w1"].rearrange("(k p) e -> p k e", p=128), writes=[wbuf])
            S.dma("pool", a1[:], W["c_a1"].rearrange("(k p) e -> p k e", p=128), writes=[wbuf])
            S.dma("pool", g1[:], W["c_g1"].rearrange("(k p) e -> p k e", p=128), writes=[wbuf])
            S.dma("pool", w2[0:64, :], W["c_w2"][:, :], writes=[wbuf])
            S.dma("pool", a2[0:64, :], W["c_a2"][:, :], writes=[wbuf])
            S.dma("pool", g2[:, :], W["c_g2"][:, :], writes=[wbuf])
            cm01 = self.sb(st, "c_cm01", [128, TT], F32)
            bones = self.sb(st, "c_bones", [128, 128], BF16)
            bm = Buf()
            S.op("pool", lambda e: e.memset(cm01[:], 1.0), writes=[bm])
            for q in range(4):
                S.op("pool", lambda e: e.memset(cm01[:, q * 128:q * 128 + 1], 0.0), writes=[bm])
            S.op("pool", lambda e: e.memset(bones[:], 0.0), writes=[bm])
            S.op("pool", lambda e: e.memset(bones[0:64, 0:64], 1.0), writes=[bm])
            S.op("pool", lambda e: e.memset(bones[64:128, 64:128], 1.0), writes=[bm])
            hr = Ring(nc, st, "c_hn", [128, KT, TT + 1], BF16, 2)
            dd = self.sb(st, "c_d", [128, KT, TT], F32)
            bdd = Buf()
            xm = [self.sb(st, "c_xm%d" % c, [128, KT, TT], BF16) for c in range(6)]
            bxm = [Buf() for _ in range(6)]
            lor = [self.sb(st, "c_lor%d" % c, [128, TT], BF16) for c in range(3)]
            blor = [Buf() for _ in range(3)]
            tr = Ring(nc, st, "c_t", [128, TT], F32, 20)
            br = Ring(nc, st, "c_b", [128, TT], BF16, 8)
            arr = Ring(nc, st, "c_ar", [128, 4, 2, 128], BF16, 2)
            vtr = Ring(nc, st, "c_vt", [128, D], BF16, 2)
            gct = self.sb(st, "c_gct", [128, KT, T // 128], F32)
            bgct = Buf()
            P_ = self.psA

            def ldh(jt):
                h, bh = hr.next()
                if jt == 0:
                    S.op("pool", lambda e: e.memset(h[:, :, 0:1], 0.0), writes=[bh])
                    S.dma("sp", h[:, :, 1:TT + 1], self.tview(self.hnT, 0), writes=[bh])
                else:
                    S.dma("sp", h[:, :, :], self.hnT.rearrange("k p t -> p k t")[:, :, jt * TT - 1:(jt + 1) * TT], writes=[bh])
                return h, bh
            nxt = ldh(0)
            for jt in range(NTT):
                sl = slice(jt * TT, (jt + 1) * TT)
                h, bh = nxt
                if jt + 1 < NTT:
                    nxt = ldh(jt + 1)
                for k in range(KT):
                    S.op("dve", lambda e: e.tensor_tensor(out=dd[:, k, :], in0=h[:, k, 0:TT], in1=h[:, k, 1:TT + 1], op=ALU.subtract),
                         reads=[bh], writes=[bdd])
                for c in (3, 4, 5, 2, 0, 1):
                    for k in range(KT):
                        S.op("dve", lambda e: e.scalar_tensor_tensor(out=xm[c][:, k, :], in0=dd[:, k, :], scalar=self.col("c_mu%d" % c, k),
                                                                     in1=h[:, k, 1:TT + 1], op0=ALU.mult, op1=ALU.add),
                             reads=[bdd, bh, self.bconst], writes=[bxm[c]])
                for li, (wt, c, fn, m) in enumerate(((w1, 3, AF.Tanh, 64), (a1, 4, AF.Copy, 64), (g1, 5, AF.Sigmoid, 128))):
                    ps, bps = P_.next()
                    for k in range(KT):
                        S.op("pe", lambda e: e.matmul(ps[0:m, :], lhsT=wt[:, k, :], rhs=xm[c][:, k, :], start=(k == 0), stop=(k == KT - 1)),
                             reads=[wbuf, bxm[c]], writes=[bps], inc=(k == KT - 1), acc=True)
                    S.op("act", lambda e: e.activation(out=lor[li][0:m, :], in_=ps[0:m, :], func=fn), reads=[bps], writes=[blor[li]])
                for blk in range(4):
                    vt, bvt = vtr.next()
                    for half in range(2):
                        ps, bps = P_.next()
                        for k in range(KT):
                            S.op("pe", lambda e: e.matmul(ps[:, :], lhsT=xm[2][:, k, blk * 128:(blk + 1) * 128],
                                                          rhs=wrkv[2][:, k, half * 512:(half + 1) * 512], start=(k == 0), stop=(k == KT - 1)),
                                 reads=[wbuf, bxm[2]], writes=[bps], inc=(k == KT - 1), acc=True)
                        S.op("act", lambda e: e.activation(out=vt[:, half * 512:(half + 1) * 512], in_=ps[:, :], func=AF.Copy),
                             reads=[bps], writes=[bvt])
                    S.dma("sp", self.c_vtok[jt * 4 + blk, :, :], vt[:], reads=[bvt])
                for e_ in range(KT):
                    es = slice(e_ * 128, (e_ + 1) * 128)
                    pss = []
                    for c in range(3):
                        ps, bps = P_.next()
                        for k in range(KT):
                            S.op("pe", lambda e: e.matmul(ps[:, :], lhsT=wrkv[c][:, k, es], rhs=xm[c][:, k, :], start=(k == 0), stop=(k == KT - 1)),
                                 reads=[wbuf, bxm[c]], writes=[bps], inc=(k == KT - 1), acc=True)
                        pss.append((ps, bps))
                    (r_ps, br_), (k_ps, bk_), (v_ps, bv_) = pss
                    wl_ps, bwl = P_.next()
                    S.op("pe", lambda e: e.matmul(wl_ps[:, :], lhsT=w2[0:64, es], rhs=lor[0][0:64, :], start=True, stop=True),
                         reads=[wbuf, blor[0]], writes=[bwl], acc=True)
                    al_ps, bal = P_.next()
                    S.op("pe", lambda e: e.matmul(al_ps[:, :], lhsT=a2[0:64, es], rhs=lor[1][0:64, :], start=True, stop=True),
                         reads=[wbuf, blor[1]], writes=[bal], acc=True)
                    g_ps, bg_ = P_.next()
                    S.op("pe", lambda e: e.matmul(g_ps[:, :], lhsT=g2[:, es], rhs=lor[2][:, :], start=True, stop=True),
                         reads=[wbuf, blor[2]], writes=[bg_], acc=True)
                    rf, brf = tr.next()
                    S.op("act", lambda e: e.activation(out=rf[:], in_=r_ps[:, :], func=AF.Copy), reads=[br_], writes=[brf])
                    kf, bkf = tr.next()
                    S.op("act", lambda e: e.activation(out=kf[:], in_=k_ps[:, :], func=AF.Copy), reads=[bk_], writes=[bkf])
                    vf, bvf = tr.next()
                    S.op("act", lambda e: e.activation(out=vf[:], in_=v_ps[:, :], func=AF.Copy), reads=[bv_], writes=[bvf])
                    gb, bgb = br.next()
                    S.op("act", lambda e: e.activation(out=gb[:], in_=g_ps[:, :], func=AF.Copy), reads=[bg_], writes=[bgb])
                    S.dma("sp", gD[e_, :, sl], gb[:], reads=[bgb])
                    sg, bsg = tr.next()
                    S.op("act", lambda e: e.activation(out=sg[:], in_=wl_ps[:, :], func=AF.Sigmoid, bias=self.col("c_w0", e_)),
                         reads=[bwl, self.bconst], writes=[bsg])
                    al, bal2 = tr.next()
                    S.op("act", lambda e: e.activation(out=al[:], in_=al_ps[:, :], func=AF.Sigmoid, bias=self.col("c_a0", e_)),
                         reads=[bal, self.bconst], writes=[bal2])
                    kk, bkk = tr.next()
                    S.op("dve", lambda e: e.tensor_scalar(out=kk[:], in0=kf[:], scalar1=self.col("c_k_k", e_), scalar2=None, op0=ALU.mult),
                         reads=[bkf, self.bconst], writes=[bkk])
                    k2, bk2 = br.next()
                    S.op("act", lambda e: e.activation(out=k2[:], in_=kk[:], func=AF.Square), reads=[bkk], writes=[bk2])
                    ss_ps, bss = P_.next()
                    S.op("pe", lambda e: e.matmul(ss_ps[:, :], lhsT=bones[:, :], rhs=k2[:], start=True, stop=True),
                         reads=[bm, bk2], writes=[bss], acc=True)
                    rn, brn = tr.next()
                    S.op("act", lambda e: e.activation(out=rn[:], in_=ss_ps[:, :], func=AF.Sqrt), reads=[bss], writes=[brn])
                    S.op("dve", lambda e: e.tensor_scalar(out=rn[:], in0=rn[:], scalar1=1e-12, scalar2=None, op0=ALU.max), reads=[brn], writes=[brn])
                    rn2, brn2 = tr.next()
                    S.op("dve", lambda e: e.reciprocal(out=rn2[:], in_=rn[:]), reads=[brn], writes=[brn2])
                    S.op("dve", lambda e: e.tensor_tensor(out=kk[:], in0=kk[:], in1=rn2[:], op=ALU.mult), reads=[bkk, brn2], writes=[bkk])
                    cs, bcs = tr.next()
                    S.op("dve", lambda e: e.tensor_tensor_scan(out=cs[:], data0=cm01[:], data1=sg[:], initial=0.0, op0=ALU.mult, op1=ALU.add),
                         reads=[bm, bsg], writes=[bcs])
                    csx, bcsx = tr.next()
                    S.op("dve", lambda e: e.tensor_tensor(out=csx[:], in0=cs[:], in1=sg[:], op=ALU.subtract), reads=[bcs, bsg], writes=[bcsx])
                    eG, beG = tr.next()
                    S.op("act", lambda e: e.activation(out=eG[:], in_=cs[:], func=AF.Exp, scale=-LAM), reads=[bcs], writes=[beG])
                    eGi, beGi = tr.next()
                    S.op("act", lambda e: e.activation(out=eGi[:], in_=cs[:], func=AF.Exp, scale=LAM), reads=[bcs], writes=[beGi])
                    S.op("act", lambda e: e.activation(out=csx[:], in_=csx[:], func=AF.Exp, scale=-LAM), reads=[bcsx], writes=[bcsx])
                    ar, bar = arr.next()
                    S.op("dve", lambda e: e.scalar_tensor_tensor(out=ar[:, :, 0, :], in0=kk[:].rearrange("p (c t) -> p c t", c=4), scalar=-1.0,
                                                                 in1=csx[:].rearrange("p (c t) -> p c t", c=4), op0=ALU.mult, op1=ALU.mult),
                         reads=[bkk, bcsx], writes=[bar])
                    S.op("dve", lambda e: e.tensor_tensor(out=kk[:], in0=kk[:], in1=al[:], op=ALU.mult), reads=[bkk, bal2], writes=[bkk])
                    bt_, bbt = br.next()
                    S.op("dve", lambda e: e.tensor_tensor(out=bt_[:], in0=kk[:], in1=eGi[:], op=ALU.mult), reads=[bkk, beGi], writes=[bbt])
                    S.dma("sp", BTd[e_, :, sl], bt_[:], reads=[bbt])
                    S.op("dve", lambda e: e.tensor_scalar(out=al[:], in0=al[:], scalar1=-1.0, scalar2=self.col("c_k_a", e_), op0=ALU.add, op1=ALU.mult),
                         reads=[bal2, self.bconst], writes=[bal2])
                    S.op("dve", lambda e: e.scalar_tensor_tensor(out=kf[:], in0=al[:], scalar=1.0, in1=kf[:], op0=ALU.add, op1=ALU.mult),
                         reads=[bal2, bkf], writes=[bkf])
                    kt_, bkt = br.next()
                    S.op("dve", lambda e: e.tensor_tensor(out=kt_[:], in0=kf[:], in1=eGi[:], op=ALU.mult), reads=[bkf, beGi], writes=[bkt])
                    S.dma("sp", KTd[e_, :, sl], kt_[:], reads=[bkt])
                    S.op("dve", lambda e: e.tensor_tensor(out=ar[:, :, 1, :], in0=rf[:].rearrange("p (c t) -> p c t", c=4),
                                                          in1=eG[:].rearrange("p (c t) -> p c t", c=4), op=ALU.mult),
                         reads=[brf, beG], writes=[bar])
                    S.dma("sp", self.c_AR[e_, :, jt * 1024:(jt + 1) * 1024], ar[:].rearrange("p a b c -> p (a b c)"), reads=[bar])
                    rk, brk = br.next()
                    S.op("dve", lambda e: e.scalar_tensor_tensor(out=rk[:], in0=rf[:], scalar=self.col("c_r_k", e_), in1=kf[:],
                                                                 op0=ALU.mult, op1=ALU.mult), reads=[brf, bkf, self.bconst], writes=[brk])
                    rk_ps, brkp = P_.next()
                    S.op("pe", lambda e: e.matmul(rk_ps[:, :], lhsT=bones[:, :], rhs=rk[:], start=True, stop=True),
                         reads=[bm, brk], writes=[brkp], acc=True)
                    S.op("dve", lambda e: e.tensor_tensor(out=vf[:], in0=rk_ps[:, :], in1=vf[:], op=ALU.mult), reads=[brkp, bvf], writes=[bvf])
                    S.dma("sp", bonD[e_, :, sl], vf[:], reads=[bvf])
                    S.op("act", lambda e: e.activation(out=gct[:, e_, jt * 4:(jt + 1) * 4], in_=eG[:, 127:TT:128], func=AF.Copy),
                         reads=[beG], writes=[bgct])
            for e_ in range(KT):
                S.dma("sp", self.c_gc[e_, :, :], gct[:, e_, :], reads=[bgct])
            self.barrier()


def make_in_maps(inp):
    f32 = np.float32
    cols = pack_cols(inp)
    wqkv = np.ascontiguousarray(inp["b_w_qkv"][0], dtype=f32)
    qk = wqkv[:, :6 * D].reshape(D, 6 * 16, 2, 32)
    wqks = np.ascontiguousarray(qk[:, :, ::-1, :].reshape(D, 6 * D))
    shared = {
        "cols": cols,
        "a_w_in": inp["a_w_in"], "a_gate_w": inp["a_gate_w"], "a_w_out": inp["a_w_out"],
        "b_w_qkv": wqkv, "b_w_qks": wqks, "b_w_out": inp["b_w_out"][0],
        "c_w_rkv": inp["c_w_rkv"][0], "c_w1": inp["c_w1"][0], "c_w2": inp["c_w2"][0],
        "c_a1": inp["c_a1"][0], "c_a2": inp["c_a2"][0], "c_g1": inp["c_g1"][0], "c_g2": inp["c_g2"][0],
        "c_w_out": inp["c_w_out"][0],
        "f_w_up": inp["f_w_up"], "f_w_down": inp["f_w_down"],
        "ple_w_proj": inp["ple_w_proj"], "ple_w_gate": inp["ple_w_gate"],
    }
    shared = {k: np.ascontiguousarray(v, dtype=f32) for k, v in shared.items()}
    maps = []
    for c in range(8):
        b = c % NB
        m = dict(shared)
        m["xT"] = np.ascontiguousarray(np.asarray(inp["x"][b], dtype=f32).T).reshape(KT, 128, T)
        m["pT"] = np.ascontiguousarray(np.transpose(np.asarray(inp["p"][:, b], dtype=f32), (0, 2, 1))).reshape(DEPTH, 2, 128, T)
        m["pos"] = np.ascontiguousarray(np.asarray(inp["positions"][b], dtype=np.int32)).reshape(1, T)
        maps.append(m)
    return maps


_PROG_CACHE = {}


def run_prog(inp, layers=DEPTH, dbg=None):
    key = (str(layers), dbg)
    if key not in _PROG_CACHE:
        _PROG_CACHE[key] = Prog(layers, dbg)
    prog = _PROG_CACHE[key]
    maps = make_in_maps(inp)
    used = set(prog.W.keys()) | {"xT", "pT", "pos", "cols"}
    maps = [{k: v for k, v in m.items() if k in used} for m in maps]
    res = run_bass_kernel_spmd(prog.nc, maps, core_ids=list(range(8)))
    out = np.stack([np.asarray(res.results[b]["yT"]).reshape(D, T).T for b in range(NB)])
    return np.ascontiguousarray(out.astype(np.float32)), res


def kernel(**inputs):
    inp = {k: np.asarray(v) for k, v in inputs.items()}
    out, _ = run_prog(inp)
    return out
```

```python
import numpy as np
import concourse.bass as bass
import concourse.mybir as mybir
from concourse.bass_utils import run_bass_kernel_spmd

F32 = mybir.dt.float32
BF16 = mybir.dt.bfloat16
I32 = mybir.dt.int32
AF = mybir.ActivationFunctionType
ALU = mybir.AluOpType
AX = mybir.AxisListType


class Buf:
    __slots__ = ("w", "r", "name")

    def __init__(self, name=""):
        self.w = None
        self.r = {}
        self.name = name


class _Eng:
    def __init__(self, name, eng, sem):
        self.name, self.e, self.sem = name, eng, sem
        self.cnt = 0
        self.seen = {}
        self.pending = []

    def wait(self, ev):
        sem, val = ev
        k = id(sem)
        if self.seen.get(k, 0) >= val:
            return
        self.e.wait_ge(sem, val)
        self.seen[k] = val


class Sched:
    NDMA = 8

    def __init__(self, nc, stack):
        self.nc = nc
        self.engs = {}
        for name, eng in (("pe", nc.tensor), ("act", nc.scalar), ("dve", nc.vector),
                          ("pool", nc.gpsimd), ("sp", nc.sync)):
            sem = stack.enter_context(nc.semaphore("sem_" + name))
            self.engs[name] = _Eng(name, eng, sem)
        self.dma_slots = {}
        for q in ("sp", "pool", "act"):
            sl = []
            for i in range(self.NDMA):
                sem = stack.enter_context(nc.semaphore("dq_%s%d" % (q, i)))
                sl.append([sem, 0])
            self.dma_slots[q] = [sl, 0]
        self.n_inst = 0

    def op(self, engname, fn, reads=(), writes=(), inc=True, acc=False):
        E = self.engs[engname]
        for b in reads:
            if b.w is not None:
                E.wait(b.w)
        for b in writes:
            if b.w is not None and not (acc and b.w[0] is E.sem):
                E.wait(b.w)
            for ev in b.r.values():
                if ev[0] is not E.sem:
                    E.wait(ev)
        inst = fn(E.e)
        self.n_inst += 1
        if inc:
            E.cnt += 1
            inst.then_inc(E.sem, 1)
            ev = (E.sem, E.cnt)
            E.pending.append((reads, writes))
            for rd, wr in E.pending:
                for b in rd:
                    b.r[id(E.sem)] = ev
                for b in wr:
                    b.w = ev
                    b.r = {}
            E.pending = []
            E.seen[id(E.sem)] = max(E.seen.get(id(E.sem), 0), 0)
        else:
            E.pending.append((reads, writes))
        return inst

    def dma(self, q, out, in_, reads=(), writes=()):
        E = self.engs[q]
        slots, idx = self.dma_slots[q]
        slot = slots[idx % self.NDMA]
        self.dma_slots[q][1] = idx + 1
        if slot[1] > 0:
            E.wait((slot[0], slot[1]))
        for b in reads:
            if b.w is not None:
                E.wait(b.w)
        for b in writes:
            if b.w is not None:
                E.wait(b.w)
            for ev in b.r.values():
                E.wait(ev)
        inst = E.e.dma_start(out=out, in_=in_)
        self.n_inst += 1
        slot[1] += 16
        inst.then_inc(slot[0], 16)
        ev = (slot[0], slot[1])
        for b in reads:
            b.r[id(slot[0])] = ev
        for b in writes:
            b.w = ev
            b.r = {}
        return ev

    def wait_all(self, engname, bufs):
        E = self.engs[engname]
        for b in bufs:
            if b.w is not None:
                E.wait(b.w)
            for ev in b.r.values():
                E.wait(ev)


class Ring:
    _uid = [0]

    def __init__(self, nc, stack, name, shape, dtype, n, psum=False):
        self.items = []
        Ring._uid[0] += 1
        name = "%s_u%d_" % (name, Ring._uid[0])
        for i in range(n):
            if psum:
                t = stack.enter_context(nc.psum_tensor("%s%d" % (name, i), shape, dtype))
            else:
                t = stack.enter_context(nc.sbuf_tensor("%s%d" % (name, i), shape, dtype))
            self.items.append((t, Buf("%s%d" % (name, i))))
        self.i = 0

    def next(self):
        it = self.items[self.i % len(self.items)]
        self.i += 1
        return it


class SubRing(Ring):
    def __init__(self, items):
        self.items = list(items)
        self.i = 0


D = 1024
T = 4096
NB = 4
DEPTH = 4
KT = D // 128
TT = 512
NTT = T // TT
FFN = 2816
FT = FFN // 128
PLE = 256
RMS_EPS = 1e-6
GN_EPS = 64e-5
LRU_C = 8.0


class ColPack:
    def __init__(self):
        self.cols = []
        self.idx = {}

    def add(self, name, vec):
        vec = np.ascontiguousarray(vec, dtype=np.float32).reshape(-1)
        assert vec.size % 128 == 0
        n = vec.size // 128
        self.idx[name] = (len(self.cols), n)
        for i in range(n):
            self.cols.append(vec[i * 128:(i + 1) * 128])

    def array(self):
        return np.ascontiguousarray(np.stack(self.cols, axis=1))


def col_layout():
    L = []
    for i in range(DEPTH):
        L += [("norm_mix%d" % i, KT), ("norm_ffn%d" % i, KT), ("norm_ple%d" % i, KT)]
        for k in range(3):
            L.append(("f_conv_w%d_%d" % (i, k), 2 * FT))
        L.append(("f_conv_b%d" % i, 2 * FT))
    L.append(("norm_final", KT))
    for j in range(2):
        for k in range(4):
            L.append(("a_conv_w%d_%d" % (j, k), KT))
        L += [("a_conv_b%d" % j, KT), ("a_gate_b%d" % j, 2 * KT), ("a_lambda%d" % j, KT)]
    for c in range(6):
        L.append(("c_mu%d" % c, KT))
    for nm in ("c_w0", "c_a0", "c_k_k", "c_k_a", "c_r_k", "c_ln_w", "c_ln_b"):
        L.append((nm, KT))
    L.append(("invf", 1))
    L.append(("rsgn", 1))
    off = {}
    o = 0
    for nm, n in L:
        off[nm] = (o, n)
        o += n
    return off, o


COLS, NCOLS = col_layout()


def pack_cols(inp):
    cp = ColPack()
    for i in range(DEPTH):
        cp.add("norm_mix%d" % i, inp["norm_mix"][i])
        cp.add("norm_ffn%d" % i, inp["norm_ffn"][i])
        cp.add("norm_ple%d" % i, inp["norm_ple"][i])
        for k in range(3):
            cp.add("f_conv_w%d_%d" % (i, k), inp["f_conv_w"][i, k])
        cp.add("f_conv_b%d" % i, inp["f_conv_b"][i])
    cp.add("norm_final", inp["norm_final"])
    for j in range(2):
        for k in range(4):
            cp.add("a_conv_w%d_%d" % (j, k), inp["a_conv_w"][j, k])
        cp.add("a_conv_b%d" % j, inp["a_conv_b"][j])
        gb = inp["a_gate_b"][j].reshape(4, 2, 256)
        cp.add("a_gate_b%d" % j, np.concatenate([gb[:, 0].reshape(-1), gb[:, 1].reshape(-1)]))
        cp.add("a_lambda%d" % j, inp["a_lambda"][j])
    for c in range(6):
        cp.add("c_mu%d" % c, inp["c_mu"][0, c])
    for nm in ("c_w0", "c_a0", "c_k_k", "c_k_a", "c_r_k", "c_ln_w", "c_ln_b"):
        cp.add(nm, inp[nm][0])
    invf = (10000.0 ** (-np.arange(0, 64, 2, dtype=np.float32) / 64)).astype(np.float32)
    cp.add("invf", np.tile(invf, 4))
    cp.add("rsgn", np.tile(np.concatenate([-np.ones(32, np.float32), np.ones(32, np.float32)]), 2))
    assert cp.idx == COLS, "col layout mismatch"
    return cp.array()


from contextlib import ExitStack


class _Cut(Exception):
    pass


class Prog:
    def cut(self, name):
        if self.dbg == name:
            raise _Cut()

    def __init__(self, layers=DEPTH, dbg=None):
        self.layers = list(range(layers)) if isinstance(layers, int) else list(layers)
        self.dbg = dbg
        nc = bass.Bass("TRN2", target_bir_lowering=False)
        self.nc = nc
        dt = nc.dram_tensor
        self.xT = dt("xT", [KT, 128, T], F32, kind="ExternalInput").ap()
        self.pT = dt("pT", [DEPTH, 2, 128, T], F32, kind="ExternalInput").ap()
        self.pos = dt("pos", [1, T], I32, kind="ExternalInput").ap()
        self.colsD = dt("cols", [128, NCOLS], F32, kind="ExternalInput").ap()
        class _LazyW(dict):
            def __init__(s2, shapes):
                s2.shapes = shapes
            def __missing__(s2, nm):
                s2[nm] = dt(nm, s2.shapes[nm], F32, kind="ExternalInput").ap()
                return s2[nm]
        shapes = {}
        for nm, shp in (("a_w_in", [2, D, 2 * D]), ("a_gate_w", [2, 4, 256, 512]), ("a_w_out", [2, D, D]),
                        ("b_w_qkv", [D, 7 * D]), ("b_w_qks", [D, 6 * D]), ("b_w_out", [D, D]),
                        ("c_w_rkv", [3, D, D]), ("c_w1", [D, 64]), ("c_w2", [64, D]), ("c_a1", [D, 64]),
                        ("c_a2", [64, D]), ("c_g1", [D, 128]), ("c_g2", [128, D]), ("c_w_out", [D, D]),
                        ("f_w_up", [DEPTH, D, 2 * FFN]), ("f_w_down", [DEPTH, FFN, D]),
                        ("ple_w_proj", [DEPTH, PLE, D]), ("ple_w_gate", [DEPTH, D, D])):
            shapes[nm] = shp
        self.W = _LazyW(shapes)
        self.yT = dt("yT", [KT, 128, T], F32, kind="ExternalOutput").ap()
        self.hT = dt("hT", [KT, 128, T], F32, kind="Internal").ap()
        self.hnT = dt("hnT", [KT, 128, T], BF16, kind="Internal").ap()
        self.gT = dt("gT", [KT, 128, T], BF16, kind="Internal").ap()
        self.actT = dt("actT", [FT, 128, T], BF16, kind="Internal").ap()
        self.s32 = [dt("s32_%d" % i, [KT, 128, T], F32, kind="Internal").ap() for i in range(6)]
        self.s16 = [dt("s16_%d" % i, [KT, 128, T], BF16, kind="Internal").ap() for i in range(8)]
        self.build()

    def col(self, name, k=0, n=1):
        o, sz = COLS[name]
        assert k + n <= sz
        return self.cols[:, o + k:o + k + n]

    def barrier(self):
        S = self.S
        evs = [(E.sem, E.cnt) for E in S.engs.values() if E.cnt > 0]
        for q in S.dma_slots:
            for sl in S.dma_slots[q][0]:
                if sl[1] > 0:
                    evs.append((sl[0], sl[1]))
        for E in S.engs.values():
            assert not E.pending
            for ev in evs:
                E.wait(ev)

    def dump(self, name, ap, bufs, shape, dtype):
        if not self.dbg:
            return
        t = self.nc.dram_tensor("dbg_" + name, list(shape), dtype, kind="Internal").ap()
        self.S.dma("sp", t, ap, reads=list(bufs))

    def sb(self, st, name, shape, dtype):
        Ring._uid[0] += 1
        return st.enter_context(self.nc.sbuf_tensor("%s_u%d" % (name, Ring._uid[0]), shape, dtype))

    def load_w(self, ring, wap2d, kt, e0, ew):
        t, b = ring.next()
        src = wap2d.rearrange("(k p) e -> p k e", p=128)[:, :, e0:e0 + ew]
        self.S.dma("pool", t[:, 0:kt, 0:ew], src, writes=[b])
        return t, b

    def rmsnorm(self, st_rings, h, bh, gain, out, bout):
        S = self.S
        sqr, psr, rsr = st_rings
        ps, bps = psr.next()
        for k in range(KT):
            sq, bsq = sqr.next()
            S.op("act", lambda e: e.activation(out=sq[:], in_=h[:, k, :], func=AF.Square), reads=[bh], writes=[bsq])
            S.op("pe", lambda e: e.matmul(ps[:, :], lhsT=self.ones_bf[:, :], rhs=sq[:], start=(k == 0), stop=(k == KT - 1)),
                 reads=[bsq, self.bconst], writes=[bps], acc=True)
        rs, brs = rsr.next()
        S.op("act", lambda e: e.activation(out=rs[:], in_=ps[:, :], func=AF.Sqrt, scale=1.0 / D, bias=self.eps_col),
             reads=[bps, self.bconst], writes=[brs])
        rs2, brs2 = rsr.next()
        S.op("dve", lambda e: e.reciprocal(out=rs2[:], in_=rs[:]), reads=[brs], writes=[brs2])
        for k in range(KT):
            S.op("dve", lambda e: e.scalar_tensor_tensor(out=out[:, k, :], in0=h[:, k, :], scalar=self.col(gain, k),
                                                         in1=rs2[:], op0=ALU.mult, op1=ALU.mult),
                 reads=[bh, brs2, self.bconst], writes=[bout])

    def norm_rings(self, st, tag):
        nc = self.nc
        return (Ring(nc, st, "sq" + tag, [128, TT], BF16, 3),
                self.psA, Ring(nc, st, "rs" + tag, [128, TT], F32, 4))

    def tview(self, ap3, j):
        return ap3.rearrange("k p t -> p k t")[:, :, j * TT:(j + 1) * TT]

    def build(self):
        nc = self.nc
        with ExitStack() as top:
            S = Sched(nc, top)
            self.S = S
            self.cols = self.sb(top, "cols", [128, NCOLS], F32)
            self.cst = self.sb(top, "cst", [128, 8], F32)
            self.ones_bf = self.sb(top, "ones_bf", [128, 128], BF16)
            self.bconst = Buf("const")
            S.dma("sp", self.cols[:], self.colsD[:, :], writes=[self.bconst])
            S.op("pool", lambda e: e.memset(self.cst[:, 0:1], RMS_EPS), writes=[self.bconst])
            S.op("pool", lambda e: e.memset(self.cst[:, 1:2], 1.0), writes=[self.bconst])
            S.op("pool", lambda e: e.memset(self.cst[:, 2:3], 0.0), writes=[self.bconst])
            S.op("pool", lambda e: e.memset(self.cst[:, 3:4], GN_EPS), writes=[self.bconst])
            S.op("pool", lambda e: e.memset(self.ones_bf[:], 1.0), writes=[self.bconst])
            self.eps_col = self.cst[:, 0:1]
            self.one_col = self.cst[:, 1:2]
            self.zero_col = self.cst[:, 2:3]
            self.gneps_col = self.cst[:, 3:4]
            self.psbig = [top.enter_context(nc.psum_tensor("psbig%d" % q, [128, 2 * TT], F32)) for q in range(4)]
            self.psA = SubRing([(self.psbig[q // 2][:, (q % 2) * TT:(q % 2 + 1) * TT], Buf("ps%d" % q)) for q in range(8)])
            self.barrier()
            self.phase_norm0(self.layers[0])
            h_src = self.xT
            for li, i in enumerate(self.layers):
                kind, j = i % 3, i // 3
                if kind == 0:
                    self.mixer_a(i, j)
                    wout = self.W["a_w_out"][j]
                elif kind == 1:
                    self.mixer_b(i)
                    wout = self.W["b_w_out"]
                else:
                    self.mixer_c(i)
                    wout = self.W["c_w_out"]
                self.mixer_out(i, wout, h_src)
                h_src = self.hT
                self.ffn_up(i)
                last = (li == len(self.layers) - 1)
                self.tail(i, last, None if last else self.layers[li + 1])
            self.barrier()

    def phase_norm0(self, i0):
        nc, S = self.nc, self.S
        with ExitStack() as st:
            hr = Ring(nc, st, "n0h", [128, KT, TT], F32, 2)
            orr = Ring(nc, st, "n0o", [128, KT, TT], BF16, 2)
            rings = self.norm_rings(st, "n0")
            def ld(j):
                h, bh = hr.next()
                S.dma("sp", h[:], self.tview(self.xT, j), writes=[bh])
                return h, bh
            nxt = ld(0)
            for j in range(NTT):
                h, bh = nxt
                if j + 1 < NTT:
                    nxt = ld(j + 1)
                o, bo = orr.next()
                self.rmsnorm(rings, h, bh, "norm_mix%d" % i0, o, bo)
                S.dma("sp", self.tview(self.hnT, j), o[:], reads=[bo])
            self.barrier()

    def mixer_out(self, i, wout, h_src):
        nc, S = self.nc, self.S
        with ExitStack() as st:
            w = self.sb(st, "mo_w", [128, KT, D], BF16)
            bw = Buf()
            for k in range(KT):
                S.dma("pool", w[:, k, :], wout[k * 128:(k + 1) * 128, :], writes=[bw])
            gr = Ring(nc, st, "mo_g", [128, KT, TT], BF16, 2)
            hr = Ring(nc, st, "mo_h", [128, KT, TT], F32, 2)
            orr = Ring(nc, st, "mo_o", [128, KT, TT], BF16, 2)
            rings = self.norm_rings(st, "mo")
            def ld(j):
                g, bg = gr.next()
                S.dma("sp", g[:], self.tview(self.gT, j), writes=[bg])
                h, bh = hr.next()
                S.dma("sp", h[:], self.tview(h_src, j), writes=[bh])
                return g, bg, h, bh
            nxt = ld(0)
            for j in range(NTT):
                g, bg, h, bh = nxt
                if j + 1 < NTT:
                    nxt = ld(j + 1)
                for e in range(KT):
                    ps, bps = self.psA.next()
                    for k in range(KT):
                        S.op("pe", lambda en: en.matmul(ps[:, :], lhsT=w[:, k, e * 128:(e + 1) * 128], rhs=g[:, k, :],
                                                        start=(k == 0), stop=(k == KT - 1)),
                             reads=[bw, bg], writes=[bps], inc=(k == KT - 1), acc=True)
                    S.op("dve", lambda en: en.tensor_tensor(out=h[:, e, :], in0=ps[:, :], in1=h[:, e, :], op=ALU.add),
                         reads=[bps, bh], writes=[bh])
                S.dma("sp", self.tview(self.hT, j), h[:], reads=[bh])
                o, bo = orr.next()
                self.rmsnorm(rings, h, bh, "norm_ffn%d" % i, o, bo)
                S.dma("sp", self.tview(self.hnT, j), o[:], reads=[bo])
            self.barrier()

    def ffn_up(self, i):
        nc, S = self.nc, self.S
        wup = self.W["f_w_up"][i]
        with ExitStack() as st:
            hn = self.sb(st, "fu_hn", [128, KT, T], BF16)
            bhn = [Buf() for _ in range(KT)]
            for k in range(KT):
                S.dma("sp", hn[:, k, :], self.hnT[k, :, :], writes=[bhn[k]])
            wr = Ring(nc, st, "fu_w", [128, KT, 128], BF16, 4)
            stg = [[self.sb(st, "fu_stg%d_%d" % (s, b), [128, 2 + T], F32) for b in range(2)] for s in range(2)]
            bstg = [[[Buf() for _ in range(NTT + 1)] for b in range(2)] for s in range(2)]
            for s in range(2):
                for b in range(2):
                    S.op("pool", lambda e: e.memset(stg[s][b][:, 0:2], 0.0), writes=[bstg[s][b][0]])
            accr = Ring(nc, st, "fu_acc", [128, TT], F32, 4)
            sgr = Ring(nc, st, "fu_sg", [128, TT], F32, 2)
            outr = Ring(nc, st, "fu_out", [128, TT], BF16, 3)
            ldw = lambda f: [self.load_w(wr, wup, KT, (s * FT + f) * 128, 128) for s in range(2)]
            wnxt = ldw(0)
            for f in range(FT):
                wt = wnxt
                if f + 1 < FT:
                    wnxt = ldw(f + 1)
                accs = [None, None]
                for j in range(NTT):
                    for s in range(2):
                        w, bw = wt[s]
                        ps, bps = self.psA.next()
                        for k in range(KT):
                            S.op("pe", lambda en: en.matmul(ps[:, :], lhsT=w[:, k, :], rhs=hn[:, k, j * TT:(j + 1) * TT],
                                                            start=(k == 0), stop=(k == KT - 1)),
                                 reads=[bw] + bhn, writes=[bps], inc=(k == KT - 1), acc=True)
                        sg_t, sg_b = stg[s][f % 2], bstg[s][f % 2]
                        c0 = 2 + j * TT
                        S.op("act", lambda en: en.activation(out=sg_t[:, c0:c0 + TT], in_=ps[:, :], func=AF.Copy),
                             reads=[bps], writes=[sg_b[j + 1]])
                        acc, bacc = accr.next()
                        ci = s * FT + f
                        S.op("act", lambda en: en.activation(out=acc[:], in_=ps[:, :], func=AF.Identity,
                                                             scale=self.col("f_conv_w%d_2" % i, ci),
                                                             bias=self.col("f_conv_b%d" % i, ci)),
                             reads=[bps, self.bconst], writes=[bacc])
                        for kk in (1, 0):
                            S.op("dve", lambda en: en.scalar_tensor_tensor(
                                out=acc[:], in0=sg_t[:, j * TT + kk:j * TT + kk + TT],
                                scalar=self.col("f_conv_w%d_%d" % (i, kk), ci), in1=acc[:], op0=ALU.mult, op1=ALU.add),
                                reads=[sg_b[j], sg_b[j + 1], bacc, self.bconst], writes=[bacc])
                        accs[s] = (acc, bacc)
                    sg, bsg = sgr.next()
                    S.op("act", lambda en: en.activation(out=sg[:], in_=accs[0][0][:], func=AF.Silu),
                         reads=[accs[0][1]], writes=[bsg])
                    o, bo = outr.next()
                    S.op("dve", lambda en: en.tensor_tensor(out=o[:], in0=sg[:], in1=accs[1][0][:], op=ALU.mult),
                         reads=[bsg, accs[1][1]], writes=[bo])
                    S.dma("sp", self.actT[f, :, j * TT:(j + 1) * TT], o[:], reads=[bo])
            self.barrier()

    def tail(self, i, last, inext):
        nc, S = self.nc, self.S
        with ExitStack() as st:
            wd = self.sb(st, "tl_wd", [128, FT, D], BF16)
            wg = self.sb(st, "tl_wg", [128, KT, D], BF16)
            wp = self.sb(st, "tl_wp", [128, 2, D], BF16)
            bwd, bwg, bwp = Buf(), Buf(), Buf()
            for f in range(FT):
                S.dma("pool", wd[:, f, :], self.W["f_w_down"][i, f * 128:(f + 1) * 128, :], writes=[bwd])
            for k in range(KT):
                S.dma("pool", wg[:, k, :], self.W["ple_w_gate"][i, k * 128:(k + 1) * 128, :], writes=[bwg])
            for k in range(2):
                S.dma("pool", wp[:, k, :], self.W["ple_w_proj"][i, k * 128:(k + 1) * 128, :], writes=[bwp])
            ar = Ring(nc, st, "tl_a", [128, FT, TT], BF16, 2)
            hr = Ring(nc, st, "tl_h", [128, KT, TT], F32, 2)
            pr = Ring(nc, st, "tl_p", [128, 2, TT], BF16, 2)
            n3r = Ring(nc, st, "tl_n3", [128, KT, TT], BF16, 1)
            sgr = Ring(nc, st, "tl_sg", [128, TT], F32, 2)
            if last:
                orr = Ring(nc, st, "tl_o", [128, KT, TT], F32, 1)
            else:
                orr = Ring(nc, st, "tl_o", [128, KT, TT], BF16, 2)
            rings = self.norm_rings(st, "tl")
            def ld(j):
                a, ba = ar.next()
                S.dma("sp", a[:], self.tview(self.actT, j), writes=[ba])
                h, bh = hr.next()
                S.dma("sp", h[:], self.tview(self.hT, j), writes=[bh])
                p, bp = pr.next()
                S.dma("pool", p[:], self.tview(self.pT[i], j), writes=[bp])
                return a, ba, h, bh, p, bp
            nxt = ld(0)
            for j in range(NTT):
                a, ba, h, bh, p, bp = nxt
                if j + 1 < NTT:
                    nxt = ld(j + 1)
                for e in range(KT):
                    ps, bps = self.psA.next()
                    for f in range(FT):
                        S.op("pe", lambda en: en.matmul(ps[:, :], lhsT=wd[:, f, e * 128:(e + 1) * 128], rhs=a[:, f, :],
                                                        start=(f == 0), stop=(f == FT - 1)),
                             reads=[bwd, ba], writes=[bps], inc=(f == FT - 1), acc=True)
                    S.op("dve", lambda en: en.tensor_tensor(out=h[:, e, :], in0=ps[:, :], in1=h[:, e, :], op=ALU.add),
                         reads=[bps, bh], writes=[bh])
                n3, bn3 = n3r.next()
                self.rmsnorm(rings, h, bh, "norm_ple%d" % i, n3, bn3)
                for e in range(KT):
                    ps, bps = self.psA.next()
                    for k in range(KT):
                        S.op("pe", lambda en: en.matmul(ps[:, :], lhsT=wg[:, k, e * 128:(e + 1) * 128], rhs=n3[:, k, :],
                                                        start=(k == 0), stop=(k == KT - 1)),
                             reads=[bwg, bn3], writes=[bps], inc=(k == KT - 1), acc=True)
                    sg, bsg = sgr.next()
                    S.op("act", lambda en: en.activation(out=sg[:], in_=ps[:, :], func=AF.Sigmoid),
                         reads=[bps], writes=[bsg])
                    ps2, bps2 = self.psA.next()
                    for k in range(2):
                        S.op("pe", lambda en: en.matmul(ps2[:, :], lhsT=wp[:, k, e * 128:(e + 1) * 128], rhs=p[:, k, :],
                                                        start=(k == 0), stop=(k == 1)),
                             reads=[bwp, bp], writes=[bps2], inc=(k == 1), acc=True)
                    S.op("dve", lambda en: en.tensor_tensor(out=sg[:], in0=ps2[:, :], in1=sg[:], op=ALU.mult),
                         reads=[bps2, bsg], writes=[bsg])
                    S.op("dve", lambda en: en.tensor_tensor(out=h[:, e, :], in0=sg[:], in1=h[:, e, :], op=ALU.add),
                         reads=[bsg, bh], writes=[bh])
                o, bo = orr.next()
                if last:
                    self.rmsnorm(rings, h, bh, "norm_final", o, bo)
                    S.dma("sp", self.tview(self.yT, j), o[:], reads=[bo])
                else:
                    S.dma("sp", self.tview(self.hT, j), h[:], reads=[bh])
                    self.rmsnorm(rings, h, bh, "norm_mix%d" % inext, o, bo)
                    S.dma("sp", self.tview(self.hnT, j), o[:], reads=[bo])
            self.barrier()

    def mixer_a(self, i, j):
        nc, S = self.nc, self.S
        ygT, xcT, xcbT = self.s32[0], self.s32[1], self.s16[0]
        win = self.W["a_w_in"][j]
        with ExitStack() as st:
            hn = self.sb(st, "a1_hn", [128, KT, T], BF16)
            bhn = [Buf() for _ in range(KT)]
            for k in range(KT):
                S.dma("sp", hn[:, k, :], self.hnT[k, :, :], writes=[bhn[k]])
            wr = Ring(nc, st, "a1_w", [128, KT, 128], BF16, 4)
            stg = [self.sb(st, "a1_stg%d" % b, [128, 3 + T], F32) for b in range(2)]
            bstg = [[Buf() for _ in range(NTT + 1)] for b in range(2)]
            for b in range(2):
                S.op("pool", lambda e: e.memset(stg[b][:, 0:3], 0.0), writes=[bstg[b][0]])
            ygr = Ring(nc, st, "a1_yg", [128, TT], F32, 3)
            accr = Ring(nc, st, "a1_acc", [128, TT], F32, 3)
            xbr = Ring(nc, st, "a1_xb", [128, TT], BF16, 3)
            wnxt = self.load_w(wr, win, KT, 0, 128)
            for e in range(2 * KT):
                w, bw = wnxt
                if e + 1 < 2 * KT:
                    wnxt = self.load_w(wr, win, KT, (e + 1) * 128, 128)
                for jt in range(NTT):
                    ps, bps = self.psA.next()
                    for k in range(KT):
                        S.op("pe", lambda en: en.matmul(ps[:, :], lhsT=w[:, k, :], rhs=hn[:, k, jt * TT:(jt + 1) * TT],
                                                        start=(k == 0), stop=(k == KT - 1)),
                             reads=[bw] + bhn, writes=[bps], inc=(k == KT - 1), acc=True)
                    if e < KT:
                        yg, byg = ygr.next()
                        S.op("act", lambda en: en.activation(out=yg[:], in_=ps[:, :], func=AF.Gelu_apprx_tanh),
                             reads=[bps], writes=[byg])
                        S.dma("sp", ygT[e, :, jt * TT:(jt + 1) * TT], yg[:], reads=[byg])
                    else:
                        ft = e - KT
                        sg_t, sg_b = stg[ft % 2], bstg[ft % 2]
                        c0 = 3 + jt * TT
                        S.op("act", lambda en: en.activation(out=sg_t[:, c0:c0 + TT], in_=ps[:, :], func=AF.Copy),
                             reads=[bps], writes=[sg_b[jt + 1]])
                        acc, bacc = accr.next()
                        S.op("act", lambda en: en.activation(out=acc[:], in_=ps[:, :], func=AF.Identity,
                                                             scale=self.col("a_conv_w%d_3" % j, ft),
                                                             bias=self.col("a_conv_b%d" % j, ft)),
                             reads=[bps, self.bconst], writes=[bacc])
                        for kk in (2, 1, 0):
                            S.op("dve", lambda en: en.scalar_tensor_tensor(
                                out=acc[:], in0=sg_t[:, jt * TT + kk:jt * TT + kk + TT],
                                scalar=self.col("a_conv_w%d_%d" % (j, kk), ft), in1=acc[:], op0=ALU.mult, op1=ALU.add),
                                reads=[sg_b[jt], sg_b[jt + 1], bacc, self.bconst], writes=[bacc])
                        xb, bxb = xbr.next()
                        S.op("pool", lambda en: en.tensor_copy(out=xb[:], in_=acc[:]), reads=[bacc], writes=[bxb])
                        S.dma("sp", xcT[ft, :, jt * TT:(jt + 1) * TT], acc[:], reads=[bacc])
                        S.dma("sp", xcbT[ft, :, jt * TT:(jt + 1) * TT], xb[:], reads=[bxb])
            self.barrier()
        with ExitStack() as st:
            cc = self.sb(st, "a2_cc", [128, 5, KT], F32)
            bc = Buf()
            lam = self.col("a_lambda%d" % j, 0, KT)
            ev, l1, t2, mk, ccol = (cc[:, q, :] for q in range(5))
            S.op("act", lambda e: e.activation(out=ev, in_=lam, func=AF.Exp, scale=-1.0), reads=[self.bconst], writes=[bc])
            S.op("act", lambda e: e.activation(out=l1, in_=ev, func=AF.Ln, bias=self.one_col), reads=[bc, self.bconst], writes=[bc])
            S.op("dve", lambda e: e.tensor_scalar(out=t2, in0=ev, scalar1=1.0 / 3.0, scalar2=-0.5, op0=ALU.mult, op1=ALU.add), reads=[bc], writes=[bc])
            S.op("dve", lambda e: e.tensor_tensor(out=t2, in0=t2, in1=ev, op=ALU.mult), reads=[bc], writes=[bc])
            S.op("dve", lambda e: e.tensor_scalar(out=t2, in0=t2, scalar1=1.0, scalar2=None, op0=ALU.add), reads=[bc], writes=[bc])
            S.op("dve", lambda e: e.tensor_tensor(out=t2, in0=t2, in1=ev, op=ALU.mult), reads=[bc], writes=[bc])
            S.op("dve", lambda e: e.tensor_scalar(out=mk, in0=ev, scalar1=0.02, scalar2=None, op0=ALU.is_lt), reads=[bc], writes=[bc])
            S.op("dve", lambda e: e.tensor_tensor(out=t2, in0=t2, in1=l1, op=ALU.subtract), reads=[bc], writes=[bc])
            S.op("dve", lambda e: e.tensor_tensor(out=t2, in0=t2, in1=mk, op=ALU.mult), reads=[bc], writes=[bc])
            S.op("dve", lambda e: e.tensor_tensor(out=l1, in0=l1, in1=t2, op=ALU.add), reads=[bc], writes=[bc])
            S.op("dve", lambda e: e.tensor_scalar(out=ccol, in0=l1, scalar1=-LRU_C, scalar2=None, op0=ALU.mult), reads=[bc], writes=[bc])

            xbr = Ring(nc, st, "a2_xb", [128, 2, T], BF16, 2)
            gwr = Ring(nc, st, "a2_gw", [128, 2, 512], BF16, 2)
            afr = Ring(nc, st, "a2_af", [128, T], F32, 2)
            bfr = Ring(nc, st, "a2_bf", [128, T], F32, 2)
            hfr = Ring(nc, st, "a2_hf", [128, T], F32, 1)
            sqr_ = Ring(nc, st, "a2_sq", [128, T], F32, 1)
            ygr = Ring(nc, st, "a2_yg", [128, T], F32, 1)
            gor = Ring(nc, st, "a2_go", [128, T], BF16, 1)
            tr = Ring(nc, st, "a2_t", [128, TT], F32, 3)
            xcr = Ring(nc, st, "a2_xc", [128, TT], F32, 3)

            def ldh(hd):
                xb, bxb = xbr.next()
                for k in range(2):
                    S.dma("sp", xb[:, k, :], xcbT[2 * hd + k, :, :], writes=[bxb])
                gw, bgw = self.load_w(gwr, self.W["a_gate_w"][j, hd], 2, 0, 512)
                return xb, bxb, gw, bgw
            nxt = ldh(0)
            for hd in range(4):
                xb, bxb, gw, bgw = nxt
                if hd + 1 < 4:
                    nxt = ldh(hd + 1)
                for ft in range(2):
                    ftg = 2 * hd + ft
                    af, baf = afr.next()
                    bf, bbf = bfr.next()
                    yg, byg = ygr.next()
                    S.dma("sp", yg[:], ygT[ftg, :, :], writes=[byg])
                    for jt in range(NTT):
                        sl = slice(jt * TT, (jt + 1) * TT)
                        xc, bxc = xcr.next()
                        S.dma("sp", xc[:], xcT[ftg, :, sl], writes=[bxc])
                        psr, bpsr = self.psA.next()
                        psi, bpsi = self.psA.next()
                        for k in range(2):
                            S.op("pe", lambda en: en.matmul(psr[:, :], lhsT=gw[:, k, ft * 128:(ft + 1) * 128], rhs=xb[:, k, sl],
                                                            start=(k == 0), stop=(k == 1)),
                                 reads=[bgw, bxb], writes=[bpsr], inc=(k == 1), acc=True)
                        for k in range(2):
                            S.op("pe", lambda en: en.matmul(psi[:, :], lhsT=gw[:, k, 256 + ft * 128:256 + (ft + 1) * 128], rhs=xb[:, k, sl],
                                                            start=(k == 0), stop=(k == 1)),
                                 reads=[bgw, bxb], writes=[bpsi], inc=(k == 1), acc=True)
                        S.op("act", lambda en: en.activation(out=af[:, sl], in_=psr[:, :], func=AF.Sigmoid,
                                                             bias=self.col("a_gate_b%d" % j, ftg)),
                             reads=[bpsr, self.bconst], writes=[baf])
                        ig, big = tr.next()
                        S.op("act", lambda en: en.activation(out=ig[:], in_=psi[:, :], func=AF.Sigmoid,
                                                             bias=self.col("a_gate_b%d" % j, KT + ftg)),
                             reads=[bpsi, self.bconst], writes=[big])
                        S.op("dve", lambda en: en.tensor_tensor(out=bf[:, sl], in0=ig[:], in1=xc[:], op=ALU.mult),
                             reads=[big, bxc], writes=[bbf])
                    S.op("act", lambda en: en.activation(out=af[:], in_=af[:], func=AF.Exp, scale=cc[:, 4, ftg:ftg + 1]),
                         reads=[baf, bc], writes=[baf])
                    sq, bsq = sqr_.next()
                    S.op("act", lambda en: en.activation(out=sq[:], in_=af[:], func=AF.Square), reads=[baf], writes=[bsq])
                    S.op("act", lambda en: en.activation(out=sq[:], in_=sq[:], func=AF.Sqrt, scale=-1.0, bias=self.one_col),
                         reads=[bsq, self.bconst], writes=[bsq])
                    S.op("dve", lambda en: en.tensor_tensor(out=bf[:], in0=bf[:], in1=sq[:], op=ALU.mult), reads=[bbf, bsq], writes=[bbf])
                    hf, bhf = hfr.next()
                    S.op("dve", lambda en: en.tensor_tensor_scan(out=hf[:], data0=af[:], data1=bf[:], initial=0.0,
                                                                  op0=ALU.mult, op1=ALU.add),
                         reads=[baf, bbf], writes=[bhf])
                    go, bgo = gor.next()
                    S.op("pool", lambda en: en.tensor_tensor(out=go[:], in0=hf[:], in1=yg[:], op=ALU.mult),
                         reads=[bhf, byg], writes=[bgo])
                    S.dma("sp", self.gT[ftg, :, :], go[:], reads=[bgo])
            self.barrier()

    def _ring_guard(self, ring, tile):
        d = getattr(self, "_rg", None)
        if d is None:
            d = self._rg = {}
        b = d.get(id(tile))
        return [b] if b is not None else []

    def _ring_set(self, ring, tile, buf):
        if getattr(self, "_rg", None) is None:
            self._rg = {}
        self._rg[id(tile)] = buf


    def mixer_b(self, i):
        with ExitStack() as st:
            try:
                self._mixer_b_body(i, st)
            except _Cut:
                pass
            self.barrier()

    def _mixer_b_body(self, i, st):
        nc, S = self.nc, self.S
        wqkv, wqks = self.W["b_w_qkv"], self.W["b_w_qks"]
        banks = self.psA.items
        if True:
            cosT = self.sb(st, "b_cos", [128, T], BF16)
            sinT = self.sb(st, "b_sin", [128, T], BF16)
            btab = Buf()
            with ExitStack() as s2:
                posi = self.sb(s2, "b_posi", [128, T], I32)
                ang = self.sb(s2, "b_ang", [128, T], F32)
                kk = self.sb(s2, "b_kk", [128, T], F32)
                cst = self.sb(s2, "b_cst", [128, 2], F32)
                bt = Buf()
                S.dma("sp", posi[:], self.pos[0:1, :].to_broadcast([128, T]), writes=[bt])
                S.op("pool", lambda e: e.memset(cst[:, 0:1], float(np.pi / 2)), writes=[bt])
                S.op("dve", lambda e: e.tensor_copy(out=ang[:], in_=posi[:]), reads=[bt], writes=[bt])
                S.op("dve", lambda e: e.tensor_scalar(out=ang[:], in0=ang[:], scalar1=self.col("invf"), scalar2=None, op0=ALU.mult),
                     reads=[bt, self.bconst], writes=[bt])
                MAGIC = 12582912.0
                S.op("dve", lambda e: e.tensor_scalar(out=kk[:], in0=ang[:], scalar1=float(1.0 / (2 * np.pi)), scalar2=MAGIC,
                                                      op0=ALU.mult, op1=ALU.add), reads=[bt], writes=[bt])
                S.op("dve", lambda e: e.tensor_scalar(out=kk[:], in0=kk[:], scalar1=-MAGIC, scalar2=None, op0=ALU.add),
                     reads=[bt], writes=[bt])
                C1 = 6.28125
                C2 = float(2 * np.pi - C1)
                S.op("dve", lambda e: e.scalar_tensor_tensor(out=ang[:], in0=kk[:], scalar=-C1, in1=ang[:], op0=ALU.mult, op1=ALU.add),
                     reads=[bt], writes=[bt])
                S.op("dve", lambda e: e.scalar_tensor_tensor(out=ang[:], in0=kk[:], scalar=-C2, in1=ang[:], op0=ALU.mult, op1=ALU.add),
                     reads=[bt], writes=[bt])
                S.op("dve", lambda e: e.tensor_scalar(out=ang[:], in0=ang[:], scalar1=float(np.pi), scalar2=float(-np.pi),
                                                      op0=ALU.min, op1=ALU.max), reads=[bt], writes=[bt])
                S.op("act", lambda e: e.activation(out=sinT[:], in_=ang[:], func=AF.Sin, scale=self.col("rsgn")),
                     reads=[bt, self.bconst], writes=[btab])
                S.op("dve", lambda e: e.scalar_tensor_tensor(out=kk[:], in0=ang[:], scalar=-1.0, in1=ang[:], op0=ALU.mult, op1=ALU.max), reads=[bt], writes=[bt])
                S.op("act", lambda e: e.activation(out=cosT[:], in_=kk[:], func=AF.Sin, scale=-1.0, bias=cst[:, 0:1]),
                     reads=[bt], writes=[btab])
                self.barrier()
            self.cut("cutA")
            self.dump("cos", cosT[:], [btab], [128, T], BF16)
            self.dump("sin", sinT[:], [btab], [128, T], BF16)
            hn = self.sb(st, "b_hn", [128, KT, T], BF16)
            bhn = [Buf() for _ in range(KT)]
            for k in range(KT):
                S.dma("sp", hn[:, k, :], self.hnT[k, :, :], writes=[bhn[k]])
            band = self.sb(st, "b_band", [128, 2, 256], BF16)
            bband = Buf()
            S.op("pool", lambda e: e.memset(band[:], 1.0), writes=[bband])
            S.op("pool", lambda e: e.affine_select(out=band[:], in_=band[:], pattern=[[0, 2], [1, 256]], compare_op=ALU.is_ge,
                                                   fill=0.0, base=0, channel_multiplier=-1), reads=[bband], writes=[bband])
            S.op("pool", lambda e: e.affine_select(out=band[:], in_=band[:], pattern=[[0, 2], [-1, 256]], compare_op=ALU.is_ge,
                                                   fill=0.0, base=128, channel_multiplier=1), reads=[bband], writes=[bband])
            wr = Ring(nc, st, "b_w", [128, KT, 128], BF16, 4)
            wvr = Ring(nc, st, "b_wv", [128, KT, 128], BF16, 2)
            qr = Ring(nc, st, "b_q", [128, T], BF16, 2)
            kr = Ring(nc, st, "b_k", [128, T], BF16, 2)
            vr = Ring(nc, st, "b_v", [128, 32, 128], BF16, 2)
            accden = self.sb(st, "b_accden", [128, 2, T], F32)
            bacc = Buf()
            tr = Ring(nc, st, "b_t", [128, TT], F32, 4)
            er = Ring(nc, st, "b_e", [128, 2, 256], BF16, 3)
            outr = Ring(nc, st, "b_o", [128, TT], BF16, 2)
            psP = SubRing(banks[0:2])
            psS = SubRing([(self.psbig[q][:, :].rearrange("p (a b) -> p a b", a=2), Buf()) for q in (1, 2)])
            psOD = SubRing([(banks[q][0].rearrange("p (a b) -> p a b", a=2), Buf()) for q in (6, 7)])

            def proj_rot(col0, dst, bdst, d):
                w, bw = self.load_w(wr, wqkv, KT, col0, 128)
                ws, bws = self.load_w(wr, wqks, KT, col0, 128)
                for jt in range(NTT):
                    sl = slice(jt * TT, (jt + 1) * TT)
                    ps, bps = psP.next()
                    ps2, bps2 = psP.next()
                    for (pp, bpp, ww, bww) in ((ps, bps, w, bw), (ps2, bps2, ws, bws)):
                        for k in range(KT):
                            S.op("pe", lambda en: en.matmul(pp[:, :], lhsT=ww[:, k, :], rhs=hn[:, k, sl],
                                                            start=(k == 0), stop=(k == KT - 1)),
                                 reads=[bww] + bhn, writes=[bpp], inc=(k == KT - 1), acc=True)
                    t1, bt1 = tr.next()
                    t2, bt2 = tr.next()
                    S.op("dve", lambda en: en.tensor_tensor(out=t1[:], in0=ps[:, :], in1=cosT[:, sl], op=ALU.mult),
                         reads=[bps, btab], writes=[bt1])
                    S.op("dve", lambda en: en.tensor_tensor(out=t2[:], in0=ps2[:, :], in1=sinT[:, sl], op=ALU.mult),
                         reads=[bps2, btab], writes=[bt2])
                    n = TT // d
                    dv = dst[:].rearrange("p (r l) -> p r l", r=d)[:, :, jt * n:(jt + 1) * n]
                    v1 = t1[:].rearrange("p (j r) -> p r j", r=d)
                    v2 = t2[:].rearrange("p (j r) -> p r j", r=d)
                    S.op("pool", lambda en: en.tensor_tensor(out=dv, in0=v1, in1=v2, op=ALU.add),
                         reads=[bt1, bt2], writes=[bdst])

            for hp in range(KT):
                wv, bwv = self.load_w(wvr, wqkv, KT, 6 * D + hp * 128, 128)
                S.op("pool", lambda e: e.memset(accden[:], 0.0), writes=[bacc])
                for g, d in enumerate((1, 4, 16)):
                    L = T // d
                    nb = L // 128
                    qT, bq = qr.next()
                    kT, bk = kr.next()
                    proj_rot(g * D + hp * 128, qT, bq, d)
                    proj_rot(3 * D + g * D + hp * 128, kT, bk, d)
                    self.cut("cutB")
                    if hp == 0:
                        self.dump("q%d" % g, qT[:], [bq], [128, T], BF16)
                        self.dump("k%d" % g, kT[:], [bk], [128, T], BF16)
                    vt, bvt = vr.next()
                    for n0 in range(0, 32, 4):
                        ps, bps = psP.next()
                        for q4 in range(4):
                            n = n0 + q4
                            r, kb = n // nb, n % nb
                            start = kb * 128 * d + r
                            for k in range(KT):
                                S.op("pe", lambda en: en.matmul(ps[:, q4 * 128:(q4 + 1) * 128],
                                                                lhsT=hn[:, k, start:start + 127 * d + 1:d], rhs=wv[:, k, :],
                                                                start=(k == 0), stop=(k == KT - 1)),
                                     reads=[bwv] + bhn, writes=[bps], inc=(k == KT - 1 and q4 == 3), acc=True)
                        S.op("act", lambda en: en.activation(out=vt[:, n0:n0 + 4, :], in_=ps[:, :].rearrange("p (a b) -> p a b", a=4),
                                                             func=AF.Copy), reads=[bps], writes=[bvt])
                    self.cut("cutC")
                    steps = [(r, kb) for r in range(d) for kb in range(nb)]

                    def stage1(r, kb):
                        kcol = r * L + kb * 128
                        nq = 256 if kb + 1 < nb else 128
                        pS, bpS = psS.next()
                        for hh in range(2):
                            S.op("pe", lambda en: en.matmul(pS[:, hh, 0:nq],
                                                            lhsT=kT[64 * hh:64 * hh + 64, kcol:kcol + 128],
                                                            rhs=qT[64 * hh:64 * hh + 64, kcol:kcol + nq],
                                                            start=True, stop=True),
                                 reads=[bk, bq], writes=[bpS], inc=(hh == 1), acc=True)
                        E, bE = er.next()
                        S.op("act", lambda en: en.activation(out=E[:, :, 0:nq], in_=pS[:, :, 0:nq],
                                                             func=AF.Exp, scale=0.125), reads=[bpS], writes=[bE])
                        meng = "dve" if (kb % 2 == 0) else "pool"
                        S.op(meng, lambda en: en.tensor_tensor(out=E[:, :, 0:nq], in0=E[:, :, 0:nq], in1=band[:, :, 0:nq], op=ALU.mult),
                             reads=[bE, bband], writes=[bE])
                        return E, bE

                    stt = {"pod": None, "bpod": None, "npod": None, "nbpod": None, "fresh": None, "nfresh": None}

                    def stage2(r, kb, E, bE):
                        n = r * nb + kb
                        if kb == 0:
                            stt["pod"], stt["bpod"] = psOD.next()
                            stt["fresh"] = [True, True]
                        halves = [(kb, 0)]
                        if kb + 1 < nb:
                            halves.append((kb + 1, 1))
                        for (qb, hf) in halves:
                            if qb % 2 == 0 and hf == 1:
                                stt["npod"], stt["nbpod"] = psOD.next()
                                stt["nfresh"] = [True, True]
                                tp, tbp, fr = stt["npod"], stt["nbpod"], stt["nfresh"]
                            else:
                                tp, tbp, fr = stt["pod"], stt["bpod"], stt["fresh"]
                            c0 = (qb % 2) * 128
                            for hh in range(2):
                                for od in range(2):
                                    lh = vt[:, n, 64 * hh:64 * hh + 64] if od == 0 else self.ones_bf[:, 0:64]
                                    st_ = fr[hh]
                                    fr[hh] = False
                                    S.op("pe", lambda en: en.matmul(tp[64 * hh:64 * hh + 64, od, c0:c0 + 128], lhsT=lh,
                                                                    rhs=E[:, hh, hf * 128:(hf + 1) * 128], start=st_, stop=True,
                                                                    skip_group_check=True),
                                         reads=[bvt, bE, self.bconst], writes=[tbp], inc=(hh == 1 and od == 1), acc=True)
                        if kb % 2 == 1 or kb == nb - 1:
                            qb0 = (kb // 2) * 2
                            ncols = (kb - qb0 + 1) * 128
                            t0 = qb0 * 128 * d + r
                            asl = accden[:, :, t0:t0 + (ncols - 1) * d + 1:d]
                            pod, bpod = stt["pod"], stt["bpod"]
                            S.op("dve", lambda en: en.tensor_tensor(out=asl, in0=pod[:, :, 0:ncols], in1=asl, op=ALU.add),
                                 reads=[bpod, bacc], writes=[bacc])
                            if kb + 1 < nb:
                                stt["pod"], stt["bpod"], stt["fresh"] = stt["npod"], stt["nbpod"], stt["nfresh"]

                    cur = stage1(*steps[0])
                    for si, (r, kb) in enumerate(steps):
                        nxt_ = stage1(*steps[si + 1]) if si + 1 < len(steps) else None
                        stage2(r, kb, *cur)
                        cur = nxt_
                    self.cut("cutD%d" % g)
                if hp == 0:
                    self.dump("acc", accden[:, 0, :], [bacc], [128, T], F32)
                    self.dump("den", accden[:, 1, :], [bacc], [128, T], F32)
                    self.dump("vt", vt[:], [bvt], [128, 32, 128], BF16)
                for jt in range(NTT):
                    sl = slice(jt * TT, (jt + 1) * TT)
                    rc, brc = tr.next()
                    S.op("dve", lambda en: en.reciprocal(out=rc[:], in_=accden[:, 1, sl]), reads=[bacc], writes=[brc])
                    o, bo = outr.next()
                    S.op("dve", lambda en: en.tensor_tensor(out=o[:], in0=accden[:, 0, sl], in1=rc[:], op=ALU.mult),
                         reads=[bacc, brc], writes=[bo])
                    S.dma("sp", self.gT[hp, :, sl], o[:], reads=[bo])
            self.barrier()


    CH = 128
    NCH = T // 128
    LAM = float(np.exp(-0.5))

    def mixer_c(self, i):
        if not hasattr(self, "c_AR"):
            dt = self.nc.dram_tensor
            self.c_AR = dt("c_AR", [KT, 128, 2 * T], BF16, kind="Internal").ap()
            self.c_vtok = dt("c_vtok", [T // 128, 128, D], BF16, kind="Internal").ap()
            self.c_gc = dt("c_gc", [KT, 128, T // 128], F32, kind="Internal").ap()
        self.mixer_c1(i)
        self.mixer_c2(i)

    def mixer_c2(self, i):
        nc, S = self.nc, self.S
        BTd, KTd, gD, bonD = self.s16[1], self.s16[2], self.s16[3], self.s32[0]
        NCH = self.NCH
        with ExitStack() as st:
            bm = Buf()
            MSK = self.sb(st, "c2_msk", [128, 2, 4, 128], BF16)
            LM = self.sb(st, "c2_lm", [128, 2, 128], BF16)
            IDN = self.sb(st, "c2_idn", [128, 2, 128], BF16)
            bonesf = self.sb(st, "c2_bones", [128, 128], F32)
            S.op("pool", lambda e: e.memset(MSK[:], 1.0), writes=[bm])
            for par in range(2):
                S.op("pool", lambda e: e.affine_select(out=MSK[:, :, par::2, :], in_=MSK[:, :, par::2, :],
                                                       pattern=[[0, 2], [0, 2], [1, 128]], compare_op=ALU.is_ge, fill=0.0,
                                                       base=par - 1, channel_multiplier=-1), reads=[bm], writes=[bm])
            S.op("pool", lambda e: e.memset(LM[:], 1.0), writes=[bm])
            S.op("pool", lambda e: e.affine_select(out=LM[:], in_=LM[:], pattern=[[0, 2], [-1, 128]], compare_op=ALU.is_ge, fill=0.0,
                                                   base=-1, channel_multiplier=1), reads=[bm], writes=[bm])
            S.op("pool", lambda e: e.memset(IDN[:], 1.0), writes=[bm])
            S.op("pool", lambda e: e.affine_select(out=IDN[:], in_=IDN[:], pattern=[[0, 2], [-1, 128]], compare_op=ALU.is_equal, fill=0.0,
                                                   base=0, channel_multiplier=1), reads=[bm], writes=[bm])
            S.op("pool", lambda e: e.memset(bonesf[:], 0.0), writes=[bm])
            S.op("pool", lambda e: e.memset(bonesf[0:64, 0:64], 1.0 / 64), writes=[bm])
            S.op("pool", lambda e: e.memset(bonesf[64:128, 64:128], 1.0 / 64), writes=[bm])
            ident = IDN[:, 0, :]
            arr = Ring(nc, st, "c2_ar", [128, NCH, 2, 128], BF16, 2)
            btr = Ring(nc, st, "c2_bt", [128, T], BF16, 2)
            ktr = Ring(nc, st, "c2_kt", [128, T], BF16, 2)
            vtr = Ring(nc, st, "c2_vt", [128, NCH, 128], BF16, 2)
            gcr = Ring(nc, st, "c2_gc", [128, NCH], F32, 2)
            scr = Ring(nc, st, "c2_sc", [128, 2, 4, 128], BF16, 2)
            mlr = Ring(nc, st, "c2_ml", [128, 2, 2, 128], BF16, 3)
            ttr = Ring(nc, st, "c2_tt", [128, 2, 128], BF16, 3)
            tokr = Ring(nc, st, "c2_tok", [128, 2, 128], BF16, 2)
            wur = Ring(nc, st, "c2_wu", [128, 64], BF16, 6)
            Pst = self.sb(st, "c2_pst", [128, 64], F32)
            PG = self.sb(st, "c2_pg", [128, 64], F32)
            Pbf = self.sb(st, "c2_pbf", [128, 64], BF16)
            bP = [Buf(), Buf()]
            bPG = [Buf(), Buf()]
            ysr = Ring(nc, st, "c2_ys", [128, TT], F32, 2)
            gtr = Ring(nc, st, "c2_gt", [128, TT], F32, 8)
            bonr = Ring(nc, st, "c2_bon", [128, TT], F32, 2)
            ggr = Ring(nc, st, "c2_gg", [128, TT], BF16, 2)
            outr = Ring(nc, st, "c2_out", [128, TT], BF16, 2)
            scA = self.psbig[0][:, :].rearrange("p (a b) -> p a b", a=2)
            bscA = Buf()
            scB = self.psbig[1][:, :].rearrange("p (a b) -> p a b", a=2)
            bscB = Buf()
            trp = self.psbig[1][:, 256:512].bitcast(BF16).rearrange("p (a b) -> p a b", a=4)
            btrp = bscB
            mlp = self.psbig[2][:, 0:512].rearrange("p (a b c) -> p a b c", a=2, b=2)
            bmlp = Buf()
            ttp = self.psbig[2][:, 512:768].rearrange("p (a b) -> p a b", a=2)
            bttp = Buf()
            sq = self.psbig[3][:, :].rearrange("p (a b) -> p a b", a=2)
            bsq = [Buf(), Buf()]

            def ldt(e_):
                ar, bar = arr.next()
                S.dma("sp", ar[:].rearrange("p a b c -> p (a b c)"), self.c_AR[e_, :, :], writes=[bar])
                bt_, bbt = btr.next()
                S.dma("sp", bt_[:], BTd[e_, :, :], writes=[bbt])
                kt_, bkt = ktr.next()
                S.dma("sp", kt_[:], KTd[e_, :, :], writes=[bkt])
                vt, bvt = vtr.next()
                S.dma("sp", vt[:], self.c_vtok.rearrange("c p f -> p c f")[:, :, e_ * 128:(e_ + 1) * 128], writes=[bvt])
                gc, bgc = gcr.next()
                S.dma("sp", gc[:], self.c_gc[e_, :, :], writes=[bgc])
                return ar, bar, bt_, bbt, kt_, bkt, vt, bvt, gc, bgc
            nxt = ldt(0)
            for e_ in range(KT):
                ar, bar, bt_, bbt, kt_, bkt, vt, bvt, gc, bgc = nxt
                if e_ + 1 < KT:
                    nxt = ldt(e_ + 1)
                for hh in range(2):
                    P = slice(64 * hh, 64 * hh + 64)
                    S.op("pool", lambda e: e.memset(Pst[P, :], 0.0), writes=[bP[hh]])
                    S.op("pool", lambda e: e.memset(Pbf[P, :], 0.0), writes=[bP[hh]])
                ys = bys = None
                for c in range(NCH):
                    cs_ = slice(c * 128, (c + 1) * 128)
                    for hh in range(2):
                        P = slice(64 * hh, 64 * hh + 64)
                        S.op("pe", lambda e: e.matmul(scA[:, hh, 0:256], lhsT=bt_[P, cs_], rhs=ar[P, c, :, :].rearrange("p a b -> p (a b)"),
                                                      start=True, stop=True), reads=[bbt, bar], writes=[bscA], inc=False, acc=True)
                    for hh in range(2):
                        P = slice(64 * hh, 64 * hh + 64)
                        S.op("pe", lambda e: e.matmul(scA[:, hh, 256:512], lhsT=kt_[P, cs_], rhs=ar[P, c, :, :].rearrange("p a b -> p (a b)"),
                                                      start=True, stop=True), reads=[bkt, bar], writes=[bscA], inc=(hh == 1), acc=True)
                    for hh in range(2):
                        P = slice(64 * hh, 64 * hh + 64)
                        S.op("pe", lambda e: e.matmul(scB[:, hh, 0:128], lhsT=ar[P, c, 0, :], rhs=bt_[P, cs_],
                                                      start=True, stop=True), reads=[bbt, bar], writes=[bscB], inc=(hh == 1), acc=True)
                    S.op("pe", lambda e: e.transpose(trp[:, 0, :], bt_[:, cs_], ident), reads=[bbt, bm], writes=[btrp], inc=False, acc=True)
                    S.op("pe", lambda e: e.transpose(trp[:, 1, :], kt_[:, cs_], ident), reads=[bkt, bm], writes=[btrp], acc=True)
                    SC, bSC = scr.next()
                    S.op("dve", lambda e: e.tensor_tensor(out=SC[:].rearrange("p a b c -> p a (b c)"), in0=scA[:, :, :],
                                                          in1=MSK[:].rearrange("p a b c -> p a (b c)"), op=ALU.mult),
                         reads=[bscA, bm], writes=[bSC])
                    ML, bML = mlr.next()
                    S.op("dve", lambda e: e.tensor_tensor(out=ML[:, :, 1, :], in0=scB[:, :, 0:128], in1=LM[:], op=ALU.mult),
                         reads=[bscB, bm], writes=[bML])
                    S.op("pool", lambda e: e.tensor_copy(out=ML[:, :, 0, :], in_=SC[:, :, 0, :]), reads=[bSC], writes=[bML])
                    tok, btok = tokr.next()
                    S.op("act", lambda e: e.activation(out=tok[:], in_=trp[:, 0:2, :], func=AF.Copy), reads=[btrp], writes=[btok])
                    TTc, bTT = ttr.next()
                    S.op("pool", lambda e: e.tensor_tensor(out=TTc[:], in0=SC[:, :, 0, :], in1=IDN[:], op=ALU.add), reads=[bSC, bm], writes=[bTT])
                    for lev in range(1, 7):
                        MLn, bMLn = mlr.next()
                        for hh in range(2):
                            if lev < 6:
                                S.op("pe", lambda e: e.matmul(mlp[:, hh, 0, :], lhsT=ML[:, hh, 1, :], rhs=ML[:, hh, 0, :], start=True, stop=True),
                                     reads=[bML], writes=[bmlp], inc=False, acc=True)
                            S.op("pe", lambda e: e.matmul(mlp[:, hh, 1, :], lhsT=ML[:, hh, 0, :], rhs=ML[:, hh, 1, :], start=True, stop=True),
                                 reads=[bML], writes=[bmlp], inc=(hh == 1), acc=True)
                        if lev < 6:
                            S.op("act", lambda e: e.activation(out=MLn[:], in_=mlp[:, :, :, :], func=AF.Copy), reads=[bmlp], writes=[bMLn])
                        else:
                            S.op("act", lambda e: e.activation(out=MLn[:, :, 1, :], in_=mlp[:, :, 1, :], func=AF.Copy), reads=[bmlp], writes=[bMLn])
                        for hh in range(2):
                            S.op("pe", lambda e: e.matmul(ttp[:, hh, :], lhsT=MLn[:, hh, 1, :], rhs=TTc[:, hh, :], start=True, stop=True),
                                 reads=[bMLn, bTT], writes=[bttp], inc=(hh == 1), acc=True)
                        TTn, bTTn = ttr.next()
                        S.op("dve", lambda e: e.tensor_tensor(out=TTn[:], in0=ttp[:, :, :], in1=TTc[:], op=ALU.add), reads=[bttp, bTT], writes=[bTTn])
                        ML, bML, TTc, bTT = MLn, bMLn, TTn, bTTn
                    if c % 4 == 0:
                        ys, bys = ysr.next()
                    for hh in range(2):
                        P = slice(64 * hh, 64 * hh + 64)
                        vh = vt[:, c, 64 * hh:64 * hh + 64]
                        S.op("pe", lambda e: e.matmul(sq[:, hh, 0:64], lhsT=SC[:, hh, 2, :], rhs=vh, start=True, stop=False),
                             reads=[bSC, bvt], writes=[bsq[hh]], inc=False, acc=True)
                        S.op("pe", lambda e: e.matmul(sq[:, hh, 0:64], lhsT=ar[P, c, 0, :], rhs=Pbf[P, :], start=False, stop=True),
                             reads=[bar, bP[hh]], writes=[bsq[hh]], acc=True)
                        Wsb, bW = wur.next()
                        S.op("act", lambda e: e.activation(out=Wsb[:], in_=sq[:, hh, 0:64], func=AF.Copy), reads=[bsq[hh]], writes=[bW])
                        S.op("pe", lambda e: e.matmul(sq[:, hh, 64:128], lhsT=TTc[:, hh, :], rhs=Wsb[:], start=True, stop=True),
                             reads=[bTT, bW], writes=[bsq[hh]], acc=True)
                        Usb, bU = wur.next()
                        S.op("act", lambda e: e.activation(out=Usb[:], in_=sq[:, hh, 64:128], func=AF.Copy), reads=[bsq[hh]], writes=[bU])
                        S.op("pe", lambda e: e.matmul(sq[P, hh, 128:256], lhsT=vh, rhs=SC[:, hh, 3, :], start=True, stop=False),
                             reads=[bvt, bSC], writes=[bsq[hh]], inc=False, acc=True)
                        S.op("pe", lambda e: e.matmul(sq[P, hh, 128:256], lhsT=Usb[:], rhs=SC[:, hh, 1, :], start=False, stop=False),
                             reads=[bU, bSC], writes=[bsq[hh]], inc=False, acc=True)
                        S.op("pe", lambda e: e.matmul(sq[P, hh, 128:256], lhsT=Pbf[P, :], rhs=ar[P, c, 1, :], start=False, stop=True),
                             reads=[bP[hh], bar], writes=[bsq[hh]], acc=True)
                        S.op("act", lambda e: e.activation(out=ys[P, (c % 4) * 128:(c % 4 + 1) * 128], in_=sq[P, hh, 128:256], func=AF.Copy),
                             reads=[bsq[hh]], writes=[bys])
                        S.op("dve", lambda e: e.tensor_scalar(out=PG[P, :], in0=Pst[P, :], scalar1=gc[P, c:c + 1], scalar2=None, op0=ALU.mult),
                             reads=[bP[hh], bgc], writes=[bPG[hh]])
                        S.op("pe", lambda e: e.matmul(sq[P, hh, 256:320], lhsT=tok[:, 0, 64 * hh:64 * hh + 64], rhs=Usb[:], start=True, stop=False),
                             reads=[btok, bU], writes=[bsq[hh]], inc=False, acc=True)
                        S.op("pe", lambda e: e.matmul(sq[P, hh, 256:320], lhsT=tok[:, 1, 64 * hh:64 * hh + 64], rhs=vh, start=False, stop=True),
                             reads=[btok, bvt], writes=[bsq[hh]], acc=True)
                        S.op("dve", lambda e: e.scalar_tensor_tensor(out=Pst[P, :], in0=sq[P, hh, 256:320], scalar=gc[P, c:c + 1], in1=PG[P, :],
                                                                     op0=ALU.mult, op1=ALU.add),
                             reads=[bsq[hh], bgc, bPG[hh]], writes=[bP[hh]])
                        S.op("act", lambda e: e.activation(out=Pbf[P, :], in_=Pst[P, :], func=AF.Copy), reads=[bP[hh]], writes=[bP[hh]])
                    if c % 4 == 3:
                        jt = c // 4
                        sl = slice(jt * TT, (jt + 1) * TT)
                        bon, bbon = bonr.next()
                        S.dma("sp", bon[:], bonD[e_, :, sl], writes=[bbon])
                        gg, bgg = ggr.next()
                        S.dma("sp", gg[:], gD[e_, :, sl], writes=[bgg])
                        mean_ps, ex2_ps = scA[:, 0, :], scA[:, 1, :]
                        ysq, bysq = gtr.next()
                        S.op("act", lambda e: e.activation(out=ysq[:], in_=ys[:], func=AF.Square), reads=[bys], writes=[bysq])
                        S.op("pe", lambda e: e.matmul(mean_ps, lhsT=bonesf[:, :], rhs=ys[:], start=True, stop=True),
                             reads=[bm, bys], writes=[bscA], inc=False, acc=True)
                        S.op("pe", lambda e: e.matmul(ex2_ps, lhsT=bonesf[:, :], rhs=ysq[:], start=True, stop=True),
                             reads=[bm, bysq], writes=[bscA], acc=True)
                        msq, bmsq = gtr.next()
                        S.op("act", lambda e: e.activation(out=msq[:], in_=mean_ps, func=AF.Square), reads=[bscA], writes=[bmsq])
                        S.op("dve", lambda e: e.tensor_tensor(out=msq[:], in0=ex2_ps, in1=msq[:], op=ALU.subtract), reads=[bscA, bmsq], writes=[bmsq])
                        S.op("dve", lambda e: e.tensor_scalar(out=msq[:], in0=msq[:], scalar1=0.0, scalar2=None, op0=ALU.max), reads=[bmsq], writes=[bmsq])
                        S.op("act", lambda e: e.activation(out=msq[:], in_=msq[:], func=AF.Sqrt, bias=self.gneps_col), reads=[bmsq, self.bconst], writes=[bmsq])
                        msq_in, bmsq_in = msq, bmsq
                        msq, bmsq = gtr.next()
                        S.op("dve", lambda e: e.reciprocal(out=msq[:], in_=msq_in[:]), reads=[bmsq_in], writes=[bmsq])
                        yc, byc = gtr.next()
                        S.op("dve", lambda e: e.tensor_tensor(out=yc[:], in0=mean_ps, in1=ys[:], op=ALU.subtract), reads=[bscA, bys], writes=[byc])
                        S.op("dve", lambda e: e.tensor_tensor(out=yc[:], in0=yc[:], in1=msq[:], op=ALU.mult), reads=[byc, bmsq], writes=[byc])
                        S.op("dve", lambda e: e.tensor_scalar(out=yc[:], in0=yc[:], scalar1=-1.0, scalar2=self.col("c_ln_w", e_), op0=ALU.mult, op1=ALU.mult),
                             reads=[byc, self.bconst], writes=[byc])
                        S.op("dve", lambda e: e.scalar_tensor_tensor(out=yc[:], in0=yc[:], scalar=self.col("c_ln_b", e_), in1=bon[:], op0=ALU.add, op1=ALU.add),
                             reads=[byc, bbon, self.bconst], writes=[byc])
                        o, bo = outr.next()
                        S.op("dve", lambda e: e.tensor_tensor(out=o[:], in0=yc[:], in1=gg[:], op=ALU.mult), reads=[byc, bgg], writes=[bo])
                        S.dma("sp", self.gT[e_, :, sl], o[:], reads=[bo])
            self.barrier()

    def mixer_c1(self, i):
        nc, S = self.nc, self.S
        W = self.W
        LAM = self.LAM
        BTd, KTd, gD, bonD = self.s16[1], self.s16[2], self.s16[3], self.s32[0]
        with ExitStack() as st:
            wbuf = Buf()
            wrkv = [self.sb(st, "c_wrkv%d" % c, [128, KT, D], BF16) for c in range(3)]
            for c in range(3):
                for k in range(KT):
                    S.dma("pool", wrkv[c][:, k, :], W["c_w_rkv"][c, k * 128:(k + 1) * 128, :], writes=[wbuf])
            w1 = self.sb(st, "c_w1", [128, KT, 64], BF16)
            a1 = self.sb(st, "c_a1", [128, KT, 64], BF16)
            g1 = self.sb(st, "c_g1", [128, KT, 128], BF16)
            w2 = self.sb(st, "c_w2", [128, D], BF16)
            a2 = self.sb(st, "c_a2", [128, D], BF16)
            g2 = self.sb(st, "c_g2", [128, D], BF16)
            S.dma("pool", w1[:], W["c_w1"].rearrange("(k p) e -> p k e", p=128), writes=[wbuf])
            S.dma("pool", a1[:], W["c_a1"].rearrange("(k p) e -> p k e", p=128), writes=[wbuf])
            S.dma("pool", g1[:], W["c_g1"].rearrange("(k p) e -> p k e", p=128), writes=[wbuf])
            S.dma("pool", w2[0:64, :], W["c_w2"][:, :], writes=[wbuf])
            S.dma("pool", a2[0:64, :], W["c_a2"][:, :], writes=[wbuf])
            S.dma("pool", g2[:, :], W["c_g2"][:, :], writes=[wbuf])
            cm01 = self.sb(st, "c_cm01", [128, TT], F32)
            bones = self.sb(st, "c_bones", [128, 128], BF16)
            bm = Buf()
            S.op("pool", lambda e: e.memset(cm01[:], 1.0), writes=[bm])
            for q in range(4):
                S.op("pool", lambda e: e.memset(cm01[:, q * 128:q * 128 + 1], 0.0), writes=[bm])
            S.op("pool", lambda e: e.memset(bones[:], 0.0), writes=[bm])
            S.op("pool", lambda e: e.memset(bones[0:64, 0:64], 1.0), writes=[bm])
            S.op("pool", lambda e: e.memset(bones[64:128, 64:128], 1.0), writes=[bm])
            hr = Ring(nc, st, "c_hn", [128, KT, TT + 1], BF16, 2)
            dd = self.sb(st, "c_d", [128, KT, TT], F32)
            bdd = Buf()
            xm = [self.sb(st, "c_xm%d" % c, [128, KT, TT], BF16) for c in range(6)]
            bxm = [Buf() for _ in range(6)]
            lor = [self.sb(st, "c_lor%d" % c, [128, TT], BF16) for c in range(3)]
            blor = [Buf() for _ in range(3)]
            trA = Ring(nc, st, "c_tA", [128, TT], F32, 12)
            trB = Ring(nc, st, "c_tB", [128, TT], F32, 8)
            brA = Ring(nc, st, "c_bA", [128, TT], BF16, 4)
            brB = Ring(nc, st, "c_bB", [128, TT], BF16, 6)
            psF = SubRing(self.psA.items[0:5])
            psB = SubRing(self.psA.items[5:8])
            arr = Ring(nc, st, "c_ar", [128, 4, 2, 128], BF16, 2)
            vtr = Ring(nc, st, "c_vt", [128, D], BF16, 2)
            gct = self.sb(st, "c_gct", [128, KT, T // 128], F32)
            bgct = Buf()
            P_ = self.psA

            def ldh(jt):
                h, bh = hr.next()
                if jt == 0:
                    S.op("pool", lambda e: e.memset(h[:, :, 0:1], 0.0), writes=[bh])
                    S.dma("sp", h[:, :, 1:TT + 1], self.tview(self.hnT, 0), writes=[bh])
                else:
                    S.dma("sp", h[:, :, :], self.hnT.rearrange("k p t -> p k t")[:, :, jt * TT - 1:(jt + 1) * TT], writes=[bh])
                return h, bh
            nxt = ldh(0)
            for jt in range(NTT):
                sl = slice(jt * TT, (jt + 1) * TT)
                h, bh = nxt
                if jt + 1 < NTT:
                    nxt = ldh(jt + 1)
                for k in range(KT):
                    S.op("dve", lambda e: e.tensor_tensor(out=dd[:, k, :], in0=h[:, k, 0:TT], in1=h[:, k, 1:TT + 1], op=ALU.subtract),
                         reads=[bh], writes=[bdd])
                for c in (3, 4, 5, 2, 0, 1):
                    for k in range(KT):
                        S.op("dve", lambda e: e.scalar_tensor_tensor(out=xm[c][:, k, :], in0=dd[:, k, :], scalar=self.col("c_mu%d" % c, k),
                                                                     in1=h[:, k, 1:TT + 1], op0=ALU.mult, op1=ALU.add),
                             reads=[bdd, bh, self.bconst], writes=[bxm[c]])
                for li, (wt, c, fn, m) in enumerate(((w1, 3, AF.Tanh, 64), (a1, 4, AF.Copy, 64), (g1, 5, AF.Sigmoid, 128))):
                    ps, bps = P_.next()
                    for k in range(KT):
                        S.op("pe", lambda e: e.matmul(ps[0:m, :], lhsT=wt[:, k, :], rhs=xm[c][:, k, :], start=(k == 0), stop=(k == KT - 1)),
                             reads=[wbuf, bxm[c]], writes=[bps], inc=(k == KT - 1), acc=True)
                    S.op("act", lambda e: e.activation(out=lor[li][0:m, :], in_=ps[0:m, :], func=fn), reads=[bps], writes=[blor[li]])
                for blk in range(4):
                    vt, bvt = vtr.next()
                    for half in range(2):
                        ps, bps = P_.next()
                        for k in range(KT):
                            S.op("pe", lambda e: e.matmul(ps[:, :], lhsT=xm[2][:, k, blk * 128:(blk + 1) * 128],
                                                          rhs=wrkv[2][:, k, half * 512:(half + 1) * 512], start=(k == 0), stop=(k == KT - 1)),
                                 reads=[wbuf, bxm[2]], writes=[bps], inc=(k == KT - 1), acc=True)
                        S.op("act", lambda e: e.activation(out=vt[:, half * 512:(half + 1) * 512], in_=ps[:, :], func=AF.Copy),
                             reads=[bps], writes=[bvt])
                    S.dma("sp", self.c_vtok[jt * 4 + blk, :, :], vt[:], reads=[bvt])
                def stageA(e_):
                    es = slice(e_ * 128, (e_ + 1) * 128)
                    pss = []
                    for c in range(3):
                        ps, bps = psF.next()
                        for k in range(KT):
                            S.op("pe", lambda e: e.matmul(ps[:, :], lhsT=wrkv[c][:, k, es], rhs=xm[c][:, k, :], start=(k == 0), stop=(k == KT - 1)),
                                 reads=[wbuf, bxm[c]], writes=[bps], inc=(k == KT - 1), acc=True)
                        pss.append((ps, bps))
                    (r_ps, br_), (k_ps, bk_), (v_ps, bv_) = pss
                    rf, brf = trA.next()
                    S.op("act", lambda e: e.activation(out=rf[:], in_=r_ps[:, :], func=AF.Copy), reads=[br_], writes=[brf])
                    kf, bkf = trA.next()
                    S.op("act", lambda e: e.activation(out=kf[:], in_=k_ps[:, :], func=AF.Copy), reads=[bk_], writes=[bkf])
                    vf, bvf = trA.next()
                    S.op("act", lambda e: e.activation(out=vf[:], in_=v_ps[:, :], func=AF.Copy), reads=[bv_], writes=[bvf])
                    wl_ps, bwl = psF.next()
                    S.op("pe", lambda e: e.matmul(wl_ps[:, :], lhsT=w2[0:64, es], rhs=lor[0][0:64, :], start=True, stop=True),
                         reads=[wbuf, blor[0]], writes=[bwl], acc=True)
                    al_ps, bal = psF.next()
                    S.op("pe", lambda e: e.matmul(al_ps[:, :], lhsT=a2[0:64, es], rhs=lor[1][0:64, :], start=True, stop=True),
                         reads=[wbuf, blor[1]], writes=[bal], acc=True)
                    g_ps, bg_ = psF.next()
                    S.op("pe", lambda e: e.matmul(g_ps[:, :], lhsT=g2[:, es], rhs=lor[2][:, :], start=True, stop=True),
                         reads=[wbuf, blor[2]], writes=[bg_], acc=True)
                    gb, bgb = brA.next()
                    S.op("act", lambda e: e.activation(out=gb[:], in_=g_ps[:, :], func=AF.Copy), reads=[bg_], writes=[bgb])
                    S.dma("sp", gD[e_, :, sl], gb[:], reads=[bgb])
                    sg, bsg = trA.next()
                    S.op("act", lambda e: e.activation(out=sg[:], in_=wl_ps[:, :], func=AF.Sigmoid, bias=self.col("c_w0", e_)),
                         reads=[bwl, self.bconst], writes=[bsg])
                    al, bal2 = trA.next()
                    S.op("act", lambda e: e.activation(out=al[:], in_=al_ps[:, :], func=AF.Sigmoid, bias=self.col("c_a0", e_)),
                         reads=[bal, self.bconst], writes=[bal2])
                    kk, bkk = trA.next()
                    S.op("dve", lambda e: e.tensor_scalar(out=kk[:], in0=kf[:], scalar1=self.col("c_k_k", e_), scalar2=None, op0=ALU.mult),
                         reads=[bkf, self.bconst], writes=[bkk])
                    k2, bk2 = brA.next()
                    S.op("act", lambda e: e.activation(out=k2[:], in_=kk[:], func=AF.Square), reads=[bkk], writes=[bk2])
                    ss_ps, bss = psB.next()
                    S.op("pe", lambda e: e.matmul(ss_ps[:, :], lhsT=bones[:, :], rhs=k2[:], start=True, stop=True),
                         reads=[bm, bk2], writes=[bss], acc=True)
                    return dict(es=es, rf=rf, brf=brf, kf=kf, bkf=bkf, vf=vf, bvf=bvf, sg=sg, bsg=bsg, al=al, bal2=bal2,
                                kk=kk, bkk=bkk, ss_ps=ss_ps, bss=bss)

                def stageB(e_, d_):
                    es = d_["es"]
                    rf, brf, kf, bkf, vf, bvf = d_["rf"], d_["brf"], d_["kf"], d_["bkf"], d_["vf"], d_["bvf"]
                    sg, bsg, al, bal2, kk, bkk, ss_ps, bss = d_["sg"], d_["bsg"], d_["al"], d_["bal2"], d_["kk"], d_["bkk"], d_["ss_ps"], d_["bss"]
                    rn, brn = trB.next()
                    S.op("act", lambda e: e.activation(out=rn[:], in_=ss_ps[:, :], func=AF.Sqrt), reads=[bss], writes=[brn])
                    S.op("dve", lambda e: e.tensor_scalar(out=rn[:], in0=rn[:], scalar1=1e-12, scalar2=None, op0=ALU.max), reads=[brn], writes=[brn])
                    rn2, brn2 = trB.next()
                    S.op("dve", lambda e: e.reciprocal(out=rn2[:], in_=rn[:]), reads=[brn], writes=[brn2])
                    S.op("dve", lambda e: e.tensor_tensor(out=kk[:], in0=kk[:], in1=rn2[:], op=ALU.mult), reads=[bkk, brn2], writes=[bkk])
                    cs, bcs = trB.next()
                    S.op("dve", lambda e: e.tensor_tensor_scan(out=cs[:], data0=cm01[:], data1=sg[:], initial=0.0, op0=ALU.mult, op1=ALU.add),
                         reads=[bm, bsg], writes=[bcs])
                    csx, bcsx = trB.next()
                    S.op("dve", lambda e: e.tensor_tensor(out=csx[:], in0=cs[:], in1=sg[:], op=ALU.subtract), reads=[bcs, bsg], writes=[bcsx])
                    eG, beG = trB.next()
                    S.op("act", lambda e: e.activation(out=eG[:], in_=cs[:], func=AF.Exp, scale=-LAM), reads=[bcs], writes=[beG])
                    eGi, beGi = trB.next()
                    S.op("act", lambda e: e.activation(out=eGi[:], in_=cs[:], func=AF.Exp, scale=LAM), reads=[bcs], writes=[beGi])
                    S.op("act", lambda e: e.activation(out=csx[:], in_=csx[:], func=AF.Exp, scale=-LAM), reads=[bcsx], writes=[bcsx])
                    ar, bar = arr.next()
                    S.op("dve", lambda e: e.scalar_tensor_tensor(out=ar[:, :, 0, :], in0=kk[:].rearrange("p (c t) -> p c t", c=4), scalar=-1.0,
                                                                 in1=csx[:].rearrange("p (c t) -> p c t", c=4), op0=ALU.mult, op1=ALU.mult),
                         reads=[bkk, bcsx], writes=[bar])
                    S.op("dve", lambda e: e.tensor_tensor(out=kk[:], in0=kk[:], in1=al[:], op=ALU.mult), reads=[bkk, bal2], writes=[bkk])
                    bt_, bbt = brB.next()
                    S.op("dve", lambda e: e.tensor_tensor(out=bt_[:], in0=kk[:], in1=eGi[:], op=ALU.mult), reads=[bkk, beGi], writes=[bbt])
                    S.dma("sp", BTd[e_, :, sl], bt_[:], reads=[bbt])
                    S.op("dve", lambda e: e.tensor_scalar(out=al[:], in0=al[:], scalar1=-1.0, scalar2=self.col("c_k_a", e_), op0=ALU.add, op1=ALU.mult),
                         reads=[bal2, self.bconst], writes=[bal2])
                    S.op("dve", lambda e: e.scalar_tensor_tensor(out=kf[:], in0=al[:], scalar=1.0, in1=kf[:], op0=ALU.add, op1=ALU.mult),
                         reads=[bal2, bkf], writes=[bkf])
                    kt_, bkt = brB.next()
                    S.op("dve", lambda e: e.tensor_tensor(out=kt_[:], in0=kf[:], in1=eGi[:], op=ALU.mult), reads=[bkf, beGi], writes=[bkt])
                    S.dma("sp", KTd[e_, :, sl], kt_[:], reads=[bkt])
                    S.op("dve", lambda e: e.tensor_tensor(out=ar[:, :, 1, :], in0=rf[:].rearrange("p (c t) -> p c t", c=4),
                                                          in1=eG[:].rearrange("p (c t) -> p c t", c=4), op=ALU.mult),
                         reads=[brf, beG], writes=[bar])
                    S.dma("sp", self.c_AR[e_, :, jt * 1024:(jt + 1) * 1024], ar[:].rearrange("p a b c -> p (a b c)"), reads=[bar])
                    rk, brk = brB.next()
                    S.op("dve", lambda e: e.scalar_tensor_tensor(out=rk[:], in0=rf[:], scalar=self.col("c_r_k", e_), in1=kf[:],
                                                                 op0=ALU.mult, op1=ALU.mult), reads=[brf, bkf, self.bconst], writes=[brk])
                    rk_ps, brkp = psB.next()
                    S.op("pe", lambda e: e.matmul(rk_ps[:, :], lhsT=bones[:, :], rhs=rk[:], start=True, stop=True),
                         reads=[bm, brk], writes=[brkp], acc=True)
                    S.op("dve", lambda e: e.tensor_tensor(out=vf[:], in0=rk_ps[:, :], in1=vf[:], op=ALU.mult), reads=[brkp, bvf], writes=[bvf])
                    S.dma("sp", bonD[e_, :, sl], vf[:], reads=[bvf])
                    S.op("act", lambda e: e.activation(out=gct[:, e_, jt * 4:(jt + 1) * 4], in_=eG[:, 127:TT:128], func=AF.Copy),
                         reads=[beG], writes=[bgct])

                curA = stageA(0)
                for e_ in range(KT):
                    nxtA = stageA(e_ + 1) if e_ + 1 < KT else None
                    stageB(e_, curA)
                    curA = nxtA
            for e_ in range(KT):
                S.dma("sp", self.c_gc[e_, :, :], gct[:, e_, :], reads=[bgct])
            self.barrier()


def make_in_maps(inp):
    f32 = np.float32
    cols = pack_cols(inp)
    wqkv = np.ascontiguousarray(inp["b_w_qkv"][0], dtype=f32)
    qk = wqkv[:, :6 * D].reshape(D, 6 * 16, 2, 32)
    wqks = np.ascontiguousarray(qk[:, :, ::-1, :].reshape(D, 6 * D))
    shared = {
        "cols": cols,
        "a_w_in": inp["a_w_in"], "a_gate_w": inp["a_gate_w"], "a_w_out": inp["a_w_out"],
        "b_w_qkv": wqkv, "b_w_qks": wqks, "b_w_out": inp["b_w_out"][0],
        "c_w_rkv": inp["c_w_rkv"][0], "c_w1": inp["c_w1"][0], "c_w2": inp["c_w2"][0],
        "c_a1": inp["c_a1"][0], "c_a2": inp["c_a2"][0], "c_g1": inp["c_g1"][0], "c_g2": inp["c_g2"][0],
        "c_w_out": inp["c_w_out"][0],
        "f_w_up": inp["f_w_up"], "f_w_down": inp["f_w_down"],
        "ple_w_proj": inp["ple_w_proj"], "ple_w_gate": inp["ple_w_gate"],
    }
    shared = {k: np.ascontiguousarray(v, dtype=f32) for k, v in shared.items()}
    maps = []
    for c in range(8):
        b = c % NB
        m = dict(shared)
        m["xT"] = np.ascontiguousarray(np.asarray(inp["x"][b], dtype=f32).T).reshape(KT, 128, T)
        m["pT"] = np.ascontiguousarray(np.transpose(np.asarray(inp["p"][:, b], dtype=f32), (0, 2, 1))).reshape(DEPTH, 2, 128, T)
        m["pos"] = np.ascontiguousarray(np.asarray(inp["positions"][b], dtype=np.int32)).reshape(1, T)
        maps.append(m)
    return maps


_PROG_CACHE = {}


def run_prog(inp, layers=DEPTH, dbg=None):
    key = (str(layers), dbg)
    if key not in _PROG_CACHE:
        _PROG_CACHE[key] = Prog(layers, dbg)
    prog = _PROG_CACHE[key]
    maps = make_in_maps(inp)
    used = set(prog.W.keys()) | {"xT", "pT", "pos", "cols"}
    maps = [{k: v for k, v in m.items() if k in used} for m in maps]
    res = run_bass_kernel_spmd(prog.nc, maps, core_ids=list(range(8)))
    out = np.stack([np.asarray(res.results[b]["yT"]).reshape(D, T).T for b in range(NB)])
    return np.ascontiguousarray(out.astype(np.float32)), res


def kernel(**inputs):
    inp = {k: np.asarray(v) for k, v in inputs.items()}
    out, _ = run_prog(inp)
    return out
```

```python
import numpy as np
import concourse.bass as bass
import concourse.mybir as mybir
from concourse.bass_utils import run_bass_kernel_spmd

F32 = mybir.dt.float32
BF16 = mybir.dt.bfloat16
I32 = mybir.dt.int32
AF = mybir.ActivationFunctionType
ALU = mybir.AluOpType
AX = mybir.AxisListType


class Buf:
    __slots__ = ("w", "r", "name")

    def __init__(self, name=""):
        self.w = None
        self.r = {}
        self.name = name


class _Eng:
    def __init__(self, name, eng, sem):
        self.name, self.e, self.sem = name, eng, sem
        self.cnt = 0
        self.seen = {}
        self.pending = []

    def wait(self, ev):
        sem, val = ev
        k = id(sem)
        if self.seen.get(k, 0) >= val:
            return
        self.e.wait_ge(sem, val)
        self.seen[k] = val


class Sched:
    NDMA = 8

    def __init__(self, nc, stack):
        self.nc = nc
        self.engs = {}
        for name, eng in (("pe", nc.tensor), ("act", nc.scalar), ("dve", nc.vector),
                          ("pool", nc.gpsimd), ("sp", nc.sync)):
            sem = stack.enter_context(nc.semaphore("sem_" + name))
            self.engs[name] = _Eng(name, eng, sem)
        self.dma_slots = {}
        for q in ("sp", "pool", "act"):
            sl = []
            for i in range(self.NDMA):
                sem = stack.enter_context(nc.semaphore("dq_%s%d" % (q, i)))
                sl.append([sem, 0])
            self.dma_slots[q] = [sl, 0]
        self.n_inst = 0

    def op(self, engname, fn, reads=(), writes=(), inc=True, acc=False):
        E = self.engs[engname]
        for b in reads:
            if b.w is not None:
                E.wait(b.w)
        for b in writes:
            if b.w is not None and not (acc and b.w[0] is E.sem):
                E.wait(b.w)
            for ev in b.r.values():
                if ev[0] is not E.sem:
                    E.wait(ev)
        inst = fn(E.e)
        self.n_inst += 1
        if inc:
            E.cnt += 1
            inst.then_inc(E.sem, 1)
            ev = (E.sem, E.cnt)
            E.pending.append((reads, writes))
            for rd, wr in E.pending:
                for b in rd:
                    b.r[id(E.sem)] = ev
                for b in wr:
                    b.w = ev
                    b.r = {}
            E.pending = []
            E.seen[id(E.sem)] = max(E.seen.get(id(E.sem), 0), 0)
        else:
            E.pending.append((reads, writes))
        return inst

    def dma(self, q, out, in_, reads=(), writes=()):
        E = self.engs[q]
        slots, idx = self.dma_slots[q]
        slot = slots[idx % self.NDMA]
        self.dma_slots[q][1] = idx + 1
        if slot[1] > 0:
            E.wait((slot[0], slot[1]))
        for b in reads:
            if b.w is not None:
                E.wait(b.w)
        for b in writes:
            if b.w is not None:
                E.wait(b.w)
            for ev in b.r.values():
                E.wait(ev)
        inst = E.e.dma_start(out=out, in_=in_)
        self.n_inst += 1
        slot[1] += 16
        inst.then_inc(slot[0], 16)
        ev = (slot[0], slot[1])
        for b in reads:
            b.r[id(slot[0])] = ev
        for b in writes:
            b.w = ev
            b.r = {}
        return ev

    def wait_all(self, engname, bufs):
        E = self.engs[engname]
        for b in bufs:
            if b.w is not None:
                E.wait(b.w)
            for ev in b.r.values():
                E.wait(ev)


class Ring:
    _uid = [0]

    def __init__(self, nc, stack, name, shape, dtype, n, psum=False):
        self.items = []
        Ring._uid[0] += 1
        name = "%s_u%d_" % (name, Ring._uid[0])
        for i in range(n):
            if psum:
                t = stack.enter_context(nc.psum_tensor("%s%d" % (name, i), shape, dtype))
            else:
                t = stack.enter_context(nc.sbuf_tensor("%s%d" % (name, i), shape, dtype))
            self.items.append((t, Buf("%s%d" % (name, i))))
        self.i = 0

    def next(self):
        it = self.items[self.i % len(self.items)]
        self.i += 1
        return it


class SubRing(Ring):
    def __init__(self, items):
        self.items = list(items)
        self.i = 0


D = 1024
T = 4096
NB = 4
DEPTH = 4
KT = D // 128
TT = 512
NTT = T // TT
FFN = 2816
FT = FFN // 128
PLE = 256
RMS_EPS = 1e-6
GN_EPS = 64e-5
LRU_C = 8.0


class ColPack:
    def __init__(self):
        self.cols = []
        self.idx = {}

    def add(self, name, vec):
        vec = np.ascontiguousarray(vec, dtype=np.float32).reshape(-1)
        assert vec.size % 128 == 0
        n = vec.size // 128
        self.idx[name] = (len(self.cols), n)
        for i in range(n):
            self.cols.append(vec[i * 128:(i + 1) * 128])

    def array(self):
        return np.ascontiguousarray(np.stack(self.cols, axis=1))


def col_layout():
    L = []
    for i in range(DEPTH):
        L += [("norm_mix%d" % i, KT), ("norm_ffn%d" % i, KT), ("norm_ple%d" % i, KT)]
        for k in range(3):
            L.append(("f_conv_w%d_%d" % (i, k), 2 * FT))
        L.append(("f_conv_b%d" % i, 2 * FT))
    L.append(("norm_final", KT))
    for j in range(2):
        for k in range(4):
            L.append(("a_conv_w%d_%d" % (j, k), KT))
        L += [("a_conv_b%d" % j, KT), ("a_gate_b%d" % j, 2 * KT), ("a_lambda%d" % j, KT)]
    for c in range(6):
        L.append(("c_mu%d" % c, KT))
    for nm in ("c_w0", "c_a0", "c_k_k", "c_k_a", "c_r_k", "c_ln_w", "c_ln_b"):
        L.append((nm, KT))
    L.append(("invf", 1))
    L.append(("rsgn", 1))
    off = {}
    o = 0
    for nm, n in L:
        off[nm] = (o, n)
        o += n
    return off, o


COLS, NCOLS = col_layout()


def pack_cols(inp):
    cp = ColPack()
    for i in range(DEPTH):
        cp.add("norm_mix%d" % i, inp["norm_mix"][i])
        cp.add("norm_ffn%d" % i, inp["norm_ffn"][i])
        cp.add("norm_ple%d" % i, inp["norm_ple"][i])
        for k in range(3):
            cp.add("f_conv_w%d_%d" % (i, k), inp["f_conv_w"][i, k])
        cp.add("f_conv_b%d" % i, inp["f_conv_b"][i])
    cp.add("norm_final", inp["norm_final"])
    for j in range(2):
        for k in range(4):
            cp.add("a_conv_w%d_%d" % (j, k), inp["a_conv_w"][j, k])
        cp.add("a_conv_b%d" % j, inp["a_conv_b"][j])
        gb = inp["a_gate_b"][j].reshape(4, 2, 256)
        cp.add("a_gate_b%d" % j, np.concatenate([gb[:, 0].reshape(-1), gb[:, 1].reshape(-1)]))
        cp.add("a_lambda%d" % j, inp["a_lambda"][j])
    for c in range(6):
        cp.add("c_mu%d" % c, inp["c_mu"][0, c])
    for nm in ("c_w0", "c_a0", "c_k_k", "c_k_a", "c_r_k", "c_ln_w", "c_ln_b"):
        cp.add(nm, inp[nm][0])
    invf = (10000.0 ** (-np.arange(0, 64, 2, dtype=np.float32) / 64)).astype(np.float32)
    cp.add("invf", np.tile(invf, 4))
    cp.add("rsgn", np.tile(np.concatenate([-np.ones(32, np.float32), np.ones(32, np.float32)]), 2))
    assert cp.idx == COLS, "col layout mismatch"
    return cp.array()


from contextlib import ExitStack


class _Cut(Exception):
    pass


class Prog:
    def cut(self, name):
        if self.dbg == name:
            raise _Cut()

    def __init__(self, layers=DEPTH, dbg=None):
        self.layers = list(range(layers)) if isinstance(layers, int) else list(layers)
        self.dbg = dbg
        nc = bass.Bass("TRN2", target_bir_lowering=False)
        self.nc = nc
        dt = nc.dram_tensor
        self.xT = dt("xT", [KT, 128, T], F32, kind="ExternalInput").ap()
        self.pT = dt("pT", [DEPTH, 2, 128, T], F32, kind="ExternalInput").ap()
        self.pos = dt("pos", [1, T], I32, kind="ExternalInput").ap()
        self.colsD = dt("cols", [128, NCOLS], F32, kind="ExternalInput").ap()
        class _LazyW(dict):
            def __init__(s2, shapes):
                s2.shapes = shapes
            def __missing__(s2, nm):
                s2[nm] = dt(nm, s2.shapes[nm], F32, kind="ExternalInput").ap()
                return s2[nm]
        shapes = {}
        for nm, shp in (("a_w_in", [2, D, 2 * D]), ("a_gate_w", [2, 4, 256, 512]), ("a_w_out", [2, D, D]),
                        ("b_w_qkv", [D, 7 * D]), ("b_w_qks", [D, 6 * D]), ("b_w_out", [D, D]),
                        ("c_w_rkv", [3, D, D]), ("c_w1", [D, 64]), ("c_w2", [64, D]), ("c_a1", [D, 64]),
                        ("c_a2", [64, D]), ("c_g1", [D, 128]), ("c_g2", [128, D]), ("c_w_out", [D, D]),
                        ("f_w_up", [DEPTH, D, 2 * FFN]), ("f_w_down", [DEPTH, FFN, D]),
                        ("ple_w_proj", [DEPTH, PLE, D]), ("ple_w_gate", [DEPTH, D, D])):
            shapes[nm] = shp
        self.W = _LazyW(shapes)
        self.yT = dt("yT", [KT, 128, T], F32, kind="ExternalOutput").ap()
        self.hT = dt("hT", [KT, 128, T], F32, kind="Internal").ap()
        self.hnT = dt("hnT", [KT, 128, T], BF16, kind="Internal").ap()
        self.gT = dt("gT", [KT, 128, T], BF16, kind="Internal").ap()
        self.actT = dt("actT", [FT, 128, T], BF16, kind="Internal").ap()
        self.s32 = [dt("s32_%d" % i, [KT, 128, T], F32, kind="Internal").ap() for i in range(6)]
        self.s16 = [dt("s16_%d" % i, [KT, 128, T], BF16, kind="Internal").ap() for i in range(8)]
        self.build()

    def col(self, name, k=0, n=1):
        o, sz = COLS[name]
        assert k + n <= sz
        return self.cols[:, o + k:o + k + n]

    def barrier(self):
        S = self.S
        evs = [(E.sem, E.cnt) for E in S.engs.values() if E.cnt > 0]
        for q in S.dma_slots:
            for sl in S.dma_slots[q][0]:
                if sl[1] > 0:
                    evs.append((sl[0], sl[1]))
        for E in S.engs.values():
            assert not E.pending
            for ev in evs:
                E.wait(ev)

    def dump(self, name, ap, bufs, shape, dtype):
        if not self.dbg:
            return
        t = self.nc.dram_tensor("dbg_" + name, list(shape), dtype, kind="Internal").ap()
        self.S.dma("sp", t, ap, reads=list(bufs))

    def sb(self, st, name, shape, dtype):
        Ring._uid[0] += 1
        return st.enter_context(self.nc.sbuf_tensor("%s_u%d" % (name, Ring._uid[0]), shape, dtype))

    def load_w(self, ring, wap2d, kt, e0, ew):
        t, b = ring.next()
        src = wap2d.rearrange("(k p) e -> p k e", p=128)[:, :, e0:e0 + ew]
        self.S.dma("pool", t[:, 0:kt, 0:ew], src, writes=[b])
        return t, b

    def rmsnorm(self, st_rings, h, bh, gain, out, bout):
        S = self.S
        sqr, psr, rsr = st_rings
        ps, bps = psr.next()
        for k in range(KT):
            sq, bsq = sqr.next()
            S.op("act", lambda e: e.activation(out=sq[:], in_=h[:, k, :], func=AF.Square), reads=[bh], writes=[bsq])
            S.op("pe", lambda e: e.matmul(ps[:, :], lhsT=self.ones_bf[:, :], rhs=sq[:], start=(k == 0), stop=(k == KT - 1)),
                 reads=[bsq, self.bconst], writes=[bps], acc=True)
        rs, brs = rsr.next()
        S.op("act", lambda e: e.activation(out=rs[:], in_=ps[:, :], func=AF.Sqrt, scale=1.0 / D, bias=self.eps_col),
             reads=[bps, self.bconst], writes=[brs])
        rs2, brs2 = rsr.next()
        S.op("dve", lambda e: e.reciprocal(out=rs2[:], in_=rs[:]), reads=[brs], writes=[brs2])
        for k in range(KT):
            S.op("dve", lambda e: e.scalar_tensor_tensor(out=out[:, k, :], in0=h[:, k, :], scalar=self.col(gain, k),
                                                         in1=rs2[:], op0=ALU.mult, op1=ALU.mult),
                 reads=[bh, brs2, self.bconst], writes=[bout])

    def norm_rings(self, st, tag):
        nc = self.nc
        return (Ring(nc, st, "sq" + tag, [128, TT], BF16, 3),
                self.psA, Ring(nc, st, "rs" + tag, [128, TT], F32, 4))

    def tview(self, ap3, j):
        return ap3.rearrange("k p t -> p k t")[:, :, j * TT:(j + 1) * TT]

    def build(self):
        nc = self.nc
        with ExitStack() as top:
            S = Sched(nc, top)
            self.S = S
            self.cols = self.sb(top, "cols", [128, NCOLS], F32)
            self.cst = self.sb(top, "cst", [128, 8], F32)
            self.ones_bf = self.sb(top, "ones_bf", [128, 128], BF16)
            self.bconst = Buf("const")
            S.dma("sp", self.cols[:], self.colsD[:, :], writes=[self.bconst])
            S.op("pool", lambda e: e.memset(self.cst[:, 0:1], RMS_EPS), writes=[self.bconst])
            S.op("pool", lambda e: e.memset(self.cst[:, 1:2], 1.0), writes=[self.bconst])
            S.op("pool", lambda e: e.memset(self.cst[:, 2:3], 0.0), writes=[self.bconst])
            S.op("pool", lambda e: e.memset(self.cst[:, 3:4], GN_EPS), writes=[self.bconst])
            S.op("pool", lambda e: e.memset(self.ones_bf[:], 1.0), writes=[self.bconst])
            self.eps_col = self.cst[:, 0:1]
            self.one_col = self.cst[:, 1:2]
            self.zero_col = self.cst[:, 2:3]
            self.gneps_col = self.cst[:, 3:4]
            self.psbig = [top.enter_context(nc.psum_tensor("psbig%d" % q, [128, 2 * TT], F32)) for q in range(4)]
            self.psA = SubRing([(self.psbig[q // 2][:, (q % 2) * TT:(q % 2 + 1) * TT], Buf("ps%d" % q)) for q in range(8)])
            self.barrier()
            self.phase_norm0(self.layers[0])
            h_src = self.xT
            for li, i in enumerate(self.layers):
                kind, j = i % 3, i // 3
                if kind == 0:
                    self.mixer_a(i, j)
                    wout = self.W["a_w_out"][j]
                elif kind == 1:
                    self.mixer_b(i)
                    wout = self.W["b_w_out"]
                else:
                    self.mixer_c(i)
                    wout = self.W["c_w_out"]
                self.mixer_out(i, wout, h_src)
                h_src = self.hT
                self.ffn_up(i)
                last = (li == len(self.layers) - 1)
                self.tail(i, last, None if last else self.layers[li + 1])
            self.barrier()

    def phase_norm0(self, i0):
        nc, S = self.nc, self.S
        with ExitStack() as st:
            hr = Ring(nc, st, "n0h", [128, KT, TT], F32, 2)
            orr = Ring(nc, st, "n0o", [128, KT, TT], BF16, 2)
            rings = self.norm_rings(st, "n0")
            def ld(j):
                h, bh = hr.next()
                S.dma("sp", h[:], self.tview(self.xT, j), writes=[bh])
                return h, bh
            nxt = ld(0)
            for j in range(NTT):
                h, bh = nxt
                if j + 1 < NTT:
                    nxt = ld(j + 1)
                o, bo = orr.next()
                self.rmsnorm(rings, h, bh, "norm_mix%d" % i0, o, bo)
                S.dma("sp", self.tview(self.hnT, j), o[:], reads=[bo])
            self.barrier()

    def mixer_out(self, i, wout, h_src):
        nc, S = self.nc, self.S
        with ExitStack() as st:
            w = self.sb(st, "mo_w", [128, KT, D], BF16)
            bw = [Buf() for _ in range(KT)]
            for k in range(KT):
                S.dma("pool", w[:, k, :], wout[k * 128:(k + 1) * 128, :], writes=[bw[k]])
            gr = Ring(nc, st, "mo_g", [128, KT, TT], BF16, 2)
            hr = Ring(nc, st, "mo_h", [128, KT, TT], F32, 2)
            orr = Ring(nc, st, "mo_o", [128, KT, TT], BF16, 2)
            rings = self.norm_rings(st, "mo")
            def ld(j):
                g, bg = gr.next()
                S.dma("sp", g[:], self.tview(self.gT, j), writes=[bg])
                h, bh = hr.next()
                S.dma("sp", h[:], self.tview(h_src, j), writes=[bh])
                return g, bg, h, bh
            nxt = ld(0)
            for j in range(NTT):
                g, bg, h, bh = nxt
                if j + 1 < NTT:
                    nxt = ld(j + 1)
                for e in range(KT):
                    ps, bps = self.psA.next()
                    for k in range(KT):
                        S.op("pe", lambda en: en.matmul(ps[:, :], lhsT=w[:, k, e * 128:(e + 1) * 128], rhs=g[:, k, :],
                                                        start=(k == 0), stop=(k == KT - 1)),
                             reads=[bw[k], bg], writes=[bps], inc=(k == KT - 1), acc=True)
                    S.op("dve", lambda en: en.tensor_tensor(out=h[:, e, :], in0=ps[:, :], in1=h[:, e, :], op=ALU.add),
                         reads=[bps, bh], writes=[bh])
                S.dma("sp", self.tview(self.hT, j), h[:], reads=[bh])
                o, bo = orr.next()
                self.rmsnorm(rings, h, bh, "norm_ffn%d" % i, o, bo)
                S.dma("sp", self.tview(self.hnT, j), o[:], reads=[bo])
            self.barrier()

    def ffn_up(self, i):
        nc, S = self.nc, self.S
        wup = self.W["f_w_up"][i]
        with ExitStack() as st:
            hn = self.sb(st, "fu_hn", [128, KT, T], BF16)
            bhn = [Buf() for _ in range(KT)]
            for k in range(KT):
                S.dma("sp", hn[:, k, :], self.hnT[k, :, :], writes=[bhn[k]])
            wr = Ring(nc, st, "fu_w", [128, KT, 128], BF16, 4)
            stg = [[self.sb(st, "fu_stg%d_%d" % (s, b), [128, 2 + T], F32) for b in range(2)] for s in range(2)]
            bstg = [[[Buf() for _ in range(NTT + 1)] for b in range(2)] for s in range(2)]
            for s in range(2):
                for b in range(2):
                    S.op("pool", lambda e: e.memset(stg[s][b][:, 0:2], 0.0), writes=[bstg[s][b][0]])
            accr = Ring(nc, st, "fu_acc", [128, TT], F32, 4)
            sgr = Ring(nc, st, "fu_sg", [128, TT], F32, 2)
            outr = Ring(nc, st, "fu_out", [128, TT], BF16, 3)
            ldw = lambda f: [self.load_w(wr, wup, KT, (s * FT + f) * 128, 128) for s in range(2)]
            wnxt = ldw(0)
            for f in range(FT):
                wt = wnxt
                if f + 1 < FT:
                    wnxt = ldw(f + 1)
                accs = [None, None]
                for j in range(NTT):
                    for s in range(2):
                        w, bw = wt[s]
                        ps, bps = self.psA.next()
                        for k in range(KT):
                            S.op("pe", lambda en: en.matmul(ps[:, :], lhsT=w[:, k, :], rhs=hn[:, k, j * TT:(j + 1) * TT],
                                                            start=(k == 0), stop=(k == KT - 1)),
                                 reads=[bw, bhn[k]], writes=[bps], inc=(k == KT - 1), acc=True)
                        sg_t, sg_b = stg[s][f % 2], bstg[s][f % 2]
                        c0 = 2 + j * TT
                        S.op("act", lambda en: en.activation(out=sg_t[:, c0:c0 + TT], in_=ps[:, :], func=AF.Copy),
                             reads=[bps], writes=[sg_b[j + 1]])
                        acc, bacc = accr.next()
                        ci = s * FT + f
                        S.op("act", lambda en: en.activation(out=acc[:], in_=ps[:, :], func=AF.Identity,
                                                             scale=self.col("f_conv_w%d_2" % i, ci),
                                                             bias=self.col("f_conv_b%d" % i, ci)),
                             reads=[bps, self.bconst], writes=[bacc])
                        for kk in (1, 0):
                            S.op("dve", lambda en: en.scalar_tensor_tensor(
                                out=acc[:], in0=sg_t[:, j * TT + kk:j * TT + kk + TT],
                                scalar=self.col("f_conv_w%d_%d" % (i, kk), ci), in1=acc[:], op0=ALU.mult, op1=ALU.add),
                                reads=[sg_b[j], sg_b[j + 1], bacc, self.bconst], writes=[bacc])
                        accs[s] = (acc, bacc)
                    sg, bsg = sgr.next()
                    S.op("act", lambda en: en.activation(out=sg[:], in_=accs[0][0][:], func=AF.Silu),
                         reads=[accs[0][1]], writes=[bsg])
                    o, bo = outr.next()
                    S.op("dve", lambda en: en.tensor_tensor(out=o[:], in0=sg[:], in1=accs[1][0][:], op=ALU.mult),
                         reads=[bsg, accs[1][1]], writes=[bo])
                    S.dma("sp", self.actT[f, :, j * TT:(j + 1) * TT], o[:], reads=[bo])
            self.barrier()

    def tail(self, i, last, inext):
        nc, S = self.nc, self.S
        with ExitStack() as st:
            wd = self.sb(st, "tl_wd", [128, FT, D], BF16)
            wg = self.sb(st, "tl_wg", [128, KT, D], BF16)
            wp = self.sb(st, "tl_wp", [128, 2, D], BF16)
            bwd, bwg, bwp = [Buf() for _ in range(FT)], [Buf() for _ in range(KT)], Buf()
            for f in range(FT):
                S.dma("pool", wd[:, f, :], self.W["f_w_down"][i, f * 128:(f + 1) * 128, :], writes=[bwd[f]])
            for k in range(KT):
                S.dma("pool", wg[:, k, :], self.W["ple_w_gate"][i, k * 128:(k + 1) * 128, :], writes=[bwg[k]])
            for k in range(2):
                S.dma("pool", wp[:, k, :], self.W["ple_w_proj"][i, k * 128:(k + 1) * 128, :], writes=[bwp])
            ar = Ring(nc, st, "tl_a", [128, FT, TT], BF16, 2)
            hr = Ring(nc, st, "tl_h", [128, KT, TT], F32, 2)
            pr = Ring(nc, st, "tl_p", [128, 2, TT], BF16, 2)
            n3r = Ring(nc, st, "tl_n3", [128, KT, TT], BF16, 1)
            sgr = Ring(nc, st, "tl_sg", [128, TT], F32, 2)
            if last:
                orr = Ring(nc, st, "tl_o", [128, KT, TT], F32, 1)
            else:
                orr = Ring(nc, st, "tl_o", [128, KT, TT], BF16, 2)
            rings = self.norm_rings(st, "tl")
            def ld(j):
                a, ba = ar.next()
                S.dma("sp", a[:], self.tview(self.actT, j), writes=[ba])
                h, bh = hr.next()
                S.dma("sp", h[:], self.tview(self.hT, j), writes=[bh])
                p, bp = pr.next()
                S.dma("pool", p[:], self.tview(self.pT[i], j), writes=[bp])
                return a, ba, h, bh, p, bp
            nxt = ld(0)
            for j in range(NTT):
                a, ba, h, bh, p, bp = nxt
                if j + 1 < NTT:
                    nxt = ld(j + 1)
                for e in range(KT):
                    ps, bps = self.psA.next()
                    for f in range(FT):
                        S.op("pe", lambda en: en.matmul(ps[:, :], lhsT=wd[:, f, e * 128:(e + 1) * 128], rhs=a[:, f, :],
                                                        start=(f == 0), stop=(f == FT - 1)),
                             reads=[bwd[f], ba], writes=[bps], inc=(f == FT - 1), acc=True)
                    S.op("dve", lambda en: en.tensor_tensor(out=h[:, e, :], in0=ps[:, :], in1=h[:, e, :], op=ALU.add),
                         reads=[bps, bh], writes=[bh])
                n3, bn3 = n3r.next()
                self.rmsnorm(rings, h, bh, "norm_ple%d" % i, n3, bn3)
                for e in range(KT):
                    ps, bps = self.psA.next()
                    for k in range(KT):
                        S.op("pe", lambda en: en.matmul(ps[:, :], lhsT=wg[:, k, e * 128:(e + 1) * 128], rhs=n3[:, k, :],
                                                        start=(k == 0), stop=(k == KT - 1)),
                             reads=[bwg[k], bn3], writes=[bps], inc=(k == KT - 1), acc=True)
                    sg, bsg = sgr.next()
                    S.op("act", lambda en: en.activation(out=sg[:], in_=ps[:, :], func=AF.Sigmoid),
                         reads=[bps], writes=[bsg])
                    ps2, bps2 = self.psA.next()
                    for k in range(2):
                        S.op("pe", lambda en: en.matmul(ps2[:, :], lhsT=wp[:, k, e * 128:(e + 1) * 128], rhs=p[:, k, :],
                                                        start=(k == 0), stop=(k == 1)),
                             reads=[bwp, bp], writes=[bps2], inc=(k == 1), acc=True)
                    S.op("dve", lambda en: en.tensor_tensor(out=sg[:], in0=ps2[:, :], in1=sg[:], op=ALU.mult),
                         reads=[bps2, bsg], writes=[bsg])
                    S.op("dve", lambda en: en.tensor_tensor(out=h[:, e, :], in0=sg[:], in1=h[:, e, :], op=ALU.add),
                         reads=[bsg, bh], writes=[bh])
                o, bo = orr.next()
                if last:
                    self.rmsnorm(rings, h, bh, "norm_final", o, bo)
                    S.dma("sp", self.tview(self.yT, j), o[:], reads=[bo])
                else:
                    S.dma("sp", self.tview(self.hT, j), h[:], reads=[bh])
                    self.rmsnorm(rings, h, bh, "norm_mix%d" % inext, o, bo)
                    S.dma("sp", self.tview(self.hnT, j), o[:], reads=[bo])
            self.barrier()

    def mixer_a(self, i, j):
        nc, S = self.nc, self.S
        ygT, xcT, xcbT = self.s32[0], self.s32[1], self.s16[0]
        win = self.W["a_w_in"][j]
        with ExitStack() as st:
            hn = self.sb(st, "a1_hn", [128, KT, T], BF16)
            bhn = [Buf() for _ in range(KT)]
            for k in range(KT):
                S.dma("sp", hn[:, k, :], self.hnT[k, :, :], writes=[bhn[k]])
            wr = Ring(nc, st, "a1_w", [128, KT, 128], BF16, 4)
            stg = [self.sb(st, "a1_stg%d" % b, [128, 3 + T], F32) for b in range(2)]
            bstg = [[Buf() for _ in range(NTT + 1)] for b in range(2)]
            for b in range(2):
                S.op("pool", lambda e: e.memset(stg[b][:, 0:3], 0.0), writes=[bstg[b][0]])
            ygr = Ring(nc, st, "a1_yg", [128, TT], F32, 3)
            accr = Ring(nc, st, "a1_acc", [128, TT], F32, 3)
            xbr = Ring(nc, st, "a1_xb", [128, TT], BF16, 3)
            wnxt = self.load_w(wr, win, KT, 0, 128)
            for e in range(2 * KT):
                w, bw = wnxt
                if e + 1 < 2 * KT:
                    wnxt = self.load_w(wr, win, KT, (e + 1) * 128, 128)
                for jt in range(NTT):
                    ps, bps = self.psA.next()
                    for k in range(KT):
                        S.op("pe", lambda en: en.matmul(ps[:, :], lhsT=w[:, k, :], rhs=hn[:, k, jt * TT:(jt + 1) * TT],
                                                        start=(k == 0), stop=(k == KT - 1)),
                             reads=[bw, bhn[k]], writes=[bps], inc=(k == KT - 1), acc=True)
                    if e < KT:
                        yg, byg = ygr.next()
                        S.op("act", lambda en: en.activation(out=yg[:], in_=ps[:, :], func=AF.Gelu_apprx_tanh),
                             reads=[bps], writes=[byg])
                        S.dma("sp", ygT[e, :, jt * TT:(jt + 1) * TT], yg[:], reads=[byg])
                    else:
                        ft = e - KT
                        sg_t, sg_b = stg[ft % 2], bstg[ft % 2]
                        c0 = 3 + jt * TT
                        S.op("act", lambda en: en.activation(out=sg_t[:, c0:c0 + TT], in_=ps[:, :], func=AF.Copy),
                             reads=[bps], writes=[sg_b[jt + 1]])
                        acc, bacc = accr.next()
                        S.op("act", lambda en: en.activation(out=acc[:], in_=ps[:, :], func=AF.Identity,
                                                             scale=self.col("a_conv_w%d_3" % j, ft),
                                                             bias=self.col("a_conv_b%d" % j, ft)),
                             reads=[bps, self.bconst], writes=[bacc])
                        for kk in (2, 1, 0):
                            S.op("dve", lambda en: en.scalar_tensor_tensor(
                                out=acc[:], in0=sg_t[:, jt * TT + kk:jt * TT + kk + TT],
                                scalar=self.col("a_conv_w%d_%d" % (j, kk), ft), in1=acc[:], op0=ALU.mult, op1=ALU.add),
                                reads=[sg_b[jt], sg_b[jt + 1], bacc, self.bconst], writes=[bacc])
                        xb, bxb = xbr.next()
                        S.op("pool", lambda en: en.tensor_copy(out=xb[:], in_=acc[:]), reads=[bacc], writes=[bxb])
                        S.dma("sp", xcT[ft, :, jt * TT:(jt + 1) * TT], acc[:], reads=[bacc])
                        S.dma("sp", xcbT[ft, :, jt * TT:(jt + 1) * TT], xb[:], reads=[bxb])
            self.barrier()
        with ExitStack() as st:
            cc = self.sb(st, "a2_cc", [128, 5, KT], F32)
            bc = Buf()
            lam = self.col("a_lambda%d" % j, 0, KT)
            ev, l1, t2, mk, ccol = (cc[:, q, :] for q in range(5))
            S.op("act", lambda e: e.activation(out=ev, in_=lam, func=AF.Exp, scale=-1.0), reads=[self.bconst], writes=[bc])
            S.op("act", lambda e: e.activation(out=l1, in_=ev, func=AF.Ln, bias=self.one_col), reads=[bc, self.bconst], writes=[bc])
            S.op("dve", lambda e: e.tensor_scalar(out=t2, in0=ev, scalar1=1.0 / 3.0, scalar2=-0.5, op0=ALU.mult, op1=ALU.add), reads=[bc], writes=[bc])
            S.op("dve", lambda e: e.tensor_tensor(out=t2, in0=t2, in1=ev, op=ALU.mult), reads=[bc], writes=[bc])
            S.op("dve", lambda e: e.tensor_scalar(out=t2, in0=t2, scalar1=1.0, scalar2=None, op0=ALU.add), reads=[bc], writes=[bc])
            S.op("dve", lambda e: e.tensor_tensor(out=t2, in0=t2, in1=ev, op=ALU.mult), reads=[bc], writes=[bc])
            S.op("dve", lambda e: e.tensor_scalar(out=mk, in0=ev, scalar1=0.02, scalar2=None, op0=ALU.is_lt), reads=[bc], writes=[bc])
            S.op("dve", lambda e: e.tensor_tensor(out=t2, in0=t2, in1=l1, op=ALU.subtract), reads=[bc], writes=[bc])
            S.op("dve", lambda e: e.tensor_tensor(out=t2, in0=t2, in1=mk, op=ALU.mult), reads=[bc], writes=[bc])
            S.op("dve", lambda e: e.tensor_tensor(out=l1, in0=l1, in1=t2, op=ALU.add), reads=[bc], writes=[bc])
            S.op("dve", lambda e: e.tensor_scalar(out=ccol, in0=l1, scalar1=-LRU_C, scalar2=None, op0=ALU.mult), reads=[bc], writes=[bc])

            xbr = Ring(nc, st, "a2_xb", [128, 2, T], BF16, 2)
            gwr = Ring(nc, st, "a2_gw", [128, 2, 512], BF16, 2)
            afr = Ring(nc, st, "a2_af", [128, T], F32, 2)
            bfr = Ring(nc, st, "a2_bf", [128, T], F32, 2)
            hfr = Ring(nc, st, "a2_hf", [128, T], F32, 1)
            sqr_ = Ring(nc, st, "a2_sq", [128, T], F32, 1)
            ygr = Ring(nc, st, "a2_yg", [128, T], F32, 1)
            gor = Ring(nc, st, "a2_go", [128, T], BF16, 1)
            tr = Ring(nc, st, "a2_t", [128, TT], F32, 3)
            xcr = Ring(nc, st, "a2_xc", [128, TT], F32, 3)

            def ldh(hd):
                xb, bxb = xbr.next()
                for k in range(2):
                    S.dma("sp", xb[:, k, :], xcbT[2 * hd + k, :, :], writes=[bxb])
                gw, bgw = self.load_w(gwr, self.W["a_gate_w"][j, hd], 2, 0, 512)
                return xb, bxb, gw, bgw
            nxt = ldh(0)
            for hd in range(4):
                xb, bxb, gw, bgw = nxt
                if hd + 1 < 4:
                    nxt = ldh(hd + 1)
                for ft in range(2):
                    ftg = 2 * hd + ft
                    af, baf = afr.next()
                    bf, bbf = bfr.next()
                    yg, byg = ygr.next()
                    S.dma("sp", yg[:], ygT[ftg, :, :], writes=[byg])
                    for jt in range(NTT):
                        sl = slice(jt * TT, (jt + 1) * TT)
                        xc, bxc = xcr.next()
                        S.dma("sp", xc[:], xcT[ftg, :, sl], writes=[bxc])
                        psr, bpsr = self.psA.next()
                        psi, bpsi = self.psA.next()
                        for k in range(2):
                            S.op("pe", lambda en: en.matmul(psr[:, :], lhsT=gw[:, k, ft * 128:(ft + 1) * 128], rhs=xb[:, k, sl],
                                                            start=(k == 0), stop=(k == 1)),
                                 reads=[bgw, bxb], writes=[bpsr], inc=(k == 1), acc=True)
                        for k in range(2):
                            S.op("pe", lambda en: en.matmul(psi[:, :], lhsT=gw[:, k, 256 + ft * 128:256 + (ft + 1) * 128], rhs=xb[:, k, sl],
                                                            start=(k == 0), stop=(k == 1)),
                                 reads=[bgw, bxb], writes=[bpsi], inc=(k == 1), acc=True)
                        S.op("act", lambda en: en.activation(out=af[:, sl], in_=psr[:, :], func=AF.Sigmoid,
                                                             bias=self.col("a_gate_b%d" % j, ftg)),
                             reads=[bpsr, self.bconst], writes=[baf])
                        ig, big = tr.next()
                        S.op("act", lambda en: en.activation(out=ig[:], in_=psi[:, :], func=AF.Sigmoid,
                                                             bias=self.col("a_gate_b%d" % j, KT + ftg)),
                             reads=[bpsi, self.bconst], writes=[big])
                        S.op("dve", lambda en: en.tensor_tensor(out=bf[:, sl], in0=ig[:], in1=xc[:], op=ALU.mult),
                             reads=[big, bxc], writes=[bbf])
                    S.op("act", lambda en: en.activation(out=af[:], in_=af[:], func=AF.Exp, scale=cc[:, 4, ftg:ftg + 1]),
                         reads=[baf, bc], writes=[baf])
                    sq, bsq = sqr_.next()
                    S.op("act", lambda en: en.activation(out=sq[:], in_=af[:], func=AF.Square), reads=[baf], writes=[bsq])
                    S.op("act", lambda en: en.activation(out=sq[:], in_=sq[:], func=AF.Sqrt, scale=-1.0, bias=self.one_col),
                         reads=[bsq, self.bconst], writes=[bsq])
                    S.op("dve", lambda en: en.tensor_tensor(out=bf[:], in0=bf[:], in1=sq[:], op=ALU.mult), reads=[bbf, bsq], writes=[bbf])
                    hf, bhf = hfr.next()
                    S.op("dve", lambda en: en.tensor_tensor_scan(out=hf[:], data0=af[:], data1=bf[:], initial=0.0,
                                                                  op0=ALU.mult, op1=ALU.add),
                         reads=[baf, bbf], writes=[bhf])
                    go, bgo = gor.next()
                    S.op("pool", lambda en: en.tensor_tensor(out=go[:], in0=hf[:], in1=yg[:], op=ALU.mult),
                         reads=[bhf, byg], writes=[bgo])
                    S.dma("sp", self.gT[ftg, :, :], go[:], reads=[bgo])
            self.barrier()

    def _ring_guard(self, ring, tile):
        d = getattr(self, "_rg", None)
        if d is None:
            d = self._rg = {}
        b = d.get(id(tile))
        return [b] if b is not None else []

    def _ring_set(self, ring, tile, buf):
        if getattr(self, "_rg", None) is None:
            self._rg = {}
        self._rg[id(tile)] = buf


    def mixer_b(self, i):
        with ExitStack() as st:
            try:
                self._mixer_b_body(i, st)
            except _Cut:
                pass
            self.barrier()

    def _mixer_b_body(self, i, st):
        nc, S = self.nc, self.S
        wqkv, wqks = self.W["b_w_qkv"], self.W["b_w_qks"]
        banks = self.psA.items
        if True:
            cosT = self.sb(st, "b_cos", [128, T], BF16)
            sinT = self.sb(st, "b_sin", [128, T], BF16)
            btab = Buf()
            with ExitStack() as s2:
                posi = self.sb(s2, "b_posi", [128, T], I32)
                ang = self.sb(s2, "b_ang", [128, T], F32)
                kk = self.sb(s2, "b_kk", [128, T], F32)
                cst = self.sb(s2, "b_cst", [128, 2], F32)
                bt = Buf()
                S.dma("sp", posi[:], self.pos[0:1, :].to_broadcast([128, T]), writes=[bt])
                S.op("pool", lambda e: e.memset(cst[:, 0:1], float(np.pi / 2)), writes=[bt])
                S.op("dve", lambda e: e.tensor_copy(out=ang[:], in_=posi[:]), reads=[bt], writes=[bt])
                S.op("dve", lambda e: e.tensor_scalar(out=ang[:], in0=ang[:], scalar1=self.col("invf"), scalar2=None, op0=ALU.mult),
                     reads=[bt, self.bconst], writes=[bt])
                MAGIC = 12582912.0
                S.op("dve", lambda e: e.tensor_scalar(out=kk[:], in0=ang[:], scalar1=float(1.0 / (2 * np.pi)), scalar2=MAGIC,
                                                      op0=ALU.mult, op1=ALU.add), reads=[bt], writes=[bt])
                S.op("dve", lambda e: e.tensor_scalar(out=kk[:], in0=kk[:], scalar1=-MAGIC, scalar2=None, op0=ALU.add),
                     reads=[bt], writes=[bt])
                C1 = 6.28125
                C2 = float(2 * np.pi - C1)
                S.op("dve", lambda e: e.scalar_tensor_tensor(out=ang[:], in0=kk[:], scalar=-C1, in1=ang[:], op0=ALU.mult, op1=ALU.add),
                     reads=[bt], writes=[bt])
                S.op("dve", lambda e: e.scalar_tensor_tensor(out=ang[:], in0=kk[:], scalar=-C2, in1=ang[:], op0=ALU.mult, op1=ALU.add),
                     reads=[bt], writes=[bt])
                S.op("dve", lambda e: e.tensor_scalar(out=ang[:], in0=ang[:], scalar1=float(np.pi), scalar2=float(-np.pi),
                                                      op0=ALU.min, op1=ALU.max), reads=[bt], writes=[bt])
                S.op("act", lambda e: e.activation(out=sinT[:], in_=ang[:], func=AF.Sin, scale=self.col("rsgn")),
                     reads=[bt, self.bconst], writes=[btab])
                S.op("dve", lambda e: e.scalar_tensor_tensor(out=kk[:], in0=ang[:], scalar=-1.0, in1=ang[:], op0=ALU.mult, op1=ALU.max), reads=[bt], writes=[bt])
                S.op("act", lambda e: e.activation(out=cosT[:], in_=kk[:], func=AF.Sin, scale=-1.0, bias=cst[:, 0:1]),
                     reads=[bt], writes=[btab])
                self.barrier()
            self.cut("cutA")
            self.dump("cos", cosT[:], [btab], [128, T], BF16)
            self.dump("sin", sinT[:], [btab], [128, T], BF16)
            hn = self.sb(st, "b_hn", [128, KT, T], BF16)
            bhn = [Buf() for _ in range(KT)]
            for k in range(KT):
                S.dma("sp", hn[:, k, :], self.hnT[k, :, :], writes=[bhn[k]])
            band = self.sb(st, "b_band", [128, 2, 256], BF16)
            bband = Buf()
            S.op("pool", lambda e: e.memset(band[:], 1.0), writes=[bband])
            S.op("pool", lambda e: e.affine_select(out=band[:], in_=band[:], pattern=[[0, 2], [1, 256]], compare_op=ALU.is_ge,
                                                   fill=0.0, base=0, channel_multiplier=-1), reads=[bband], writes=[bband])
            S.op("pool", lambda e: e.affine_select(out=band[:], in_=band[:], pattern=[[0, 2], [-1, 256]], compare_op=ALU.is_ge,
                                                   fill=0.0, base=128, channel_multiplier=1), reads=[bband], writes=[bband])
            wr = Ring(nc, st, "b_w", [128, KT, 128], BF16, 4)
            wvr = Ring(nc, st, "b_wv", [128, KT, 128], BF16, 2)
            qr = Ring(nc, st, "b_q", [128, T], BF16, 2)
            kr = Ring(nc, st, "b_k", [128, T], BF16, 2)
            vr = Ring(nc, st, "b_v", [128, 32, 128], BF16, 2)
            accden = self.sb(st, "b_accden", [128, 2, T], F32)
            bacc = Buf()
            tr = Ring(nc, st, "b_t", [128, TT], F32, 4)
            er = Ring(nc, st, "b_e", [128, 2, 256], BF16, 3)
            outr = Ring(nc, st, "b_o", [128, TT], BF16, 2)
            psP = SubRing(banks[0:2])
            psS = SubRing([(self.psbig[q][:, :].rearrange("p (a b) -> p a b", a=2), Buf()) for q in (1, 2)])
            psOD = SubRing([(banks[q][0].rearrange("p (a b) -> p a b", a=2), Buf()) for q in (6, 7)])

            def proj_rot(col0, dst, bdst, d):
                w, bw = self.load_w(wr, wqkv, KT, col0, 128)
                ws, bws = self.load_w(wr, wqks, KT, col0, 128)
                for jt in range(NTT):
                    sl = slice(jt * TT, (jt + 1) * TT)
                    ps, bps = psP.next()
                    ps2, bps2 = psP.next()
                    for (pp, bpp, ww, bww) in ((ps, bps, w, bw), (ps2, bps2, ws, bws)):
                        for k in range(KT):
                            S.op("pe", lambda en: en.matmul(pp[:, :], lhsT=ww[:, k, :], rhs=hn[:, k, sl],
                                                            start=(k == 0), stop=(k == KT - 1)),
                                 reads=[bww] + bhn, writes=[bpp], inc=(k == KT - 1), acc=True)
                    t1, bt1 = tr.next()
                    t2, bt2 = tr.next()
                    S.op("dve", lambda en: en.tensor_tensor(out=t1[:], in0=ps[:, :], in1=cosT[:, sl], op=ALU.mult),
                         reads=[bps, btab], writes=[bt1])
                    S.op("dve", lambda en: en.tensor_tensor(out=t2[:], in0=ps2[:, :], in1=sinT[:, sl], op=ALU.mult),
                         reads=[bps2, btab], writes=[bt2])
                    n = TT // d
                    dv = dst[:].rearrange("p (r l) -> p r l", r=d)[:, :, jt * n:(jt + 1) * n]
                    v1 = t1[:].rearrange("p (j r) -> p r j", r=d)
                    v2 = t2[:].rearrange("p (j r) -> p r j", r=d)
                    S.op("pool", lambda en: en.tensor_tensor(out=dv, in0=v1, in1=v2, op=ALU.add),
                         reads=[bt1, bt2], writes=[bdst])

            for hp in range(KT):
                wv, bwv = self.load_w(wvr, wqkv, KT, 6 * D + hp * 128, 128)
                S.op("pool", lambda e: e.memset(accden[:], 0.0), writes=[bacc])
                for g, d in enumerate((1, 4, 16)):
                    L = T // d
                    nb = L // 128
                    qT, bq = qr.next()
                    kT, bk = kr.next()
                    proj_rot(g * D + hp * 128, qT, bq, d)
                    proj_rot(3 * D + g * D + hp * 128, kT, bk, d)
                    self.cut("cutB")
                    if hp == 0:
                        self.dump("q%d" % g, qT[:], [bq], [128, T], BF16)
                        self.dump("k%d" % g, kT[:], [bk], [128, T], BF16)
                    vt, bvt = vr.next()
                    for n0 in range(0, 32, 4):
                        ps, bps = psP.next()
                        for q4 in range(4):
                            n = n0 + q4
                            r, kb = n // nb, n % nb
                            start = kb * 128 * d + r
                            for k in range(KT):
                                S.op("pe", lambda en: en.matmul(ps[:, q4 * 128:(q4 + 1) * 128],
                                                                lhsT=hn[:, k, start:start + 127 * d + 1:d], rhs=wv[:, k, :],
                                                                start=(k == 0), stop=(k == KT - 1)),
                                     reads=[bwv] + bhn, writes=[bps], inc=(k == KT - 1 and q4 == 3), acc=True)
                        S.op("act", lambda en: en.activation(out=vt[:, n0:n0 + 4, :], in_=ps[:, :].rearrange("p (a b) -> p a b", a=4),
                                                             func=AF.Copy), reads=[bps], writes=[bvt])
                    self.cut("cutC")
                    steps = [(r, kb) for r in range(d) for kb in range(nb)]

                    def stage1(r, kb):
                        kcol = r * L + kb * 128
                        nq = 256 if kb + 1 < nb else 128
                        pS, bpS = psS.next()
                        for hh in range(2):
                            S.op("pe", lambda en: en.matmul(pS[:, hh, 0:nq],
                                                            lhsT=kT[64 * hh:64 * hh + 64, kcol:kcol + 128],
                                                            rhs=qT[64 * hh:64 * hh + 64, kcol:kcol + nq],
                                                            start=True, stop=True),
                                 reads=[bk, bq], writes=[bpS], inc=(hh == 1), acc=True)
                        E, bE = er.next()
                        S.op("act", lambda en: en.activation(out=E[:, :, 0:nq], in_=pS[:, :, 0:nq],
                                                             func=AF.Exp, scale=0.125), reads=[bpS], writes=[bE])
                        meng = "dve" if (kb % 2 == 0) else "pool"
                        S.op(meng, lambda en: en.tensor_tensor(out=E[:, :, 0:nq], in0=E[:, :, 0:nq], in1=band[:, :, 0:nq], op=ALU.mult),
                             reads=[bE, bband], writes=[bE])
                        return E, bE

                    stt = {"pod": None, "bpod": None, "npod": None, "nbpod": None, "fresh": None, "nfresh": None}

                    def stage2(r, kb, E, bE):
                        n = r * nb + kb
                        if kb == 0:
                            stt["pod"], stt["bpod"] = psOD.next()
                            stt["fresh"] = [True, True]
                        halves = [(kb, 0)]
                        if kb + 1 < nb:
                            halves.append((kb + 1, 1))
                        for (qb, hf) in halves:
                            if qb % 2 == 0 and hf == 1:
                                stt["npod"], stt["nbpod"] = psOD.next()
                                stt["nfresh"] = [True, True]
                                tp, tbp, fr = stt["npod"], stt["nbpod"], stt["nfresh"]
                            else:
                                tp, tbp, fr = stt["pod"], stt["bpod"], stt["fresh"]
                            c0 = (qb % 2) * 128
                            for hh in range(2):
                                for od in range(2):
                                    lh = vt[:, n, 64 * hh:64 * hh + 64] if od == 0 else self.ones_bf[:, 0:64]
                                    st_ = fr[hh]
                                    fr[hh] = False
                                    S.op("pe", lambda en: en.matmul(tp[64 * hh:64 * hh + 64, od, c0:c0 + 128], lhsT=lh,
                                                                    rhs=E[:, hh, hf * 128:(hf + 1) * 128], start=st_, stop=True,
                                                                    skip_group_check=True),
                                         reads=[bvt, bE, self.bconst], writes=[tbp], inc=(hh == 1 and od == 1), acc=True)
                        if kb % 2 == 1 or kb == nb - 1:
                            qb0 = (kb // 2) * 2
                            ncols = (kb - qb0 + 1) * 128
                            t0 = qb0 * 128 * d + r
                            asl = accden[:, :, t0:t0 + (ncols - 1) * d + 1:d]
                            pod, bpod = stt["pod"], stt["bpod"]
                            S.op("dve", lambda en: en.tensor_tensor(out=asl, in0=pod[:, :, 0:ncols], in1=asl, op=ALU.add),
                                 reads=[bpod, bacc], writes=[bacc])
                            if kb + 1 < nb:
                                stt["pod"], stt["bpod"], stt["fresh"] = stt["npod"], stt["nbpod"], stt["nfresh"]

                    cur = stage1(*steps[0])
                    for si, (r, kb) in enumerate(steps):
                        nxt_ = stage1(*steps[si + 1]) if si + 1 < len(steps) else None
                        stage2(r, kb, *cur)
                        cur = nxt_
                    self.cut("cutD%d" % g)
                if hp == 0:
                    self.dump("acc", accden[:, 0, :], [bacc], [128, T], F32)
                    self.dump("den", accden[:, 1, :], [bacc], [128, T], F32)
                    self.dump("vt", vt[:], [bvt], [128, 32, 128], BF16)
                for jt in range(NTT):
                    sl = slice(jt * TT, (jt + 1) * TT)
                    rc, brc = tr.next()
                    S.op("dve", lambda en: en.reciprocal(out=rc[:], in_=accden[:, 1, sl]), reads=[bacc], writes=[brc])
                    o, bo = outr.next()
                    S.op("dve", lambda en: en.tensor_tensor(out=o[:], in0=accden[:, 0, sl], in1=rc[:], op=ALU.mult),
                         reads=[bacc, brc], writes=[bo])
                    S.dma("sp", self.gT[hp, :, sl], o[:], reads=[bo])
            self.barrier()


    CH = 128
    NCH = T // 128
    LAM = float(np.exp(-0.5))

    def mixer_c(self, i):
        if not hasattr(self, "c_AR"):
            dt = self.nc.dram_tensor
            self.c_AR = dt("c_AR", [KT, 128, 2 * T], BF16, kind="Internal").ap()
            self.c_vtok = dt("c_vtok", [T // 128, 128, D], BF16, kind="Internal").ap()
            self.c_gc = dt("c_gc", [KT, 128, T // 128], F32, kind="Internal").ap()
        self.mixer_c1(i)
        self.mixer_c2(i)

    def mixer_c2(self, i):
        nc, S = self.nc, self.S
        BTd, KTd, gD, bonD = self.s16[1], self.s16[2], self.s16[3], self.s32[0]
        NCH = self.NCH
        with ExitStack() as st:
            bm = Buf()
            MSK = self.sb(st, "c2_msk", [128, 2, 4, 128], BF16)
            LM = self.sb(st, "c2_lm", [128, 2, 128], BF16)
            IDN = self.sb(st, "c2_idn", [128, 2, 128], BF16)
            bonesf = self.sb(st, "c2_bones", [128, 128], F32)
            S.op("pool", lambda e: e.memset(MSK[:], 1.0), writes=[bm])
            for par in range(2):
                S.op("pool", lambda e: e.affine_select(out=MSK[:, :, par::2, :], in_=MSK[:, :, par::2, :],
                                                       pattern=[[0, 2], [0, 2], [1, 128]], compare_op=ALU.is_ge, fill=0.0,
                                                       base=par - 1, channel_multiplier=-1), reads=[bm], writes=[bm])
            S.op("pool", lambda e: e.memset(LM[:], 1.0), writes=[bm])
            S.op("pool", lambda e: e.affine_select(out=LM[:], in_=LM[:], pattern=[[0, 2], [-1, 128]], compare_op=ALU.is_ge, fill=0.0,
                                                   base=-1, channel_multiplier=1), reads=[bm], writes=[bm])
            S.op("pool", lambda e: e.memset(IDN[:], 1.0), writes=[bm])
            S.op("pool", lambda e: e.affine_select(out=IDN[:], in_=IDN[:], pattern=[[0, 2], [-1, 128]], compare_op=ALU.is_equal, fill=0.0,
                                                   base=0, channel_multiplier=1), reads=[bm], writes=[bm])
            S.op("pool", lambda e: e.memset(bonesf[:], 0.0), writes=[bm])
            S.op("pool", lambda e: e.memset(bonesf[0:64, 0:64], 1.0 / 64), writes=[bm])
            S.op("pool", lambda e: e.memset(bonesf[64:128, 64:128], 1.0 / 64), writes=[bm])
            ident = IDN[:, 0, :]
            arr = Ring(nc, st, "c2_ar", [128, NCH, 2, 128], BF16, 2)
            btr = Ring(nc, st, "c2_bt", [128, T], BF16, 2)
            ktr = Ring(nc, st, "c2_kt", [128, T], BF16, 2)
            vtr = Ring(nc, st, "c2_vt", [128, NCH, 128], BF16, 2)
            gcr = Ring(nc, st, "c2_gc", [128, NCH], F32, 2)
            scr = Ring(nc, st, "c2_sc", [128, 2, 4, 128], BF16, 2)
            mlr = Ring(nc, st, "c2_ml", [128, 2, 2, 128], BF16, 3)
            ttr = Ring(nc, st, "c2_tt", [128, 2, 128], BF16, 3)
            tokr = Ring(nc, st, "c2_tok", [128, 2, 128], BF16, 2)
            wur = Ring(nc, st, "c2_wu", [128, 64], BF16, 6)
            Pst = self.sb(st, "c2_pst", [128, 64], F32)
            PG = self.sb(st, "c2_pg", [128, 64], F32)
            Pbf = self.sb(st, "c2_pbf", [128, 64], BF16)
            bP = [Buf(), Buf()]
            bPG = [Buf(), Buf()]
            ysr = Ring(nc, st, "c2_ys", [128, TT], F32, 2)
            gtr = Ring(nc, st, "c2_gt", [128, TT], F32, 8)
            bonr = Ring(nc, st, "c2_bon", [128, TT], F32, 2)
            ggr = Ring(nc, st, "c2_gg", [128, TT], BF16, 2)
            outr = Ring(nc, st, "c2_out", [128, TT], BF16, 2)
            scA = self.psbig[0][:, :].rearrange("p (a b) -> p a b", a=2)
            bscA = Buf()
            scB = self.psbig[1][:, :].rearrange("p (a b) -> p a b", a=2)
            bscB = Buf()
            trp = self.psbig[1][:, 256:512].bitcast(BF16).rearrange("p (a b) -> p a b", a=4)
            btrp = bscB
            mlp = self.psbig[2][:, 0:512].rearrange("p (a b c) -> p a b c", a=2, b=2)
            bmlp = Buf()
            ttp = self.psbig[2][:, 512:768].rearrange("p (a b) -> p a b", a=2)
            bttp = Buf()
            sq = self.psbig[3][:, :].rearrange("p (a b) -> p a b", a=2)
            bsq = [Buf(), Buf()]

            def ldt(e_):
                ar, bar = arr.next()
                S.dma("sp", ar[:].rearrange("p a b c -> p (a b c)"), self.c_AR[e_, :, :], writes=[bar])
                bt_, bbt = btr.next()
                S.dma("sp", bt_[:], BTd[e_, :, :], writes=[bbt])
                kt_, bkt = ktr.next()
                S.dma("sp", kt_[:], KTd[e_, :, :], writes=[bkt])
                vt, bvt = vtr.next()
                S.dma("sp", vt[:], self.c_vtok.rearrange("c p f -> p c f")[:, :, e_ * 128:(e_ + 1) * 128], writes=[bvt])
                gc, bgc = gcr.next()
                S.dma("sp", gc[:], self.c_gc[e_, :, :], writes=[bgc])
                return ar, bar, bt_, bbt, kt_, bkt, vt, bvt, gc, bgc
            nxt = ldt(0)
            for e_ in range(KT):
                ar, bar, bt_, bbt, kt_, bkt, vt, bvt, gc, bgc = nxt
                if e_ + 1 < KT:
                    nxt = ldt(e_ + 1)
                for hh in range(2):
                    P = slice(64 * hh, 64 * hh + 64)
                    S.op("pool", lambda e: e.memset(Pst[P, :], 0.0), writes=[bP[hh]])
                    S.op("pool", lambda e: e.memset(Pbf[P, :], 0.0), writes=[bP[hh]])
                ys = bys = None
                for c in range(NCH):
                    cs_ = slice(c * 128, (c + 1) * 128)
                    for hh in range(2):
                        P = slice(64 * hh, 64 * hh + 64)
                        S.op("pe", lambda e: e.matmul(scA[:, hh, 0:256], lhsT=bt_[P, cs_], rhs=ar[P, c, :, :].rearrange("p a b -> p (a b)"),
                                                      start=True, stop=True), reads=[bbt, bar], writes=[bscA], inc=False, acc=True)
                    for hh in range(2):
                        P = slice(64 * hh, 64 * hh + 64)
                        S.op("pe", lambda e: e.matmul(scA[:, hh, 256:512], lhsT=kt_[P, cs_], rhs=ar[P, c, :, :].rearrange("p a b -> p (a b)"),
                                                      start=True, stop=True), reads=[bkt, bar], writes=[bscA], inc=(hh == 1), acc=True)
                    for hh in range(2):
                        P = slice(64 * hh, 64 * hh + 64)
                        S.op("pe", lambda e: e.matmul(scB[:, hh, 0:128], lhsT=ar[P, c, 0, :], rhs=bt_[P, cs_],
                                                      start=True, stop=True), reads=[bbt, bar], writes=[bscB], inc=(hh == 1), acc=True)
                    S.op("pe", lambda e: e.transpose(trp[:, 0, :], bt_[:, cs_], ident), reads=[bbt, bm], writes=[btrp], inc=False, acc=True)
                    S.op("pe", lambda e: e.transpose(trp[:, 1, :], kt_[:, cs_], ident), reads=[bkt, bm], writes=[btrp], acc=True)
                    SC, bSC = scr.next()
                    S.op("dve", lambda e: e.tensor_tensor(out=SC[:].rearrange("p a b c -> p a (b c)"), in0=scA[:, :, :],
                                                          in1=MSK[:].rearrange("p a b c -> p a (b c)"), op=ALU.mult),
                         reads=[bscA, bm], writes=[bSC])
                    ML, bML = mlr.next()
                    S.op("dve", lambda e: e.tensor_tensor(out=ML[:, :, 1, :], in0=scB[:, :, 0:128], in1=LM[:], op=ALU.mult),
                         reads=[bscB, bm], writes=[bML])
                    S.op("pool", lambda e: e.tensor_copy(out=ML[:, :, 0, :], in_=SC[:, :, 0, :]), reads=[bSC], writes=[bML])
                    tok, btok = tokr.next()
                    S.op("act", lambda e: e.activation(out=tok[:], in_=trp[:, 0:2, :], func=AF.Copy), reads=[btrp], writes=[btok])
                    TTc, bTT = ttr.next()
                    S.op("pool", lambda e: e.tensor_tensor(out=TTc[:], in0=SC[:, :, 0, :], in1=IDN[:], op=ALU.add), reads=[bSC, bm], writes=[bTT])
                    for lev in range(1, 7):
                        MLn, bMLn = mlr.next()
                        for hh in range(2):
                            if lev < 6:
                                S.op("pe", lambda e: e.matmul(mlp[:, hh, 0, :], lhsT=ML[:, hh, 1, :], rhs=ML[:, hh, 0, :], start=True, stop=True),
                                     reads=[bML], writes=[bmlp], inc=False, acc=True)
                            S.op("pe", lambda e: e.matmul(mlp[:, hh, 1, :], lhsT=ML[:, hh, 0, :], rhs=ML[:, hh, 1, :], start=True, stop=True),
                                 reads=[bML], writes=[bmlp], inc=(hh == 1), acc=True)
                        if lev < 6:
                            S.op("act", lambda e: e.activation(out=MLn[:], in_=mlp[:, :, :, :], func=AF.Copy), reads=[bmlp], writes=[bMLn])
                        else:
                            S.op("act", lambda e: e.activation(out=MLn[:, :, 1, :], in_=mlp[:, :, 1, :], func=AF.Copy), reads=[bmlp], writes=[bMLn])
                        for hh in range(2):
                            S.op("pe", lambda e: e.matmul(ttp[:, hh, :], lhsT=MLn[:, hh, 1, :], rhs=TTc[:, hh, :], start=True, stop=True),
                                 reads=[bMLn, bTT], writes=[bttp], inc=(hh == 1), acc=True)
                        TTn, bTTn = ttr.next()
                        S.op("dve", lambda e: e.tensor_tensor(out=TTn[:], in0=ttp[:, :, :], in1=TTc[:], op=ALU.add), reads=[bttp, bTT], writes=[bTTn])
                        ML, bML, TTc, bTT = MLn, bMLn, TTn, bTTn
                    if c % 4 == 0:
                        ys, bys = ysr.next()
                    for hh in range(2):
                        P = slice(64 * hh, 64 * hh + 64)
                        vh = vt[:, c, 64 * hh:64 * hh + 64]
                        S.op("pe", lambda e: e.matmul(sq[:, hh, 0:64], lhsT=SC[:, hh, 2, :], rhs=vh, start=True, stop=False),
                             reads=[bSC, bvt], writes=[bsq[hh]], inc=False, acc=True)
                        S.op("pe", lambda e: e.matmul(sq[:, hh, 0:64], lhsT=ar[P, c, 0, :], rhs=Pbf[P, :], start=False, stop=True),
                             reads=[bar, bP[hh]], writes=[bsq[hh]], acc=True)
                        Wsb, bW = wur.next()
                        S.op("act", lambda e: e.activation(out=Wsb[:], in_=sq[:, hh, 0:64], func=AF.Copy), reads=[bsq[hh]], writes=[bW])
                        S.op("pe", lambda e: e.matmul(sq[:, hh, 64:128], lhsT=TTc[:, hh, :], rhs=Wsb[:], start=True, stop=True),
                             reads=[bTT, bW], writes=[bsq[hh]], acc=True)
                        Usb, bU = wur.next()
                        S.op("act", lambda e: e.activation(out=Usb[:], in_=sq[:, hh, 64:128], func=AF.Copy), reads=[bsq[hh]], writes=[bU])
                        S.op("pe", lambda e: e.matmul(sq[P, hh, 128:256], lhsT=vh, rhs=SC[:, hh, 3, :], start=True, stop=False),
                             reads=[bvt, bSC], writes=[bsq[hh]], inc=False, acc=True)
                        S.op("pe", lambda e: e.matmul(sq[P, hh, 128:256], lhsT=Usb[:], rhs=SC[:, hh, 1, :], start=False, stop=False),
                             reads=[bU, bSC], writes=[bsq[hh]], inc=False, acc=True)
                        S.op("pe", lambda e: e.matmul(sq[P, hh, 128:256], lhsT=Pbf[P, :], rhs=ar[P, c, 1, :], start=False, stop=True),
                             reads=[bP[hh], bar], writes=[bsq[hh]], acc=True)
                        S.op("act", lambda e: e.activation(out=ys[P, (c % 4) * 128:(c % 4 + 1) * 128], in_=sq[P, hh, 128:256], func=AF.Copy),
                             reads=[bsq[hh]], writes=[bys])
                        S.op("dve", lambda e: e.tensor_scalar(out=PG[P, :], in0=Pst[P, :], scalar1=gc[P, c:c + 1], scalar2=None, op0=ALU.mult),
                             reads=[bP[hh], bgc], writes=[bPG[hh]])
                        S.op("pe", lambda e: e.matmul(sq[P, hh, 256:320], lhsT=tok[:, 0, 64 * hh:64 * hh + 64], rhs=Usb[:], start=True, stop=False),
                             reads=[btok, bU], writes=[bsq[hh]], inc=False, acc=True)
                        S.op("pe", lambda e: e.matmul(sq[P, hh, 256:320], lhsT=tok[:, 1, 64 * hh:64 * hh + 64], rhs=vh, start=False, stop=True),
                             reads=[btok, bvt], writes=[bsq[hh]], acc=True)
                        S.op("dve", lambda e: e.scalar_tensor_tensor(out=Pst[P, :], in0=sq[P, hh, 256:320], scalar=gc[P, c:c + 1], in1=PG[P, :],
                                                                     op0=ALU.mult, op1=ALU.add),
                             reads=[bsq[hh], bgc, bPG[hh]], writes=[bP[hh]])
                        S.op("act", lambda e: e.activation(out=Pbf[P, :], in_=Pst[P, :], func=AF.Copy), reads=[bP[hh]], writes=[bP[hh]])
                    if c % 4 == 3:
                        jt = c // 4
                        sl = slice(jt * TT, (jt + 1) * TT)
                        bon, bbon = bonr.next()
                        S.dma("sp", bon[:], bonD[e_, :, sl], writes=[bbon])
                        gg, bgg = ggr.next()
                        S.dma("sp", gg[:], gD[e_, :, sl], writes=[bgg])
                        mean_ps, ex2_ps = scA[:, 0, :], scA[:, 1, :]
                        ysq, bysq = gtr.next()
                        S.op("act", lambda e: e.activation(out=ysq[:], in_=ys[:], func=AF.Square), reads=[bys], writes=[bysq])
                        S.op("pe", lambda e: e.matmul(mean_ps, lhsT=bonesf[:, :], rhs=ys[:], start=True, stop=True),
                             reads=[bm, bys], writes=[bscA], inc=False, acc=True)
                        S.op("pe", lambda e: e.matmul(ex2_ps, lhsT=bonesf[:, :], rhs=ysq[:], start=True, stop=True),
                             reads=[bm, bysq], writes=[bscA], acc=True)
                        msq, bmsq = gtr.next()
                        S.op("act", lambda e: e.activation(out=msq[:], in_=mean_ps, func=AF.Square), reads=[bscA], writes=[bmsq])
                        S.op("dve", lambda e: e.tensor_tensor(out=msq[:], in0=ex2_ps, in1=msq[:], op=ALU.subtract), reads=[bscA, bmsq], writes=[bmsq])
                        S.op("dve", lambda e: e.tensor_scalar(out=msq[:], in0=msq[:], scalar1=0.0, scalar2=None, op0=ALU.max), reads=[bmsq], writes=[bmsq])
                        S.op("act", lambda e: e.activation(out=msq[:], in_=msq[:], func=AF.Sqrt, bias=self.gneps_col), reads=[bmsq, self.bconst], writes=[bmsq])
                        msq_in, bmsq_in = msq, bmsq
                        msq, bmsq = gtr.next()
                        S.op("dve", lambda e: e.reciprocal(out=msq[:], in_=msq_in[:]), reads=[bmsq_in], writes=[bmsq])
                        yc, byc = gtr.next()
                        S.op("dve", lambda e: e.tensor_tensor(out=yc[:], in0=mean_ps, in1=ys[:], op=ALU.subtract), reads=[bscA, bys], writes=[byc])
                        S.op("dve", lambda e: e.tensor_tensor(out=yc[:], in0=yc[:], in1=msq[:], op=ALU.mult), reads=[byc, bmsq], writes=[byc])
                        S.op("dve", lambda e: e.tensor_scalar(out=yc[:], in0=yc[:], scalar1=-1.0, scalar2=self.col("c_ln_w", e_), op0=ALU.mult, op1=ALU.mult),
                             reads=[byc, self.bconst], writes=[byc])
                        S.op("dve", lambda e: e.scalar_tensor_tensor(out=yc[:], in0=yc[:], scalar=self.col("c_ln_b", e_), in1=bon[:], op0=ALU.add, op1=ALU.add),
                             reads=[byc, bbon, self.bconst], writes=[byc])
                        o, bo = outr.next()
                        S.op("dve", lambda e: e.tensor_tensor(out=o[:], in0=yc[:], in1=gg[:], op=ALU.mult), reads=[byc, bgg], writes=[bo])
                        S.dma("sp", self.gT[e_, :, sl], o[:], reads=[bo])
            self.barrier()

    def mixer_c1(self, i):
        nc, S = self.nc, self.S
        W = self.W
        LAM = self.LAM
        BTd, KTd, gD, bonD = self.s16[1], self.s16[2], self.s16[3], self.s32[0]
        with ExitStack() as st:
            wbuf = Buf()
            wrkv = [self.sb(st, "c_wrkv%d" % c, [128, KT, D], BF16) for c in range(3)]
            for c in range(3):
                for k in range(KT):
                    S.dma("pool", wrkv[c][:, k, :], W["c_w_rkv"][c, k * 128:(k + 1) * 128, :], writes=[wbuf])
            w1 = self.sb(st, "c_w1", [128, KT, 64], BF16)
            a1 = self.sb(st, "c_a1", [128, KT, 64], BF16)
            g1 = self.sb(st, "c_g1", [128, KT, 128], BF16)
            w2 = self.sb(st, "c_w2", [128, D], BF16)
            a2 = self.sb(st, "c_a2", [128, D], BF16)
            g2 = self.sb(st, "c_g2", [128, D], BF16)
            S.dma("pool", w1[:], W["c_w1"].rearrange("(k p) e -> p k e", p=128), writes=[wbuf])
            S.dma("pool", a1[:], W["c_a1"].rearrange("(k p) e -> p k e", p=128), writes=[wbuf])
            S.dma("pool", g1[:], W["c_g1"].rearrange("(k p) e -> p k e", p=128), writes=[wbuf])
            S.dma("pool", w2[0:64, :], W["c_w2"][:, :], writes=[wbuf])
            S.dma("pool", a2[0:64, :], W["c_a2"][:, :], writes=[wbuf])
            S.dma("pool", g2[:, :], W["c_g2"][:, :], writes=[wbuf])
            cm01 = self.sb(st, "c_cm01", [128, TT], F32)
            bones = self.sb(st, "c_bones", [128, 128], BF16)
            bm = Buf()
            S.op("pool", lambda e: e.memset(cm01[:], 1.0), writes=[bm])
            for q in range(4):
                S.op("pool", lambda e: e.memset(cm01[:, q * 128:q * 128 + 1], 0.0), writes=[bm])
            S.op("pool", lambda e: e.memset(bones[:], 0.0), writes=[bm])
            S.op("pool", lambda e: e.memset(bones[0:64, 0:64], 1.0), writes=[bm])
            S.op("pool", lambda e: e.memset(bones[64:128, 64:128], 1.0), writes=[bm])
            hr = Ring(nc, st, "c_hn", [128, KT, TT + 1], BF16, 2)
            dd = self.sb(st, "c_d", [128, KT, TT], F32)
            bdd = Buf()
            xm = [self.sb(st, "c_xm%d" % c, [128, KT, TT], BF16) for c in range(6)]
            bxm = [Buf() for _ in range(6)]
            lor = [self.sb(st, "c_lor%d" % c, [128, TT], BF16) for c in range(3)]
            blor = [Buf() for _ in range(3)]
            trA = Ring(nc, st, "c_tA", [128, TT], F32, 12)
            trB = Ring(nc, st, "c_tB", [128, TT], F32, 8)
            brA = Ring(nc, st, "c_bA", [128, TT], BF16, 4)
            brB = Ring(nc, st, "c_bB", [128, TT], BF16, 6)
            psF = SubRing(self.psA.items[0:5])
            psB = SubRing(self.psA.items[5:8])
            arr = Ring(nc, st, "c_ar", [128, 4, 2, 128], BF16, 2)
            vtr = Ring(nc, st, "c_vt", [128, D], BF16, 2)
            gct = self.sb(st, "c_gct", [128, KT, T // 128], F32)
            bgct = Buf()
            P_ = self.psA

            def ldh(jt):
                h, bh = hr.next()
                if jt == 0:
                    S.op("pool", lambda e: e.memset(h[:, :, 0:1], 0.0), writes=[bh])
                    S.dma("sp", h[:, :, 1:TT + 1], self.tview(self.hnT, 0), writes=[bh])
                else:
                    S.dma("sp", h[:, :, :], self.hnT.rearrange("k p t -> p k t")[:, :, jt * TT - 1:(jt + 1) * TT], writes=[bh])
                return h, bh
            nxt = ldh(0)
            for jt in range(NTT):
                sl = slice(jt * TT, (jt + 1) * TT)
                h, bh = nxt
                if jt + 1 < NTT:
                    nxt = ldh(jt + 1)
                for k in range(KT):
                    S.op("dve", lambda e: e.tensor_tensor(out=dd[:, k, :], in0=h[:, k, 0:TT], in1=h[:, k, 1:TT + 1], op=ALU.subtract),
                         reads=[bh], writes=[bdd])
                for c in (3, 4, 5, 2, 0, 1):
                    for k in range(KT):
                        S.op("dve", lambda e: e.scalar_tensor_tensor(out=xm[c][:, k, :], in0=dd[:, k, :], scalar=self.col("c_mu%d" % c, k),
                                                                     in1=h[:, k, 1:TT + 1], op0=ALU.mult, op1=ALU.add),
                             reads=[bdd, bh, self.bconst], writes=[bxm[c]])
                for li, (wt, c, fn, m) in enumerate(((w1, 3, AF.Tanh, 64), (a1, 4, AF.Copy, 64), (g1, 5, AF.Sigmoid, 128))):
                    ps, bps = P_.next()
                    for k in range(KT):
                        S.op("pe", lambda e: e.matmul(ps[0:m, :], lhsT=wt[:, k, :], rhs=xm[c][:, k, :], start=(k == 0), stop=(k == KT - 1)),
                             reads=[wbuf, bxm[c]], writes=[bps], inc=(k == KT - 1), acc=True)
                    S.op("act", lambda e: e.activation(out=lor[li][0:m, :], in_=ps[0:m, :], func=fn), reads=[bps], writes=[blor[li]])
                for blk in range(4):
                    vt, bvt = vtr.next()
                    for half in range(2):
                        ps, bps = P_.next()
                        for k in range(KT):
                            S.op("pe", lambda e: e.matmul(ps[:, :], lhsT=xm[2][:, k, blk * 128:(blk + 1) * 128],
                                                          rhs=wrkv[2][:, k, half * 512:(half + 1) * 512], start=(k == 0), stop=(k == KT - 1)),
                                 reads=[wbuf, bxm[2]], writes=[bps], inc=(k == KT - 1), acc=True)
                        S.op("act", lambda e: e.activation(out=vt[:, half * 512:(half + 1) * 512], in_=ps[:, :], func=AF.Copy),
                             reads=[bps], writes=[bvt])
                    S.dma("sp", self.c_vtok[jt * 4 + blk, :, :], vt[:], reads=[bvt])
                def stageA(e_):
                    es = slice(e_ * 128, (e_ + 1) * 128)
                    pss = []
                    for c in range(3):
                        ps, bps = psF.next()
                        for k in range(KT):
                            S.op("pe", lambda e: e.matmul(ps[:, :], lhsT=wrkv[c][:, k, es], rhs=xm[c][:, k, :], start=(k == 0), stop=(k == KT - 1)),
                                 reads=[wbuf, bxm[c]], writes=[bps], inc=(k == KT - 1), acc=True)
                        pss.append((ps, bps))
                    (r_ps, br_), (k_ps, bk_), (v_ps, bv_) = pss
                    rf, brf = trA.next()
                    S.op("act", lambda e: e.activation(out=rf[:], in_=r_ps[:, :], func=AF.Copy), reads=[br_], writes=[brf])
                    kf, bkf = trA.next()
                    S.op("act", lambda e: e.activation(out=kf[:], in_=k_ps[:, :], func=AF.Copy), reads=[bk_], writes=[bkf])
                    vf, bvf = trA.next()
                    S.op("act", lambda e: e.activation(out=vf[:], in_=v_ps[:, :], func=AF.Copy), reads=[bv_], writes=[bvf])
                    wl_ps, bwl = psF.next()
                    S.op("pe", lambda e: e.matmul(wl_ps[:, :], lhsT=w2[0:64, es], rhs=lor[0][0:64, :], start=True, stop=True),
                         reads=[wbuf, blor[0]], writes=[bwl], acc=True)
                    al_ps, bal = psF.next()
                    S.op("pe", lambda e: e.matmul(al_ps[:, :], lhsT=a2[0:64, es], rhs=lor[1][0:64, :], start=True, stop=True),
                         reads=[wbuf, blor[1]], writes=[bal], acc=True)
                    g_ps, bg_ = psF.next()
                    S.op("pe", lambda e: e.matmul(g_ps[:, :], lhsT=g2[:, es], rhs=lor[2][:, :], start=True, stop=True),
                         reads=[wbuf, blor[2]], writes=[bg_], acc=True)
                    gb, bgb = brA.next()
                    S.op("act", lambda e: e.activation(out=gb[:], in_=g_ps[:, :], func=AF.Copy), reads=[bg_], writes=[bgb])
                    S.dma("sp", gD[e_, :, sl], gb[:], reads=[bgb])
                    sg, bsg = trA.next()
                    S.op("act", lambda e: e.activation(out=sg[:], in_=wl_ps[:, :], func=AF.Sigmoid, bias=self.col("c_w0", e_)),
                         reads=[bwl, self.bconst], writes=[bsg])
                    al, bal2 = trA.next()
                    S.op("act", lambda e: e.activation(out=al[:], in_=al_ps[:, :], func=AF.Sigmoid, bias=self.col("c_a0", e_)),
                         reads=[bal, self.bconst], writes=[bal2])
                    kk, bkk = trA.next()
                    S.op("dve", lambda e: e.tensor_scalar(out=kk[:], in0=kf[:], scalar1=self.col("c_k_k", e_), scalar2=None, op0=ALU.mult),
                         reads=[bkf, self.bconst], writes=[bkk])
                    k2, bk2 = brA.next()
                    S.op("act", lambda e: e.activation(out=k2[:], in_=kk[:], func=AF.Square), reads=[bkk], writes=[bk2])
                    ss_ps, bss = psB.next()
                    S.op("pe", lambda e: e.matmul(ss_ps[:, :], lhsT=bones[:, :], rhs=k2[:], start=True, stop=True),
                         reads=[bm, bk2], writes=[bss], acc=True)
                    return dict(es=es, rf=rf, brf=brf, kf=kf, bkf=bkf, vf=vf, bvf=bvf, sg=sg, bsg=bsg, al=al, bal2=bal2,
                                kk=kk, bkk=bkk, ss_ps=ss_ps, bss=bss)

                def stageB(e_, d_):
                    es = d_["es"]
                    rf, brf, kf, bkf, vf, bvf = d_["rf"], d_["brf"], d_["kf"], d_["bkf"], d_["vf"], d_["bvf"]
                    sg, bsg, al, bal2, kk, bkk, ss_ps, bss = d_["sg"], d_["bsg"], d_["al"], d_["bal2"], d_["kk"], d_["bkk"], d_["ss_ps"], d_["bss"]
                    rn, brn = trB.next()
                    S.op("act", lambda e: e.activation(out=rn[:], in_=ss_ps[:, :], func=AF.Sqrt), reads=[bss], writes=[brn])
                    S.op("dve", lambda e: e.tensor_scalar(out=rn[:], in0=rn[:], scalar1=1e-12, scalar2=None, op0=ALU.max), reads=[brn], writes=[brn])
                    rn2, brn2 = trB.next()
                    S.op("dve", lambda e: e.reciprocal(out=rn2[:], in_=rn[:]), reads=[brn], writes=[brn2])
                    S.op("dve", lambda e: e.tensor_tensor(out=kk[:], in0=kk[:], in1=rn2[:], op=ALU.mult), reads=[bkk, brn2], writes=[bkk])
                    cs, bcs = trB.next()
                    S.op("dve", lambda e: e.tensor_tensor_scan(out=cs[:], data0=cm01[:], data1=sg[:], initial=0.0, op0=ALU.mult, op1=ALU.add),
                         reads=[bm, bsg], writes=[bcs])
                    csx, bcsx = trB.next()
                    S.op("dve", lambda e: e.tensor_tensor(out=csx[:], in0=cs[:], in1=sg[:], op=ALU.subtract), reads=[bcs, bsg], writes=[bcsx])
                    eG, beG = trB.next()
                    S.op("act", lambda e: e.activation(out=eG[:], in_=cs[:], func=AF.Exp, scale=-LAM), reads=[bcs], writes=[beG])
                    eGi, beGi = trB.next()
                    S.op("act", lambda e: e.activation(out=eGi[:], in_=cs[:], func=AF.Exp, scale=LAM), reads=[bcs], writes=[beGi])
                    S.op("act", lambda e: e.activation(out=csx[:], in_=csx[:], func=AF.Exp, scale=-LAM), reads=[bcsx], writes=[bcsx])
                    ar, bar = arr.next()
                    S.op("dve", lambda e: e.scalar_tensor_tensor(out=ar[:, :, 0, :], in0=kk[:].rearrange("p (c t) -> p c t", c=4), scalar=-1.0,
                                                                 in1=csx[:].rearrange("p (c t) -> p c t", c=4), op0=ALU.mult, op1=ALU.mult),
                         reads=[bkk, bcsx], writes=[bar])
                    S.op("dve", lambda e: e.tensor_tensor(out=kk[:], in0=kk[:], in1=al[:], op=ALU.mult), reads=[bkk, bal2], writes=[bkk])
                    bt_, bbt = brB.next()
                    S.op("dve", lambda e: e.tensor_tensor(out=bt_[:], in0=kk[:], in1=eGi[:], op=ALU.mult), reads=[bkk, beGi], writes=[bbt])
                    S.dma("sp", BTd[e_, :, sl], bt_[:], reads=[bbt])
                    S.op("dve", lambda e: e.tensor_scalar(out=al[:], in0=al[:], scalar1=-1.0, scalar2=self.col("c_k_a", e_), op0=ALU.add, op1=ALU.mult),
                         reads=[bal2, self.bconst], writes=[bal2])
                    S.op("dve", lambda e: e.scalar_tensor_tensor(out=kf[:], in0=al[:], scalar=1.0, in1=kf[:], op0=ALU.add, op1=ALU.mult),
                         reads=[bal2, bkf], writes=[bkf])
                    kt_, bkt = brB.next()
                    S.op("dve", lambda e: e.tensor_tensor(out=kt_[:], in0=kf[:], in1=eGi[:], op=ALU.mult), reads=[bkf, beGi], writes=[bkt])
                    S.dma("sp", KTd[e_, :, sl], kt_[:], reads=[bkt])
                    S.op("dve", lambda e: e.tensor_tensor(out=ar[:, :, 1, :], in0=rf[:].rearrange("p (c t) -> p c t", c=4),
                                                          in1=eG[:].rearrange("p (c t) -> p c t", c=4), op=ALU.mult),
                         reads=[brf, beG], writes=[bar])
                    S.dma("sp", self.c_AR[e_, :, jt * 1024:(jt + 1) * 1024], ar[:].rearrange("p a b c -> p (a b c)"), reads=[bar])
                    rk, brk = brB.next()
                    S.op("dve", lambda e: e.scalar_tensor_tensor(out=rk[:], in0=rf[:], scalar=self.col("c_r_k", e_), in1=kf[:],
                                                                 op0=ALU.mult, op1=ALU.mult), reads=[brf, bkf, self.bconst], writes=[brk])
                    rk_ps, brkp = psB.next()
                    S.op("pe", lambda e: e.matmul(rk_ps[:, :], lhsT=bones[:, :], rhs=rk[:], start=True, stop=True),
                         reads=[bm, brk], writes=[brkp], acc=True)
                    S.op("dve", lambda e: e.tensor_tensor(out=vf[:], in0=rk_ps[:, :], in1=vf[:], op=ALU.mult), reads=[brkp, bvf], writes=[bvf])
                    S.dma("sp", bonD[e_, :, sl], vf[:], reads=[bvf])
                    S.op("act", lambda e: e.activation(out=gct[:, e_, jt * 4:(jt + 1) * 4], in_=eG[:, 127:TT:128], func=AF.Copy),
                         reads=[beG], writes=[bgct])

                curA = stageA(0)
                for e_ in range(KT):
                    nxtA = stageA(e_ + 1) if e_ + 1 < KT else None
                    stageB(e_, curA)
                    curA = nxtA
            for e_ in range(KT):
                S.dma("sp", self.c_gc[e_, :, :], gct[:, e_, :], reads=[bgct])
            self.barrier()


def make_in_maps(inp):
    f32 = np.float32
    cols = pack_cols(inp)
    wqkv = np.ascontiguousarray(inp["b_w_qkv"][0], dtype=f32)
    qk = wqkv[:, :6 * D].reshape(D, 6 * 16, 2, 32)
    wqks = np.ascontiguousarray(qk[:, :, ::-1, :].reshape(D, 6 * D))
    shared = {
        "cols": cols,
        "a_w_in": inp["a_w_in"], "a_gate_w": inp["a_gate_w"], "a_w_out": inp["a_w_out"],
        "b_w_qkv": wqkv, "b_w_qks": wqks, "b_w_out": inp["b_w_out"][0],
        "c_w_rkv": inp["c_w_rkv"][0], "c_w1": inp["c_w1"][0], "c_w2": inp["c_w2"][0],
        "c_a1": inp["c_a1"][0], "c_a2": inp["c_a2"][0], "c_g1": inp["c_g1"][0], "c_g2": inp["c_g2"][0],
        "c_w_out": inp["c_w_out"][0],
        "f_w_up": inp["f_w_up"], "f_w_down": inp["f_w_down"],
        "ple_w_proj": inp["ple_w_proj"], "ple_w_gate": inp["ple_w_gate"],
    }
    shared = {k: np.ascontiguousarray(v, dtype=f32) for k, v in shared.items()}
    maps = []
    for c in range(8):
        b = c % NB
        m = dict(shared)
        m["xT"] = np.ascontiguousarray(np.asarray(inp["x"][b], dtype=f32).T).reshape(KT, 128, T)
        m["pT"] = np.ascontiguousarray(np.transpose(np.asarray(inp["p"][:, b], dtype=f32), (0, 2, 1))).reshape(DEPTH, 2, 128, T)
        m["pos"] = np.ascontiguousarray(np.asarray(inp["positions"][b], dtype=np.int32)).reshape(1, T)
        maps.append(m)
    return maps


_PROG_CACHE = {}


def run_prog(inp, layers=DEPTH, dbg=None):
    key = (str(layers), dbg)
    if key not in _PROG_CACHE:
        _PROG_CACHE[key] = Prog(layers, dbg)
    prog = _PROG_CACHE[key]
    maps = make_in_maps(inp)
    used = set(prog.W.keys()) | {"xT", "pT", "pos", "cols"}
    maps = [{k: v for k, v in m.items() if k in used} for m in maps]
    res = run_bass_kernel_spmd(prog.nc, maps, core_ids=list(range(8)))
    out = np.stack([np.asarray(res.results[b]["yT"]).reshape(D, T).T for b in range(NB)])
    return np.ascontiguousarray(out.astype(np.float32)), res


def kernel(**inputs):
    inp = {k: np.asarray(v) for k, v in inputs.items()}
    out, _ = run_prog(inp)
    return out
```

```python
import numpy as np
import concourse.bass as bass
import concourse.mybir as mybir
from concourse.bass_utils import run_bass_kernel_spmd

F32 = mybir.dt.float32
BF16 = mybir.dt.bfloat16
I32 = mybir.dt.int32
AF = mybir.ActivationFunctionType
ALU = mybir.AluOpType
AX = mybir.AxisListType


class Buf:
    __slots__ = ("w", "r", "name", "x")

    def __init__(self, name="", x=False):
        self.w = None
        self.r = {}
        self.name = name
        self.x = x


class _Eng:
    def __init__(self, name, eng, sem):
        self.name, self.e, self.sem = name, eng, sem
        self.cnt = 0
        self.seen = {}
        self.pending = []

    def wait(self, ev):
        sem, val = ev
        k = id(sem)
        if self.seen.get(k, 0) >= val:
            return
        self.e.wait_ge(sem, val)
        self.seen[k] = val


class Sched:
    NDMA = 8

    def __init__(self, nc, stack):
        self.nc = nc
        self.engs = {}
        for name, eng in (("pe", nc.tensor), ("act", nc.scalar), ("dve", nc.vector),
                          ("pool", nc.gpsimd), ("sp", nc.sync)):
            sem = stack.enter_context(nc.semaphore("sem_" + name))
            self.engs[name] = _Eng(name, eng, sem)
        self.dma_slots = {}
        for q in ("sp", "pool", "act"):
            sl = []
            for i in range(self.NDMA):
                sem = stack.enter_context(nc.semaphore("dq_%s%d" % (q, i)))
                sl.append([sem, 0])
            self.dma_slots[q] = [sl, 0]
        self.n_inst = 0

    def op(self, engname, fn, reads=(), writes=(), inc=True, acc=False):
        E = self.engs[engname]
        for b in reads:
            if b.w is not None:
                E.wait(b.w)
            if b.x:
                for ev in b.r.values():
                    if ev[0] is not E.sem:
                        E.wait(ev)
        for b in writes:
            if b.w is not None and not (acc and b.w[0] is E.sem):
                E.wait(b.w)
            for ev in b.r.values():
                if ev[0] is not E.sem:
                    E.wait(ev)
        inst = fn(E.e)
        self.n_inst += 1
        if inc:
            E.cnt += 1
            inst.then_inc(E.sem, 1)
            ev = (E.sem, E.cnt)
            E.pending.append((reads, writes))
            for rd, wr in E.pending:
                for b in rd:
                    b.r[id(E.sem)] = ev
                for b in wr:
                    b.w = ev
                    b.r = {}
            E.pending = []
            E.seen[id(E.sem)] = max(E.seen.get(id(E.sem), 0), 0)
        else:
            E.pending.append((reads, writes))
        return inst

    def dma(self, q, out, in_, reads=(), writes=()):
        E = self.engs[q]
        slots, idx = self.dma_slots[q]
        slot = slots[idx % self.NDMA]
        self.dma_slots[q][1] = idx + 1
        if slot[1] > 0:
            E.wait((slot[0], slot[1]))
        for b in reads:
            if b.w is not None:
                E.wait(b.w)
        for b in writes:
            if b.w is not None:
                E.wait(b.w)
            for ev in b.r.values():
                E.wait(ev)
        inst = E.e.dma_start(out=out, in_=in_)
        self.n_inst += 1
        slot[1] += 16
        inst.then_inc(slot[0], 16)
        ev = (slot[0], slot[1])
        for b in reads:
            b.r[id(slot[0])] = ev
        for b in writes:
            b.w = ev
            b.r = {}
        return ev

    def wait_all(self, engname, bufs):
        E = self.engs[engname]
        for b in bufs:
            if b.w is not None:
                E.wait(b.w)
            for ev in b.r.values():
                E.wait(ev)


class Ring:
    _uid = [0]

    def __init__(self, nc, stack, name, shape, dtype, n, psum=False):
        self.items = []
        Ring._uid[0] += 1
        name = "%s_u%d_" % (name, Ring._uid[0])
        for i in range(n):
            if psum:
                t = stack.enter_context(nc.psum_tensor("%s%d" % (name, i), shape, dtype))
            else:
                t = stack.enter_context(nc.sbuf_tensor("%s%d" % (name, i), shape, dtype))
            self.items.append((t, Buf("%s%d" % (name, i))))
        self.i = 0

    def next(self):
        it = self.items[self.i % len(self.items)]
        self.i += 1
        return it


class SubRing(Ring):
    def __init__(self, items):
        self.items = list(items)
        self.i = 0


D = 1024
T = 4096
NB = 4
DEPTH = 4
KT = D // 128
TT = 512
NTT = T // TT
FFN = 2816
FT = FFN // 128
PLE = 256
RMS_EPS = 1e-6
GN_EPS = 64e-5
LRU_C = 8.0


class ColPack:
    def __init__(self):
        self.cols = []
        self.idx = {}

    def add(self, name, vec):
        vec = np.ascontiguousarray(vec, dtype=np.float32).reshape(-1)
        assert vec.size % 128 == 0
        n = vec.size // 128
        self.idx[name] = (len(self.cols), n)
        for i in range(n):
            self.cols.append(vec[i * 128:(i + 1) * 128])

    def array(self):
        return np.ascontiguousarray(np.stack(self.cols, axis=1))


def col_layout():
    L = []
    for i in range(DEPTH):
        L += [("norm_mix%d" % i, KT), ("norm_ffn%d" % i, KT), ("norm_ple%d" % i, KT)]
        for k in range(3):
            L.append(("f_conv_w%d_%d" % (i, k), 2 * FT))
        L.append(("f_conv_b%d" % i, 2 * FT))
    L.append(("norm_final", KT))
    for j in range(2):
        for k in range(4):
            L.append(("a_conv_w%d_%d" % (j, k), KT))
        L += [("a_conv_b%d" % j, KT), ("a_gate_b%d" % j, 2 * KT), ("a_lambda%d" % j, KT)]
    for c in range(6):
        L.append(("c_mu%d" % c, KT))
    for nm in ("c_w0", "c_a0", "c_k_k", "c_k_a", "c_r_k", "c_ln_w", "c_ln_b"):
        L.append((nm, KT))
    L.append(("invf", 1))
    L.append(("rsgn", 1))
    off = {}
    o = 0
    for nm, n in L:
        off[nm] = (o, n)
        o += n
    return off, o


COLS, NCOLS = col_layout()


def pack_cols(inp):
    cp = ColPack()
    for i in range(DEPTH):
        cp.add("norm_mix%d" % i, inp["norm_mix"][i])
        cp.add("norm_ffn%d" % i, inp["norm_ffn"][i])
        cp.add("norm_ple%d" % i, inp["norm_ple"][i])
        for k in range(3):
            cp.add("f_conv_w%d_%d" % (i, k), inp["f_conv_w"][i, k])
        cp.add("f_conv_b%d" % i, inp["f_conv_b"][i])
    cp.add("norm_final", inp["norm_final"])
    for j in range(2):
        for k in range(4):
            cp.add("a_conv_w%d_%d" % (j, k), inp["a_conv_w"][j, k])
        cp.add("a_conv_b%d" % j, inp["a_conv_b"][j])
        gb = inp["a_gate_b"][j].reshape(4, 2, 256)
        cp.add("a_gate_b%d" % j, np.concatenate([gb[:, 0].reshape(-1), gb[:, 1].reshape(-1)]))
        cp.add("a_lambda%d" % j, inp["a_lambda"][j])
    for c in range(6):
        cp.add("c_mu%d" % c, inp["c_mu"][0, c])
    for nm in ("c_w0", "c_a0", "c_k_k", "c_k_a", "c_r_k", "c_ln_w", "c_ln_b"):
        cp.add(nm, inp[nm][0])
    invf = (10000.0 ** (-np.arange(0, 64, 2, dtype=np.float32) / 64)).astype(np.float32)
    cp.add("invf", np.tile(invf, 4))
    cp.add("rsgn", np.tile(np.concatenate([-np.ones(32, np.float32), np.ones(32, np.float32)]), 2))
    assert cp.idx == COLS, "col layout mismatch"
    return cp.array()


from contextlib import ExitStack


class _Cut(Exception):
    pass


class Prog:
    def cut(self, name):
        if self.dbg == name:
            raise _Cut()

    def __init__(self, layers=DEPTH, dbg=None):
        self.layers = list(range(layers)) if isinstance(layers, int) else list(layers)
        self.dbg = dbg
        nc = bass.Bass("TRN2", target_bir_lowering=False)
        self.nc = nc
        dt = nc.dram_tensor
        self.xT = dt("xT", [KT, 128, T], F32, kind="ExternalInput").ap()
        self.pT = dt("pT", [DEPTH, 2, 128, T], F32, kind="ExternalInput").ap()
        self.pos = dt("pos", [1, T], I32, kind="ExternalInput").ap()
        self.colsD = dt("cols", [128, NCOLS], F32, kind="ExternalInput").ap()
        class _LazyW(dict):
            def __init__(s2, shapes):
                s2.shapes = shapes
            def __missing__(s2, nm):
                s2[nm] = dt(nm, s2.shapes[nm], F32, kind="ExternalInput").ap()
                return s2[nm]
        shapes = {}
        for nm, shp in (("a_w_in", [2, D, 2 * D]), ("a_gate_w", [2, 4, 256, 512]), ("a_w_out", [2, D, D]),
                        ("b_w_qkv", [D, 7 * D]), ("b_w_qks", [D, 6 * D]), ("b_w_out", [D, D]),
                        ("c_w_rkv", [3, D, D]), ("c_w1", [D, 64]), ("c_w2", [64, D]), ("c_a1", [D, 64]),
                        ("c_a2", [64, D]), ("c_g1", [D, 128]), ("c_g2", [128, D]), ("c_w_out", [D, D]),
                        ("f_w_up", [DEPTH, D, 2 * FFN]), ("f_w_down", [DEPTH, FFN, D]),
                        ("ple_w_proj", [DEPTH, PLE, D]), ("ple_w_gate", [DEPTH, D, D])):
            shapes[nm] = shp
        self.W = _LazyW(shapes)
        self.yT = dt("yT", [KT, 128, T], F32, kind="ExternalOutput").ap()
        self.hT = dt("hT", [KT, 128, T], F32, kind="Internal").ap()
        self.hnT = dt("hnT", [KT, 128, T], BF16, kind="Internal").ap()
        self.gT = dt("gT", [KT, 128, T], BF16, kind="Internal").ap()
        self.actT = dt("actT", [FT, 128, T], BF16, kind="Internal").ap()
        self.s32 = [dt("s32_%d" % i, [KT, 128, T], F32, kind="Internal").ap() for i in range(6)]
        self.s16 = [dt("s16_%d" % i, [KT, 128, T], BF16, kind="Internal").ap() for i in range(8)]
        self.build()

    def col(self, name, k=0, n=1):
        o, sz = COLS[name]
        assert k + n <= sz
        return self.cols[:, o + k:o + k + n]

    def barrier(self):
        S = self.S
        evs = [(E.sem, E.cnt) for E in S.engs.values() if E.cnt > 0]
        for q in S.dma_slots:
            for sl in S.dma_slots[q][0]:
                if sl[1] > 0:
                    evs.append((sl[0], sl[1]))
        for E in S.engs.values():
            assert not E.pending
            for ev in evs:
                E.wait(ev)

    def dump(self, name, ap, bufs, shape, dtype):
        if not self.dbg:
            return
        t = self.nc.dram_tensor("dbg_" + name, list(shape), dtype, kind="Internal").ap()
        self.S.dma("sp", t, ap, reads=list(bufs))

    def sb(self, st, name, shape, dtype):
        Ring._uid[0] += 1
        return st.enter_context(self.nc.sbuf_tensor("%s_u%d" % (name, Ring._uid[0]), shape, dtype))

    def load_w(self, ring, wap2d, kt, e0, ew):
        t, b = ring.next()
        src = wap2d.rearrange("(k p) e -> p k e", p=128)[:, :, e0:e0 + ew]
        self.S.dma("pool", t[:, 0:kt, 0:ew], src, writes=[b])
        return t, b

    def rmsnorm(self, st_rings, h, bh, gain, out, bout):
        S = self.S
        sqr, psr, rsr = st_rings
        ps, bps = psr.next()
        for k in range(KT):
            sq, bsq = sqr.next()
            S.op("act", lambda e: e.activation(out=sq[:], in_=h[:, k, :], func=AF.Square), reads=[bh], writes=[bsq])
            S.op("pe", lambda e: e.matmul(ps[:, :], lhsT=self.ones_bf[:, :], rhs=sq[:], start=(k == 0), stop=(k == KT - 1)),
                 reads=[bsq, self.bconst], writes=[bps], acc=True)
        rs, brs = rsr.next()
        S.op("act", lambda e: e.activation(out=rs[:], in_=ps[:, :], func=AF.Sqrt, scale=1.0 / D, bias=self.eps_col),
             reads=[bps, self.bconst], writes=[brs])
        rs2, brs2 = rsr.next()
        S.op("dve", lambda e: e.reciprocal(out=rs2[:], in_=rs[:]), reads=[brs], writes=[brs2])
        for k in range(KT):
            S.op("dve", lambda e: e.scalar_tensor_tensor(out=out[:, k, :], in0=h[:, k, :], scalar=self.col(gain, k),
                                                         in1=rs2[:], op0=ALU.mult, op1=ALU.mult),
                 reads=[bh, brs2, self.bconst], writes=[bout])

    def norm_rings(self, st, tag):
        nc = self.nc
        return (Ring(nc, st, "sq" + tag, [128, TT], BF16, 3),
                self.psA, Ring(nc, st, "rs" + tag, [128, TT], F32, 4))

    def tview(self, ap3, j):
        return ap3.rearrange("k p t -> p k t")[:, :, j * TT:(j + 1) * TT]

    def build(self):
        nc = self.nc
        with ExitStack() as top:
            S = Sched(nc, top)
            self.S = S
            self.cols = self.sb(top, "cols", [128, NCOLS], F32)
            self.cst = self.sb(top, "cst", [128, 8], F32)
            self.ones_bf = self.sb(top, "ones_bf", [128, 128], BF16)
            self.bconst = Buf("const")
            S.dma("sp", self.cols[:], self.colsD[:, :], writes=[self.bconst])
            S.op("pool", lambda e: e.memset(self.cst[:, 0:1], RMS_EPS), writes=[self.bconst])
            S.op("pool", lambda e: e.memset(self.cst[:, 1:2], 1.0), writes=[self.bconst])
            S.op("pool", lambda e: e.memset(self.cst[:, 2:3], 0.0), writes=[self.bconst])
            S.op("pool", lambda e: e.memset(self.cst[:, 3:4], GN_EPS), writes=[self.bconst])
            S.op("pool", lambda e: e.memset(self.ones_bf[:], 1.0), writes=[self.bconst])
            self.eps_col = self.cst[:, 0:1]
            self.one_col = self.cst[:, 1:2]
            self.zero_col = self.cst[:, 2:3]
            self.gneps_col = self.cst[:, 3:4]
            self.psbig = [top.enter_context(nc.psum_tensor("psbig%d" % q, [128, 2 * TT], F32)) for q in range(4)]
            self.psA = SubRing([(self.psbig[q // 2][:, (q % 2) * TT:(q % 2 + 1) * TT], Buf("ps%d" % q, x=True)) for q in range(8)])
            self.barrier()
            self.phase_norm0(self.layers[0])
            h_src = self.xT
            for li, i in enumerate(self.layers):
                kind, j = i % 3, i // 3
                if kind == 0:
                    self.mixer_a(i, j)
                    wout = self.W["a_w_out"][j]
                elif kind == 1:
                    self.mixer_b(i)
                    wout = self.W["b_w_out"]
                else:
                    self.mixer_c(i)
                    wout = self.W["c_w_out"]
                self.mixer_out(i, wout, h_src)
                h_src = self.hT
                self.ffn_up(i)
                last = (li == len(self.layers) - 1)
                self.tail(i, last, None if last else self.layers[li + 1])
            self.barrier()

    def phase_norm0(self, i0):
        nc, S = self.nc, self.S
        with ExitStack() as st:
            hr = Ring(nc, st, "n0h", [128, KT, TT], F32, 2)
            orr = Ring(nc, st, "n0o", [128, KT, TT], BF16, 2)
            rings = self.norm_rings(st, "n0")
            def ld(j):
                h, bh = hr.next()
                S.dma("sp", h[:], self.tview(self.xT, j), writes=[bh])
                return h, bh
            nxt = ld(0)
            for j in range(NTT):
                h, bh = nxt
                if j + 1 < NTT:
                    nxt = ld(j + 1)
                o, bo = orr.next()
                self.rmsnorm(rings, h, bh, "norm_mix%d" % i0, o, bo)
                S.dma("sp", self.tview(self.hnT, j), o[:], reads=[bo])
            self.barrier()

    def mixer_out(self, i, wout, h_src):
        nc, S = self.nc, self.S
        with ExitStack() as st:
            w = self.sb(st, "mo_w", [128, KT, D], BF16)
            bw = [Buf() for _ in range(KT)]
            for k in range(KT):
                S.dma("pool", w[:, k, :], wout[k * 128:(k + 1) * 128, :], writes=[bw[k]])
            gr = Ring(nc, st, "mo_g", [128, KT, TT], BF16, 2)
            hr = Ring(nc, st, "mo_h", [128, KT, TT], F32, 2)
            orr = Ring(nc, st, "mo_o", [128, KT, TT], BF16, 2)
            rings = self.norm_rings(st, "mo")
            def ld(j):
                g, bg = gr.next()
                S.dma("sp", g[:], self.tview(self.gT, j), writes=[bg])
                h, bh = hr.next()
                S.dma("sp", h[:], self.tview(h_src, j), writes=[bh])
                return g, bg, h, bh
            nxt = ld(0)
            for j in range(NTT):
                g, bg, h, bh = nxt
                if j + 1 < NTT:
                    nxt = ld(j + 1)
                for e in range(KT):
                    ps, bps = self.psA.next()
                    for k in range(KT):
                        S.op("pe", lambda en: en.matmul(ps[:, :], lhsT=w[:, k, e * 128:(e + 1) * 128], rhs=g[:, k, :],
                                                        start=(k == 0), stop=(k == KT - 1)),
                             reads=[bw[k], bg], writes=[bps], inc=(k == KT - 1), acc=True)
                    S.op("dve", lambda en: en.tensor_tensor(out=h[:, e, :], in0=ps[:, :], in1=h[:, e, :], op=ALU.add),
                         reads=[bps, bh], writes=[bh])
                S.dma("sp", self.tview(self.hT, j), h[:], reads=[bh])
                o, bo = orr.next()
                self.rmsnorm(rings, h, bh, "norm_ffn%d" % i, o, bo)
                S.dma("sp", self.tview(self.hnT, j), o[:], reads=[bo])
            self.barrier()

    def ffn_up(self, i):
        nc, S = self.nc, self.S
        wup = self.W["f_w_up"][i]
        with ExitStack() as st:
            hn = self.sb(st, "fu_hn", [128, KT, T], BF16)
            bhn = [Buf() for _ in range(KT)]
            for k in range(KT):
                S.dma("sp", hn[:, k, :], self.hnT[k, :, :], writes=[bhn[k]])
            wr = Ring(nc, st, "fu_w", [128, KT, 128], BF16, 4)
            stg = [[self.sb(st, "fu_stg%d_%d" % (s, b), [128, 2 + T], F32) for b in range(2)] for s in range(2)]
            bstg = [[[Buf() for _ in range(NTT + 1)] for b in range(2)] for s in range(2)]
            for s in range(2):
                for b in range(2):
                    S.op("pool", lambda e: e.memset(stg[s][b][:, 0:2], 0.0), writes=[bstg[s][b][0]])
            accr = Ring(nc, st, "fu_acc", [128, TT], F32, 4)
            sgr = Ring(nc, st, "fu_sg", [128, TT], F32, 2)
            outr = Ring(nc, st, "fu_out", [128, TT], BF16, 3)
            ldw = lambda f: [self.load_w(wr, wup, KT, (s * FT + f) * 128, 128) for s in range(2)]
            wnxt = ldw(0)
            for f in range(FT):
                wt = wnxt
                if f + 1 < FT:
                    wnxt = ldw(f + 1)
                accs = [None, None]
                for j in range(NTT):
                    for s in range(2):
                        w, bw = wt[s]
                        ps, bps = self.psA.next()
                        for k in range(KT):
                            S.op("pe", lambda en: en.matmul(ps[:, :], lhsT=w[:, k, :], rhs=hn[:, k, j * TT:(j + 1) * TT],
                                                            start=(k == 0), stop=(k == KT - 1)),
                                 reads=[bw, bhn[k]], writes=[bps], inc=(k == KT - 1), acc=True)
                        sg_t, sg_b = stg[s][f % 2], bstg[s][f % 2]
                        c0 = 2 + j * TT
                        S.op("act", lambda en: en.activation(out=sg_t[:, c0:c0 + TT], in_=ps[:, :], func=AF.Copy),
                             reads=[bps], writes=[sg_b[j + 1]])
                        acc, bacc = accr.next()
                        ci = s * FT + f
                        S.op("act", lambda en: en.activation(out=acc[:], in_=ps[:, :], func=AF.Identity,
                                                             scale=self.col("f_conv_w%d_2" % i, ci),
                                                             bias=self.col("f_conv_b%d" % i, ci)),
                             reads=[bps, self.bconst], writes=[bacc])
                        for kk in (1, 0):
                            S.op("dve", lambda en: en.scalar_tensor_tensor(
                                out=acc[:], in0=sg_t[:, j * TT + kk:j * TT + kk + TT],
                                scalar=self.col("f_conv_w%d_%d" % (i, kk), ci), in1=acc[:], op0=ALU.mult, op1=ALU.add),
                                reads=[sg_b[j], sg_b[j + 1], bacc, self.bconst], writes=[bacc])
                        accs[s] = (acc, bacc)
                    sg, bsg = sgr.next()
                    S.op("act", lambda en: en.activation(out=sg[:], in_=accs[0][0][:], func=AF.Silu),
                         reads=[accs[0][1]], writes=[bsg])
                    o, bo = outr.next()
                    S.op("dve", lambda en: en.tensor_tensor(out=o[:], in0=sg[:], in1=accs[1][0][:], op=ALU.mult),
                         reads=[bsg, accs[1][1]], writes=[bo])
                    S.dma("sp", self.actT[f, :, j * TT:(j + 1) * TT], o[:], reads=[bo])
            self.barrier()

    def tail(self, i, last, inext):
        nc, S = self.nc, self.S
        with ExitStack() as st:
            wd = self.sb(st, "tl_wd", [128, FT, D], BF16)
            wg = self.sb(st, "tl_wg", [128, KT, D], BF16)
            wp = self.sb(st, "tl_wp", [128, 2, D], BF16)
            bwd, bwg, bwp = [Buf() for _ in range(FT)], [Buf() for _ in range(KT)], Buf()
            for f in range(FT):
                S.dma("pool", wd[:, f, :], self.W["f_w_down"][i, f * 128:(f + 1) * 128, :], writes=[bwd[f]])
            for k in range(KT):
                S.dma("pool", wg[:, k, :], self.W["ple_w_gate"][i, k * 128:(k + 1) * 128, :], writes=[bwg[k]])
            for k in range(2):
                S.dma("pool", wp[:, k, :], self.W["ple_w_proj"][i, k * 128:(k + 1) * 128, :], writes=[bwp])
            ar = Ring(nc, st, "tl_a", [128, FT, TT], BF16, 2)
            hr = Ring(nc, st, "tl_h", [128, KT, TT], F32, 2)
            pr = Ring(nc, st, "tl_p", [128, 2, TT], BF16, 2)
            n3r = Ring(nc, st, "tl_n3", [128, KT, TT], BF16, 1)
            sgr = Ring(nc, st, "tl_sg", [128, TT], F32, 2)
            if last:
                orr = Ring(nc, st, "tl_o", [128, KT, TT], F32, 1)
            else:
                orr = Ring(nc, st, "tl_o", [128, KT, TT], BF16, 2)
            rings = self.norm_rings(st, "tl")
            def ld(j):
                a, ba = ar.next()
                S.dma("sp", a[:], self.tview(self.actT, j), writes=[ba])
                h, bh = hr.next()
                S.dma("sp", h[:], self.tview(self.hT, j), writes=[bh])
                p, bp = pr.next()
                S.dma("pool", p[:], self.tview(self.pT[i], j), writes=[bp])
                return a, ba, h, bh, p, bp
            nxt = ld(0)
            for j in range(NTT):
                a, ba, h, bh, p, bp = nxt
                if j + 1 < NTT:
                    nxt = ld(j + 1)
                for e in range(KT):
                    ps, bps = self.psA.next()
                    for f in range(FT):
                        S.op("pe", lambda en: en.matmul(ps[:, :], lhsT=wd[:, f, e * 128:(e + 1) * 128], rhs=a[:, f, :],
                                                        start=(f == 0), stop=(f == FT - 1)),
                             reads=[bwd[f], ba], writes=[bps], inc=(f == FT - 1), acc=True)
                    S.op("dve", lambda en: en.tensor_tensor(out=h[:, e, :], in0=ps[:, :], in1=h[:, e, :], op=ALU.add),
                         reads=[bps, bh], writes=[bh])
                n3, bn3 = n3r.next()
                self.rmsnorm(rings, h, bh, "norm_ple%d" % i, n3, bn3)
                for e in range(KT):
                    ps, bps = self.psA.next()
                    for k in range(KT):
                        S.op("pe", lambda en: en.matmul(ps[:, :], lhsT=wg[:, k, e * 128:(e + 1) * 128], rhs=n3[:, k, :],
                                                        start=(k == 0), stop=(k == KT - 1)),
                             reads=[bwg[k], bn3], writes=[bps], inc=(k == KT - 1), acc=True)
                    sg, bsg = sgr.next()
                    S.op("act", lambda en: en.activation(out=sg[:], in_=ps[:, :], func=AF.Sigmoid),
                         reads=[bps], writes=[bsg])
                    ps2, bps2 = self.psA.next()
                    for k in range(2):
                        S.op("pe", lambda en: en.matmul(ps2[:, :], lhsT=wp[:, k, e * 128:(e + 1) * 128], rhs=p[:, k, :],
                                                        start=(k == 0), stop=(k == 1)),
                             reads=[bwp, bp], writes=[bps2], inc=(k == 1), acc=True)
                    S.op("dve", lambda en: en.tensor_tensor(out=sg[:], in0=ps2[:, :], in1=sg[:], op=ALU.mult),
                         reads=[bps2, bsg], writes=[bsg])
                    S.op("dve", lambda en: en.tensor_tensor(out=h[:, e, :], in0=sg[:], in1=h[:, e, :], op=ALU.add),
                         reads=[bsg, bh], writes=[bh])
                o, bo = orr.next()
                if last:
                    self.rmsnorm(rings, h, bh, "norm_final", o, bo)
                    S.dma("sp", self.tview(self.yT, j), o[:], reads=[bo])
                else:
                    S.dma("sp", self.tview(self.hT, j), h[:], reads=[bh])
                    self.rmsnorm(rings, h, bh, "norm_mix%d" % inext, o, bo)
                    S.dma("sp", self.tview(self.hnT, j), o[:], reads=[bo])
            self.barrier()

    def mixer_a(self, i, j):
        nc, S = self.nc, self.S
        ygT, xcT, xcbT = self.s32[0], self.s32[1], self.s16[0]
        win = self.W["a_w_in"][j]
        with ExitStack() as st:
            hn = self.sb(st, "a1_hn", [128, KT, T], BF16)
            bhn = [Buf() for _ in range(KT)]
            for k in range(KT):
                S.dma("sp", hn[:, k, :], self.hnT[k, :, :], writes=[bhn[k]])
            wr = Ring(nc, st, "a1_w", [128, KT, 128], BF16, 4)
            stg = [self.sb(st, "a1_stg%d" % b, [128, 3 + T], F32) for b in range(2)]
            bstg = [[Buf() for _ in range(NTT + 1)] for b in range(2)]
            for b in range(2):
                S.op("pool", lambda e: e.memset(stg[b][:, 0:3], 0.0), writes=[bstg[b][0]])
            ygr = Ring(nc, st, "a1_yg", [128, TT], F32, 3)
            accr = Ring(nc, st, "a1_acc", [128, TT], F32, 3)
            xbr = Ring(nc, st, "a1_xb", [128, TT], BF16, 3)
            wnxt = self.load_w(wr, win, KT, 0, 128)
            for e in range(2 * KT):
                w, bw = wnxt
                if e + 1 < 2 * KT:
                    wnxt = self.load_w(wr, win, KT, (e + 1) * 128, 128)
                for jt in range(NTT):
                    ps, bps = self.psA.next()
                    for k in range(KT):
                        S.op("pe", lambda en: en.matmul(ps[:, :], lhsT=w[:, k, :], rhs=hn[:, k, jt * TT:(jt + 1) * TT],
                                                        start=(k == 0), stop=(k == KT - 1)),
                             reads=[bw, bhn[k]], writes=[bps], inc=(k == KT - 1), acc=True)
                    if e < KT:
                        yg, byg = ygr.next()
                        S.op("act", lambda en: en.activation(out=yg[:], in_=ps[:, :], func=AF.Gelu_apprx_tanh),
                             reads=[bps], writes=[byg])
                        S.dma("sp", ygT[e, :, jt * TT:(jt + 1) * TT], yg[:], reads=[byg])
                    else:
                        ft = e - KT
                        sg_t, sg_b = stg[ft % 2], bstg[ft % 2]
                        c0 = 3 + jt * TT
                        S.op("act", lambda en: en.activation(out=sg_t[:, c0:c0 + TT], in_=ps[:, :], func=AF.Copy),
                             reads=[bps], writes=[sg_b[jt + 1]])
                        acc, bacc = accr.next()
                        S.op("act", lambda en: en.activation(out=acc[:], in_=ps[:, :], func=AF.Identity,
                                                             scale=self.col("a_conv_w%d_3" % j, ft),
                                                             bias=self.col("a_conv_b%d" % j, ft)),
                             reads=[bps, self.bconst], writes=[bacc])
                        for kk in (2, 1, 0):
                            S.op("dve", lambda en: en.scalar_tensor_tensor(
                                out=acc[:], in0=sg_t[:, jt * TT + kk:jt * TT + kk + TT],
                                scalar=self.col("a_conv_w%d_%d" % (j, kk), ft), in1=acc[:], op0=ALU.mult, op1=ALU.add),
                                reads=[sg_b[jt], sg_b[jt + 1], bacc, self.bconst], writes=[bacc])
                        xb, bxb = xbr.next()
                        S.op("pool", lambda en: en.tensor_copy(out=xb[:], in_=acc[:]), reads=[bacc], writes=[bxb])
                        S.dma("sp", xcT[ft, :, jt * TT:(jt + 1) * TT], acc[:], reads=[bacc])
                        S.dma("sp", xcbT[ft, :, jt * TT:(jt + 1) * TT], xb[:], reads=[bxb])
            self.barrier()
        with ExitStack() as st:
            cc = self.sb(st, "a2_cc", [128, 5, KT], F32)
            bc = Buf()
            lam = self.col("a_lambda%d" % j, 0, KT)
            ev, l1, t2, mk, ccol = (cc[:, q, :] for q in range(5))
            S.op("act", lambda e: e.activation(out=ev, in_=lam, func=AF.Exp, scale=-1.0), reads=[self.bconst], writes=[bc])
            S.op("act", lambda e: e.activation(out=l1, in_=ev, func=AF.Ln, bias=self.one_col), reads=[bc, self.bconst], writes=[bc])
            S.op("dve", lambda e: e.tensor_scalar(out=t2, in0=ev, scalar1=1.0 / 3.0, scalar2=-0.5, op0=ALU.mult, op1=ALU.add), reads=[bc], writes=[bc])
            S.op("dve", lambda e: e.tensor_tensor(out=t2, in0=t2, in1=ev, op=ALU.mult), reads=[bc], writes=[bc])
            S.op("dve", lambda e: e.tensor_scalar(out=t2, in0=t2, scalar1=1.0, scalar2=None, op0=ALU.add), reads=[bc], writes=[bc])
            S.op("dve", lambda e: e.tensor_tensor(out=t2, in0=t2, in1=ev, op=ALU.mult), reads=[bc], writes=[bc])
            S.op("dve", lambda e: e.tensor_scalar(out=mk, in0=ev, scalar1=0.02, scalar2=None, op0=ALU.is_lt), reads=[bc], writes=[bc])
            S.op("dve", lambda e: e.tensor_tensor(out=t2, in0=t2, in1=l1, op=ALU.subtract), reads=[bc], writes=[bc])
            S.op("dve", lambda e: e.tensor_tensor(out=t2, in0=t2, in1=mk, op=ALU.mult), reads=[bc], writes=[bc])
            S.op("dve", lambda e: e.tensor_tensor(out=l1, in0=l1, in1=t2, op=ALU.add), reads=[bc], writes=[bc])
            S.op("dve", lambda e: e.tensor_scalar(out=ccol, in0=l1, scalar1=-LRU_C, scalar2=None, op0=ALU.mult), reads=[bc], writes=[bc])

            xbr = Ring(nc, st, "a2_xb", [128, 2, T], BF16, 2)
            gwr = Ring(nc, st, "a2_gw", [128, 2, 512], BF16, 2)
            afr = Ring(nc, st, "a2_af", [128, T], F32, 2)
            bfr = Ring(nc, st, "a2_bf", [128, T], F32, 2)
            hfr = Ring(nc, st, "a2_hf", [128, T], F32, 1)
            sqr_ = Ring(nc, st, "a2_sq", [128, T], F32, 1)
            ygr = Ring(nc, st, "a2_yg", [128, T], F32, 1)
            gor = Ring(nc, st, "a2_go", [128, T], BF16, 1)
            tr = Ring(nc, st, "a2_t", [128, TT], F32, 3)
            xcr = Ring(nc, st, "a2_xc", [128, TT], F32, 3)

            def ldh(hd):
                xb, bxb = xbr.next()
                for k in range(2):
                    S.dma("sp", xb[:, k, :], xcbT[2 * hd + k, :, :], writes=[bxb])
                gw, bgw = self.load_w(gwr, self.W["a_gate_w"][j, hd], 2, 0, 512)
                return xb, bxb, gw, bgw
            nxt = ldh(0)
            for hd in range(4):
                xb, bxb, gw, bgw = nxt
                if hd + 1 < 4:
                    nxt = ldh(hd + 1)
                for ft in range(2):
                    ftg = 2 * hd + ft
                    af, baf = afr.next()
                    bf, bbf = bfr.next()
                    yg, byg = ygr.next()
                    S.dma("sp", yg[:], ygT[ftg, :, :], writes=[byg])
                    for jt in range(NTT):
                        sl = slice(jt * TT, (jt + 1) * TT)
                        xc, bxc = xcr.next()
                        S.dma("sp", xc[:], xcT[ftg, :, sl], writes=[bxc])
                        psr, bpsr = self.psA.next()
                        psi, bpsi = self.psA.next()
                        for k in range(2):
                            S.op("pe", lambda en: en.matmul(psr[:, :], lhsT=gw[:, k, ft * 128:(ft + 1) * 128], rhs=xb[:, k, sl],
                                                            start=(k == 0), stop=(k == 1)),
                                 reads=[bgw, bxb], writes=[bpsr], inc=(k == 1), acc=True)
                        for k in range(2):
                            S.op("pe", lambda en: en.matmul(psi[:, :], lhsT=gw[:, k, 256 + ft * 128:256 + (ft + 1) * 128], rhs=xb[:, k, sl],
                                                            start=(k == 0), stop=(k == 1)),
                                 reads=[bgw, bxb], writes=[bpsi], inc=(k == 1), acc=True)
                        S.op("act", lambda en: en.activation(out=af[:, sl], in_=psr[:, :], func=AF.Sigmoid,
                                                             bias=self.col("a_gate_b%d" % j, ftg)),
                             reads=[bpsr, self.bconst], writes=[baf])
                        ig, big = tr.next()
                        S.op("act", lambda en: en.activation(out=ig[:], in_=psi[:, :], func=AF.Sigmoid,
                                                             bias=self.col("a_gate_b%d" % j, KT + ftg)),
                             reads=[bpsi, self.bconst], writes=[big])
                        S.op("dve", lambda en: en.tensor_tensor(out=bf[:, sl], in0=ig[:], in1=xc[:], op=ALU.mult),
                             reads=[big, bxc], writes=[bbf])
                    S.op("act", lambda en: en.activation(out=af[:], in_=af[:], func=AF.Exp, scale=cc[:, 4, ftg:ftg + 1]),
                         reads=[baf, bc], writes=[baf])
                    sq, bsq = sqr_.next()
                    S.op("act", lambda en: en.activation(out=sq[:], in_=af[:], func=AF.Square), reads=[baf], writes=[bsq])
                    S.op("act", lambda en: en.activation(out=sq[:], in_=sq[:], func=AF.Sqrt, scale=-1.0, bias=self.one_col),
                         reads=[bsq, self.bconst], writes=[bsq])
                    S.op("dve", lambda en: en.tensor_tensor(out=bf[:], in0=bf[:], in1=sq[:], op=ALU.mult), reads=[bbf, bsq], writes=[bbf])
                    hf, bhf = hfr.next()
                    S.op("dve", lambda en: en.tensor_tensor_scan(out=hf[:], data0=af[:], data1=bf[:], initial=0.0,
                                                                  op0=ALU.mult, op1=ALU.add),
                         reads=[baf, bbf], writes=[bhf])
                    go, bgo = gor.next()
                    S.op("pool", lambda en: en.tensor_tensor(out=go[:], in0=hf[:], in1=yg[:], op=ALU.mult),
                         reads=[bhf, byg], writes=[bgo])
                    S.dma("sp", self.gT[ftg, :, :], go[:], reads=[bgo])
            self.barrier()

    def _ring_guard(self, ring, tile):
        d = getattr(self, "_rg", None)
        if d is None:
            d = self._rg = {}
        b = d.get(id(tile))
        return [b] if b is not None else []

    def _ring_set(self, ring, tile, buf):
        if getattr(self, "_rg", None) is None:
            self._rg = {}
        self._rg[id(tile)] = buf


    def mixer_b(self, i):
        with ExitStack() as st:
            try:
                self._mixer_b_body(i, st)
            except _Cut:
                pass
            self.barrier()

    def _mixer_b_body(self, i, st):
        nc, S = self.nc, self.S
        wqkv, wqks = self.W["b_w_qkv"], self.W["b_w_qks"]
        banks = self.psA.items
        if True:
            cosT = self.sb(st, "b_cos", [128, T], BF16)
            sinT = self.sb(st, "b_sin", [128, T], BF16)
            btab = Buf()
            with ExitStack() as s2:
                posi = self.sb(s2, "b_posi", [128, T], I32)
                ang = self.sb(s2, "b_ang", [128, T], F32)
                kk = self.sb(s2, "b_kk", [128, T], F32)
                cst = self.sb(s2, "b_cst", [128, 2], F32)
                bt = Buf()
                S.dma("sp", posi[:], self.pos[0:1, :].to_broadcast([128, T]), writes=[bt])
                S.op("pool", lambda e: e.memset(cst[:, 0:1], float(np.pi / 2)), writes=[bt])
                S.op("dve", lambda e: e.tensor_copy(out=ang[:], in_=posi[:]), reads=[bt], writes=[bt])
                S.op("dve", lambda e: e.tensor_scalar(out=ang[:], in0=ang[:], scalar1=self.col("invf"), scalar2=None, op0=ALU.mult),
                     reads=[bt, self.bconst], writes=[bt])
                MAGIC = 12582912.0
                S.op("dve", lambda e: e.tensor_scalar(out=kk[:], in0=ang[:], scalar1=float(1.0 / (2 * np.pi)), scalar2=MAGIC,
                                                      op0=ALU.mult, op1=ALU.add), reads=[bt], writes=[bt])
                S.op("dve", lambda e: e.tensor_scalar(out=kk[:], in0=kk[:], scalar1=-MAGIC, scalar2=None, op0=ALU.add),
                     reads=[bt], writes=[bt])
                C1 = 6.28125
                C2 = float(2 * np.pi - C1)
                S.op("dve", lambda e: e.scalar_tensor_tensor(out=ang[:], in0=kk[:], scalar=-C1, in1=ang[:], op0=ALU.mult, op1=ALU.add),
                     reads=[bt], writes=[bt])
                S.op("dve", lambda e: e.scalar_tensor_tensor(out=ang[:], in0=kk[:], scalar=-C2, in1=ang[:], op0=ALU.mult, op1=ALU.add),
                     reads=[bt], writes=[bt])
                S.op("dve", lambda e: e.tensor_scalar(out=ang[:], in0=ang[:], scalar1=float(np.pi), scalar2=float(-np.pi),
                                                      op0=ALU.min, op1=ALU.max), reads=[bt], writes=[bt])
                S.op("act", lambda e: e.activation(out=sinT[:], in_=ang[:], func=AF.Sin, scale=self.col("rsgn")),
                     reads=[bt, self.bconst], writes=[btab])
                S.op("dve", lambda e: e.scalar_tensor_tensor(out=kk[:], in0=ang[:], scalar=-1.0, in1=ang[:], op0=ALU.mult, op1=ALU.max), reads=[bt], writes=[bt])
                S.op("act", lambda e: e.activation(out=cosT[:], in_=kk[:], func=AF.Sin, scale=-1.0, bias=cst[:, 0:1]),
                     reads=[bt], writes=[btab])
                self.barrier()
            self.cut("cutA")
            self.dump("cos", cosT[:], [btab], [128, T], BF16)
            self.dump("sin", sinT[:], [btab], [128, T], BF16)
            hn = self.sb(st, "b_hn", [128, KT, T], BF16)
            bhn = [Buf() for _ in range(KT)]
            for k in range(KT):
                S.dma("sp", hn[:, k, :], self.hnT[k, :, :], writes=[bhn[k]])
            band = self.sb(st, "b_band", [128, 2, 256], BF16)
            bband = Buf()
            S.op("pool", lambda e: e.memset(band[:], 1.0), writes=[bband])
            S.op("pool", lambda e: e.affine_select(out=band[:], in_=band[:], pattern=[[0, 2], [1, 256]], compare_op=ALU.is_ge,
                                                   fill=0.0, base=0, channel_multiplier=-1), reads=[bband], writes=[bband])
            S.op("pool", lambda e: e.affine_select(out=band[:], in_=band[:], pattern=[[0, 2], [-1, 256]], compare_op=ALU.is_ge,
                                                   fill=0.0, base=128, channel_multiplier=1), reads=[bband], writes=[bband])
            wr = Ring(nc, st, "b_w", [128, KT, 128], BF16, 4)
            wvr = Ring(nc, st, "b_wv", [128, KT, 128], BF16, 2)
            qr = Ring(nc, st, "b_q", [128, T], BF16, 2)
            kr = Ring(nc, st, "b_k", [128, T], BF16, 2)
            vr = Ring(nc, st, "b_v", [128, 32, 128], BF16, 2)
            accden = self.sb(st, "b_accden", [128, 2, T], F32)
            bacc = Buf()
            tr = Ring(nc, st, "b_t", [128, TT], F32, 4)
            er = Ring(nc, st, "b_e", [128, 2, 256], BF16, 3)
            outr = Ring(nc, st, "b_o", [128, TT], BF16, 2)
            psP = SubRing(banks[0:2])
            psS = SubRing([(self.psbig[q][:, :].rearrange("p (a b) -> p a b", a=2), Buf(x=True)) for q in (1, 2)])
            psOD = SubRing([(banks[q][0].rearrange("p (a b) -> p a b", a=2), Buf(x=True)) for q in (6, 7)])

            def proj_rot(col0, dst, bdst, d):
                w, bw = self.load_w(wr, wqkv, KT, col0, 128)
                ws, bws = self.load_w(wr, wqks, KT, col0, 128)
                for jt in range(NTT):
                    sl = slice(jt * TT, (jt + 1) * TT)
                    ps, bps = psP.next()
                    ps2, bps2 = psP.next()
                    for (pp, bpp, ww, bww) in ((ps, bps, w, bw), (ps2, bps2, ws, bws)):
                        for k in range(KT):
                            S.op("pe", lambda en: en.matmul(pp[:, :], lhsT=ww[:, k, :], rhs=hn[:, k, sl],
                                                            start=(k == 0), stop=(k == KT - 1)),
                                 reads=[bww] + bhn, writes=[bpp], inc=(k == KT - 1), acc=True)
                    t1, bt1 = tr.next()
                    t2, bt2 = tr.next()
                    S.op("dve", lambda en: en.tensor_tensor(out=t1[:], in0=ps[:, :], in1=cosT[:, sl], op=ALU.mult),
                         reads=[bps, btab], writes=[bt1])
                    S.op("dve", lambda en: en.tensor_tensor(out=t2[:], in0=ps2[:, :], in1=sinT[:, sl], op=ALU.mult),
                         reads=[bps2, btab], writes=[bt2])
                    n = TT // d
                    dv = dst[:].rearrange("p (r l) -> p r l", r=d)[:, :, jt * n:(jt + 1) * n]
                    v1 = t1[:].rearrange("p (j r) -> p r j", r=d)
                    v2 = t2[:].rearrange("p (j r) -> p r j", r=d)
                    S.op("pool", lambda en: en.tensor_tensor(out=dv, in0=v1, in1=v2, op=ALU.add),
                         reads=[bt1, bt2], writes=[bdst])

            for hp in range(KT):
                wv, bwv = self.load_w(wvr, wqkv, KT, 6 * D + hp * 128, 128)
                S.op("pool", lambda e: e.memset(accden[:], 0.0), writes=[bacc])
                for g, d in enumerate((1, 4, 16)):
                    L = T // d
                    nb = L // 128
                    qT, bq = qr.next()
                    kT, bk = kr.next()
                    proj_rot(g * D + hp * 128, qT, bq, d)
                    proj_rot(3 * D + g * D + hp * 128, kT, bk, d)
                    self.cut("cutB")
                    if hp == 0:
                        self.dump("q%d" % g, qT[:], [bq], [128, T], BF16)
                        self.dump("k%d" % g, kT[:], [bk], [128, T], BF16)
                    vt, bvt = vr.next()
                    for n0 in range(0, 32, 4):
                        ps, bps = psP.next()
                        for q4 in range(4):
                            n = n0 + q4
                            r, kb = n // nb, n % nb
                            start = kb * 128 * d + r
                            for k in range(KT):
                                S.op("pe", lambda en: en.matmul(ps[:, q4 * 128:(q4 + 1) * 128],
                                                                lhsT=hn[:, k, start:start + 127 * d + 1:d], rhs=wv[:, k, :],
                                                                start=(k == 0), stop=(k == KT - 1)),
                                     reads=[bwv] + bhn, writes=[bps], inc=(k == KT - 1 and q4 == 3), acc=True)
                        S.op("act", lambda en: en.activation(out=vt[:, n0:n0 + 4, :], in_=ps[:, :].rearrange("p (a b) -> p a b", a=4),
                                                             func=AF.Copy), reads=[bps], writes=[bvt])
                    self.cut("cutC")
                    steps = [(r, kb) for r in range(d) for kb in range(nb)]

                    def stage1(r, kb):
                        kcol = r * L + kb * 128
                        nq = 256 if kb + 1 < nb else 128
                        pS, bpS = psS.next()
                        for hh in range(2):
                            S.op("pe", lambda en: en.matmul(pS[:, hh, 0:nq],
                                                            lhsT=kT[64 * hh:64 * hh + 64, kcol:kcol + 128],
                                                            rhs=qT[64 * hh:64 * hh + 64, kcol:kcol + nq],
                                                            start=True, stop=True),
                                 reads=[bk, bq], writes=[bpS], inc=(hh == 1), acc=True)
                        E, bE = er.next()
                        S.op("act", lambda en: en.activation(out=E[:, :, 0:nq], in_=pS[:, :, 0:nq],
                                                             func=AF.Exp, scale=0.125), reads=[bpS], writes=[bE])
                        meng = "dve" if (kb % 2 == 0) else "pool"
                        S.op(meng, lambda en: en.tensor_tensor(out=E[:, :, 0:nq], in0=E[:, :, 0:nq], in1=band[:, :, 0:nq], op=ALU.mult),
                             reads=[bE, bband], writes=[bE])
                        return E, bE

                    stt = {"pod": None, "bpod": None, "npod": None, "nbpod": None, "fresh": None, "nfresh": None}

                    def stage2(r, kb, E, bE):
                        n = r * nb + kb
                        if kb == 0:
                            stt["pod"], stt["bpod"] = psOD.next()
                            stt["fresh"] = [True, True]
                        halves = [(kb, 0)]
                        if kb + 1 < nb:
                            halves.append((kb + 1, 1))
                        for (qb, hf) in halves:
                            if qb % 2 == 0 and hf == 1:
                                stt["npod"], stt["nbpod"] = psOD.next()
                                stt["nfresh"] = [True, True]
                                tp, tbp, fr = stt["npod"], stt["nbpod"], stt["nfresh"]
                            else:
                                tp, tbp, fr = stt["pod"], stt["bpod"], stt["fresh"]
                            c0 = (qb % 2) * 128
                            for hh in range(2):
                                for od in range(2):
                                    lh = vt[:, n, 64 * hh:64 * hh + 64] if od == 0 else self.ones_bf[:, 0:64]
                                    st_ = fr[hh]
                                    fr[hh] = False
                                    S.op("pe", lambda en: en.matmul(tp[64 * hh:64 * hh + 64, od, c0:c0 + 128], lhsT=lh,
                                                                    rhs=E[:, hh, hf * 128:(hf + 1) * 128], start=st_, stop=True,
                                                                    skip_group_check=True),
                                         reads=[bvt, bE, self.bconst], writes=[tbp], inc=(hh == 1 and od == 1), acc=True)
                        if kb % 2 == 1 or kb == nb - 1:
                            qb0 = (kb // 2) * 2
                            ncols = (kb - qb0 + 1) * 128
                            t0 = qb0 * 128 * d + r
                            asl = accden[:, :, t0:t0 + (ncols - 1) * d + 1:d]
                            pod, bpod = stt["pod"], stt["bpod"]
                            S.op("dve", lambda en: en.tensor_tensor(out=asl, in0=pod[:, :, 0:ncols], in1=asl, op=ALU.add),
                                 reads=[bpod, bacc], writes=[bacc])
                            if kb + 1 < nb:
                                stt["pod"], stt["bpod"], stt["fresh"] = stt["npod"], stt["nbpod"], stt["nfresh"]

                    cur = stage1(*steps[0])
                    for si, (r, kb) in enumerate(steps):
                        nxt_ = stage1(*steps[si + 1]) if si + 1 < len(steps) else None
                        stage2(r, kb, *cur)
                        cur = nxt_
                    self.cut("cutD%d" % g)
                if hp == 0:
                    self.dump("acc", accden[:, 0, :], [bacc], [128, T], F32)
                    self.dump("den", accden[:, 1, :], [bacc], [128, T], F32)
                    self.dump("vt", vt[:], [bvt], [128, 32, 128], BF16)
                for jt in range(NTT):
                    sl = slice(jt * TT, (jt + 1) * TT)
                    rc, brc = tr.next()
                    S.op("dve", lambda en: en.reciprocal(out=rc[:], in_=accden[:, 1, sl]), reads=[bacc], writes=[brc])
                    o, bo = outr.next()
                    S.op("dve", lambda en: en.tensor_tensor(out=o[:], in0=accden[:, 0, sl], in1=rc[:], op=ALU.mult),
                         reads=[bacc, brc], writes=[bo])
                    S.dma("sp", self.gT[hp, :, sl], o[:], reads=[bo])
            self.barrier()


    CH = 128
    NCH = T // 128
    LAM = float(np.exp(-0.5))

    def mixer_c(self, i):
        if not hasattr(self, "c_AR"):
            dt = self.nc.dram_tensor
            self.c_AR = dt("c_AR", [KT, 128, 2 * T], BF16, kind="Internal").ap()
            self.c_vtok = dt("c_vtok", [T // 128, 128, D], BF16, kind="Internal").ap()
            self.c_gc = dt("c_gc", [KT, 128, T // 128], F32, kind="Internal").ap()
        self.mixer_c1(i)
        self.mixer_c2(i)

    def mixer_c2(self, i):
        nc, S = self.nc, self.S
        BTd, KTd, gD, bonD = self.s16[1], self.s16[2], self.s16[3], self.s32[0]
        NCH = self.NCH
        with ExitStack() as st:
            bm = Buf()
            MSK = self.sb(st, "c2_msk", [128, 2, 4, 128], BF16)
            LM = self.sb(st, "c2_lm", [128, 2, 128], BF16)
            IDN = self.sb(st, "c2_idn", [128, 2, 128], BF16)
            bonesf = self.sb(st, "c2_bones", [128, 128], F32)
            S.op("pool", lambda e: e.memset(MSK[:], 1.0), writes=[bm])
            for par in range(2):
                S.op("pool", lambda e: e.affine_select(out=MSK[:, :, par::2, :], in_=MSK[:, :, par::2, :],
                                                       pattern=[[0, 2], [0, 2], [1, 128]], compare_op=ALU.is_ge, fill=0.0,
                                                       base=par - 1, channel_multiplier=-1), reads=[bm], writes=[bm])
            S.op("pool", lambda e: e.memset(LM[:], 1.0), writes=[bm])
            S.op("pool", lambda e: e.affine_select(out=LM[:], in_=LM[:], pattern=[[0, 2], [-1, 128]], compare_op=ALU.is_ge, fill=0.0,
                                                   base=-1, channel_multiplier=1), reads=[bm], writes=[bm])
            S.op("pool", lambda e: e.memset(IDN[:], 1.0), writes=[bm])
            S.op("pool", lambda e: e.affine_select(out=IDN[:], in_=IDN[:], pattern=[[0, 2], [-1, 128]], compare_op=ALU.is_equal, fill=0.0,
                                                   base=0, channel_multiplier=1), reads=[bm], writes=[bm])
            S.op("pool", lambda e: e.memset(bonesf[:], 0.0), writes=[bm])
            S.op("pool", lambda e: e.memset(bonesf[0:64, 0:64], 1.0 / 64), writes=[bm])
            S.op("pool", lambda e: e.memset(bonesf[64:128, 64:128], 1.0 / 64), writes=[bm])
            ident = IDN[:, 0, :]
            arr = Ring(nc, st, "c2_ar", [128, NCH, 2, 128], BF16, 2)
            btr = Ring(nc, st, "c2_bt", [128, T], BF16, 2)
            ktr = Ring(nc, st, "c2_kt", [128, T], BF16, 2)
            vtr = Ring(nc, st, "c2_vt", [128, NCH, 128], BF16, 2)
            gcr = Ring(nc, st, "c2_gc", [128, NCH], F32, 2)
            scr = Ring(nc, st, "c2_sc", [128, 2, 4, 128], BF16, 2)
            lr_ = Ring(nc, st, "c2_l", [128, 2, 128], BF16, 3)
            mr_ = Ring(nc, st, "c2_m", [128, 2, 128], BF16, 3)
            ttr = Ring(nc, st, "c2_tt", [128, 2, 128], BF16, 3)
            tokr = Ring(nc, st, "c2_tok", [128, 2, 128], BF16, 2)
            wur = Ring(nc, st, "c2_wu", [128, 64], BF16, 6)
            Pst = self.sb(st, "c2_pst", [128, 64], F32)
            PG = self.sb(st, "c2_pg", [128, 64], F32)
            Pbf = self.sb(st, "c2_pbf", [128, 64], BF16)
            bP = [Buf(), Buf()]
            bPb = [Buf(), Buf()]
            bPG = [Buf(), Buf()]
            ysr = Ring(nc, st, "c2_ys", [128, TT], F32, 2)
            gtr = Ring(nc, st, "c2_gt", [128, TT], F32, 8)
            bonr = Ring(nc, st, "c2_bon", [128, TT], F32, 2)
            ggr = Ring(nc, st, "c2_gg", [128, TT], BF16, 2)
            outr = Ring(nc, st, "c2_out", [128, TT], BF16, 2)
            scA = self.psbig[0][:, :].rearrange("p (a b) -> p a b", a=2)
            bscA = Buf(x=True)
            scB = self.psbig[1][:, :].rearrange("p (a b) -> p a b", a=2)
            bscB = Buf(x=True)
            trp = self.psbig[1][:, 256:512].bitcast(BF16).rearrange("p (a b) -> p a b", a=4)
            btrp = bscB
            mlp = self.psbig[2][:, 0:512].rearrange("p (a b c) -> p a b c", a=2, b=2)
            bmlp = Buf(x=True)
            ttp = self.psbig[2][:, 512:768].rearrange("p (a b) -> p a b", a=2)
            bttp = Buf(x=True)
            sq = self.psbig[3][:, :].rearrange("p (a b) -> p a b", a=2)
            bsq = [Buf(x=True), Buf(x=True)]

            def ldt(e_):
                ar, bar = arr.next()
                S.dma("sp", ar[:].rearrange("p a b c -> p (a b c)"), self.c_AR[e_, :, :], writes=[bar])
                bt_, bbt = btr.next()
                S.dma("sp", bt_[:], BTd[e_, :, :], writes=[bbt])
                kt_, bkt = ktr.next()
                S.dma("sp", kt_[:], KTd[e_, :, :], writes=[bkt])
                vt, bvt = vtr.next()
                S.dma("sp", vt[:], self.c_vtok.rearrange("c p f -> p c f")[:, :, e_ * 128:(e_ + 1) * 128], writes=[bvt])
                gc, bgc = gcr.next()
                S.dma("sp", gc[:], self.c_gc[e_, :, :], writes=[bgc])
                return ar, bar, bt_, bbt, kt_, bkt, vt, bvt, gc, bgc
            nxt = ldt(0)
            for e_ in range(KT):
                ar, bar, bt_, bbt, kt_, bkt, vt, bvt, gc, bgc = nxt
                if e_ + 1 < KT:
                    nxt = ldt(e_ + 1)
                for hh in range(2):
                    P = slice(64 * hh, 64 * hh + 64)
                    S.op("pool", lambda e: e.memset(Pst[P, :], 0.0), writes=[bP[hh]])
                    S.op("pool", lambda e: e.memset(Pbf[P, :], 0.0), writes=[bPb[hh]])
                ys = bys = None
                for c in range(NCH):
                    cs_ = slice(c * 128, (c + 1) * 128)
                    for hh in range(2):
                        P = slice(64 * hh, 64 * hh + 64)
                        S.op("pe", lambda e: e.matmul(scA[:, hh, 0:256], lhsT=bt_[P, cs_], rhs=ar[P, c, :, :].rearrange("p a b -> p (a b)"),
                                                      start=True, stop=True), reads=[bbt, bar], writes=[bscA], inc=False, acc=True)
                    for hh in range(2):
                        P = slice(64 * hh, 64 * hh + 64)
                        S.op("pe", lambda e: e.matmul(scA[:, hh, 256:512], lhsT=kt_[P, cs_], rhs=ar[P, c, :, :].rearrange("p a b -> p (a b)"),
                                                      start=True, stop=True), reads=[bkt, bar], writes=[bscA], inc=(hh == 1), acc=True)
                    for hh in range(2):
                        P = slice(64 * hh, 64 * hh + 64)
                        S.op("pe", lambda e: e.matmul(scB[:, hh, 0:128], lhsT=ar[P, c, 0, :], rhs=bt_[P, cs_],
                                                      start=True, stop=True), reads=[bbt, bar], writes=[bscB], inc=(hh == 1), acc=True)
                    S.op("pe", lambda e: e.transpose(trp[:, 0, :], bt_[:, cs_], ident), reads=[bbt, bm], writes=[btrp], inc=False, acc=True)
                    S.op("pe", lambda e: e.transpose(trp[:, 1, :], kt_[:, cs_], ident), reads=[bkt, bm], writes=[btrp], acc=True)
                    SC, bSC = scr.next()
                    S.op("dve", lambda e: e.tensor_tensor(out=SC[:].rearrange("p a b c -> p a (b c)"), in0=scA[:, :, :],
                                                          in1=MSK[:].rearrange("p a b c -> p a (b c)"), op=ALU.mult),
                         reads=[bscA, bm], writes=[bSC])
                    Lt, bLt = lr_.next()
                    S.op("dve", lambda e: e.tensor_tensor(out=Lt[:], in0=scB[:, :, 0:128], in1=LM[:], op=ALU.mult),
                         reads=[bscB, bm], writes=[bLt])
                    tok, btok = tokr.next()
                    S.op("act", lambda e: e.activation(out=tok[:], in_=trp[:, 0:2, :], func=AF.Copy), reads=[btrp], writes=[btok])
                    TTc, bTT = ttr.next()
                    S.op("pool", lambda e: e.tensor_tensor(out=TTc[:], in0=SC[:, :, 0, :], in1=IDN[:], op=ALU.add), reads=[bSC, bm], writes=[bTT])
                    Mt, bMt = None, bSC
                    for lev in range(1, 7):
                        for hh in range(2):
                            Mop = SC[:, hh, 0, :] if lev == 1 else Mt[:, hh, :]
                            if lev < 6:
                                S.op("pe", lambda e: e.matmul(mlp[:, hh, 0, :], lhsT=Lt[:, hh, :], rhs=Mop, start=True, stop=True),
                                     reads=[bLt, bMt], writes=[bmlp], inc=False, acc=True)
                            S.op("pe", lambda e: e.matmul(mlp[:, hh, 1, :], lhsT=Mop, rhs=Lt[:, hh, :], start=True, stop=True),
                                 reads=[bLt, bMt], writes=[bmlp], inc=(hh == 1), acc=True)
                        Ln, bLn = lr_.next()
                        S.op("act", lambda e: e.activation(out=Ln[:], in_=mlp[:, :, 1, :], func=AF.Copy), reads=[bmlp], writes=[bLn])
                        if lev < 6:
                            Mn, bMn = mr_.next()
                            S.op("dve", lambda e: e.tensor_copy(out=Mn[:], in_=mlp[:, :, 0, :]), reads=[bmlp], writes=[bMn])
                        else:
                            Mn, bMn = None, None
                        for hh in range(2):
                            S.op("pe", lambda e: e.matmul(ttp[:, hh, :], lhsT=Ln[:, hh, :], rhs=TTc[:, hh, :], start=True, stop=True),
                                 reads=[bLn, bTT], writes=[bttp], inc=(hh == 1), acc=True)
                        TTn, bTTn = ttr.next()
                        S.op("dve", lambda e: e.tensor_tensor(out=TTn[:], in0=ttp[:, :, :], in1=TTc[:], op=ALU.add), reads=[bttp, bTT], writes=[bTTn])
                        Lt, bLt, Mt, bMt, TTc, bTT = Ln, bLn, Mn, bMn, TTn, bTTn
                    if c % 4 == 0:
                        ys, bys = ysr.next()
                    for hh in range(2):
                        P = slice(64 * hh, 64 * hh + 64)
                        vh = vt[:, c, 64 * hh:64 * hh + 64]
                        S.op("pe", lambda e: e.matmul(sq[:, hh, 0:64], lhsT=SC[:, hh, 2, :], rhs=vh, start=True, stop=False),
                             reads=[bSC, bvt], writes=[bsq[hh]], inc=False, acc=True)
                        S.op("pe", lambda e: e.matmul(sq[:, hh, 0:64], lhsT=ar[P, c, 0, :], rhs=Pbf[P, :], start=False, stop=True),
                             reads=[bar, bPb[hh]], writes=[bsq[hh]], acc=True)
                        Wsb, bW = wur.next()
                        S.op("act", lambda e: e.activation(out=Wsb[:], in_=sq[:, hh, 0:64], func=AF.Copy), reads=[bsq[hh]], writes=[bW])
                        S.op("pe", lambda e: e.matmul(sq[:, hh, 64:128], lhsT=TTc[:, hh, :], rhs=Wsb[:], start=True, stop=True),
                             reads=[bTT, bW], writes=[bsq[hh]], acc=True)
                        Usb, bU = wur.next()
                        S.op("act", lambda e: e.activation(out=Usb[:], in_=sq[:, hh, 64:128], func=AF.Copy), reads=[bsq[hh]], writes=[bU])
                        S.op("pe", lambda e: e.matmul(sq[P, hh, 128:256], lhsT=vh, rhs=SC[:, hh, 3, :], start=True, stop=False),
                             reads=[bvt, bSC], writes=[bsq[hh]], inc=False, acc=True)
                        S.op("pe", lambda e: e.matmul(sq[P, hh, 128:256], lhsT=Usb[:], rhs=SC[:, hh, 1, :], start=False, stop=False),
                             reads=[bU, bSC], writes=[bsq[hh]], inc=False, acc=True)
                        S.op("pe", lambda e: e.matmul(sq[P, hh, 128:256], lhsT=Pbf[P, :], rhs=ar[P, c, 1, :], start=False, stop=True),
                             reads=[bPb[hh], bar], writes=[bsq[hh]], acc=True)
                        S.op("act", lambda e: e.activation(out=ys[P, (c % 4) * 128:(c % 4 + 1) * 128], in_=sq[P, hh, 128:256], func=AF.Copy),
                             reads=[bsq[hh]], writes=[bys])
                        S.op("dve", lambda e: e.tensor_scalar(out=PG[P, :], in0=Pst[P, :], scalar1=gc[P, c:c + 1], scalar2=None, op0=ALU.mult),
                             reads=[bP[hh], bgc], writes=[bPG[hh]])
                        S.op("pe", lambda e: e.matmul(sq[P, hh, 256:320], lhsT=tok[:, 0, 64 * hh:64 * hh + 64], rhs=Usb[:], start=True, stop=False),
                             reads=[btok, bU], writes=[bsq[hh]], inc=False, acc=True)
                        S.op("pe", lambda e: e.matmul(sq[P, hh, 256:320], lhsT=tok[:, 1, 64 * hh:64 * hh + 64], rhs=vh, start=False, stop=True),
                             reads=[btok, bvt], writes=[bsq[hh]], acc=True)
                        S.op("dve", lambda e: e.scalar_tensor_tensor(out=Pbf[P, :], in0=sq[P, hh, 256:320], scalar=gc[P, c:c + 1], in1=PG[P, :],
                                                                     op0=ALU.mult, op1=ALU.add),
                             reads=[bsq[hh], bgc, bPG[hh]], writes=[bPb[hh]])
                        S.op("dve", lambda e: e.scalar_tensor_tensor(out=Pst[P, :], in0=sq[P, hh, 256:320], scalar=gc[P, c:c + 1], in1=PG[P, :],
                                                                     op0=ALU.mult, op1=ALU.add),
                             reads=[bsq[hh], bgc, bPG[hh]], writes=[bP[hh]])
                    if c % 4 == 3:
                        jt = c // 4
                        sl = slice(jt * TT, (jt + 1) * TT)
                        bon, bbon = bonr.next()
                        S.dma("sp", bon[:], bonD[e_, :, sl], writes=[bbon])
                        gg, bgg = ggr.next()
                        S.dma("sp", gg[:], gD[e_, :, sl], writes=[bgg])
                        mean_ps, ex2_ps = scA[:, 0, :], scA[:, 1, :]
                        ysq, bysq = gtr.next()
                        S.op("act", lambda e: e.activation(out=ysq[:], in_=ys[:], func=AF.Square), reads=[bys], writes=[bysq])
                        S.op("pe", lambda e: e.matmul(mean_ps, lhsT=bonesf[:, :], rhs=ys[:], start=True, stop=True),
                             reads=[bm, bys], writes=[bscA], inc=False, acc=True)
                        S.op("pe", lambda e: e.matmul(ex2_ps, lhsT=bonesf[:, :], rhs=ysq[:], start=True, stop=True),
                             reads=[bm, bysq], writes=[bscA], acc=True)
                        msq, bmsq = gtr.next()
                        S.op("act", lambda e: e.activation(out=msq[:], in_=mean_ps, func=AF.Square), reads=[bscA], writes=[bmsq])
                        S.op("dve", lambda e: e.tensor_tensor(out=msq[:], in0=ex2_ps, in1=msq[:], op=ALU.subtract), reads=[bscA, bmsq], writes=[bmsq])
                        S.op("dve", lambda e: e.tensor_scalar(out=msq[:], in0=msq[:], scalar1=0.0, scalar2=None, op0=ALU.max), reads=[bmsq], writes=[bmsq])
                        S.op("act", lambda e: e.activation(out=msq[:], in_=msq[:], func=AF.Sqrt, bias=self.gneps_col), reads=[bmsq, self.bconst], writes=[bmsq])
                        msq_in, bmsq_in = msq, bmsq
                        msq, bmsq = gtr.next()
                        S.op("dve", lambda e: e.reciprocal(out=msq[:], in_=msq_in[:]), reads=[bmsq_in], writes=[bmsq])
                        yc, byc = gtr.next()
                        S.op("dve", lambda e: e.tensor_tensor(out=yc[:], in0=mean_ps, in1=ys[:], op=ALU.subtract), reads=[bscA, bys], writes=[byc])
                        S.op("dve", lambda e: e.tensor_tensor(out=yc[:], in0=yc[:], in1=msq[:], op=ALU.mult), reads=[byc, bmsq], writes=[byc])
                        S.op("dve", lambda e: e.tensor_scalar(out=yc[:], in0=yc[:], scalar1=-1.0, scalar2=self.col("c_ln_w", e_), op0=ALU.mult, op1=ALU.mult),
                             reads=[byc, self.bconst], writes=[byc])
                        S.op("dve", lambda e: e.scalar_tensor_tensor(out=yc[:], in0=yc[:], scalar=self.col("c_ln_b", e_), in1=bon[:], op0=ALU.add, op1=ALU.add),
                             reads=[byc, bbon, self.bconst], writes=[byc])
                        o, bo = outr.next()
                        S.op("dve", lambda e: e.tensor_tensor(out=o[:], in0=yc[:], in1=gg[:], op=ALU.mult), reads=[byc, bgg], writes=[bo])
                        S.dma("sp", self.gT[e_, :, sl], o[:], reads=[bo])
            self.barrier()

    def mixer_c1(self, i):
        nc, S = self.nc, self.S
        W = self.W
        LAM = self.LAM
        BTd, KTd, gD, bonD = self.s16[1], self.s16[2], self.s16[3], self.s32[0]
        with ExitStack() as st:
            wbuf = Buf()
            wrkv = [self.sb(st, "c_wrkv%d" % c, [128, KT, D], BF16) for c in range(3)]
            for c in range(3):
                for k in range(KT):
                    S.dma("pool", wrkv[c][:, k, :], W["c_w_rkv"][c, k * 128:(k + 1) * 128, :], writes=[wbuf])
            w1 = self.sb(st, "c_w1", [128, KT, 64], BF16)
            a1 = self.sb(st, "c_a1", [128, KT, 64], BF16)
            g1 = self.sb(st, "c_g1", [128, KT, 128], BF16)
            w2 = self.sb(st, "c_w2", [128, D], BF16)
            a2 = self.sb(st, "c_a2", [128, D], BF16)
            g2 = self.sb(st, "c_g2", [128, D], BF16)
            S.dma("pool", w1[:], W["c_w1"].rearrange("(k p) e -> p k e", p=128), writes=[wbuf])
            S.dma("pool", a1[:], W["c_a1"].rearrange("(k p) e -> p k e", p=128), writes=[wbuf])
            S.dma("pool", g1[:], W["c_g1"].rearrange("(k p) e -> p k e", p=128), writes=[wbuf])
            S.dma("pool", w2[0:64, :], W["c_w2"][:, :], writes=[wbuf])
            S.dma("pool", a2[0:64, :], W["c_a2"][:, :], writes=[wbuf])
            S.dma("pool", g2[:, :], W["c_g2"][:, :], writes=[wbuf])
            cm01 = self.sb(st, "c_cm01", [128, TT], F32)
            bones = self.sb(st, "c_bones", [128, 128], BF16)
            bm = Buf()
            S.op("pool", lambda e: e.memset(cm01[:], 1.0), writes=[bm])
            for q in range(4):
                S.op("pool", lambda e: e.memset(cm01[:, q * 128:q * 128 + 1], 0.0), writes=[bm])
            S.op("pool", lambda e: e.memset(bones[:], 0.0), writes=[bm])
            S.op("pool", lambda e: e.memset(bones[0:64, 0:64], 1.0), writes=[bm])
            S.op("pool", lambda e: e.memset(bones[64:128, 64:128], 1.0), writes=[bm])
            hr = Ring(nc, st, "c_hn", [128, KT, TT + 1], BF16, 2)
            dd = self.sb(st, "c_d", [128, KT, TT], F32)
            bdd = Buf()
            xm = [self.sb(st, "c_xm%d" % c, [128, KT, TT], BF16) for c in range(6)]
            bxm = [Buf() for _ in range(6)]
            lor = [self.sb(st, "c_lor%d" % c, [128, TT], BF16) for c in range(3)]
            blor = [Buf() for _ in range(3)]
            trA = Ring(nc, st, "c_tA", [128, TT], F32, 12)
            trB = Ring(nc, st, "c_tB", [128, TT], F32, 8)
            brA = Ring(nc, st, "c_bA", [128, TT], BF16, 4)
            brB = Ring(nc, st, "c_bB", [128, TT], BF16, 6)
            psF = SubRing(self.psA.items[0:5])
            psB = SubRing(self.psA.items[5:8])
            arr = Ring(nc, st, "c_ar", [128, 4, 2, 128], BF16, 2)
            vtr = Ring(nc, st, "c_vt", [128, D], BF16, 2)
            gct = self.sb(st, "c_gct", [128, KT, T // 128], F32)
            bgct = Buf()
            P_ = self.psA

            def ldh(jt):
                h, bh = hr.next()
                if jt == 0:
                    S.op("pool", lambda e: e.memset(h[:, :, 0:1], 0.0), writes=[bh])
                    S.dma("sp", h[:, :, 1:TT + 1], self.tview(self.hnT, 0), writes=[bh])
                else:
                    S.dma("sp", h[:, :, :], self.hnT.rearrange("k p t -> p k t")[:, :, jt * TT - 1:(jt + 1) * TT], writes=[bh])
                return h, bh
            nxt = ldh(0)
            for jt in range(NTT):
                sl = slice(jt * TT, (jt + 1) * TT)
                h, bh = nxt
                if jt + 1 < NTT:
                    nxt = ldh(jt + 1)
                for k in range(KT):
                    S.op("dve", lambda e: e.tensor_tensor(out=dd[:, k, :], in0=h[:, k, 0:TT], in1=h[:, k, 1:TT + 1], op=ALU.subtract),
                         reads=[bh], writes=[bdd])
                for c in (3, 4, 5, 2, 0, 1):
                    for k in range(KT):
                        S.op("dve", lambda e: e.scalar_tensor_tensor(out=xm[c][:, k, :], in0=dd[:, k, :], scalar=self.col("c_mu%d" % c, k),
                                                                     in1=h[:, k, 1:TT + 1], op0=ALU.mult, op1=ALU.add),
                             reads=[bdd, bh, self.bconst], writes=[bxm[c]])
                for li, (wt, c, fn, m) in enumerate(((w1, 3, AF.Tanh, 64), (a1, 4, AF.Copy, 64), (g1, 5, AF.Sigmoid, 128))):
                    ps, bps = P_.next()
                    for k in range(KT):
                        S.op("pe", lambda e: e.matmul(ps[0:m, :], lhsT=wt[:, k, :], rhs=xm[c][:, k, :], start=(k == 0), stop=(k == KT - 1)),
                             reads=[wbuf, bxm[c]], writes=[bps], inc=(k == KT - 1), acc=True)
                    S.op("act", lambda e: e.activation(out=lor[li][0:m, :], in_=ps[0:m, :], func=fn), reads=[bps], writes=[blor[li]])
                for blk in range(4):
                    vt, bvt = vtr.next()
                    for half in range(2):
                        ps, bps = P_.next()
                        for k in range(KT):
                            S.op("pe", lambda e: e.matmul(ps[:, :], lhsT=xm[2][:, k, blk * 128:(blk + 1) * 128],
                                                          rhs=wrkv[2][:, k, half * 512:(half + 1) * 512], start=(k == 0), stop=(k == KT - 1)),
                                 reads=[wbuf, bxm[2]], writes=[bps], inc=(k == KT - 1), acc=True)
                        S.op("act", lambda e: e.activation(out=vt[:, half * 512:(half + 1) * 512], in_=ps[:, :], func=AF.Copy),
                             reads=[bps], writes=[bvt])
                    S.dma("sp", self.c_vtok[jt * 4 + blk, :, :], vt[:], reads=[bvt])
                def stageA(e_):
                    es = slice(e_ * 128, (e_ + 1) * 128)
                    pss = []
                    for c in range(3):
                        ps, bps = psF.next()
                        for k in range(KT):
                            S.op("pe", lambda e: e.matmul(ps[:, :], lhsT=wrkv[c][:, k, es], rhs=xm[c][:, k, :], start=(k == 0), stop=(k == KT - 1)),
                                 reads=[wbuf, bxm[c]], writes=[bps], inc=(k == KT - 1), acc=True)
                        pss.append((ps, bps))
                    (r_ps, br_), (k_ps, bk_), (v_ps, bv_) = pss
                    rf, brf = trA.next()
                    S.op("act", lambda e: e.activation(out=rf[:], in_=r_ps[:, :], func=AF.Copy), reads=[br_], writes=[brf])
                    kf, bkf = trA.next()
                    S.op("act", lambda e: e.activation(out=kf[:], in_=k_ps[:, :], func=AF.Copy), reads=[bk_], writes=[bkf])
                    vf, bvf = trA.next()
                    S.op("act", lambda e: e.activation(out=vf[:], in_=v_ps[:, :], func=AF.Copy), reads=[bv_], writes=[bvf])
                    wl_ps, bwl = psF.next()
                    S.op("pe", lambda e: e.matmul(wl_ps[:, :], lhsT=w2[0:64, es], rhs=lor[0][0:64, :], start=True, stop=True),
                         reads=[wbuf, blor[0]], writes=[bwl], acc=True)
                    al_ps, bal = psF.next()
                    S.op("pe", lambda e: e.matmul(al_ps[:, :], lhsT=a2[0:64, es], rhs=lor[1][0:64, :], start=True, stop=True),
                         reads=[wbuf, blor[1]], writes=[bal], acc=True)
                    g_ps, bg_ = psF.next()
                    S.op("pe", lambda e: e.matmul(g_ps[:, :], lhsT=g2[:, es], rhs=lor[2][:, :], start=True, stop=True),
                         reads=[wbuf, blor[2]], writes=[bg_], acc=True)
                    gb, bgb = brA.next()
                    S.op("act", lambda e: e.activation(out=gb[:], in_=g_ps[:, :], func=AF.Copy), reads=[bg_], writes=[bgb])
                    S.dma("sp", gD[e_, :, sl], gb[:], reads=[bgb])
                    sg, bsg = trA.next()
                    S.op("act", lambda e: e.activation(out=sg[:], in_=wl_ps[:, :], func=AF.Sigmoid, bias=self.col("c_w0", e_)),
                         reads=[bwl, self.bconst], writes=[bsg])
                    al, bal2 = trA.next()
                    S.op("act", lambda e: e.activation(out=al[:], in_=al_ps[:, :], func=AF.Sigmoid, bias=self.col("c_a0", e_)),
                         reads=[bal, self.bconst], writes=[bal2])
                    kk, bkk = trA.next()
                    S.op("dve", lambda e: e.tensor_scalar(out=kk[:], in0=kf[:], scalar1=self.col("c_k_k", e_), scalar2=None, op0=ALU.mult),
                         reads=[bkf, self.bconst], writes=[bkk])
                    k2, bk2 = brA.next()
                    S.op("act", lambda e: e.activation(out=k2[:], in_=kk[:], func=AF.Square), reads=[bkk], writes=[bk2])
                    ss_ps, bss = psB.next()
                    S.op("pe", lambda e: e.matmul(ss_ps[:, :], lhsT=bones[:, :], rhs=k2[:], start=True, stop=True),
                         reads=[bm, bk2], writes=[bss], acc=True)
                    return dict(es=es, rf=rf, brf=brf, kf=kf, bkf=bkf, vf=vf, bvf=bvf, sg=sg, bsg=bsg, al=al, bal2=bal2,
                                kk=kk, bkk=bkk, ss_ps=ss_ps, bss=bss)

                def stageB(e_, d_):
                    es = d_["es"]
                    rf, brf, kf, bkf, vf, bvf = d_["rf"], d_["brf"], d_["kf"], d_["bkf"], d_["vf"], d_["bvf"]
                    sg, bsg, al, bal2, kk, bkk, ss_ps, bss = d_["sg"], d_["bsg"], d_["al"], d_["bal2"], d_["kk"], d_["bkk"], d_["ss_ps"], d_["bss"]
                    rn, brn = trB.next()
                    S.op("act", lambda e: e.activation(out=rn[:], in_=ss_ps[:, :], func=AF.Sqrt), reads=[bss], writes=[brn])
                    S.op("dve", lambda e: e.tensor_scalar(out=rn[:], in0=rn[:], scalar1=1e-12, scalar2=None, op0=ALU.max), reads=[brn], writes=[brn])
                    rn2, brn2 = trB.next()
                    S.op("dve", lambda e: e.reciprocal(out=rn2[:], in_=rn[:]), reads=[brn], writes=[brn2])
                    S.op("dve", lambda e: e.tensor_tensor(out=kk[:], in0=kk[:], in1=rn2[:], op=ALU.mult), reads=[bkk, brn2], writes=[bkk])
                    cs, bcs = trB.next()
                    S.op("dve", lambda e: e.tensor_tensor_scan(out=cs[:], data0=cm01[:], data1=sg[:], initial=0.0, op0=ALU.mult, op1=ALU.add),
                         reads=[bm, bsg], writes=[bcs])
                    csx, bcsx = trB.next()
                    S.op("dve", lambda e: e.tensor_tensor(out=csx[:], in0=cs[:], in1=sg[:], op=ALU.subtract), reads=[bcs, bsg], writes=[bcsx])
                    eG, beG = trB.next()
                    S.op("act", lambda e: e.activation(out=eG[:], in_=cs[:], func=AF.Exp, scale=-LAM), reads=[bcs], writes=[beG])
                    eGi, beGi = trB.next()
                    S.op("act", lambda e: e.activation(out=eGi[:], in_=cs[:], func=AF.Exp, scale=LAM), reads=[bcs], writes=[beGi])
                    S.op("act", lambda e: e.activation(out=csx[:], in_=csx[:], func=AF.Exp, scale=-LAM), reads=[bcsx], writes=[bcsx])
                    ar, bar = arr.next()
                    S.op("dve", lambda e: e.scalar_tensor_tensor(out=ar[:, :, 0, :], in0=kk[:].rearrange("p (c t) -> p c t", c=4), scalar=-1.0,
                                                                 in1=csx[:].rearrange("p (c t) -> p c t", c=4), op0=ALU.mult, op1=ALU.mult),
                         reads=[bkk, bcsx], writes=[bar])
                    S.op("dve", lambda e: e.tensor_tensor(out=kk[:], in0=kk[:], in1=al[:], op=ALU.mult), reads=[bkk, bal2], writes=[bkk])
                    bt_, bbt = brB.next()
                    S.op("dve", lambda e: e.tensor_tensor(out=bt_[:], in0=kk[:], in1=eGi[:], op=ALU.mult), reads=[bkk, beGi], writes=[bbt])
                    S.dma("sp", BTd[e_, :, sl], bt_[:], reads=[bbt])
                    S.op("dve", lambda e: e.tensor_scalar(out=al[:], in0=al[:], scalar1=-1.0, scalar2=self.col("c_k_a", e_), op0=ALU.add, op1=ALU.mult),
                         reads=[bal2, self.bconst], writes=[bal2])
                    S.op("dve", lambda e: e.scalar_tensor_tensor(out=kf[:], in0=al[:], scalar=1.0, in1=kf[:], op0=ALU.add, op1=ALU.mult),
                         reads=[bal2, bkf], writes=[bkf])
                    kt_, bkt = brB.next()
                    S.op("dve", lambda e: e.tensor_tensor(out=kt_[:], in0=kf[:], in1=eGi[:], op=ALU.mult), reads=[bkf, beGi], writes=[bkt])
                    S.dma("sp", KTd[e_, :, sl], kt_[:], reads=[bkt])
                    S.op("dve", lambda e: e.tensor_tensor(out=ar[:, :, 1, :], in0=rf[:].rearrange("p (c t) -> p c t", c=4),
                                                          in1=eG[:].rearrange("p (c t) -> p c t", c=4), op=ALU.mult),
                         reads=[brf, beG], writes=[bar])
                    S.dma("sp", self.c_AR[e_, :, jt * 1024:(jt + 1) * 1024], ar[:].rearrange("p a b c -> p (a b c)"), reads=[bar])
                    rk, brk = brB.next()
                    S.op("dve", lambda e: e.scalar_tensor_tensor(out=rk[:], in0=rf[:], scalar=self.col("c_r_k", e_), in1=kf[:],
                                                                 op0=ALU.mult, op1=ALU.mult), reads=[brf, bkf, self.bconst], writes=[brk])
                    rk_ps, brkp = psB.next()
                    S.op("pe", lambda e: e.matmul(rk_ps[:, :], lhsT=bones[:, :], rhs=rk[:], start=True, stop=True),
                         reads=[bm, brk], writes=[brkp], acc=True)
                    S.op("dve", lambda e: e.tensor_tensor(out=vf[:], in0=rk_ps[:, :], in1=vf[:], op=ALU.mult), reads=[brkp, bvf], writes=[bvf])
                    S.dma("sp", bonD[e_, :, sl], vf[:], reads=[bvf])
                    S.op("act", lambda e: e.activation(out=gct[:, e_, jt * 4:(jt + 1) * 4], in_=eG[:, 127:TT:128], func=AF.Copy),
                         reads=[beG], writes=[bgct])

                curA = stageA(0)
                for e_ in range(KT):
                    nxtA = stageA(e_ + 1) if e_ + 1 < KT else None
                    stageB(e_, curA)
                    curA = nxtA
            for e_ in range(KT):
                S.dma("sp", self.c_gc[e_, :, :], gct[:, e_, :], reads=[bgct])
            self.barrier()


def make_in_maps(inp):
    f32 = np.float32
    cols = pack_cols(inp)
    wqkv = np.ascontiguousarray(inp["b_w_qkv"][0], dtype=f32)
    qk = wqkv[:, :6 * D].reshape(D, 6 * 16, 2, 32)
    wqks = np.ascontiguousarray(qk[:, :, ::-1, :].reshape(D, 6 * D))
    shared = {
        "cols": cols,
        "a_w_in": inp["a_w_in"], "a_gate_w": inp["a_gate_w"], "a_w_out": inp["a_w_out"],
        "b_w_qkv": wqkv, "b_w_qks": wqks, "b_w_out": inp["b_w_out"][0],
        "c_w_rkv": inp["c_w_rkv"][0], "c_w1": inp["c_w1"][0], "c_w2": inp["c_w2"][0],
        "c_a1": inp["c_a1"][0], "c_a2": inp["c_a2"][0], "c_g1": inp["c_g1"][0], "c_g2": inp["c_g2"][0],
        "c_w_out": inp["c_w_out"][0],
        "f_w_up": inp["f_w_up"], "f_w_down": inp["f_w_down"],
        "ple_w_proj": inp["ple_w_proj"], "ple_w_gate": inp["ple_w_gate"],
    }
    shared = {k: np.ascontiguousarray(v, dtype=f32) for k, v in shared.items()}
    maps = []
    for c in range(8):
        b = c % NB
        m = dict(shared)
        m["xT"] = np.ascontiguousarray(np.asarray(inp["x"][b], dtype=f32).T).reshape(KT, 128, T)
        m["pT"] = np.ascontiguousarray(np.transpose(np.asarray(inp["p"][:, b], dtype=f32), (0, 2, 1))).reshape(DEPTH, 2, 128, T)
        m["pos"] = np.ascontiguousarray(np.asarray(inp["positions"][b], dtype=np.int32)).reshape(1, T)
        maps.append(m)
    return maps


_PROG_CACHE = {}


def run_prog(inp, layers=DEPTH, dbg=None):
    key = (str(layers), dbg)
    if key not in _PROG_CACHE:
        _PROG_CACHE[key] = Prog(layers, dbg)
    prog = _PROG_CACHE[key]
    maps = make_in_maps(inp)
    used = set(prog.W.keys()) | {"xT", "pT", "pos", "cols"}
    maps = [{k: v for k, v in m.items() if k in used} for m in maps]
    res = run_bass_kernel_spmd(prog.nc, maps, core_ids=list(range(8)))
    out = np.stack([np.asarray(res.results[b]["yT"]).reshape(D, T).T for b in range(NB)])
    return np.ascontiguousarray(out.astype(np.float32)), res


def kernel(**inputs):
    inp = {k: np.asarray(v) for k, v in inputs.items()}
    out, _ = run_prog(inp)
    return out
```

```python
import numpy as np
import concourse.bass as bass
import concourse.mybir as mybir
from concourse.bass_utils import run_bass_kernel_spmd

F32 = mybir.dt.float32
BF16 = mybir.dt.bfloat16
I32 = mybir.dt.int32
AF = mybir.ActivationFunctionType
ALU = mybir.AluOpType
AX = mybir.AxisListType


class Buf:
    __slots__ = ("w", "r", "name", "x")

    def __init__(self, name="", x=False):
        self.w = None
        self.r = {}
        self.name = name
        self.x = x


class _Eng:
    def __init__(self, name, eng, sem):
        self.name, self.e, self.sem = name, eng, sem
        self.cnt = 0
        self.seen = {}
        self.pending = []

    def wait(self, ev):
        sem, val = ev
        k = id(sem)
        if self.seen.get(k, 0) >= val:
            return
        self.e.wait_ge(sem, val)
        self.seen[k] = val


class Sched:
    NDMA = 8

    def __init__(self, nc, stack):
        self.nc = nc
        self.engs = {}
        for name, eng in (("pe", nc.tensor), ("act", nc.scalar), ("dve", nc.vector),
                          ("pool", nc.gpsimd), ("sp", nc.sync)):
            sem = stack.enter_context(nc.semaphore("sem_" + name))
            self.engs[name] = _Eng(name, eng, sem)
        self.dma_slots = {}
        for q in ("sp", "pool", "act"):
            sl = []
            for i in range(self.NDMA):
                sem = stack.enter_context(nc.semaphore("dq_%s%d" % (q, i)))
                sl.append([sem, 0])
            self.dma_slots[q] = [sl, 0]
        self.n_inst = 0

    def op(self, engname, fn, reads=(), writes=(), inc=True, acc=False):
        E = self.engs[engname]
        for b in reads:
            if b.w is not None:
                E.wait(b.w)
            if b.x:
                for ev in b.r.values():
                    if ev[0] is not E.sem:
                        E.wait(ev)
        for b in writes:
            if b.w is not None and not (acc and b.w[0] is E.sem):
                E.wait(b.w)
            for ev in b.r.values():
                if ev[0] is not E.sem:
                    E.wait(ev)
        inst = fn(E.e)
        self.n_inst += 1
        if inc:
            E.cnt += 1
            inst.then_inc(E.sem, 1)
            ev = (E.sem, E.cnt)
            E.pending.append((reads, writes))
            for rd, wr in E.pending:
                for b in rd:
                    b.r[id(E.sem)] = ev
                for b in wr:
                    b.w = ev
                    b.r = {}
            E.pending = []
            E.seen[id(E.sem)] = max(E.seen.get(id(E.sem), 0), 0)
        else:
            E.pending.append((reads, writes))
        return inst

    def dma(self, q, out, in_, reads=(), writes=()):
        E = self.engs[q]
        slots, idx = self.dma_slots[q]
        slot = slots[idx % self.NDMA]
        self.dma_slots[q][1] = idx + 1
        if slot[1] > 0:
            E.wait((slot[0], slot[1]))
        for b in reads:
            if b.w is not None:
                E.wait(b.w)
        for b in writes:
            if b.w is not None:
                E.wait(b.w)
            for ev in b.r.values():
                E.wait(ev)
        inst = E.e.dma_start(out=out, in_=in_)
        self.n_inst += 1
        slot[1] += 16
        inst.then_inc(slot[0], 16)
        ev = (slot[0], slot[1])
        for b in reads:
            b.r[id(slot[0])] = ev
        for b in writes:
            b.w = ev
            b.r = {}
        return ev

    def wait_all(self, engname, bufs):
        E = self.engs[engname]
        for b in bufs:
            if b.w is not None:
                E.wait(b.w)
            for ev in b.r.values():
                E.wait(ev)


class Ring:
    _uid = [0]

    def __init__(self, nc, stack, name, shape, dtype, n, psum=False):
        self.items = []
        Ring._uid[0] += 1
        name = "%s_u%d_" % (name, Ring._uid[0])
        for i in range(n):
            if psum:
                t = stack.enter_context(nc.psum_tensor("%s%d" % (name, i), shape, dtype))
            else:
                t = stack.enter_context(nc.sbuf_tensor("%s%d" % (name, i), shape, dtype))
            self.items.append((t, Buf("%s%d" % (name, i))))
        self.i = 0

    def next(self):
        it = self.items[self.i % len(self.items)]
        self.i += 1
        return it


class SubRing(Ring):
    def __init__(self, items):
        self.items = list(items)
        self.i = 0


D = 1024
T = 4096
NB = 4
DEPTH = 4
KT = D // 128
TT = 512
NTT = T // TT
FFN = 2816
FT = FFN // 128
PLE = 256
RMS_EPS = 1e-6
GN_EPS = 64e-5
LRU_C = 8.0


class ColPack:
    def __init__(self):
        self.cols = []
        self.idx = {}

    def add(self, name, vec):
        vec = np.ascontiguousarray(vec, dtype=np.float32).reshape(-1)
        assert vec.size % 128 == 0
        n = vec.size // 128
        self.idx[name] = (len(self.cols), n)
        for i in range(n):
            self.cols.append(vec[i * 128:(i + 1) * 128])

    def array(self):
        return np.ascontiguousarray(np.stack(self.cols, axis=1))


def col_layout():
    L = []
    for i in range(DEPTH):
        L += [("norm_mix%d" % i, KT), ("norm_ffn%d" % i, KT), ("norm_ple%d" % i, KT)]
        for k in range(3):
            L.append(("f_conv_w%d_%d" % (i, k), 2 * FT))
        L.append(("f_conv_b%d" % i, 2 * FT))
    L.append(("norm_final", KT))
    for j in range(2):
        for k in range(4):
            L.append(("a_conv_w%d_%d" % (j, k), KT))
        L += [("a_conv_b%d" % j, KT), ("a_gate_b%d" % j, 2 * KT), ("a_lambda%d" % j, KT)]
    for c in range(6):
        L.append(("c_mu%d" % c, KT))
    for nm in ("c_w0", "c_a0", "c_k_k", "c_k_a", "c_r_k", "c_ln_w", "c_ln_b"):
        L.append((nm, KT))
    L.append(("invf", 1))
    L.append(("rsgn", 1))
    off = {}
    o = 0
    for nm, n in L:
        off[nm] = (o, n)
        o += n
    return off, o


COLS, NCOLS = col_layout()


def pack_cols(inp):
    cp = ColPack()
    for i in range(DEPTH):
        cp.add("norm_mix%d" % i, inp["norm_mix"][i])
        cp.add("norm_ffn%d" % i, inp["norm_ffn"][i])
        cp.add("norm_ple%d" % i, inp["norm_ple"][i])
        for k in range(3):
            cp.add("f_conv_w%d_%d" % (i, k), inp["f_conv_w"][i, k])
        cp.add("f_conv_b%d" % i, inp["f_conv_b"][i])
    cp.add("norm_final", inp["norm_final"])
    for j in range(2):
        for k in range(4):
            cp.add("a_conv_w%d_%d" % (j, k), inp["a_conv_w"][j, k])
        cp.add("a_conv_b%d" % j, inp["a_conv_b"][j])
        gb = inp["a_gate_b"][j].reshape(4, 2, 256)
        cp.add("a_gate_b%d" % j, np.concatenate([gb[:, 0].reshape(-1), gb[:, 1].reshape(-1)]))
        cp.add("a_lambda%d" % j, inp["a_lambda"][j])
    for c in range(6):
        cp.add("c_mu%d" % c, inp["c_mu"][0, c])
    for nm in ("c_w0", "c_a0", "c_k_k", "c_k_a", "c_r_k", "c_ln_w", "c_ln_b"):
        cp.add(nm, inp[nm][0])
    invf = (10000.0 ** (-np.arange(0, 64, 2, dtype=np.float32) / 64)).astype(np.float32)
    cp.add("invf", np.tile(invf, 4))
    cp.add("rsgn", np.tile(np.concatenate([-np.ones(32, np.float32), np.ones(32, np.float32)]), 2))
    assert cp.idx == COLS, "col layout mismatch"
    return cp.array()


from contextlib import ExitStack


class _Cut(Exception):
    pass


class Prog:
    def cut(self, name):
        if self.dbg == name:
            raise _Cut()

    def __init__(self, layers=DEPTH, dbg=None):
        self.layers = list(range(layers)) if isinstance(layers, int) else list(layers)
        self.dbg = dbg
        nc = bass.Bass("TRN2", target_bir_lowering=False)
        self.nc = nc
        dt = nc.dram_tensor
        self.xT = dt("xT", [KT, 128, T], F32, kind="ExternalInput").ap()
        self.pT = dt("pT", [DEPTH, 2, 128, T], F32, kind="ExternalInput").ap()
        self.pos = dt("pos", [1, T], I32, kind="ExternalInput").ap()
        self.colsD = dt("cols", [128, NCOLS], F32, kind="ExternalInput").ap()
        class _LazyW(dict):
            def __init__(s2, shapes):
                s2.shapes = shapes
            def __missing__(s2, nm):
                s2[nm] = dt(nm, s2.shapes[nm], F32, kind="ExternalInput").ap()
                return s2[nm]
        shapes = {}
        for nm, shp in (("a_w_in", [2, D, 2 * D]), ("a_gate_w", [2, 4, 256, 512]), ("a_w_out", [2, D, D]),
                        ("b_w_qkv", [D, 7 * D]), ("b_w_qks", [D, 6 * D]), ("b_w_out", [D, D]),
                        ("c_w_rkv", [3, D, D]), ("c_w1", [D, 64]), ("c_w2", [64, D]), ("c_a1", [D, 64]),
                        ("c_a2", [64, D]), ("c_g1", [D, 128]), ("c_g2", [128, D]), ("c_w_out", [D, D]),
                        ("f_w_up", [DEPTH, D, 2 * FFN]), ("f_w_down", [DEPTH, FFN, D]),
                        ("ple_w_proj", [DEPTH, PLE, D]), ("ple_w_gate", [DEPTH, D, D])):
            shapes[nm] = shp
        self.W = _LazyW(shapes)
        self.yT = dt("yT", [KT, 128, T], F32, kind="ExternalOutput").ap()
        self.hT = dt("hT", [KT, 128, T], F32, kind="Internal").ap()
        self.hnT = dt("hnT", [KT, 128, T], BF16, kind="Internal").ap()
        self.gT = dt("gT", [KT, 128, T], BF16, kind="Internal").ap()
        self.actT = dt("actT", [FT, 128, T], BF16, kind="Internal").ap()
        self.s32 = [dt("s32_%d" % i, [KT, 128, T], F32, kind="Internal").ap() for i in range(6)]
        self.s16 = [dt("s16_%d" % i, [KT, 128, T], BF16, kind="Internal").ap() for i in range(8)]
        self.build()

    def col(self, name, k=0, n=1):
        o, sz = COLS[name]
        assert k + n <= sz
        return self.cols[:, o + k:o + k + n]

    def barrier(self):
        S = self.S
        evs = [(E.sem, E.cnt) for E in S.engs.values() if E.cnt > 0]
        for q in S.dma_slots:
            for sl in S.dma_slots[q][0]:
                if sl[1] > 0:
                    evs.append((sl[0], sl[1]))
        for E in S.engs.values():
            assert not E.pending
            for ev in evs:
                E.wait(ev)

    def dump(self, name, ap, bufs, shape, dtype):
        if not self.dbg:
            return
        t = self.nc.dram_tensor("dbg_" + name, list(shape), dtype, kind="Internal").ap()
        self.S.dma("sp", t, ap, reads=list(bufs))

    def sb(self, st, name, shape, dtype):
        Ring._uid[0] += 1
        return st.enter_context(self.nc.sbuf_tensor("%s_u%d" % (name, Ring._uid[0]), shape, dtype))

    def load_w(self, ring, wap2d, kt, e0, ew):
        t, b = ring.next()
        src = wap2d.rearrange("(k p) e -> p k e", p=128)[:, :, e0:e0 + ew]
        self.S.dma("pool", t[:, 0:kt, 0:ew], src, writes=[b])
        return t, b

    def rmsnorm(self, st_rings, h, bh, gain, out, bout):
        S = self.S
        sqr, psr, rsr = st_rings
        ps, bps = psr.next()
        for k in range(KT):
            sq, bsq = sqr.next()
            S.op("act", lambda e: e.activation(out=sq[:], in_=h[:, k, :], func=AF.Square), reads=[bh], writes=[bsq])
            S.op("pe", lambda e: e.matmul(ps[:, :], lhsT=self.ones_bf[:, :], rhs=sq[:], start=(k == 0), stop=(k == KT - 1)),
                 reads=[bsq, self.bconst], writes=[bps], acc=True)
        rs, brs = rsr.next()
        S.op("act", lambda e: e.activation(out=rs[:], in_=ps[:, :], func=AF.Sqrt, scale=1.0 / D, bias=self.eps_col),
             reads=[bps, self.bconst], writes=[brs])
        rs2, brs2 = rsr.next()
        S.op("dve", lambda e: e.reciprocal(out=rs2[:], in_=rs[:]), reads=[brs], writes=[brs2])
        for k in range(KT):
            S.op("dve", lambda e: e.scalar_tensor_tensor(out=out[:, k, :], in0=h[:, k, :], scalar=self.col(gain, k),
                                                         in1=rs2[:], op0=ALU.mult, op1=ALU.mult),
                 reads=[bh, brs2, self.bconst], writes=[bout])

    def norm_rings(self, st, tag):
        nc = self.nc
        return (Ring(nc, st, "sq" + tag, [128, TT], BF16, 3),
                self.psA, Ring(nc, st, "rs" + tag, [128, TT], F32, 4))

    def tview(self, ap3, j):
        return ap3.rearrange("k p t -> p k t")[:, :, j * TT:(j + 1) * TT]

    def build(self):
        nc = self.nc
        with ExitStack() as top:
            S = Sched(nc, top)
            self.S = S
            self.cols = self.sb(top, "cols", [128, NCOLS], F32)
            self.cst = self.sb(top, "cst", [128, 8], F32)
            self.ones_bf = self.sb(top, "ones_bf", [128, 128], BF16)
            self.bconst = Buf("const")
            S.dma("sp", self.cols[:], self.colsD[:, :], writes=[self.bconst])
            S.op("pool", lambda e: e.memset(self.cst[:, 0:1], RMS_EPS), writes=[self.bconst])
            S.op("pool", lambda e: e.memset(self.cst[:, 1:2], 1.0), writes=[self.bconst])
            S.op("pool", lambda e: e.memset(self.cst[:, 2:3], 0.0), writes=[self.bconst])
            S.op("pool", lambda e: e.memset(self.cst[:, 3:4], GN_EPS), writes=[self.bconst])
            S.op("pool", lambda e: e.memset(self.ones_bf[:], 1.0), writes=[self.bconst])
            self.eps_col = self.cst[:, 0:1]
            self.one_col = self.cst[:, 1:2]
            self.zero_col = self.cst[:, 2:3]
            self.gneps_col = self.cst[:, 3:4]
            self.psbig = [top.enter_context(nc.psum_tensor("psbig%d" % q, [128, 2 * TT], F32)) for q in range(4)]
            self.psA = SubRing([(self.psbig[q // 2][:, (q % 2) * TT:(q % 2 + 1) * TT], Buf("ps%d" % q, x=True)) for q in range(8)])
            self.barrier()
            self.phase_norm0(self.layers[0])
            h_src = self.xT
            for li, i in enumerate(self.layers):
                kind, j = i % 3, i // 3
                if kind == 0:
                    self.mixer_a(i, j)
                    wout = self.W["a_w_out"][j]
                elif kind == 1:
                    self.mixer_b(i)
                    wout = self.W["b_w_out"]
                else:
                    self.mixer_c(i)
                    wout = self.W["c_w_out"]
                self.mixer_out(i, wout, h_src)
                h_src = self.hT
                self.ffn_up(i)
                last = (li == len(self.layers) - 1)
                self.tail(i, last, None if last else self.layers[li + 1])
            self.barrier()

    def phase_norm0(self, i0):
        nc, S = self.nc, self.S
        with ExitStack() as st:
            hr = Ring(nc, st, "n0h", [128, KT, TT], F32, 2)
            orr = Ring(nc, st, "n0o", [128, KT, TT], BF16, 2)
            rings = self.norm_rings(st, "n0")
            def ld(j):
                h, bh = hr.next()
                S.dma("sp", h[:], self.tview(self.xT, j), writes=[bh])
                return h, bh
            nxt = ld(0)
            for j in range(NTT):
                h, bh = nxt
                if j + 1 < NTT:
                    nxt = ld(j + 1)
                o, bo = orr.next()
                self.rmsnorm(rings, h, bh, "norm_mix%d" % i0, o, bo)
                S.dma("sp", self.tview(self.hnT, j), o[:], reads=[bo])
            self.barrier()

    def mixer_out(self, i, wout, h_src):
        nc, S = self.nc, self.S
        with ExitStack() as st:
            w = self.sb(st, "mo_w", [128, KT, D], BF16)
            bw = [Buf() for _ in range(KT)]
            for k in range(KT):
                S.dma("pool", w[:, k, :], wout[k * 128:(k + 1) * 128, :], writes=[bw[k]])
            gr = Ring(nc, st, "mo_g", [128, KT, TT], BF16, 2)
            hr = Ring(nc, st, "mo_h", [128, KT, TT], F32, 2)
            orr = Ring(nc, st, "mo_o", [128, KT, TT], BF16, 2)
            rings = self.norm_rings(st, "mo")
            def ld(j):
                g, bg = gr.next()
                S.dma("sp", g[:], self.tview(self.gT, j), writes=[bg])
                h, bh = hr.next()
                S.dma("sp", h[:], self.tview(h_src, j), writes=[bh])
                return g, bg, h, bh
            nxt = ld(0)
            for j in range(NTT):
                g, bg, h, bh = nxt
                if j + 1 < NTT:
                    nxt = ld(j + 1)
                for e in range(KT):
                    ps, bps = self.psA.next()
                    for k in range(KT):
                        S.op("pe", lambda en: en.matmul(ps[:, :], lhsT=w[:, k, e * 128:(e + 1) * 128], rhs=g[:, k, :],
                                                        start=(k == 0), stop=(k == KT - 1)),
                             reads=[bw[k], bg], writes=[bps], inc=(k == KT - 1), acc=True)
                    S.op("dve", lambda en: en.tensor_tensor(out=h[:, e, :], in0=ps[:, :], in1=h[:, e, :], op=ALU.add),
                         reads=[bps, bh], writes=[bh])
                S.dma("sp", self.tview(self.hT, j), h[:], reads=[bh])
                o, bo = orr.next()
                self.rmsnorm(rings, h, bh, "norm_ffn%d" % i, o, bo)
                S.dma("sp", self.tview(self.hnT, j), o[:], reads=[bo])
            self.barrier()

    def ffn_up(self, i):
        nc, S = self.nc, self.S
        wup = self.W["f_w_up"][i]
        with ExitStack() as st:
            hn = self.sb(st, "fu_hn", [128, KT, T], BF16)
            bhn = [Buf() for _ in range(KT)]
            for k in range(KT):
                S.dma("sp", hn[:, k, :], self.hnT[k, :, :], writes=[bhn[k]])
            wr = Ring(nc, st, "fu_w", [128, KT, 128], BF16, 4)
            stg = [[self.sb(st, "fu_stg%d_%d" % (s, b), [128, 2 + T], F32) for b in range(2)] for s in range(2)]
            bstg = [[[Buf() for _ in range(NTT + 1)] for b in range(2)] for s in range(2)]
            for s in range(2):
                for b in range(2):
                    S.op("pool", lambda e: e.memset(stg[s][b][:, 0:2], 0.0), writes=[bstg[s][b][0]])
            accr = Ring(nc, st, "fu_acc", [128, TT], F32, 4)
            sgr = Ring(nc, st, "fu_sg", [128, TT], F32, 2)
            outr = Ring(nc, st, "fu_out", [128, TT], BF16, 3)
            ldw = lambda f: [self.load_w(wr, wup, KT, (s * FT + f) * 128, 128) for s in range(2)]
            wnxt = ldw(0)
            for f in range(FT):
                wt = wnxt
                if f + 1 < FT:
                    wnxt = ldw(f + 1)
                accs = [None, None]
                for j in range(NTT):
                    for s in range(2):
                        w, bw = wt[s]
                        ps, bps = self.psA.next()
                        for k in range(KT):
                            S.op("pe", lambda en: en.matmul(ps[:, :], lhsT=w[:, k, :], rhs=hn[:, k, j * TT:(j + 1) * TT],
                                                            start=(k == 0), stop=(k == KT - 1)),
                                 reads=[bw, bhn[k]], writes=[bps], inc=(k == KT - 1), acc=True)
                        sg_t, sg_b = stg[s][f % 2], bstg[s][f % 2]
                        c0 = 2 + j * TT
                        S.op("act", lambda en: en.activation(out=sg_t[:, c0:c0 + TT], in_=ps[:, :], func=AF.Copy),
                             reads=[bps], writes=[sg_b[j + 1]])
                        acc, bacc = accr.next()
                        ci = s * FT + f
                        S.op("act", lambda en: en.activation(out=acc[:], in_=ps[:, :], func=AF.Identity,
                                                             scale=self.col("f_conv_w%d_2" % i, ci),
                                                             bias=self.col("f_conv_b%d" % i, ci)),
                             reads=[bps, self.bconst], writes=[bacc])
                        for kk in (1, 0):
                            S.op("dve", lambda en: en.scalar_tensor_tensor(
                                out=acc[:], in0=sg_t[:, j * TT + kk:j * TT + kk + TT],
                                scalar=self.col("f_conv_w%d_%d" % (i, kk), ci), in1=acc[:], op0=ALU.mult, op1=ALU.add),
                                reads=[sg_b[j], sg_b[j + 1], bacc, self.bconst], writes=[bacc])
                        accs[s] = (acc, bacc)
                    sg, bsg = sgr.next()
                    S.op("act", lambda en: en.activation(out=sg[:], in_=accs[0][0][:], func=AF.Silu),
                         reads=[accs[0][1]], writes=[bsg])
                    o, bo = outr.next()
                    S.op("dve", lambda en: en.tensor_tensor(out=o[:], in0=sg[:], in1=accs[1][0][:], op=ALU.mult),
                         reads=[bsg, accs[1][1]], writes=[bo])
                    S.dma("sp", self.actT[f, :, j * TT:(j + 1) * TT], o[:], reads=[bo])
            self.barrier()

    def tail(self, i, last, inext):
        nc, S = self.nc, self.S
        with ExitStack() as st:
            wd = self.sb(st, "tl_wd", [128, FT, D], BF16)
            wg = self.sb(st, "tl_wg", [128, KT, D], BF16)
            wp = self.sb(st, "tl_wp", [128, 2, D], BF16)
            bwd, bwg, bwp = [Buf() for _ in range(FT)], [Buf() for _ in range(KT)], Buf()
            for f in range(FT):
                S.dma("pool", wd[:, f, :], self.W["f_w_down"][i, f * 128:(f + 1) * 128, :], writes=[bwd[f]])
            for k in range(KT):
                S.dma("pool", wg[:, k, :], self.W["ple_w_gate"][i, k * 128:(k + 1) * 128, :], writes=[bwg[k]])
            for k in range(2):
                S.dma("pool", wp[:, k, :], self.W["ple_w_proj"][i, k * 128:(k + 1) * 128, :], writes=[bwp])
            ar = Ring(nc, st, "tl_a", [128, FT, TT], BF16, 2)
            hr = Ring(nc, st, "tl_h", [128, KT, TT], F32, 2)
            pr = Ring(nc, st, "tl_p", [128, 2, TT], BF16, 2)
            n3r = Ring(nc, st, "tl_n3", [128, KT, TT], BF16, 1)
            sgr = Ring(nc, st, "tl_sg", [128, TT], F32, 2)
            if last:
                orr = Ring(nc, st, "tl_o", [128, KT, TT], F32, 1)
            else:
                orr = Ring(nc, st, "tl_o", [128, KT, TT], BF16, 2)
            rings = self.norm_rings(st, "tl")
            def ld(j):
                a, ba = ar.next()
                S.dma("sp", a[:], self.tview(self.actT, j), writes=[ba])
                h, bh = hr.next()
                S.dma("sp", h[:], self.tview(self.hT, j), writes=[bh])
                p, bp = pr.next()
                S.dma("pool", p[:], self.tview(self.pT[i], j), writes=[bp])
                return a, ba, h, bh, p, bp
            nxt = ld(0)
            for j in range(NTT):
                a, ba, h, bh, p, bp = nxt
                if j + 1 < NTT:
                    nxt = ld(j + 1)
                for e in range(KT):
                    ps, bps = self.psA.next()
                    for f in range(FT):
                        S.op("pe", lambda en: en.matmul(ps[:, :], lhsT=wd[:, f, e * 128:(e + 1) * 128], rhs=a[:, f, :],
                                                        start=(f == 0), stop=(f == FT - 1)),
                             reads=[bwd[f], ba], writes=[bps], inc=(f == FT - 1), acc=True)
                    S.op("dve", lambda en: en.tensor_tensor(out=h[:, e, :], in0=ps[:, :], in1=h[:, e, :], op=ALU.add),
                         reads=[bps, bh], writes=[bh])
                n3, bn3 = n3r.next()
                self.rmsnorm(rings, h, bh, "norm_ple%d" % i, n3, bn3)
                for e in range(KT):
                    ps, bps = self.psA.next()
                    for k in range(KT):
                        S.op("pe", lambda en: en.matmul(ps[:, :], lhsT=wg[:, k, e * 128:(e + 1) * 128], rhs=n3[:, k, :],
                                                        start=(k == 0), stop=(k == KT - 1)),
                             reads=[bwg[k], bn3], writes=[bps], inc=(k == KT - 1), acc=True)
                    sg, bsg = sgr.next()
                    S.op("act", lambda en: en.activation(out=sg[:], in_=ps[:, :], func=AF.Sigmoid),
                         reads=[bps], writes=[bsg])
                    ps2, bps2 = self.psA.next()
                    for k in range(2):
                        S.op("pe", lambda en: en.matmul(ps2[:, :], lhsT=wp[:, k, e * 128:(e + 1) * 128], rhs=p[:, k, :],
                                                        start=(k == 0), stop=(k == 1)),
                             reads=[bwp, bp], writes=[bps2], inc=(k == 1), acc=True)
                    S.op("dve", lambda en: en.tensor_tensor(out=sg[:], in0=ps2[:, :], in1=sg[:], op=ALU.mult),
                         reads=[bps2, bsg], writes=[bsg])
                    S.op("dve", lambda en: en.tensor_tensor(out=h[:, e, :], in0=sg[:], in1=h[:, e, :], op=ALU.add),
                         reads=[bsg, bh], writes=[bh])
                o, bo = orr.next()
                if last:
                    self.rmsnorm(rings, h, bh, "norm_final", o, bo)
                    S.dma("sp", self.tview(self.yT, j), o[:], reads=[bo])
                else:
                    S.dma("sp", self.tview(self.hT, j), h[:], reads=[bh])
                    self.rmsnorm(rings, h, bh, "norm_mix%d" % inext, o, bo)
                    S.dma("sp", self.tview(self.hnT, j), o[:], reads=[bo])
            self.barrier()

    def mixer_a(self, i, j):
        nc, S = self.nc, self.S
        ygT, xcT, xcbT = self.s32[0], self.s32[1], self.s16[0]
        win = self.W["a_w_in"][j]
        with ExitStack() as st:
            hn = self.sb(st, "a1_hn", [128, KT, T], BF16)
            bhn = [Buf() for _ in range(KT)]
            for k in range(KT):
                S.dma("sp", hn[:, k, :], self.hnT[k, :, :], writes=[bhn[k]])
            wr = Ring(nc, st, "a1_w", [128, KT, 128], BF16, 4)
            stg = [self.sb(st, "a1_stg%d" % b, [128, 3 + T], F32) for b in range(2)]
            bstg = [[Buf() for _ in range(NTT + 1)] for b in range(2)]
            for b in range(2):
                S.op("pool", lambda e: e.memset(stg[b][:, 0:3], 0.0), writes=[bstg[b][0]])
            ygr = Ring(nc, st, "a1_yg", [128, TT], F32, 3)
            accr = Ring(nc, st, "a1_acc", [128, TT], F32, 3)
            xbr = Ring(nc, st, "a1_xb", [128, TT], BF16, 3)
            wnxt = self.load_w(wr, win, KT, 0, 128)
            for e in range(2 * KT):
                w, bw = wnxt
                if e + 1 < 2 * KT:
                    wnxt = self.load_w(wr, win, KT, (e + 1) * 128, 128)
                for jt in range(NTT):
                    ps, bps = self.psA.next()
                    for k in range(KT):
                        S.op("pe", lambda en: en.matmul(ps[:, :], lhsT=w[:, k, :], rhs=hn[:, k, jt * TT:(jt + 1) * TT],
                                                        start=(k == 0), stop=(k == KT - 1)),
                             reads=[bw, bhn[k]], writes=[bps], inc=(k == KT - 1), acc=True)
                    if e < KT:
                        yg, byg = ygr.next()
                        S.op("act", lambda en: en.activation(out=yg[:], in_=ps[:, :], func=AF.Gelu_apprx_tanh),
                             reads=[bps], writes=[byg])
                        S.dma("sp", ygT[e, :, jt * TT:(jt + 1) * TT], yg[:], reads=[byg])
                    else:
                        ft = e - KT
                        sg_t, sg_b = stg[ft % 2], bstg[ft % 2]
                        c0 = 3 + jt * TT
                        S.op("act", lambda en: en.activation(out=sg_t[:, c0:c0 + TT], in_=ps[:, :], func=AF.Copy),
                             reads=[bps], writes=[sg_b[jt + 1]])
                        acc, bacc = accr.next()
                        S.op("act", lambda en: en.activation(out=acc[:], in_=ps[:, :], func=AF.Identity,
                                                             scale=self.col("a_conv_w%d_3" % j, ft),
                                                             bias=self.col("a_conv_b%d" % j, ft)),
                             reads=[bps, self.bconst], writes=[bacc])
                        for kk in (2, 1, 0):
                            S.op("dve", lambda en: en.scalar_tensor_tensor(
                                out=acc[:], in0=sg_t[:, jt * TT + kk:jt * TT + kk + TT],
                                scalar=self.col("a_conv_w%d_%d" % (j, kk), ft), in1=acc[:], op0=ALU.mult, op1=ALU.add),
                                reads=[sg_b[jt], sg_b[jt + 1], bacc, self.bconst], writes=[bacc])
                        xb, bxb = xbr.next()
                        S.op("pool", lambda en: en.tensor_copy(out=xb[:], in_=acc[:]), reads=[bacc], writes=[bxb])
                        S.dma("sp", xcT[ft, :, jt * TT:(jt + 1) * TT], acc[:], reads=[bacc])
                        S.dma("sp", xcbT[ft, :, jt * TT:(jt + 1) * TT], xb[:], reads=[bxb])
            self.barrier()
        with ExitStack() as st:
            cc = self.sb(st, "a2_cc", [128, 5, KT], F32)
            bc = Buf()
            lam = self.col("a_lambda%d" % j, 0, KT)
            ev, l1, t2, mk, ccol = (cc[:, q, :] for q in range(5))
            S.op("act", lambda e: e.activation(out=ev, in_=lam, func=AF.Exp, scale=-1.0), reads=[self.bconst], writes=[bc])
            S.op("act", lambda e: e.activation(out=l1, in_=ev, func=AF.Ln, bias=self.one_col), reads=[bc, self.bconst], writes=[bc])
            S.op("dve", lambda e: e.tensor_scalar(out=t2, in0=ev, scalar1=1.0 / 3.0, scalar2=-0.5, op0=ALU.mult, op1=ALU.add), reads=[bc], writes=[bc])
            S.op("dve", lambda e: e.tensor_tensor(out=t2, in0=t2, in1=ev, op=ALU.mult), reads=[bc], writes=[bc])
            S.op("dve", lambda e: e.tensor_scalar(out=t2, in0=t2, scalar1=1.0, scalar2=None, op0=ALU.add), reads=[bc], writes=[bc])
            S.op("dve", lambda e: e.tensor_tensor(out=t2, in0=t2, in1=ev, op=ALU.mult), reads=[bc], writes=[bc])
            S.op("dve", lambda e: e.tensor_scalar(out=mk, in0=ev, scalar1=0.02, scalar2=None, op0=ALU.is_lt), reads=[bc], writes=[bc])
            S.op("dve", lambda e: e.tensor_tensor(out=t2, in0=t2, in1=l1, op=ALU.subtract), reads=[bc], writes=[bc])
            S.op("dve", lambda e: e.tensor_tensor(out=t2, in0=t2, in1=mk, op=ALU.mult), reads=[bc], writes=[bc])
            S.op("dve", lambda e: e.tensor_tensor(out=l1, in0=l1, in1=t2, op=ALU.add), reads=[bc], writes=[bc])
            S.op("dve", lambda e: e.tensor_scalar(out=ccol, in0=l1, scalar1=-LRU_C, scalar2=None, op0=ALU.mult), reads=[bc], writes=[bc])

            xbr = Ring(nc, st, "a2_xb", [128, 2, T], BF16, 2)
            gwr = Ring(nc, st, "a2_gw", [128, 2, 512], BF16, 2)
            afr = Ring(nc, st, "a2_af", [128, T], F32, 2)
            bfr = Ring(nc, st, "a2_bf", [128, T], F32, 2)
            hfr = Ring(nc, st, "a2_hf", [128, T], F32, 1)
            sqr_ = Ring(nc, st, "a2_sq", [128, T], F32, 1)
            ygr = Ring(nc, st, "a2_yg", [128, T], F32, 1)
            gor = Ring(nc, st, "a2_go", [128, T], BF16, 1)
            tr = Ring(nc, st, "a2_t", [128, TT], F32, 3)
            xcr = Ring(nc, st, "a2_xc", [128, TT], F32, 3)

            def ldh(hd):
                xb, bxb = xbr.next()
                for k in range(2):
                    S.dma("sp", xb[:, k, :], xcbT[2 * hd + k, :, :], writes=[bxb])
                gw, bgw = self.load_w(gwr, self.W["a_gate_w"][j, hd], 2, 0, 512)
                return xb, bxb, gw, bgw
            nxt = ldh(0)
            for hd in range(4):
                xb, bxb, gw, bgw = nxt
                if hd + 1 < 4:
                    nxt = ldh(hd + 1)
                for ft in range(2):
                    ftg = 2 * hd + ft
                    af, baf = afr.next()
                    bf, bbf = bfr.next()
                    yg, byg = ygr.next()
                    S.dma("sp", yg[:], ygT[ftg, :, :], writes=[byg])
                    for jt in range(NTT):
                        sl = slice(jt * TT, (jt + 1) * TT)
                        xc, bxc = xcr.next()
                        S.dma("sp", xc[:], xcT[ftg, :, sl], writes=[bxc])
                        psr, bpsr = self.psA.next()
                        psi, bpsi = self.psA.next()
                        for k in range(2):
                            S.op("pe", lambda en: en.matmul(psr[:, :], lhsT=gw[:, k, ft * 128:(ft + 1) * 128], rhs=xb[:, k, sl],
                                                            start=(k == 0), stop=(k == 1)),
                                 reads=[bgw, bxb], writes=[bpsr], inc=(k == 1), acc=True)
                        for k in range(2):
                            S.op("pe", lambda en: en.matmul(psi[:, :], lhsT=gw[:, k, 256 + ft * 128:256 + (ft + 1) * 128], rhs=xb[:, k, sl],
                                                            start=(k == 0), stop=(k == 1)),
                                 reads=[bgw, bxb], writes=[bpsi], inc=(k == 1), acc=True)
                        S.op("act", lambda en: en.activation(out=af[:, sl], in_=psr[:, :], func=AF.Sigmoid,
                                                             bias=self.col("a_gate_b%d" % j, ftg)),
                             reads=[bpsr, self.bconst], writes=[baf])
                        ig, big = tr.next()
                        S.op("act", lambda en: en.activation(out=ig[:], in_=psi[:, :], func=AF.Sigmoid,
                                                             bias=self.col("a_gate_b%d" % j, KT + ftg)),
                             reads=[bpsi, self.bconst], writes=[big])
                        S.op("dve", lambda en: en.tensor_tensor(out=bf[:, sl], in0=ig[:], in1=xc[:], op=ALU.mult),
                             reads=[big, bxc], writes=[bbf])
                    S.op("act", lambda en: en.activation(out=af[:], in_=af[:], func=AF.Exp, scale=cc[:, 4, ftg:ftg + 1]),
                         reads=[baf, bc], writes=[baf])
                    sq, bsq = sqr_.next()
                    S.op("act", lambda en: en.activation(out=sq[:], in_=af[:], func=AF.Square), reads=[baf], writes=[bsq])
                    S.op("act", lambda en: en.activation(out=sq[:], in_=sq[:], func=AF.Sqrt, scale=-1.0, bias=self.one_col),
                         reads=[bsq, self.bconst], writes=[bsq])
                    S.op("dve", lambda en: en.tensor_tensor(out=bf[:], in0=bf[:], in1=sq[:], op=ALU.mult), reads=[bbf, bsq], writes=[bbf])
                    hf, bhf = hfr.next()
                    S.op("dve", lambda en: en.tensor_tensor_scan(out=hf[:], data0=af[:], data1=bf[:], initial=0.0,
                                                                  op0=ALU.mult, op1=ALU.add),
                         reads=[baf, bbf], writes=[bhf])
                    go, bgo = gor.next()
                    S.op("pool", lambda en: en.tensor_tensor(out=go[:], in0=hf[:], in1=yg[:], op=ALU.mult),
                         reads=[bhf, byg], writes=[bgo])
                    S.dma("sp", self.gT[ftg, :, :], go[:], reads=[bgo])
            self.barrier()

    def _ring_guard(self, ring, tile):
        d = getattr(self, "_rg", None)
        if d is None:
            d = self._rg = {}
        b = d.get(id(tile))
        return [b] if b is not None else []

    def _ring_set(self, ring, tile, buf):
        if getattr(self, "_rg", None) is None:
            self._rg = {}
        self._rg[id(tile)] = buf


    def mixer_b(self, i):
        with ExitStack() as st:
            try:
                self._mixer_b_body(i, st)
            except _Cut:
                pass
            self.barrier()

    def _mixer_b_body(self, i, st):
        nc, S = self.nc, self.S
        wqkv, wqks = self.W["b_w_qkv"], self.W["b_w_qks"]
        banks = self.psA.items
        if True:
            cosT = self.sb(st, "b_cos", [128, T], BF16)
            sinT = self.sb(st, "b_sin", [128, T], BF16)
            btab = Buf()
            with ExitStack() as s2:
                posi = self.sb(s2, "b_posi", [128, T], I32)
                ang = self.sb(s2, "b_ang", [128, T], F32)
                kk = self.sb(s2, "b_kk", [128, T], F32)
                cst = self.sb(s2, "b_cst", [128, 2], F32)
                bt = Buf()
                S.dma("sp", posi[:], self.pos[0:1, :].to_broadcast([128, T]), writes=[bt])
                S.op("pool", lambda e: e.memset(cst[:, 0:1], float(np.pi / 2)), writes=[bt])
                S.op("dve", lambda e: e.tensor_copy(out=ang[:], in_=posi[:]), reads=[bt], writes=[bt])
                S.op("dve", lambda e: e.tensor_scalar(out=ang[:], in0=ang[:], scalar1=self.col("invf"), scalar2=None, op0=ALU.mult),
                     reads=[bt, self.bconst], writes=[bt])
                MAGIC = 12582912.0
                S.op("dve", lambda e: e.tensor_scalar(out=kk[:], in0=ang[:], scalar1=float(1.0 / (2 * np.pi)), scalar2=MAGIC,
                                                      op0=ALU.mult, op1=ALU.add), reads=[bt], writes=[bt])
                S.op("dve", lambda e: e.tensor_scalar(out=kk[:], in0=kk[:], scalar1=-MAGIC, scalar2=None, op0=ALU.add),
                     reads=[bt], writes=[bt])
                C1 = 6.28125
                C2 = float(2 * np.pi - C1)
                S.op("dve", lambda e: e.scalar_tensor_tensor(out=ang[:], in0=kk[:], scalar=-C1, in1=ang[:], op0=ALU.mult, op1=ALU.add),
                     reads=[bt], writes=[bt])
                S.op("dve", lambda e: e.scalar_tensor_tensor(out=ang[:], in0=kk[:], scalar=-C2, in1=ang[:], op0=ALU.mult, op1=ALU.add),
                     reads=[bt], writes=[bt])
                S.op("dve", lambda e: e.tensor_scalar(out=ang[:], in0=ang[:], scalar1=float(np.pi), scalar2=float(-np.pi),
                                                      op0=ALU.min, op1=ALU.max), reads=[bt], writes=[bt])
                S.op("act", lambda e: e.activation(out=sinT[:], in_=ang[:], func=AF.Sin, scale=self.col("rsgn")),
                     reads=[bt, self.bconst], writes=[btab])
                S.op("dve", lambda e: e.scalar_tensor_tensor(out=kk[:], in0=ang[:], scalar=-1.0, in1=ang[:], op0=ALU.mult, op1=ALU.max), reads=[bt], writes=[bt])
                S.op("act", lambda e: e.activation(out=cosT[:], in_=kk[:], func=AF.Sin, scale=-1.0, bias=cst[:, 0:1]),
                     reads=[bt], writes=[btab])
                self.barrier()
            self.cut("cutA")
            self.dump("cos", cosT[:], [btab], [128, T], BF16)
            self.dump("sin", sinT[:], [btab], [128, T], BF16)
            hn = self.sb(st, "b_hn", [128, KT, T], BF16)
            bhn = [Buf() for _ in range(KT)]
            for k in range(KT):
                S.dma("sp", hn[:, k, :], self.hnT[k, :, :], writes=[bhn[k]])
            band = self.sb(st, "b_band", [128, 2, 256], BF16)
            bband = Buf()
            S.op("pool", lambda e: e.memset(band[:], 1.0), writes=[bband])
            S.op("pool", lambda e: e.affine_select(out=band[:], in_=band[:], pattern=[[0, 2], [1, 256]], compare_op=ALU.is_ge,
                                                   fill=0.0, base=0, channel_multiplier=-1), reads=[bband], writes=[bband])
            S.op("pool", lambda e: e.affine_select(out=band[:], in_=band[:], pattern=[[0, 2], [-1, 256]], compare_op=ALU.is_ge,
                                                   fill=0.0, base=128, channel_multiplier=1), reads=[bband], writes=[bband])
            wr = Ring(nc, st, "b_w", [128, KT, 128], BF16, 4)
            wvr = Ring(nc, st, "b_wv", [128, KT, 128], BF16, 2)
            qr = Ring(nc, st, "b_q", [128, T], BF16, 2)
            kr = Ring(nc, st, "b_k", [128, T], BF16, 2)
            vr = Ring(nc, st, "b_v", [128, 32, 128], BF16, 2)
            accden = self.sb(st, "b_accden", [128, 2, T], F32)
            bacc = Buf()
            tr = Ring(nc, st, "b_t", [128, TT], F32, 4)
            er = Ring(nc, st, "b_e", [128, 2, 256], BF16, 3)
            outr = Ring(nc, st, "b_o", [128, TT], BF16, 2)
            psP = SubRing(banks[0:2])
            psS = SubRing([(self.psbig[q][:, :].rearrange("p (a b) -> p a b", a=2), Buf(x=True)) for q in (1, 2)])
            psOD = SubRing([(banks[q][0].rearrange("p (a b) -> p a b", a=2), Buf(x=True)) for q in (6, 7)])

            def proj_rot(col0, dst, bdst, d):
                w, bw = self.load_w(wr, wqkv, KT, col0, 128)
                ws, bws = self.load_w(wr, wqks, KT, col0, 128)
                for jt in range(NTT):
                    sl = slice(jt * TT, (jt + 1) * TT)
                    ps, bps = psP.next()
                    ps2, bps2 = psP.next()
                    for (pp, bpp, ww, bww) in ((ps, bps, w, bw), (ps2, bps2, ws, bws)):
                        for k in range(KT):
                            S.op("pe", lambda en: en.matmul(pp[:, :], lhsT=ww[:, k, :], rhs=hn[:, k, sl],
                                                            start=(k == 0), stop=(k == KT - 1)),
                                 reads=[bww] + bhn, writes=[bpp], inc=(k == KT - 1), acc=True)
                    t1, bt1 = tr.next()
                    t2, bt2 = tr.next()
                    S.op("dve", lambda en: en.tensor_tensor(out=t1[:], in0=ps[:, :], in1=cosT[:, sl], op=ALU.mult),
                         reads=[bps, btab], writes=[bt1])
                    S.op("dve", lambda en: en.tensor_tensor(out=t2[:], in0=ps2[:, :], in1=sinT[:, sl], op=ALU.mult),
                         reads=[bps2, btab], writes=[bt2])
                    n = TT // d
                    dv = dst[:].rearrange("p (r l) -> p r l", r=d)[:, :, jt * n:(jt + 1) * n]
                    v1 = t1[:].rearrange("p (j r) -> p r j", r=d)
                    v2 = t2[:].rearrange("p (j r) -> p r j", r=d)
                    S.op("pool", lambda en: en.tensor_tensor(out=dv, in0=v1, in1=v2, op=ALU.add),
                         reads=[bt1, bt2], writes=[bdst])

            for hp in range(KT):
                wv, bwv = self.load_w(wvr, wqkv, KT, 6 * D + hp * 128, 128)
                S.op("pool", lambda e: e.memset(accden[:], 0.0), writes=[bacc])
                for g, d in enumerate((1, 4, 16)):
                    L = T // d
                    nb = L // 128
                    qT, bq = qr.next()
                    kT, bk = kr.next()
                    proj_rot(g * D + hp * 128, qT, bq, d)
                    proj_rot(3 * D + g * D + hp * 128, kT, bk, d)
                    self.cut("cutB")
                    if hp == 0:
                        self.dump("q%d" % g, qT[:], [bq], [128, T], BF16)
                        self.dump("k%d" % g, kT[:], [bk], [128, T], BF16)
                    vt, bvt = vr.next()
                    for n0 in range(0, 32, 4):
                        ps, bps = psP.next()
                        for q4 in range(4):
                            n = n0 + q4
                            r, kb = n // nb, n % nb
                            start = kb * 128 * d + r
                            for k in range(KT):
                                S.op("pe", lambda en: en.matmul(ps[:, q4 * 128:(q4 + 1) * 128],
                                                                lhsT=hn[:, k, start:start + 127 * d + 1:d], rhs=wv[:, k, :],
                                                                start=(k == 0), stop=(k == KT - 1)),
                                     reads=[bwv] + bhn, writes=[bps], inc=(k == KT - 1 and q4 == 3), acc=True)
                        S.op("act", lambda en: en.activation(out=vt[:, n0:n0 + 4, :], in_=ps[:, :].rearrange("p (a b) -> p a b", a=4),
                                                             func=AF.Copy), reads=[bps], writes=[bvt])
                    self.cut("cutC")
                    steps = [(r, kb) for r in range(d) for kb in range(nb)]

                    def stage1(r, kb):
                        kcol = r * L + kb * 128
                        nq = 256 if kb + 1 < nb else 128
                        pS, bpS = psS.next()
                        for hh in range(2):
                            S.op("pe", lambda en: en.matmul(pS[:, hh, 0:nq],
                                                            lhsT=kT[64 * hh:64 * hh + 64, kcol:kcol + 128],
                                                            rhs=qT[64 * hh:64 * hh + 64, kcol:kcol + nq],
                                                            start=True, stop=True),
                                 reads=[bk, bq], writes=[bpS], inc=(hh == 1), acc=True)
                        E, bE = er.next()
                        S.op("act", lambda en: en.activation(out=E[:, :, 0:nq], in_=pS[:, :, 0:nq],
                                                             func=AF.Exp, scale=0.125), reads=[bpS], writes=[bE])
                        meng = "dve" if (kb % 2 == 0) else "pool"
                        S.op(meng, lambda en: en.tensor_tensor(out=E[:, :, 0:nq], in0=E[:, :, 0:nq], in1=band[:, :, 0:nq], op=ALU.mult),
                             reads=[bE, bband], writes=[bE])
                        return E, bE

                    stt = {"pod": None, "bpod": None, "npod": None, "nbpod": None, "fresh": None, "nfresh": None}

                    def stage2(r, kb, E, bE):
                        n = r * nb + kb
                        if kb == 0:
                            stt["pod"], stt["bpod"] = psOD.next()
                            stt["fresh"] = [True, True]
                        halves = [(kb, 0)]
                        if kb + 1 < nb:
                            halves.append((kb + 1, 1))
                        for (qb, hf) in halves:
                            if qb % 2 == 0 and hf == 1:
                                stt["npod"], stt["nbpod"] = psOD.next()
                                stt["nfresh"] = [True, True]
                                tp, tbp, fr = stt["npod"], stt["nbpod"], stt["nfresh"]
                            else:
                                tp, tbp, fr = stt["pod"], stt["bpod"], stt["fresh"]
                            c0 = (qb % 2) * 128
                            for hh in range(2):
                                for od in range(2):
                                    lh = vt[:, n, 64 * hh:64 * hh + 64] if od == 0 else self.ones_bf[:, 0:64]
                                    st_ = fr[hh]
                                    fr[hh] = False
                                    S.op("pe", lambda en: en.matmul(tp[64 * hh:64 * hh + 64, od, c0:c0 + 128], lhsT=lh,
                                                                    rhs=E[:, hh, hf * 128:(hf + 1) * 128], start=st_, stop=True,
                                                                    skip_group_check=True),
                                         reads=[bvt, bE, self.bconst], writes=[tbp], inc=(hh == 1 and od == 1), acc=True)
                        if kb % 2 == 1 or kb == nb - 1:
                            qb0 = (kb // 2) * 2
                            ncols = (kb - qb0 + 1) * 128
                            t0 = qb0 * 128 * d + r
                            asl = accden[:, :, t0:t0 + (ncols - 1) * d + 1:d]
                            pod, bpod = stt["pod"], stt["bpod"]
                            S.op("dve", lambda en: en.tensor_tensor(out=asl, in0=pod[:, :, 0:ncols], in1=asl, op=ALU.add),
                                 reads=[bpod, bacc], writes=[bacc])
                            if kb + 1 < nb:
                                stt["pod"], stt["bpod"], stt["fresh"] = stt["npod"], stt["nbpod"], stt["nfresh"]

                    cur = stage1(*steps[0])
                    for si, (r, kb) in enumerate(steps):
                        nxt_ = stage1(*steps[si + 1]) if si + 1 < len(steps) else None
                        stage2(r, kb, *cur)
                        cur = nxt_
                    self.cut("cutD%d" % g)
                if hp == 0:
                    self.dump("acc", accden[:, 0, :], [bacc], [128, T], F32)
                    self.dump("den", accden[:, 1, :], [bacc], [128, T], F32)
                    self.dump("vt", vt[:], [bvt], [128, 32, 128], BF16)
                for jt in range(NTT):
                    sl = slice(jt * TT, (jt + 1) * TT)
                    rc, brc = tr.next()
                    S.op("dve", lambda en: en.reciprocal(out=rc[:], in_=accden[:, 1, sl]), reads=[bacc], writes=[brc])
                    o, bo = outr.next()
                    S.op("dve", lambda en: en.tensor_tensor(out=o[:], in0=accden[:, 0, sl], in1=rc[:], op=ALU.mult),
                         reads=[bacc, brc], writes=[bo])
                    S.dma("sp", self.gT[hp, :, sl], o[:], reads=[bo])
            self.barrier()


    CH = 128
    NCH = T // 128
    LAM = float(np.exp(-0.5))

    def mixer_c(self, i):
        if not hasattr(self, "c_AR"):
            dt = self.nc.dram_tensor
            self.c_AR = dt("c_AR", [KT, 128, 2 * T], BF16, kind="Internal").ap()
            self.c_vtok = dt("c_vtok", [T // 128, 128, D], BF16, kind="Internal").ap()
            self.c_gc = dt("c_gc", [KT, 128, T // 128], F32, kind="Internal").ap()
        self.mixer_c1(i)
        self.mixer_c2(i)

    def mixer_c2(self, i):
        nc, S = self.nc, self.S
        BTd, KTd, gD, bonD = self.s16[1], self.s16[2], self.s16[3], self.s32[0]
        NCH = self.NCH
        with ExitStack() as st:
            bm = Buf()
            MSK = self.sb(st, "c2_msk", [128, 2, 4, 128], BF16)
            LM = self.sb(st, "c2_lm", [128, 2, 128], BF16)
            IDN = self.sb(st, "c2_idn", [128, 2, 128], BF16)
            bonesf = self.sb(st, "c2_bones", [128, 128], F32)
            S.op("pool", lambda e: e.memset(MSK[:], 1.0), writes=[bm])
            for par in range(2):
                S.op("pool", lambda e: e.affine_select(out=MSK[:, :, par::2, :], in_=MSK[:, :, par::2, :],
                                                       pattern=[[0, 2], [0, 2], [1, 128]], compare_op=ALU.is_ge, fill=0.0,
                                                       base=par - 1, channel_multiplier=-1), reads=[bm], writes=[bm])
            S.op("pool", lambda e: e.memset(LM[:], 1.0), writes=[bm])
            S.op("pool", lambda e: e.affine_select(out=LM[:], in_=LM[:], pattern=[[0, 2], [-1, 128]], compare_op=ALU.is_ge, fill=0.0,
                                                   base=-1, channel_multiplier=1), reads=[bm], writes=[bm])
            S.op("pool", lambda e: e.memset(IDN[:], 1.0), writes=[bm])
            S.op("pool", lambda e: e.affine_select(out=IDN[:], in_=IDN[:], pattern=[[0, 2], [-1, 128]], compare_op=ALU.is_equal, fill=0.0,
                                                   base=0, channel_multiplier=1), reads=[bm], writes=[bm])
            S.op("pool", lambda e: e.memset(bonesf[:], 0.0), writes=[bm])
            S.op("pool", lambda e: e.memset(bonesf[0:64, 0:64], 1.0 / 64), writes=[bm])
            S.op("pool", lambda e: e.memset(bonesf[64:128, 64:128], 1.0 / 64), writes=[bm])
            ident = IDN[:, 0, :]
            arr = Ring(nc, st, "c2_ar", [128, NCH, 2, 128], BF16, 2)
            btr = Ring(nc, st, "c2_bt", [128, T], BF16, 2)
            ktr = Ring(nc, st, "c2_kt", [128, T], BF16, 2)
            vtr = Ring(nc, st, "c2_vt", [128, NCH, 128], BF16, 2)
            gcr = Ring(nc, st, "c2_gc", [128, NCH], F32, 2)
            scr = Ring(nc, st, "c2_sc", [128, 2, 4, 128], BF16, 4)
            mlrs = [Ring(nc, st, "c2_ml%d" % X, [128, 2, 2, 128], BF16, 3) for X in range(2)]
            ttrs = [Ring(nc, st, "c2_tt%d" % X, [128, 2, 128], BF16, 3) for X in range(2)]
            ttfr = Ring(nc, st, "c2_ttf", [128, 2, 128], BF16, 4)
            tokr = Ring(nc, st, "c2_tok", [128, 2, 128], BF16, 4)
            wur = Ring(nc, st, "c2_wu", [128, 64], BF16, 6)
            Pst = self.sb(st, "c2_pst", [128, 64], F32)
            PG = self.sb(st, "c2_pg", [128, 64], F32)
            Pbf = self.sb(st, "c2_pbf", [128, 64], BF16)
            bP = [Buf(), Buf()]
            bPG = [Buf(), Buf()]
            ysr = Ring(nc, st, "c2_ys", [128, TT], F32, 2)
            gtr = Ring(nc, st, "c2_gt", [128, TT], F32, 8)
            bonr = Ring(nc, st, "c2_bon", [128, TT], F32, 2)
            ggr = Ring(nc, st, "c2_gg", [128, TT], BF16, 2)
            outr = Ring(nc, st, "c2_out", [128, TT], BF16, 2)
            scA = self.psbig[0][:, :].rearrange("p (a b) -> p a b", a=2)
            bscA = Buf(x=True)
            scB = self.psbig[1][:, :].rearrange("p (a b) -> p a b", a=2)
            bscB = Buf(x=True)
            trp = self.psbig[1][:, 256:512].bitcast(BF16).rearrange("p (a b) -> p a b", a=4)
            btrp = bscB
            b2b1 = Buf(x=True)
            mlps = [(self.psbig[2][:, 0:512].rearrange("p (a b c) -> p a b c", a=2, b=2), Buf(x=True)),
                    (self.psbig[1][:, 512:1024].rearrange("p (a b c) -> p a b c", a=2, b=2), Buf(x=True))]
            ttps = [(self.psbig[2][:, 512:768].rearrange("p (a b) -> p a b", a=2), b2b1),
                    (self.psbig[1][:, 0:512].rearrange("p (a b) -> p a b", a=2)[:, :, 128:256], bscB)]
            nps = [self.psbig[1][:, 0:128], self.psbig[2][:, 768:896]]
            bnps = [bscB, b2b1]
            sq = self.psbig[3][:, :].rearrange("p (a b) -> p a b", a=2)
            bsq = [Buf(x=True), Buf(x=True)]

            def ldt(e_):
                ar, bar = arr.next()
                S.dma("sp", ar[:].rearrange("p a b c -> p (a b c)"), self.c_AR[e_, :, :], writes=[bar])
                bt_, bbt = btr.next()
                S.dma("sp", bt_[:], BTd[e_, :, :], writes=[bbt])
                kt_, bkt = ktr.next()
                S.dma("sp", kt_[:], KTd[e_, :, :], writes=[bkt])
                vt, bvt = vtr.next()
                S.dma("sp", vt[:], self.c_vtok.rearrange("c p f -> p c f")[:, :, e_ * 128:(e_ + 1) * 128], writes=[bvt])
                gc, bgc = gcr.next()
                S.dma("sp", gc[:], self.c_gc[e_, :, :], writes=[bgc])
                return ar, bar, bt_, bbt, kt_, bkt, vt, bvt, gc, bgc
            nxt = ldt(0)
            for e_ in range(KT):
                ar, bar, bt_, bbt, kt_, bkt, vt, bvt, gc, bgc = nxt
                if e_ + 1 < KT:
                    nxt = ldt(e_ + 1)
                for hh in range(2):
                    P = slice(64 * hh, 64 * hh + 64)
                    S.op("pool", lambda e: e.memset(Pst[P, :], 0.0), writes=[bP[hh]])
                    S.op("pool", lambda e: e.memset(Pbf[P, :], 0.0), writes=[bP[hh]])
                ys = bys = None
                for c0 in range(0, NCH, 2):
                    st2 = []
                    for X in range(2):
                        c = c0 + X
                        cs_ = slice(c * 128, (c + 1) * 128)
                        for hh in range(2):
                            P = slice(64 * hh, 64 * hh + 64)
                            S.op("pe", lambda e: e.matmul(scA[:, hh, 0:256], lhsT=bt_[P, cs_], rhs=ar[P, c, :, :].rearrange("p a b -> p (a b)"),
                                                          start=True, stop=True), reads=[bbt, bar], writes=[bscA], inc=False, acc=True)
                        for hh in range(2):
                            P = slice(64 * hh, 64 * hh + 64)
                            S.op("pe", lambda e: e.matmul(scA[:, hh, 256:512], lhsT=kt_[P, cs_], rhs=ar[P, c, :, :].rearrange("p a b -> p (a b)"),
                                                          start=True, stop=True), reads=[bkt, bar], writes=[bscA], inc=(hh == 1), acc=True)
                        for hh in range(2):
                            P = slice(64 * hh, 64 * hh + 64)
                            S.op("pe", lambda e: e.matmul(nps[hh], lhsT=ar[P, c, 0, :], rhs=bt_[P, cs_],
                                                          start=True, stop=True), reads=[bbt, bar], writes=[bnps[hh]], inc=True, acc=True)
                        S.op("pe", lambda e: e.transpose(trp[:, 0, :], bt_[:, cs_], ident), reads=[bbt, bm], writes=[btrp], inc=False, acc=True)
                        S.op("pe", lambda e: e.transpose(trp[:, 1, :], kt_[:, cs_], ident), reads=[bkt, bm], writes=[btrp], acc=True)
                        SC, bSC = scr.next()
                        S.op("dve", lambda e: e.tensor_tensor(out=SC[:].rearrange("p a b c -> p a (b c)"), in0=scA[:, :, :],
                                                              in1=MSK[:].rearrange("p a b c -> p a (b c)"), op=ALU.mult),
                             reads=[bscA, bm], writes=[bSC])
                        ML, bML = mlrs[X].next()
                        for hh in range(2):
                            S.op("dve", lambda e: e.tensor_tensor(out=ML[:, hh, 1, :], in0=nps[hh], in1=LM[:, hh, :], op=ALU.mult),
                                 reads=[bnps[hh], bm], writes=[bML])
                        S.op("pool", lambda e: e.tensor_copy(out=ML[:, :, 0, :], in_=SC[:, :, 0, :]), reads=[bSC], writes=[bML])
                        tok, btok = tokr.next()
                        S.op("act", lambda e: e.activation(out=tok[:], in_=trp[:, 0:2, :], func=AF.Copy), reads=[btrp], writes=[btok])
                        TTc, bTT = ttrs[X].next()
                        S.op("pool", lambda e: e.tensor_tensor(out=TTc[:], in0=SC[:, :, 0, :], in1=IDN[:], op=ALU.add), reads=[bSC, bm], writes=[bTT])
                        st2.append(dict(SC=SC, bSC=bSC, tok=tok, btok=btok, ML=ML, bML=bML, TT=TTc, bTT=bTT))
                    for lev in range(1, 7):
                        nxt2 = []
                        for X in range(2):
                            d_ = st2[X]
                            ML, bML = d_["ML"], d_["bML"]
                            mlpX, bmlpX = mlps[X]
                            for hh in range(2):
                                if lev < 6:
                                    S.op("pe", lambda e: e.matmul(mlpX[:, hh, 0, :], lhsT=ML[:, hh, 1, :], rhs=ML[:, hh, 0, :], start=True, stop=True),
                                         reads=[bML], writes=[bmlpX], inc=False, acc=True)
                                S.op("pe", lambda e: e.matmul(mlpX[:, hh, 1, :], lhsT=ML[:, hh, 0, :], rhs=ML[:, hh, 1, :], start=True, stop=True),
                                     reads=[bML], writes=[bmlpX], inc=(hh == 1), acc=True)
                        for X in range(2):
                            mlpX, bmlpX = mlps[X]
                            MLn, bMLn = mlrs[X].next()
                            if lev < 6:
                                S.op("act", lambda e: e.activation(out=MLn[:], in_=mlpX[:, :, :, :], func=AF.Copy), reads=[bmlpX], writes=[bMLn])
                            else:
                                S.op("act", lambda e: e.activation(out=MLn[:, :, 1, :], in_=mlpX[:, :, 1, :], func=AF.Copy), reads=[bmlpX], writes=[bMLn])
                            nxt2.append((MLn, bMLn))
                        for X in range(2):
                            d_ = st2[X]
                            MLn, bMLn = nxt2[X]
                            ttpX, bttpX = ttps[X]
                            for hh in range(2):
                                S.op("pe", lambda e: e.matmul(ttpX[:, hh, :], lhsT=MLn[:, hh, 1, :], rhs=d_["TT"][:, hh, :], start=True, stop=True),
                                     reads=[bMLn, d_["bTT"]], writes=[bttpX], inc=(hh == 1), acc=True)
                        for X in range(2):
                            d_ = st2[X]
                            ttpX, bttpX = ttps[X]
                            TTn, bTTn = ttrs[X].next() if lev < 6 else ttfr.next()
                            S.op("dve", lambda e: e.tensor_tensor(out=TTn[:], in0=ttpX[:, :, :], in1=d_["TT"][:], op=ALU.add),
                                 reads=[bttpX, d_["bTT"]], writes=[bTTn])
                            d_["ML"], d_["bML"] = nxt2[X]
                            d_["TT"], d_["bTT"] = TTn, bTTn
                    for X in range(2):
                      c = c0 + X
                      cs_ = slice(c * 128, (c + 1) * 128)
                      SC, bSC, tok, btok, TTc, bTT = (st2[X][k_] for k_ in ("SC", "bSC", "tok", "btok", "TT", "bTT"))
                      if c % 4 == 0:
                          ys, bys = ysr.next()
                      for hh in range(2):
                          P = slice(64 * hh, 64 * hh + 64)
                          vh = vt[:, c, 64 * hh:64 * hh + 64]
                          S.op("pe", lambda e: e.matmul(sq[:, hh, 0:64], lhsT=SC[:, hh, 2, :], rhs=vh, start=True, stop=False),
                               reads=[bSC, bvt], writes=[bsq[hh]], inc=False, acc=True)
                          S.op("pe", lambda e: e.matmul(sq[:, hh, 0:64], lhsT=ar[P, c, 0, :], rhs=Pbf[P, :], start=False, stop=True),
                               reads=[bar, bP[hh]], writes=[bsq[hh]], acc=True)
                          Wsb, bW = wur.next()
                          S.op("act", lambda e: e.activation(out=Wsb[:], in_=sq[:, hh, 0:64], func=AF.Copy), reads=[bsq[hh]], writes=[bW])
                          S.op("pe", lambda e: e.matmul(sq[:, hh, 64:128], lhsT=TTc[:, hh, :], rhs=Wsb[:], start=True, stop=True),
                               reads=[bTT, bW], writes=[bsq[hh]], acc=True)
                          Usb, bU = wur.next()
                          S.op("act", lambda e: e.activation(out=Usb[:], in_=sq[:, hh, 64:128], func=AF.Copy), reads=[bsq[hh]], writes=[bU])
                          S.op("pe", lambda e: e.matmul(sq[P, hh, 128:256], lhsT=vh, rhs=SC[:, hh, 3, :], start=True, stop=False),
                               reads=[bvt, bSC], writes=[bsq[hh]], inc=False, acc=True)
                          S.op("pe", lambda e: e.matmul(sq[P, hh, 128:256], lhsT=Usb[:], rhs=SC[:, hh, 1, :], start=False, stop=False),
                               reads=[bU, bSC], writes=[bsq[hh]], inc=False, acc=True)
                          S.op("pe", lambda e: e.matmul(sq[P, hh, 128:256], lhsT=Pbf[P, :], rhs=ar[P, c, 1, :], start=False, stop=True),
                               reads=[bP[hh], bar], writes=[bsq[hh]], acc=True)
                          S.op("act", lambda e: e.activation(out=ys[P, (c % 4) * 128:(c % 4 + 1) * 128], in_=sq[P, hh, 128:256], func=AF.Copy),
                               reads=[bsq[hh]], writes=[bys])
                          S.op("dve", lambda e: e.tensor_scalar(out=PG[P, :], in0=Pst[P, :], scalar1=gc[P, c:c + 1], scalar2=None, op0=ALU.mult),
                               reads=[bP[hh], bgc], writes=[bPG[hh]])
                          S.op("pe", lambda e: e.matmul(sq[P, hh, 256:320], lhsT=tok[:, 0, 64 * hh:64 * hh + 64], rhs=Usb[:], start=True, stop=False),
                               reads=[btok, bU], writes=[bsq[hh]], inc=False, acc=True)
                          S.op("pe", lambda e: e.matmul(sq[P, hh, 256:320], lhsT=tok[:, 1, 64 * hh:64 * hh + 64], rhs=vh, start=False, stop=True),
                               reads=[btok, bvt], writes=[bsq[hh]], acc=True)
                          S.op("dve", lambda e: e.scalar_tensor_tensor(out=Pst[P, :], in0=sq[P, hh, 256:320], scalar=gc[P, c:c + 1], in1=PG[P, :],
                                                                       op0=ALU.mult, op1=ALU.add),
                               reads=[bsq[hh], bgc, bPG[hh]], writes=[bP[hh]])
                          S.op("act", lambda e: e.activation(out=Pbf[P, :], in_=Pst[P, :], func=AF.Copy), reads=[bP[hh]], writes=[bP[hh]])
                      if c % 4 == 3:
                          jt = c // 4
                          sl = slice(jt * TT, (jt + 1) * TT)
                          bon, bbon = bonr.next()
                          S.dma("sp", bon[:], bonD[e_, :, sl], writes=[bbon])
                          gg, bgg = ggr.next()
                          S.dma("sp", gg[:], gD[e_, :, sl], writes=[bgg])
                          mean_ps, ex2_ps = scA[:, 0, :], scA[:, 1, :]
                          ysq, bysq = gtr.next()
                          S.op("act", lambda e: e.activation(out=ysq[:], in_=ys[:], func=AF.Square), reads=[bys], writes=[bysq])
                          S.op("pe", lambda e: e.matmul(mean_ps, lhsT=bonesf[:, :], rhs=ys[:], start=True, stop=True),
                               reads=[bm, bys], writes=[bscA], inc=False, acc=True)
                          S.op("pe", lambda e: e.matmul(ex2_ps, lhsT=bonesf[:, :], rhs=ysq[:], start=True, stop=True),
                               reads=[bm, bysq], writes=[bscA], acc=True)
                          msq, bmsq = gtr.next()
                          S.op("act", lambda e: e.activation(out=msq[:], in_=mean_ps, func=AF.Square), reads=[bscA], writes=[bmsq])
                          S.op("dve", lambda e: e.tensor_tensor(out=msq[:], in0=ex2_ps, in1=msq[:], op=ALU.subtract), reads=[bscA, bmsq], writes=[bmsq])
                          S.op("dve", lambda e: e.tensor_scalar(out=msq[:], in0=msq[:], scalar1=0.0, scalar2=None, op0=ALU.max), reads=[bmsq], writes=[bmsq])
                          S.op("act", lambda e: e.activation(out=msq[:], in_=msq[:], func=AF.Sqrt, bias=self.gneps_col), reads=[bmsq, self.bconst], writes=[bmsq])
                          msq_in, bmsq_in = msq, bmsq
                          msq, bmsq = gtr.next()
                          S.op("dve", lambda e: e.reciprocal(out=msq[:], in_=msq_in[:]), reads=[bmsq_in], writes=[bmsq])
                          yc, byc = gtr.next()
                          S.op("dve", lambda e: e.tensor_tensor(out=yc[:], in0=mean_ps, in1=ys[:], op=ALU.subtract), reads=[bscA, bys], writes=[byc])
                          S.op("dve", lambda e: e.tensor_tensor(out=yc[:], in0=yc[:], in1=msq[:], op=ALU.mult), reads=[byc, bmsq], writes=[byc])
                          S.op("dve", lambda e: e.tensor_scalar(out=yc[:], in0=yc[:], scalar1=-1.0, scalar2=self.col("c_ln_w", e_), op0=ALU.mult, op1=ALU.mult),
                               reads=[byc, self.bconst], writes=[byc])
                          S.op("dve", lambda e: e.scalar_tensor_tensor(out=yc[:], in0=yc[:], scalar=self.col("c_ln_b", e_), in1=bon[:], op0=ALU.add, op1=ALU.add),
                               reads=[byc, bbon, self.bconst], writes=[byc])
                          o, bo = outr.next()
                          S.op("dve", lambda e: e.tensor_tensor(out=o[:], in0=yc[:], in1=gg[:], op=ALU.mult), reads=[byc, bgg], writes=[bo])
                          S.dma("sp", self.gT[e_, :, sl], o[:], reads=[bo])
            self.barrier()

    def mixer_c1(self, i):
        nc, S = self.nc, self.S
        W = self.W
        LAM = self.LAM
        BTd, KTd, gD, bonD = self.s16[1], self.s16[2], self.s16[3], self.s32[0]
        with ExitStack() as st:
            wbuf = Buf()
            wrkv = [self.sb(st, "c_wrkv%d" % c, [128, KT, D], BF16) for c in range(3)]
            for c in range(3):
                for k in range(KT):
                    S.dma("pool", wrkv[c][:, k, :], W["c_w_rkv"][c, k * 128:(k + 1) * 128, :], writes=[wbuf])
            w1 = self.sb(st, "c_w1", [128, KT, 64], BF16)
            a1 = self.sb(st, "c_a1", [128, KT, 64], BF16)
            g1 = self.sb(st, "c_g1", [128, KT, 128], BF16)
            w2 = self.sb(st, "c_w2", [128, D], BF16)
            a2 = self.sb(st, "c_a2", [128, D], BF16)
            g2 = self.sb(st, "c_g2", [128, D], BF16)
            S.dma("pool", w1[:], W["c_w1"].rearrange("(k p) e -> p k e", p=128), writes=[wbuf])
            S.dma("pool", a1[:], W["c_a1"].rearrange("(k p) e -> p k e", p=128), writes=[wbuf])
            S.dma("pool", g1[:], W["c_g1"].rearrange("(k p) e -> p k e", p=128), writes=[wbuf])
            S.dma("pool", w2[0:64, :], W["c_w2"][:, :], writes=[wbuf])
            S.dma("pool", a2[0:64, :], W["c_a2"][:, :], writes=[wbuf])
            S.dma("pool", g2[:, :], W["c_g2"][:, :], writes=[wbuf])
            cm01 = self.sb(st, "c_cm01", [128, TT], F32)
            bones = self.sb(st, "c_bones", [128, 128], BF16)
            bm = Buf()
            S.op("pool", lambda e: e.memset(cm01[:], 1.0), writes=[bm])
            for q in range(4):
                S.op("pool", lambda e: e.memset(cm01[:, q * 128:q * 128 + 1], 0.0), writes=[bm])
            S.op("pool", lambda e: e.memset(bones[:], 0.0), writes=[bm])
            S.op("pool", lambda e: e.memset(bones[0:64, 0:64], 1.0), writes=[bm])
            S.op("pool", lambda e: e.memset(bones[64:128, 64:128], 1.0), writes=[bm])
            hr = Ring(nc, st, "c_hn", [128, KT, TT + 1], BF16, 2)
            dd = self.sb(st, "c_d", [128, KT, TT], F32)
            bdd = Buf()
            xm = [self.sb(st, "c_xm%d" % c, [128, KT, TT], BF16) for c in range(6)]
            bxm = [Buf() for _ in range(6)]
            lor = [self.sb(st, "c_lor%d" % c, [128, TT], BF16) for c in range(3)]
            blor = [Buf() for _ in range(3)]
            trA = Ring(nc, st, "c_tA", [128, TT], F32, 12)
            trB = Ring(nc, st, "c_tB", [128, TT], F32, 8)
            brA = Ring(nc, st, "c_bA", [128, TT], BF16, 4)
            brB = Ring(nc, st, "c_bB", [128, TT], BF16, 6)
            psF = SubRing(self.psA.items[0:5])
            psB = SubRing(self.psA.items[5:8])
            arr = Ring(nc, st, "c_ar", [128, 4, 2, 128], BF16, 2)
            vtr = Ring(nc, st, "c_vt", [128, D], BF16, 2)
            gct = self.sb(st, "c_gct", [128, KT, T // 128], F32)
            bgct = Buf()
            P_ = self.psA

            def ldh(jt):
                h, bh = hr.next()
                if jt == 0:
                    S.op("pool", lambda e: e.memset(h[:, :, 0:1], 0.0), writes=[bh])
                    S.dma("sp", h[:, :, 1:TT + 1], self.tview(self.hnT, 0), writes=[bh])
                else:
                    S.dma("sp", h[:, :, :], self.hnT.rearrange("k p t -> p k t")[:, :, jt * TT - 1:(jt + 1) * TT], writes=[bh])
                return h, bh
            nxt = ldh(0)
            for jt in range(NTT):
                sl = slice(jt * TT, (jt + 1) * TT)
                h, bh = nxt
                if jt + 1 < NTT:
                    nxt = ldh(jt + 1)
                for k in range(KT):
                    S.op("dve", lambda e: e.tensor_tensor(out=dd[:, k, :], in0=h[:, k, 0:TT], in1=h[:, k, 1:TT + 1], op=ALU.subtract),
                         reads=[bh], writes=[bdd])
                for c in (3, 4, 5, 2, 0, 1):
                    for k in range(KT):
                        S.op("dve", lambda e: e.scalar_tensor_tensor(out=xm[c][:, k, :], in0=dd[:, k, :], scalar=self.col("c_mu%d" % c, k),
                                                                     in1=h[:, k, 1:TT + 1], op0=ALU.mult, op1=ALU.add),
                             reads=[bdd, bh, self.bconst], writes=[bxm[c]])
                for li, (wt, c, fn, m) in enumerate(((w1, 3, AF.Tanh, 64), (a1, 4, AF.Copy, 64), (g1, 5, AF.Sigmoid, 128))):
                    ps, bps = P_.next()
                    for k in range(KT):
                        S.op("pe", lambda e: e.matmul(ps[0:m, :], lhsT=wt[:, k, :], rhs=xm[c][:, k, :], start=(k == 0), stop=(k == KT - 1)),
                             reads=[wbuf, bxm[c]], writes=[bps], inc=(k == KT - 1), acc=True)
                    S.op("act", lambda e: e.activation(out=lor[li][0:m, :], in_=ps[0:m, :], func=fn), reads=[bps], writes=[blor[li]])
                for blk in range(4):
                    vt, bvt = vtr.next()
                    for half in range(2):
                        ps, bps = P_.next()
                        for k in range(KT):
                            S.op("pe", lambda e: e.matmul(ps[:, :], lhsT=xm[2][:, k, blk * 128:(blk + 1) * 128],
                                                          rhs=wrkv[2][:, k, half * 512:(half + 1) * 512], start=(k == 0), stop=(k == KT - 1)),
                                 reads=[wbuf, bxm[2]], writes=[bps], inc=(k == KT - 1), acc=True)
                        S.op("act", lambda e: e.activation(out=vt[:, half * 512:(half + 1) * 512], in_=ps[:, :], func=AF.Copy),
                             reads=[bps], writes=[bvt])
                    S.dma("sp", self.c_vtok[jt * 4 + blk, :, :], vt[:], reads=[bvt])
                def stageA(e_):
                    es = slice(e_ * 128, (e_ + 1) * 128)
                    pss = []
                    for c in range(3):
                        ps, bps = psF.next()
                        for k in range(KT):
                            S.op("pe", lambda e: e.matmul(ps[:, :], lhsT=wrkv[c][:, k, es], rhs=xm[c][:, k, :], start=(k == 0), stop=(k == KT - 1)),
                                 reads=[wbuf, bxm[c]], writes=[bps], inc=(k == KT - 1), acc=True)
                        pss.append((ps, bps))
                    (r_ps, br_), (k_ps, bk_), (v_ps, bv_) = pss
                    rf, brf = trA.next()
                    S.op("act", lambda e: e.activation(out=rf[:], in_=r_ps[:, :], func=AF.Copy), reads=[br_], writes=[brf])
                    kf, bkf = trA.next()
                    S.op("act", lambda e: e.activation(out=kf[:], in_=k_ps[:, :], func=AF.Copy), reads=[bk_], writes=[bkf])
                    vf, bvf = trA.next()
                    S.op("act", lambda e: e.activation(out=vf[:], in_=v_ps[:, :], func=AF.Copy), reads=[bv_], writes=[bvf])
                    wl_ps, bwl = psF.next()
                    S.op("pe", lambda e: e.matmul(wl_ps[:, :], lhsT=w2[0:64, es], rhs=lor[0][0:64, :], start=True, stop=True),
                         reads=[wbuf, blor[0]], writes=[bwl], acc=True)
                    al_ps, bal = psF.next()
                    S.op("pe", lambda e: e.matmul(al_ps[:, :], lhsT=a2[0:64, es], rhs=lor[1][0:64, :], start=True, stop=True),
                         reads=[wbuf, blor[1]], writes=[bal], acc=True)
                    g_ps, bg_ = psF.next()
                    S.op("pe", lambda e: e.matmul(g_ps[:, :], lhsT=g2[:, es], rhs=lor[2][:, :], start=True, stop=True),
                         reads=[wbuf, blor[2]], writes=[bg_], acc=True)
                    gb, bgb = brA.next()
                    S.op("act", lambda e: e.activation(out=gb[:], in_=g_ps[:, :], func=AF.Copy), reads=[bg_], writes=[bgb])
                    S.dma("sp", gD[e_, :, sl], gb[:], reads=[bgb])
                    sg, bsg = trA.next()
                    S.op("act", lambda e: e.activation(out=sg[:], in_=wl_ps[:, :], func=AF.Sigmoid, bias=self.col("c_w0", e_)),
                         reads=[bwl, self.bconst], writes=[bsg])
                    al, bal2 = trA.next()
                    S.op("act", lambda e: e.activation(out=al[:], in_=al_ps[:, :], func=AF.Sigmoid, bias=self.col("c_a0", e_)),
                         reads=[bal, self.bconst], writes=[bal2])
                    kk, bkk = trA.next()
                    S.op("dve", lambda e: e.tensor_scalar(out=kk[:], in0=kf[:], scalar1=self.col("c_k_k", e_), scalar2=None, op0=ALU.mult),
                         reads=[bkf, self.bconst], writes=[bkk])
                    k2, bk2 = brA.next()
                    S.op("act", lambda e: e.activation(out=k2[:], in_=kk[:], func=AF.Square), reads=[bkk], writes=[bk2])
                    ss_ps, bss = psB.next()
                    S.op("pe", lambda e: e.matmul(ss_ps[:, :], lhsT=bones[:, :], rhs=k2[:], start=True, stop=True),
                         reads=[bm, bk2], writes=[bss], acc=True)
                    return dict(es=es, rf=rf, brf=brf, kf=kf, bkf=bkf, vf=vf, bvf=bvf, sg=sg, bsg=bsg, al=al, bal2=bal2,
                                kk=kk, bkk=bkk, ss_ps=ss_ps, bss=bss)

                def stageB(e_, d_):
                    es = d_["es"]
                    rf, brf, kf, bkf, vf, bvf = d_["rf"], d_["brf"], d_["kf"], d_["bkf"], d_["vf"], d_["bvf"]
                    sg, bsg, al, bal2, kk, bkk, ss_ps, bss = d_["sg"], d_["bsg"], d_["al"], d_["bal2"], d_["kk"], d_["bkk"], d_["ss_ps"], d_["bss"]
                    rn, brn = trB.next()
                    S.op("act", lambda e: e.activation(out=rn[:], in_=ss_ps[:, :], func=AF.Sqrt), reads=[bss], writes=[brn])
                    S.op("dve", lambda e: e.tensor_scalar(out=rn[:], in0=rn[:], scalar1=1e-12, scalar2=None, op0=ALU.max), reads=[brn], writes=[brn])
                    rn2, brn2 = trB.next()
                    S.op("dve", lambda e: e.reciprocal(out=rn2[:], in_=rn[:]), reads=[brn], writes=[brn2])
                    S.op("dve", lambda e: e.tensor_tensor(out=kk[:], in0=kk[:], in1=rn2[:], op=ALU.mult), reads=[bkk, brn2], writes=[bkk])
                    cs, bcs = trB.next()
                    S.op("dve", lambda e: e.tensor_tensor_scan(out=cs[:], data0=cm01[:], data1=sg[:], initial=0.0, op0=ALU.mult, op1=ALU.add),
                         reads=[bm, bsg], writes=[bcs])
                    csx, bcsx = trB.next()
                    S.op("dve", lambda e: e.tensor_tensor(out=csx[:], in0=cs[:], in1=sg[:], op=ALU.subtract), reads=[bcs, bsg], writes=[bcsx])
                    eG, beG = trB.next()
                    S.op("act", lambda e: e.activation(out=eG[:], in_=cs[:], func=AF.Exp, scale=-LAM), reads=[bcs], writes=[beG])
                    eGi, beGi = trB.next()
                    S.op("act", lambda e: e.activation(out=eGi[:], in_=cs[:], func=AF.Exp, scale=LAM), reads=[bcs], writes=[beGi])
                    S.op("act", lambda e: e.activation(out=csx[:], in_=csx[:], func=AF.Exp, scale=-LAM), reads=[bcsx], writes=[bcsx])
                    ar, bar = arr.next()
                    S.op("dve", lambda e: e.scalar_tensor_tensor(out=ar[:, :, 0, :], in0=kk[:].rearrange("p (c t) -> p c t", c=4), scalar=-1.0,
                                                                 in1=csx[:].rearrange("p (c t) -> p c t", c=4), op0=ALU.mult, op1=ALU.mult),
                         reads=[bkk, bcsx], writes=[bar])
                    S.op("dve", lambda e: e.tensor_tensor(out=kk[:], in0=kk[:], in1=al[:], op=ALU.mult), reads=[bkk, bal2], writes=[bkk])
                    bt_, bbt = brB.next()
                    S.op("dve", lambda e: e.tensor_tensor(out=bt_[:], in0=kk[:], in1=eGi[:], op=ALU.mult), reads=[bkk, beGi], writes=[bbt])
                    S.dma("sp", BTd[e_, :, sl], bt_[:], reads=[bbt])
                    S.op("dve", lambda e: e.tensor_scalar(out=al[:], in0=al[:], scalar1=-1.0, scalar2=self.col("c_k_a", e_), op0=ALU.add, op1=ALU.mult),
                         reads=[bal2, self.bconst], writes=[bal2])
                    S.op("dve", lambda e: e.scalar_tensor_tensor(out=kf[:], in0=al[:], scalar=1.0, in1=kf[:], op0=ALU.add, op1=ALU.mult),
                         reads=[bal2, bkf], writes=[bkf])
                    kt_, bkt = brB.next()
                    S.op("dve", lambda e: e.tensor_tensor(out=kt_[:], in0=kf[:], in1=eGi[:], op=ALU.mult), reads=[bkf, beGi], writes=[bkt])
                    S.dma("sp", KTd[e_, :, sl], kt_[:], reads=[bkt])
                    S.op("dve", lambda e: e.tensor_tensor(out=ar[:, :, 1, :], in0=rf[:].rearrange("p (c t) -> p c t", c=4),
                                                          in1=eG[:].rearrange("p (c t) -> p c t", c=4), op=ALU.mult),
                         reads=[brf, beG], writes=[bar])
                    S.dma("sp", self.c_AR[e_, :, jt * 1024:(jt + 1) * 1024], ar[:].rearrange("p a b c -> p (a b c)"), reads=[bar])
                    rk, brk = brB.next()
                    S.op("dve", lambda e: e.scalar_tensor_tensor(out=rk[:], in0=rf[:], scalar=self.col("c_r_k", e_), in1=kf[:],
                                                                 op0=ALU.mult, op1=ALU.mult), reads=[brf, bkf, self.bconst], writes=[brk])
                    rk_ps, brkp = psB.next()
                    S.op("pe", lambda e: e.matmul(rk_ps[:, :], lhsT=bones[:, :], rhs=rk[:], start=True, stop=True),
                         reads=[bm, brk], writes=[brkp], acc=True)
                    S.op("dve", lambda e: e.tensor_tensor(out=vf[:], in0=rk_ps[:, :], in1=vf[:], op=ALU.mult), reads=[brkp, bvf], writes=[bvf])
                    S.dma("sp", bonD[e_, :, sl], vf[:], reads=[bvf])
                    S.op("act", lambda e: e.activation(out=gct[:, e_, jt * 4:(jt + 1) * 4], in_=eG[:, 127:TT:128], func=AF.Copy),
                         reads=[beG], writes=[bgct])

                curA = stageA(0)
                for e_ in range(KT):
                    nxtA = stageA(e_ + 1) if e_ + 1 < KT else None
                    stageB(e_, curA)
                    curA = nxtA
            for e_ in range(KT):
                S.dma("sp", self.c_gc[e_, :, :], gct[:, e_, :], reads=[bgct])
            self.barrier()


def make_in_maps(inp):
    f32 = np.float32
    cols = pack_cols(inp)
    wqkv = np.ascontiguousarray(inp["b_w_qkv"][0], dtype=f32)
    qk = wqkv[:, :6 * D].reshape(D, 6 * 16, 2, 32)
    wqks = np.ascontiguousarray(qk[:, :, ::-1, :].reshape(D, 6 * D))
    shared = {
        "cols": cols,
        "a_w_in": inp["a_w_in"], "a_gate_w": inp["a_gate_w"], "a_w_out": inp["a_w_out"],
        "b_w_qkv": wqkv, "b_w_qks": wqks, "b_w_out": inp["b_w_out"][0],
        "c_w_rkv": inp["c_w_rkv"][0], "c_w1": inp["c_w1"][0], "c_w2": inp["c_w2"][0],
        "c_a1": inp["c_a1"][0], "c_a2": inp["c_a2"][0], "c_g1": inp["c_g1"][0], "c_g2": inp["c_g2"][0],
        "c_w_out": inp["c_w_out"][0],
        "f_w_up": inp["f_w_up"], "f_w_down": inp["f_w_down"],
        "ple_w_proj": inp["ple_w_proj"], "ple_w_gate": inp["ple_w_gate"],
    }
    shared = {k: np.ascontiguousarray(v, dtype=f32) for k, v in shared.items()}
    maps = []
    for c in range(8):
        b = c % NB
        m = dict(shared)
        m["xT"] = np.ascontiguousarray(np.asarray(inp["x"][b], dtype=f32).T).reshape(KT, 128, T)
        m["pT"] = np.ascontiguousarray(np.transpose(np.asarray(inp["p"][:, b], dtype=f32), (0, 2, 1))).reshape(DEPTH, 2, 128, T)
        m["pos"] = np.ascontiguousarray(np.asarray(inp["positions"][b], dtype=np.int32)).reshape(1, T)
        maps.append(m)
    return maps


_PROG_CACHE = {}


def run_prog(inp, layers=DEPTH, dbg=None):
    key = (str(layers), dbg)
    if key not in _PROG_CACHE:
        _PROG_CACHE[key] = Prog(layers, dbg)
    prog = _PROG_CACHE[key]
    maps = make_in_maps(inp)
    used = set(prog.W.keys()) | {"xT", "pT", "pos", "cols"}
    maps = [{k: v for k, v in m.items() if k in used} for m in maps]
    res = run_bass_kernel_spmd(prog.nc, maps, core_ids=list(range(8)))
    out = np.stack([np.asarray(res.results[b]["yT"]).reshape(D, T).T for b in range(NB)])
    return np.ascontiguousarray(out.astype(np.float32)), res


def kernel(**inputs):
    inp = {k: np.asarray(v) for k, v in inputs.items()}
    out, _ = run_prog(inp)
    return out
```
